# Optimizing a Trainium2 kernel written in Bass

```python
import math
import jax, jax.numpy as jnp
from jax import lax
import numpy as np

D_MODEL = 1024
BATCH = 8
SEQ = 2048
DEPTH = 4
DEC_BATCH = 128
DEC_SEQ = 8
PAST_LEN = 8192
PAGE_SIZE = 128

N_MIXERS = 3
N_ATTN_LAYERS = (DEPTH + 2) // 3
N_SSM_LAYERS = (DEPTH + 1) // 3
N_POOL_LAYERS = DEPTH // 3

HEAD_DIM = 64
N_HEADS = D_MODEL // HEAD_DIM
N_KV_HEADS = 4
GQA_GROUP = N_HEADS // N_KV_HEADS
WINDOW = 128
BLOCK_Q = 128
Q_DIM = N_HEADS * HEAD_DIM
KV_DIM = N_KV_HEADS * HEAD_DIM
QKV_DIM = Q_DIM + 2 * KV_DIM
REL_BUCKETS = 32
REL_MAX_DIST = 128

D_INNER = 2 * D_MODEL
SSM_HEAD_DIM = 64
SSM_HEADS = D_INNER // SSM_HEAD_DIM
SSM_GROUPS = 4
SSM_HEADS_PER_GROUP = SSM_HEADS // SSM_GROUPS
D_STATE = 128
CONV_WIDTH = 4
CONV_DIM = D_INNER + 2 * SSM_GROUPS * D_STATE
SSM_IN_DIM = D_INNER + CONV_DIM + SSM_HEADS
SSD_CHUNK = 128
RMS_EPS = 1e-5

POOL_WINDOWS = (2, 4, 8, 16)
POOL_GROUPS = len(POOL_WINDOWS)
POOL_GROUP_DIM = D_MODEL // POOL_GROUPS
POOL_STATE_LEN = max(POOL_WINDOWS) - 1

D_FF = -(-8 * D_MODEL // (3 * 256)) * 256

DEEPNORM_ALPHA = (2 * DEPTH) ** 0.25
DEEPNORM_BETA = (8 * DEPTH) ** -0.25
LN_EPS = 1e-5

kernel_name = 'hybrid_swa_ssd_pool_decoder_step'


def _layer_norm(x, g, b):
    xf = x.astype(jnp.float32)
    mu = xf.mean(-1, keepdims=True)
    var = jnp.square(xf - mu).mean(-1, keepdims=True)
    return (xf - mu) * lax.rsqrt(var + LN_EPS) * g + b


def _swiglu(x, wg, wu, wd):
    return (jax.nn.silu(x @ wg) * (x @ wu)) @ wd


def _t5_bucket(dist):
    n = jnp.maximum(dist, 0)
    max_exact = REL_BUCKETS // 2
    nf = jnp.maximum(n, 1).astype(jnp.float32)
    large = max_exact + (jnp.log(nf / max_exact) / math.log(REL_MAX_DIST / max_exact)
                         * (REL_BUCKETS - max_exact)).astype(jnp.int32)
    large = jnp.minimum(large, REL_BUCKETS - 1)
    return jnp.where(n < max_exact, n, large)


def _qkv(x, w_qkv, b_qkv):
    h = x @ w_qkv + b_qkv
    lead = x.shape[:-1]
    q = h[..., :Q_DIM].reshape(*lead, N_KV_HEADS, GQA_GROUP, HEAD_DIM)
    k = h[..., Q_DIM:Q_DIM + KV_DIM].reshape(*lead, N_KV_HEADS, HEAD_DIM)
    v = h[..., Q_DIM + KV_DIM:].reshape(*lead, N_KV_HEADS, HEAD_DIM)
    return q, k, v


def _sink_attention(q, k, v, dist, valid, rel_bias, sinks):
    s = jnp.einsum('...qkgd,...skd->...kgqs', q, k).astype(jnp.float32) * (HEAD_DIM ** -0.5)
    bias = rel_bias.astype(jnp.float32)[_t5_bucket(dist)]
    bias = jnp.moveaxis(bias, -1, 0).reshape(N_KV_HEADS, GQA_GROUP, *dist.shape)
    s = jnp.where(valid, s + bias, -jnp.inf)
    sink = sinks.astype(jnp.float32).reshape(N_KV_HEADS, GQA_GROUP, 1, 1)
    m = jnp.maximum(s.max(-1, keepdims=True), sink)
    p = jnp.exp(s - m)
    w = p / (p.sum(-1, keepdims=True) + jnp.exp(sink - m))
    return jnp.einsum('...kgqs,...skd->...qkgd', w, v.astype(jnp.float32))


def _swa_prompt(x, w_qkv, b_qkv, w_o, b_o, sinks, rel_bias):
    b, l, _ = x.shape
    nb = l // BLOCK_Q
    q, k, v = _qkv(x, w_qkv, b_qkv)
    qb = q.reshape(b, nb, BLOCK_Q, N_KV_HEADS, GQA_GROUP, HEAD_DIM)

    def band(t):
        tb = t.reshape(b, nb, BLOCK_Q, N_KV_HEADS, HEAD_DIM)
        prev = jnp.concatenate([jnp.zeros_like(tb[:, :1]), tb[:, :-1]], axis=1)
        return jnp.concatenate([prev, tb], axis=2)

    qi = jnp.arange(BLOCK_Q, dtype=jnp.int32)
    si = jnp.arange(2 * BLOCK_Q, dtype=jnp.int32)
    dist = qi[:, None] + BLOCK_Q - si[None, :]
    kpos = jnp.arange(nb, dtype=jnp.int32)[:, None] * BLOCK_Q + si[None, :] - BLOCK_Q
    valid = (dist >= 0) & (dist < WINDOW) & (kpos[:, None, :] >= 0)
    o = _sink_attention(qb, band(k), band(v), dist, valid[:, None, None], rel_bias, sinks)
    y = o.reshape(b, l, Q_DIM) @ w_o + b_o
    return y, k[:, -WINDOW:], v[:, -WINDOW:]


def _swa_sample(x, k_buf, v_buf, start, w_qkv, b_qkv, w_o, b_o, sinks, rel_bias):
    b, l, _ = x.shape
    q, k, v = _qkv(x, w_qkv, b_qkv)
    k_all = jnp.concatenate([k_buf.astype(k.dtype), k], axis=1)
    v_all = jnp.concatenate([v_buf.astype(v.dtype), v], axis=1)
    qpos = start + jnp.arange(l, dtype=jnp.int32)
    kpos = start - WINDOW + jnp.arange(WINDOW + l, dtype=jnp.int32)
    dist = qpos[:, None] - kpos[None, :]
    valid = (dist >= 0) & (dist < WINDOW)
    o = _sink_attention(q, k_all, v_all, dist, valid, rel_bias, sinks)
    y = o.reshape(b, l, Q_DIM) @ w_o + b_o
    return y, k_all[:, -WINDOW:], v_all[:, -WINDOW:]


def _ssd_scan(x, dt, a, bm, cm, h0, chunk):
    b, l = x.shape[:2]
    nc = l // chunk

    def to_chunks(t):
        return jnp.moveaxis(t.astype(jnp.float32).reshape(b, nc, chunk, *t.shape[2:]), 1, 0)

    causal = jnp.tril(jnp.ones((chunk, chunk), dtype=bool))[None, :, :, None, None]

    def step(h, inp):
        xc, dtc, bc, cc = inp
        acum = jnp.cumsum(dtc * a, axis=1)
        seg = acum[:, :, None] - acum[:, None, :]
        lmat = jnp.exp(jnp.where(causal, seg, -jnp.inf))
        cb = jnp.einsum('btgn,bsgn->btsg', cc, bc)
        w = cb[..., None] * lmat * dtc[:, None]
        y = jnp.einsum('btsgr,bsgrp->btgrp', w, xc)
        y = y + jnp.einsum('btgn,bgrpn->btgrp', cc, h) * jnp.exp(acum)[..., None]
        decay = jnp.exp(acum[:, -1:] - acum) * dtc
        h = h * jnp.exp(acum[:, -1])[..., None, None] + jnp.einsum('bsgn,bsgrp->bgrpn', bc, decay[..., None] * xc)
        return h, y

    h, ys = lax.scan(step, h0, (to_chunks(x), to_chunks(dt), to_chunks(bm), to_chunks(cm)))
    return jnp.moveaxis(ys, 0, 1).reshape(x.shape), h


def _mamba2(x, conv_state, ssm_state, w_in, conv_w, conv_b, dt_bias, a_log, d_skip, norm_w, w_out):
    b, l, _ = x.shape
    zxbcdt = x @ w_in
    z = zxbcdt[..., :D_INNER]
    xbc = zxbcdt[..., D_INNER:D_INNER + CONV_DIM]
    dt = zxbcdt[..., D_INNER + CONV_DIM:]
    xpad = jnp.concatenate([conv_state.astype(xbc.dtype), xbc], axis=1)
    conv = conv_b + sum(xpad[:, j:j + l] * conv_w[j] for j in range(CONV_WIDTH))
    xbc = jax.nn.silu(conv)
    gbn = SSM_GROUPS * D_STATE
    xs = xbc[..., :D_INNER].reshape(b, l, SSM_GROUPS, SSM_HEADS_PER_GROUP, SSM_HEAD_DIM)
    bm = xbc[..., D_INNER:D_INNER + gbn].reshape(b, l, SSM_GROUPS, D_STATE)
    cm = xbc[..., D_INNER + gbn:].reshape(b, l, SSM_GROUPS, D_STATE)
    dt = jax.nn.softplus(dt.astype(jnp.float32) + dt_bias).reshape(b, l, SSM_GROUPS, SSM_HEADS_PER_GROUP)
    a = -jnp.exp(a_log.astype(jnp.float32)).reshape(SSM_GROUPS, SSM_HEADS_PER_GROUP)
    h0 = ssm_state.astype(jnp.float32).reshape(b, SSM_GROUPS, SSM_HEADS_PER_GROUP, SSM_HEAD_DIM, D_STATE)
    y, h = _ssd_scan(xs, dt, a, bm, cm, h0, min(SSD_CHUNK, l))
    y = y + d_skip.reshape(SSM_GROUPS, SSM_HEADS_PER_GROUP, 1) * xs
    y = y.reshape(b, l, D_INNER) * jax.nn.silu(z.astype(jnp.float32))
    yg = y.reshape(b, l, SSM_GROUPS, D_INNER // SSM_GROUPS)
    yg = yg * lax.rsqrt(jnp.mean(yg * yg, -1, keepdims=True) + RMS_EPS)
    y = yg.reshape(b, l, D_INNER) * norm_w
    new_h = h.reshape(b, SSM_HEADS, SSM_HEAD_DIM, D_STATE)
    return y @ w_out, xpad[:, -(CONV_WIDTH - 1):], new_h


def _pool_mixer(x, prefix, start, pool_w, pool_scale):
    b, l, _ = x.shape
    xf = x.astype(jnp.float32)
    xp = jnp.concatenate([prefix.astype(jnp.float32), xf], axis=1)
    cs = jnp.concatenate([jnp.zeros((b, 1, D_MODEL), jnp.float32), jnp.cumsum(xp, axis=1)], axis=1)
    hi = cs[:, POOL_STATE_LEN + 1:]
    pos = start + jnp.arange(l, dtype=jnp.int32)
    outs = []
    for g, w in enumerate(POOL_WINDOWS):
        sl = slice(g * POOL_GROUP_DIM, (g + 1) * POOL_GROUP_DIM)
        lo = cs[:, POOL_STATE_LEN + 1 - w:POOL_STATE_LEN + 1 - w + l, sl]
        cnt = jnp.minimum(pos + 1, w).astype(jnp.float32)[None, :, None]
        diff = (hi[..., sl] - lo) / cnt - xf[..., sl]
        outs.append(diff @ pool_w[g])
    y = jnp.concatenate(outs, axis=-1) * pool_scale
    return y, xp[:, -POOL_STATE_LEN:]


def setup_inputs(seed: int = 0) -> dict:
    key = jax.random.key(seed)
    ks = iter(jax.random.split(key, 32))

    def nrm(shape, scale):
        return jax.random.normal(next(ks), shape, jnp.float32) * scale

    def unif(shape, lo, hi):
        return jax.random.uniform(next(ks), shape, jnp.float32, lo, hi)

    beta = DEEPNORM_BETA
    x_prompt = nrm((BATCH, SEQ, D_MODEL), 1.0)
    x_sample = nrm((DEC_BATCH, DEC_SEQ, D_MODEL), 1.0)
    cache_k = nrm((N_ATTN_LAYERS, DEC_BATCH, WINDOW, N_KV_HEADS, HEAD_DIM), 1.0)
    cache_v = nrm((N_ATTN_LAYERS, DEC_BATCH, WINDOW, N_KV_HEADS, HEAD_DIM), beta)
    state_conv = nrm((N_SSM_LAYERS, DEC_BATCH, CONV_WIDTH - 1, CONV_DIM), 1.0)
    state_ssm = nrm((N_SSM_LAYERS, DEC_BATCH, SSM_HEADS, SSM_HEAD_DIM, D_STATE), 0.1)
    state_pool = nrm((N_POOL_LAYERS, DEC_BATCH, POOL_STATE_LEN, D_MODEL), 1.0)
    rel_bias = nrm((REL_BUCKETS, N_HEADS), 0.5)
    w_qk = nrm((N_ATTN_LAYERS, D_MODEL, Q_DIM + KV_DIM), D_MODEL ** -0.5)
    w_v = nrm((N_ATTN_LAYERS, D_MODEL, KV_DIM), D_MODEL ** -0.5 * beta)
    attn_w_qkv = jnp.concatenate([w_qk, w_v], axis=-1)
    attn_b_qkv = nrm((N_ATTN_LAYERS, QKV_DIM), 0.02)
    attn_w_o = nrm((N_ATTN_LAYERS, Q_DIM, D_MODEL), Q_DIM ** -0.5 * beta)
    attn_b_o = nrm((N_ATTN_LAYERS, D_MODEL), 0.02)
    attn_sinks = nrm((N_ATTN_LAYERS, N_HEADS), 1.0)
    ssm_w_in = nrm((N_SSM_LAYERS, D_MODEL, SSM_IN_DIM), D_MODEL ** -0.5)
    ssm_conv_w = nrm((N_SSM_LAYERS, CONV_WIDTH, CONV_DIM), CONV_WIDTH ** -0.5)
    ssm_conv_b = nrm((N_SSM_LAYERS, CONV_DIM), 0.02)
    dt0 = jnp.exp(unif((N_SSM_LAYERS, SSM_HEADS), math.log(1e-3), math.log(1e-1)))
    ssm_dt_bias = dt0 + jnp.log(-jnp.expm1(-dt0))
    ssm_a_log = jnp.log(unif((N_SSM_LAYERS, SSM_HEADS), 1.0, 16.0))
    ssm_d = 1.0 + nrm((N_SSM_LAYERS, SSM_HEADS), 0.02)
    ssm_norm_w = 1.0 + nrm((N_SSM_LAYERS, D_INNER), 0.02)
    ssm_w_out = nrm((N_SSM_LAYERS, D_INNER, D_MODEL), D_INNER ** -0.5 * beta)
    pool_w = nrm((N_POOL_LAYERS, POOL_GROUPS, POOL_GROUP_DIM, POOL_GROUP_DIM), POOL_GROUP_DIM ** -0.5 * beta)
    pool_scale = 1.0 + nrm((N_POOL_LAYERS, D_MODEL), 0.02)
    ffn_w_gate = nrm((DEPTH, D_MODEL, D_FF), D_MODEL ** -0.5 * beta)
    ffn_w_up = nrm((DEPTH, D_MODEL, D_FF), D_MODEL ** -0.5 * beta)
    ffn_w_down = nrm((DEPTH, D_FF, D_MODEL), D_FF ** -0.5 * beta)
    ln_g = 1.0 + nrm((DEPTH, 2, D_MODEL), 0.02)
    ln_b = nrm((DEPTH, 2, D_MODEL), 0.02)
    return {'x_prompt': x_prompt, 'x_sample': x_sample, 'cache_k': cache_k, 'cache_v': cache_v,
            'state_conv': state_conv, 'state_ssm': state_ssm, 'state_pool': state_pool,
            'rel_bias': rel_bias, 'attn_w_qkv': attn_w_qkv, 'attn_b_qkv': attn_b_qkv,
            'attn_w_o': attn_w_o, 'attn_b_o': attn_b_o, 'attn_sinks': attn_sinks,
            'ssm_w_in': ssm_w_in, 'ssm_conv_w': ssm_conv_w, 'ssm_conv_b': ssm_conv_b,
            'ssm_dt_bias': ssm_dt_bias, 'ssm_a_log': ssm_a_log, 'ssm_d': ssm_d,
            'ssm_norm_w': ssm_norm_w, 'ssm_w_out': ssm_w_out, 'pool_w': pool_w,
            'pool_scale': pool_scale, 'ffn_w_gate': ffn_w_gate, 'ffn_w_up': ffn_w_up,
            'ffn_w_down': ffn_w_down, 'ln_g': ln_g, 'ln_b': ln_b}


def reference(x_prompt, x_sample, cache_k, cache_v, state_conv, state_ssm, state_pool, rel_bias,
              attn_w_qkv, attn_b_qkv, attn_w_o, attn_b_o, attn_sinks, ssm_w_in, ssm_conv_w,
              ssm_conv_b, ssm_dt_bias, ssm_a_log, ssm_d, ssm_norm_w, ssm_w_out, pool_w,
              pool_scale, ffn_w_gate, ffn_w_up, ffn_w_down, ln_g, ln_b):
    xp, xs = x_prompt, x_sample
    bp = xp.shape[0]
    nk_p, nv_p, nc_p, nh_p, npool_p = [], [], [], [], []
    nk_s, nv_s, nc_s, nh_s, npool_s = [], [], [], [], []
    for i in range(DEPTH):
        j = i // N_MIXERS
        kind = i % N_MIXERS
        if kind == 0:
            mp, kp, vp = _swa_prompt(xp, attn_w_qkv[j], attn_b_qkv[j], attn_w_o[j], attn_b_o[j],
                                     attn_sinks[j], rel_bias)
            ms, ks_, vs_ = _swa_sample(xs, cache_k[j], cache_v[j], PAST_LEN, attn_w_qkv[j], attn_b_qkv[j],
                                       attn_w_o[j], attn_b_o[j], attn_sinks[j], rel_bias)
            nk_p.append(kp); nv_p.append(vp); nk_s.append(ks_); nv_s.append(vs_)
        elif kind == 1:
            ssm_args = (ssm_w_in[j], ssm_conv_w[j], ssm_conv_b[j], ssm_dt_bias[j], ssm_a_log[j],
                        ssm_d[j], ssm_norm_w[j], ssm_w_out[j])
            conv0 = jnp.zeros((bp, CONV_WIDTH - 1, CONV_DIM), xp.dtype)
            h0 = jnp.zeros((bp, SSM_HEADS, SSM_HEAD_DIM, D_STATE), jnp.float32)
            mp, cp, hp = _mamba2(xp, conv0, h0, *ssm_args)
            ms, cs_, hs_ = _mamba2(xs, state_conv[j], state_ssm[j], *ssm_args)
            nc_p.append(cp); nh_p.append(hp); nc_s.append(cs_); nh_s.append(hs_)
        else:
            pool0 = jnp.zeros((bp, POOL_STATE_LEN, D_MODEL), jnp.float32)
            mp, pp = _pool_mixer(xp, pool0, 0, pool_w[j], pool_scale[j])
            ms, ps_ = _pool_mixer(xs, state_pool[j], PAST_LEN, pool_w[j], pool_scale[j])
            npool_p.append(pp); npool_s.append(ps_)
        xp = _layer_norm(DEEPNORM_ALPHA * xp + mp, ln_g[i, 0], ln_b[i, 0])
        xs = _layer_norm(DEEPNORM_ALPHA * xs + ms, ln_g[i, 0], ln_b[i, 0])
        xp = _layer_norm(DEEPNORM_ALPHA * xp + _swiglu(xp, ffn_w_gate[i], ffn_w_up[i], ffn_w_down[i]),
                         ln_g[i, 1], ln_b[i, 1])
        xs = _layer_norm(DEEPNORM_ALPHA * xs + _swiglu(xs, ffn_w_gate[i], ffn_w_up[i], ffn_w_down[i]),
                         ln_g[i, 1], ln_b[i, 1])
    return (xp, xs,
            jnp.stack(nk_p), jnp.stack(nv_p), jnp.stack(nc_p), jnp.stack(nh_p), jnp.stack(npool_p),
            jnp.stack(nk_s), jnp.stack(nv_s), jnp.stack(nc_s), jnp.stack(nh_s), jnp.stack(npool_s))
```

```python
import math
from contextlib import ExitStack

import ml_dtypes
import numpy as np

import concourse.bass as bass
import concourse.mybir as mybir
from concourse.bass_utils import run_bass_kernel_spmd

F32 = mybir.dt.float32
BF16 = mybir.dt.bfloat16
AF = mybir.ActivationFunctionType
ALU = mybir.AluOpType
AX = mybir.AxisListType

NCORES = 8
D = 1024
SEQ = 2048
NT = 17
NTOK = NT * 128
DEPTH = 4
DFF = 2816
ALPHA = (2 * DEPTH) ** 0.25
LN_EPS = 1e-5
RMS_EPS = 1e-5
NEG = -30000.0
BLKS = [(0, 4), (4, 8), (8, 12), (12, 16), (16, 17)]
ARENA_F32 = 23552
QHEADS = [(0, 4), (1, 5), (2, 6), (3, 7), (8, 12), (9, 13), (10, 14), (11, 15)]
POOL_W = (2, 4, 8, 16)


class Buf:
    __slots__ = ("name", "w", "r", "dsem", "excl")

    def __init__(self, name, r=None, excl=False):
        self.name = name
        self.excl = excl
        self.w = None
        self.r = dict(r) if r else {}
        self.dsem = None


class Sched:
    ENG = ("pe", "act", "dve", "pool", "sp")

    def __init__(self, nc, es):
        self.nc = nc
        self.es = es
        self.sems = {}
        self.cnt = {}
        self.seen = {e: {} for e in self.ENG}
        self.ops = {e: [] for e in self.ENG}
        for e in ("pe", "act", "dve", "pool"):
            self._newsem(e)
        self.nd = 0
        self.free_dsems = []

    def _newsem(self, key):
        self.sems[key] = self.es.enter_context(self.nc.semaphore(f"s_{key}"))
        self.cnt[key] = 0

    def _deps(self, eng, reads, writes):
        need = {}

        def add(k, v):
            if v > need.get(k, 0):
                need[k] = v
        for b in reads:
            if b.w is not None:
                add(*b.w)
            if b.excl:
                for k, v in b.r.items():
                    if k != eng:
                        add(k, v)
        for b in writes:
            if b.w is not None and b.w[0] != eng:
                add(*b.w)
            for k, v in b.r.items():
                if k != eng:
                    add(k, v)
        waits = []
        seen = self.seen[eng]
        for k, v in need.items():
            if seen.get(k, 0) < v:
                seen[k] = v
                waits.append((k, v))
        return waits

    def op(self, eng, fn, reads=(), writes=(), inc=True):
        waits = self._deps(eng, reads, writes)
        val = self.cnt[eng] + 1
        if inc:
            self.cnt[eng] = val
        for b in reads:
            b.r[eng] = val
        for b in writes:
            b.w = (eng, val)
            b.r = {}
        self.ops[eng].append((waits, fn, (eng, 1) if inc else None))

    def dma(self, q, fn, reads=(), writes=(), own=None):
        waits = self._deps(q, reads, writes)
        own = own or (writes[0] if writes else reads[0])
        if own.dsem is None:
            if self.free_dsems:
                own.dsem = self.free_dsems.pop()
            else:
                own.dsem = f"d{self.nd}"
                self.nd += 1
                self._newsem(own.dsem)
        k = own.dsem
        self.cnt[k] += 16
        val = self.cnt[k]
        for b in reads:
            b.r[k] = val
        for b in writes:
            b.w = (k, val)
            b.r = {}
        self.ops[q].append((waits, fn, (k, 16)))

    def wait_all(self, eng, bufs):
        waits = self._deps(eng, (), bufs)
        self.ops[eng].append((waits, None, None))

    def emit(self):
        with self.nc.Block() as block:
            def run(name):
                def body(e):
                    for waits, fn, inc in self.ops[name]:
                        for k, v in waits:
                            e.wait_ge(self.sems[k], v)
                        if fn is not None:
                            ins = fn(e)
                            if inc is not None:
                                ins.then_inc(self.sems[inc[0]], inc[1])
                return body
            block.sync(run("sp"))
            block.scalar(run("act"))
            block.vector(run("dve"))
            block.gpsimd(run("pool"))
            block.tensor(run("pe"))


def _t5_bucket(n):
    n = np.maximum(n, 0)
    nf = np.maximum(n, 1).astype(np.float32)
    large = 16 + (np.log(nf / np.float32(16)) / np.float32(math.log(8.0)) * np.float32(16)).astype(np.int32)
    large = np.minimum(large, 31)
    return np.where(n < 16, n, large)


def host_constants():
    bf = ml_dtypes.bfloat16
    c = {}
    c["c_identb"] = np.eye(128, dtype=np.float32).astype(bf)
    c["c_identf"] = np.eye(128, dtype=np.float32)
    c["c_antib"] = np.eye(128, dtype=np.float32)[::-1].copy().astype(bf)
    c["c_onesf"] = np.ones((128, 128), np.float32)
    i = np.arange(384)
    dist = 255 - i
    valid = (dist >= 0) & (dist < 128)
    oh = np.zeros((33, 384), np.float32)
    bk = _t5_bucket(dist)
    for ii in range(384):
        if valid[ii]:
            oh[bk[ii], ii] = 1.0
        else:
            oh[32, ii] = NEG
    c["c_onehot"] = oh
    p = np.arange(128)
    sels = np.zeros((128, 128), np.float32)
    sels[127 - (p % 8), p] = 1.0
    c["c_sels"] = sels.astype(bf)
    negb = np.where((p[:, None] // 8) == (p[None, :] // 8), 0.0, NEG).astype(np.float32)
    c["c_negb"] = negb.astype(bf)
    bm16 = np.zeros((128, 16, 128), np.float32)
    for b in range(16):
        bm16[:, b, 8 * b:8 * b + 8] = 1.0
    c["c_bm16"] = bm16.astype(bf)
    s_ = p[:, None]
    t_ = p[None, :]
    negm_p = np.where(t_ < s_, NEG, 0.0).astype(np.float32)
    negm_s = np.where((t_ >= s_) & (t_ // 8 == s_ // 8), 0.0, NEG).astype(np.float32)
    c["c_negm_p"] = np.tile(negm_p[:, None, :], (1, 8, 1)).reshape(128, 1024).astype(bf)
    c["c_negm_s"] = np.tile(negm_s[:, None, :], (1, 8, 1)).reshape(128, 1024).astype(bf)
    sel8 = np.zeros((8, 8, 128), np.float32)
    for r in range(8):
        sel8[r, r, :] = 1.0
    c["c_sel8"] = sel8
    sm_p = np.ones((8, 512), np.float32)
    sm_p[:, ::128] = 0.0
    sm_s = np.ones((8, 128), np.float32)
    sm_s[:, ::8] = 0.0
    c["c_scan_p"] = sm_p
    c["c_scan_s"] = sm_s
    apar = np.zeros((8, 128), np.float32)
    for k in range(8):
        apar[k, (k % 2) * 64:(k % 2) * 64 + 64] = 1.0
    c["c_apar"] = apar
    selj = np.zeros((8, 16, 4), np.float32)
    for k in range(8):
        selj[k, :, k // 2] = 1.0
    c["c_selj"] = selj
    seqm = np.zeros((128, 16), np.float32)
    seqm[p, p // 8] = 1.0
    c["c_seqm"] = seqm
    mt = np.zeros((24, 128, 128), np.float32)
    for g, w in enumerate(POOL_W):
        for t in range(128):
            for s in range(t - w + 1, t + 1):
                if s >= 0:
                    mt[4 + g, s, t] += 1.0 / w
                else:
                    mt[0 + g, s + 128, t] += 1.0 / w
            mt[4 + g, t, t] -= 1.0
            cnt = min(t + 1, w)
            for s in range(max(0, t - w + 1), t + 1):
                mt[8 + g, s, t] += 1.0 / cnt
            mt[8 + g, t, t] -= 1.0
            b, tt = t // 8, t % 8
            for pos in range(tt - w + 1, tt + 1):
                if pos >= 0:
                    mt[12 + g, b * 8 + pos, t] += 1.0 / w
                else:
                    j = pos + 15
                    if b < 8:
                        mt[16 + g, b * 15 + j, t] += 1.0 / w
                    else:
                        mt[20 + g, (b - 8) * 15 + j, t] += 1.0 / w
            mt[12 + g, t, t] -= 1.0
    c["c_poolmt"] = np.ascontiguousarray(mt.transpose(1, 0, 2))
    return c


class KB:
    def __init__(self, nc, S, es, dr, stop_after=None):
        self.nc, self.S, self.es, self.dr = nc, S, es, dr
        self.stop_after = stop_after
        sb = lambda name, shape, dt: es.enter_context(nc.sbuf_tensor(name, shape, dt))
        self.XR = sb("XR", [128, NT, D], F32)
        self.XT = sb("XT", [128, 8, NTOK], BF16)
        self.bxr = [Buf(f"xr{t}") for t in range(NT)]
        self.bxt = [Buf(f"xt{t}") for t in range(NT)]
        self.lng = sb("lng", [128, D], F32)
        self.lnb = sb("lnb", [128, D], F32)
        self.blng, self.blnb = Buf("lng"), Buf("lnb")
        self.xb16 = sb("xb16", [128, 2, D], BF16)
        self.bxb16 = [Buf("xb16_0"), Buf("xb16_1")]
        self.xbi = 0
        self.identb = sb("identb", [128, 128], BF16)
        self.identf = sb("identf", [128, 128], F32)
        self.onesf = sb("onesf", [128, 128], F32)
        self.mhalf = sb("mhalf", [128, 1], F32)
        self.bconst = Buf("const")
        self.stat = sb("stat", [128, 4, 16], F32)
        self.bstat = [Buf(f"stat{i}") for i in range(4)]
        self.sti = 0
        self.arena = sb("arena", [128, ARENA_F32], F32)
        self.aoff = 0
        self.abufs = []
        self.retired = {}
        self.psf = es.enter_context(nc.psum_tensor("psf", [128, 6, 512], F32))
        self.psb = es.enter_context(nc.psum_tensor("psb", [128, 2, 1024], BF16))
        self.bpf = [Buf(f"pf{i}", excl=True) for i in range(6)]
        self.bpb = [Buf("pb0", excl=True), Buf("pb1", excl=True)]
        self.pbi = 0
        self.bd2d = Buf("d2d")
        self.outbufs = [self.bd2d]
        self.evi = 0

    def op(self, eng, name, R, W, inc=True, **kw):
        self.S.op(eng, lambda e, n=name, kw=kw: getattr(e, n)(**kw), R, W, inc)

    def mm(self, out, lhsT, rhs, start, stop, R, W, inc=None):
        if inc is None:
            inc = stop
        self.S.op("pe", lambda e: e.matmul(out, lhsT=lhsT, rhs=rhs, start=start, stop=stop), R, W, inc)

    def tr(self, out, in_, ident, R, W, inc):
        self.S.op("pe", lambda e: e.transpose(out=out, in_=in_, identity=ident), R, W, inc)

    def act(self, out, in_, func, R, W, **kw):
        self.S.op("act", lambda e: e.activation(out=out, in_=in_, func=func, **kw), R, W)

    def dma(self, q, out, in_, R, W, own=None, **kw):
        self.S.dma(q, lambda e: e.dma_start(out=out, in_=in_, **kw), R, W, own)

    def evac(self, out, in_, R, W, eng=None):
        if eng is None:
            eng = ("act", "dve")[self.evi % 2]
            self.evi += 1
        if eng == "act":
            self.act(out, in_, AF.Copy, R, W)
        else:
            self.op(eng, "tensor_copy", R, W, out=out, in_=in_)

    def carve(self, name, shape, dt):
        n = int(np.prod(shape[1:]))
        nb = n * (2 if dt == BF16 else 4)
        n4 = (nb + 3) // 4
        assert self.aoff + n4 <= ARENA_F32, f"arena overflow at {name}: {self.aoff}+{n4}"
        v = self.arena[:, self.aoff:self.aoff + n4]
        self.aoff += n4
        if dt != F32:
            v = v.bitcast(dt)
        v = v[0:shape[0], 0:n]
        if len(shape) == 3:
            v = v.rearrange("p (a b) -> p a b", a=shape[1])
        elif len(shape) == 4:
            v = v.rearrange("p (a b c) -> p a b c", a=shape[1], b=shape[2])
        b = Buf(name, self.retired)
        self.abufs.append((self.aoff, b))
        return v, b

    def mark(self):
        return self.aoff

    def release(self, mark=0):
        keep = []
        for off, b in self.abufs:
            if off > mark:
                for k, v in ([b.w] if b.w else []) + list(b.r.items()):
                    if v > self.retired.get(k, 0):
                        self.retired[k] = v
                if b.dsem is not None:
                    self.S.free_dsems.append(b.dsem)
                    b.dsem = None
            else:
                keep.append((off, b))
        self.abufs = keep
        self.aoff = mark

    def pair(self, i):
        return self.psf[:, 2 * i:2 * i + 2, :].rearrange("p a b -> p (a b)")

    def bpair(self, i):
        return [self.bpf[2 * i], self.bpf[2 * i + 1]]

    def next_pb(self):
        i = self.pbi % 2
        self.pbi += 1
        return self.psb[:, i, :], self.bpb[i]

    def next_stat(self):
        i = self.sti % 4
        self.sti += 1
        return self.stat[:, i, :], self.bstat[i]

    def load_consts(self):
        d = self.dr
        W = [self.bconst]
        self.dma("sp", self.identb[:], d["c_identb"], [], W)
        self.dma("sp", self.identf[:], d["c_identf"], [], W)
        self.dma("sp", self.onesf[:], d["c_onesf"], [], W)
        self.op("pool", "memset", [], W, ap=self.mhalf[:], constant=-0.5)

    def load_x(self):
        d = self.dr
        self.dma("sp", self.XR[:, 0:16, :], d["x_p"].rearrange("(i p) d -> p i d", p=128), [], self.bxr[0:16])
        self.dma("sp", self.XR[:, 16, :], d["x_s"], [], [self.bxr[16]])
        for t in range(NT):
            slot = self.cast_xb(t)
            self.make_xt(t, slot)

    def cast_xb(self, t):
        slot = self.xbi % 2
        self.xbi += 1
        self.act(self.xb16[:, slot, :], self.XR[:, t, :], AF.Copy, [self.bxr[t]], [self.bxb16[slot]])
        return slot

    def make_xt(self, t, slot):
        pb, bpb = self.next_pb()
        pv = pb.rearrange("p (c n) -> p c n", c=8)
        for c in range(8):
            self.tr(pv[:, c, :], self.xb16[:, slot, c * 128:(c + 1) * 128], self.identb[:],
                    [self.bxb16[slot], self.bconst], [bpb], inc=(c == 7))
        self.evac(self.XT[:, :, t * 128:(t + 1) * 128], pv, [bpb], [self.bxt[t]])

    def load_ln(self, l, which):
        d = self.dr
        self.dma("sp", self.lng[:], d["ln_g"][l, which].partition_broadcast(128), [], [self.blng])
        self.dma("sp", self.lnb[:], d["ln_b"][l, which].partition_broadcast(128), [], [self.blnb])

    def ln1(self, t, final=False):
        st, bst = self.next_stat()
        x = self.XR[:, t, :]
        bx = self.bxr[t]
        for i in range(2):
            self.op("dve", "bn_stats", [bx], [bst], out=st[:, 6 * i:6 * i + 6], in_=x[:, i * 512:(i + 1) * 512])
        self.op("dve", "bn_aggr", [bst], [bst], out=st[:, 12:14], in_=st[:, 0:12])
        self.op("dve", "tensor_scalar", [bst], [bst], out=st[:, 14:15], in0=st[:, 13:14], scalar1=LN_EPS,
                scalar2=None, op0=ALU.add)
        self.op("pool", "tensor_tensor", [bst, self.bconst], [bst], out=st[:, 15:16], in0=st[:, 14:15],
                in1=self.mhalf[:], op=ALU.pow)
        self.op("dve", "tensor_scalar", [bx, bst], [bx], out=x, in0=x, scalar1=st[:, 12:13], scalar2=st[:, 15:16],
                op0=ALU.subtract, op1=ALU.mult)
        self.op("pool", "tensor_tensor", [bx, self.blng], [bx], out=x, in0=x, in1=self.lng[:], op=ALU.mult)
        self.op("dve", "tensor_tensor", [bx, self.blnb], [bx], out=x, in0=x, in1=self.lnb[:], op=ALU.add)
        if final:
            d = self.dr
            if t < 16:
                self.dma("sp", d["y_p"][t * 128:(t + 1) * 128, :], x, [bx], [])
            else:
                self.dma("sp", d["y_s"], x, [bx], [])
            return None
        return self.cast_xb(t)

    def ln_all(self, pend, t, final=False):
        slot = self.ln1(t, final)
        if pend is not None:
            self.make_xt(*pend)
        return None if final else (t, slot)

    def ffn(self, l, final):
        d = self.dr
        self.release(0)
        groups = [(i * 512, 512) for i in range(5)] + [(2560, 256)]
        wg, wu, wd, bw = [], [], [], []
        for s in range(2):
            a, b1 = self.carve(f"wg{s}", [128, 8, 512], BF16)
            u, b2 = self.carve(f"wu{s}", [128, 8, 512], BF16)
            dd, b3 = self.carve(f"wd{s}", [128, 4, 1024], BF16)
            wg.append(a), wu.append(u), wd.append(dd), bw.append((b1, b2, b3))
        at, bat, sg, bsg = [], [], [], []
        for s in range(2):
            a, b = self.carve(f"at{s}", [128, 4, 512], BF16)
            at.append(a), bat.append(b)
            a, b = self.carve(f"sg{s}", [128, 512], F32)
            sg.append(a), bsg.append(b)

        def load(gi):
            ff0, fw = groups[gi]
            s = gi % 2
            nfc = fw // 128
            self.dma("pool", wg[s][:, :, 0:fw], d["ffn_w_gate"][l][:, ff0:ff0 + fw].rearrange("(c p) n -> p c n", p=128),
                     [], [bw[s][0]])
            self.dma("pool", wu[s][:, :, 0:fw], d["ffn_w_up"][l][:, ff0:ff0 + fw].rearrange("(c p) n -> p c n", p=128),
                     [], [bw[s][1]])
            self.dma("pool", wd[s][:, 0:nfc, :], d["ffn_w_down"][l][ff0:ff0 + fw, :].rearrange("(c p) n -> p c n", p=128),
                     [], [bw[s][2]])

        load(0)
        load(1)
        self.load_ln(l, 1)
        ai = 0
        gi_pg = 0
        pend = None
        for gi, (ff0, fw) in enumerate(groups):
            s = gi % 2
            nfc = fw // 128
            last = gi == len(groups) - 1
            for (t0, t1) in BLKS:
                c0, n = t0 * 128, (t1 - t0) * 128
                a_s = ai % 2
                ai += 1
                for fc in range(nfc):
                    pgi = gi_pg % 2
                    gi_pg += 1
                    pg, bpg = self.psf[:, pgi, 0:n], self.bpf[pgi]
                    pu, bpu = self.psf[:, 2 + pgi, 0:n], self.bpf[2 + pgi]
                    for k in range(8):
                        self.mm(pg, wg[s][:, k, fc * 128:(fc + 1) * 128], self.XT[:, k, c0:c0 + n], k == 0, k == 7,
                                [bw[s][0]] + self.bxt[t0:t1], [bpg])
                    for k in range(8):
                        self.mm(pu, wu[s][:, k, fc * 128:(fc + 1) * 128], self.XT[:, k, c0:c0 + n], k == 0, k == 7,
                                [bw[s][1]] + self.bxt[t0:t1], [bpu])
                    self.act(sg[pgi][:, 0:n], pg, AF.Silu, [bpg], [bsg[pgi]])
                    self.op("dve", "tensor_tensor", [bsg[pgi], bpu], [bat[a_s]], out=at[a_s][:, fc, 0:n],
                            in0=sg[pgi][:, 0:n], in1=pu, op=ALU.mult)
                for t in range(t0, t1):
                    tl = t - t0
                    for half in range(2):
                        pd, bpd = self.psf[:, 4 + half, :], self.bpf[4 + half]
                        for fc in range(nfc):
                            self.mm(pd, at[a_s][:, fc, tl * 128:(tl + 1) * 128], wd[s][:, fc, half * 512:(half + 1) * 512],
                                    fc == 0, fc == nfc - 1, [bat[a_s], bw[s][2]], [bpd])
                        xs_ = self.XR[:, t, half * 512:(half + 1) * 512]
                        if gi == 0:
                            self.op("dve", "scalar_tensor_tensor", [self.bxr[t], bpd], [self.bxr[t]], out=xs_, in0=xs_,
                                    scalar=ALPHA, in1=pd, op0=ALU.mult, op1=ALU.add)
                        else:
                            self.op("dve", "tensor_tensor", [self.bxr[t], bpd], [self.bxr[t]], out=xs_, in0=xs_, in1=pd,
                                    op=ALU.add)
                    if last:
                        pend = self.ln_all(pend, t, final)
            if gi + 2 < len(groups):
                load(gi + 2)
        if pend is not None:
            self.make_xt(*pend)

    def build_bias_tables(self):
        d = self.dr
        m = self.mark()
        rb, brb = self.carve("rb33", [33, 16], F32)
        oh, boh = self.carve("oh33", [33, 384], F32)
        gs, bgs = self.carve("gsb", [16, 384], F32)
        self.dma("sp", rb[0:32, :], d["rel_bias"], [], [brb])
        self.op("pool", "memset", [], [brb], ap=rb[32:33, :], constant=1.0)
        self.dma("sp", oh, d["c_onehot"], [], [boh])
        ps, bps = self.psf[0:16, 0, 0:384], self.bpf[0]
        self.mm(ps, rb, oh, True, True, [brb, boh], [bps])
        self.evac(gs, ps, [bps], [bgs], "act")
        self.bgtab = Buf("gtab")
        self.dma("sp", d["gtab"], gs, [bgs], [self.bgtab], own=bgs)
        self.release(m)

    def build_bm(self, bm, bbm, U, bU, sample, cst):
        bU = bU if isinstance(bU, list) else [bU]
        antib, sels, negb, bcs = cst
        for q4 in range(4):
            pr = self.pair(q4 % 2)
            bpr = self.bpair(q4 % 2)
            pv = pr.rearrange("p (h k) -> p h k", h=4)
            if not sample:
                for hh2 in range(2):
                    self.mm(pr[:, hh2 * 512:(hh2 + 1) * 512], antib,
                            U[:, 4 * q4 + 2 * hh2:4 * q4 + 2 * hh2 + 2, :].rearrange("p h k -> p (h k)"), True, True,
                            bU + [bcs], [bpr[hh2]])
            else:
                for i in range(4):
                    h = 4 * q4 + i
                    first = (i % 2 == 0)
                    self.mm(pv[:, i, 0:128], sels, U[:, h, 0:128], first, False, bU + [bcs], [bpr[i // 2]], inc=False)
                    for b in range(16):
                        self.mm(pv[:, i, 128 + 8 * b:136 + 8 * b], sels, U[:, h, 128:136], False, False, bU + [bcs],
                                [bpr[i // 2]], inc=False)
                    self.mm(pv[:, i, 128:256], self.identb[:], negb, False, True, [bcs, self.bconst], [bpr[i // 2]],
                            inc=True)
            self.evac(bm[:, 4 * q4:4 * q4 + 4, :], pv, bpr, [bbm])

    def attn(self, j, l):
        d = self.dr
        self.release(0)
        wo, bwo = self.carve("wo", [128, 8, 1024], BF16)
        kT, bkT = self.carve("kT", [128, 2, NTOK], BF16)
        Vb, _ = self.carve("Vb", [128, NT, 256], BF16)
        bVb = [Buf(f"Vb{t}", self.retired) for t in range(NT)]
        bkTt = [Buf(f"kT{t}", self.retired) for t in range(NT)]
        for b in bVb + bkTt:
            self.abufs.append((self.aoff, b))
        qTf, bqT = self.carve("qT", [128, 4096], BF16)
        qT = qTf.rearrange("p (c n) -> p c n", c=8)
        bm, bbm = self.carve("bm", [128, 16, 256], BF16)
        sm, bsm = self.carve("attn_small", [128, 64], F32)
        bq, bk, snk, nsnk = sm[:, 0:8], sm[:, 8:10], sm[:, 16:32], sm[:, 32:48]
        brow, bbrow = self.carve("brow", [65, 512], F32)
        cstt, bcs = self.carve("acst", [128, 3, 128], BF16)
        w8, _ = self.carve("w8", [128, 4096], BF16)
        P_ = w8[:, 0:1024].rearrange("p (h k) -> p h k", h=4)
        PT = w8[:, 1024:2048].rearrange("p (h a n) -> p h a n", h=4, a=2)
        Ob = w8[:, 2048:3072].rearrange("p (h e) -> p h e", h=16)
        OT = w8[:, 3072:4096].rearrange("p (c n) -> p c n", c=8)
        bP, bPT, bOb, bOT = (Buf(n_, self.retired) for n_ in ("P", "PT", "Ob", "OT"))
        bw8 = [bP, bPT, bOb, bOT]
        for b in bw8:
            self.abufs.append((self.aoff, b))
        st2, bst2 = self.carve("ast", [128, 64], F32)
        kvo, bkvo = self.carve("kvo", [128, 512], F32)
        mW = self.mark()
        wq, bwq = self.carve("wq", [128, 8, 8, 128], BF16)
        wkv, bwkv = self.carve("wkv", [128, 8, 512], BF16)
        U, bU = qTf.rearrange("p (h k) -> p h k", h=16), bqT

        wsrc = d["attn_w_qkv"][j]
        for hh in range(2):
            for cg in range(2):
                c0 = cg * 512 + hh * 256
                for ci in range(4):
                    self.dma("pool", wq[:, :, 4 * cg + ci, hh * 64:(hh + 1) * 64],
                             wsrc[:, c0 + 64 * ci:c0 + 64 * ci + 64].rearrange("(k p) n -> p k n", p=128), [], [bwq])
        self.dma("pool", wkv, wsrc[:, 1024:1536].rearrange("(k p) n -> p k n", p=128), [], [bwkv])
        self.dma("pool", wo, d["attn_w_o"][j].rearrange("(k p) n -> p k n", p=128), [], [bwo])
        bsrc = d["attn_b_qkv"][j]
        for hh in range(2):
            for cg in range(2):
                c0 = cg * 512 + hh * 256
                self.dma("sp", bq[hh * 64:(hh + 1) * 64, 4 * cg:4 * cg + 4],
                         bsrc[c0:c0 + 256].rearrange("(c n) -> n c", n=64), [], [bsm], allow_slow_non_contiguous=True)
        self.dma("sp", bk, bsrc[1024:1280].rearrange("(c p) -> p c", p=128), [], [bsm], allow_slow_non_contiguous=True)
        self.dma("sp", snk, d["attn_sinks"][j].partition_broadcast(128), [], [bsm])
        self.dma("sp", brow[0:1, :], bsrc[1024:1536].rearrange("(o n) -> o n", o=1), [], [bbrow])
        self.dma("sp", brow[32:33, :], d["attn_b_o"][j, 0:512].rearrange("(o n) -> o n", o=1), [], [bbrow])
        self.dma("sp", brow[64:65, :], d["attn_b_o"][j, 512:1024].rearrange("(o n) -> o n", o=1), [], [bbrow])
        self.dma("sp", cstt[:, 0, :], d["c_antib"], [], [bcs])
        self.dma("sp", cstt[:, 1, :], d["c_sels"], [], [bcs])
        self.dma("sp", cstt[:, 2, :], d["c_negb"], [], [bcs])
        self.op("dve", "tensor_scalar", [bsm], [bsm], out=bq, in0=bq, scalar1=0.125, scalar2=None, op0=ALU.mult)
        self.op("dve", "tensor_scalar", [bsm], [bsm], out=nsnk, in0=snk, scalar1=-1.0, scalar2=None, op0=ALU.mult)
        gt = d["gtab"]
        self.dma("pool", U, bass.AP(gt.tensor, 0, [[1, 128], [384, 16], [1, 256]]), [self.bgtab], [bU])
        cst = (cstt[:, 0, :], cstt[:, 1, :], cstt[:, 2, :], bcs)
        import os
        dbg = os.environ.get("KATT", "")
        if dbg == "a0":
            return
        self.build_bm(bm, bbm, U, bU, False, cst)
        self.load_ln(l, 0)
        if dbg == "a1":
            return

        pend = None
        pri = 0
        for bi, (t0, t1) in enumerate(BLKS):
            c0, n = t0 * 128, (t1 - t0) * 128
            xtb = self.bxt[t0:t1]
            for c in range(2):
                bi_ = pri % 4
                pri += 1
                ps, bps = self.psf[:, bi_, 0:n], self.bpf[bi_]
                for k in range(8):
                    self.mm(ps, wkv[:, k, c * 128:(c + 1) * 128], self.XT[:, k, c0:c0 + n], k == 0, k == 7,
                            [bwkv] + xtb, [bps])
                self.act(kT[:, c, c0:c0 + n], ps, AF.Identity, [bps, bsm], bkTt[t0:t1], bias=bk[:, c:c + 1], scale=1.0)
            for c in range(8):
                bi_ = pri % 4
                pri += 1
                ps, bps = self.psf[:, bi_, 0:n], self.bpf[bi_]
                for k in range(8):
                    self.mm(ps, wq[:, k, c, :], self.XT[:, k, c0:c0 + n], k == 0, k == 7, [bwq] + xtb, [bps])
                self.act(qT[:, c, 0:n], ps, AF.Identity, [bps, bsm], [bqT], bias=bq[:, c:c + 1], scale=0.125)
            for t in range(t0, t1):
                if dbg == "p1":
                    continue
                bi_ = pri % 4
                pri += 1
                ps, bps = self.psf[:, bi_, :], self.bpf[bi_]
                self.mm(ps, self.onesf[0:1, :], brow[0:1, :], True, False, [self.bconst, bbrow], [bps], inc=False)
                for k in range(8):
                    self.mm(ps, self.XT[:, k, t * 128:(t + 1) * 128], wkv[:, k, :], False, k == 7, [bwkv, self.bxt[t]], [bps])
                if t < 15:
                    self.evac(Vb[:, t, :], ps[:, 256:512], [bps], [bVb[t]])
                if t >= 15 and dbg != "p2":
                    self.evac(kvo, ps, [bps], [bkvo], "act")
                    self.op("dve", "tensor_copy", [bkvo], [bVb[t]], out=Vb[:, t, :], in_=kvo[:, 256:512])
                    if dbg == "p3a" and t == 16:
                        continue
                    if dbg == "p3c":
                        continue
                    if dbg == "p3d":
                        if t == 15:
                            self.dma("sp", d["nk_p"][j], kvo[:, 0:256], [bkvo], [])
                        continue
                    if dbg == "p3b" and t == 15:
                        continue
                    if t == 15:
                        self.dma("sp", d["nk_p"][j], kvo[:, 0:256], [bkvo], [])
                        self.dma("sp", d["nv_p"][j], kvo[:, 256:512], [bkvo], [])
                    else:
                        for b in range(16):
                            self.dma("sp", d["nk_s"][j, b, 120:128, :], kvo[8 * b:8 * b + 8, 0:256], [bkvo], [])
                            self.dma("sp", d["nv_s"][j, b, 120:128, :], kvo[8 * b:8 * b + 8, 256:512], [bkvo], [])
            if dbg in ("p1", "p2", "p3", "p3a", "p3b", "p3c", "p3d"):
                continue
            if t0 == 16:
                self.dma("sp", d["nk_s"][j, :, 0:120, :], d["cache_k"][j, :, 8:128, :], [self.bd2d], [], own=self.bd2d)
                self.dma("sp", d["nv_s"][j, :, 0:120, :], d["cache_v"][j, :, 8:128, :], [self.bd2d], [], own=self.bd2d)
                if dbg == "p4":
                    continue
                kcr = w8.rearrange("p (b c) -> p b c", b=16)
                self.dma("pool", kcr, bass.AP(gt.tensor, 0, [[1, 128], [384, 16], [1, 256]]), [self.bgtab], bw8)
                self.build_bm(bm, bbm, kcr, bw8, True, cst)
                self.release(mW)
                vc, bvc = self.carve("vc", [128, 16, 256], BF16)
                kcT, bkcT = self.carve("kcT", [128, 2, 16, 128], BF16)
                bm16, bbm16 = self.carve("bm16", [128, 16, 128], BF16)
                QM1, bQM1 = self.carve("QM", [128, 16, 128], BF16)
                QM, bQM = [QM1, QM1], [bQM1, bQM1]
                self.dma("sp", bm16, d["c_bm16"], [], [bbm16])
                self.dma("pool", kcr, d["cache_k"][j].rearrange("b k c -> k b c"), [], bw8)
                self.dma("pool", vc, d["cache_v"][j].rearrange("b k c -> k b c"), [], [bvc])
                for kc in range(2):
                    for b8 in range(2):
                        pb, bpb = self.next_pb()
                        pv = pb.rearrange("p (b n) -> p b n", b=8)
                        for bb in range(8):
                            b = b8 * 8 + bb
                            self.tr(pv[:, bb, :], kcr[:, b, kc * 128:(kc + 1) * 128], self.identb[:], bw8 + [self.bconst],
                                    [bpb], inc=(bb == 7))
                        self.evac(kcT[:, kc, b8 * 8:b8 * 8 + 8, :], pv, [bpb], [bkcT])
                smp = (kcT, bkcT, vc, bvc, QM, bQM, bm16, bbm16)
            if dbg == "a2":
                continue
            for t in range(t0, t1):
                tl = t - t0
                sample = t == 16
                if dbg == "a3" and sample:
                    continue
                den, rden = st2[:, 32:48], st2[:, 48:64]
                Opr, bOpr = self.pair(2), self.bpair(2)
                for hg in range(4):
                    Spr, bSpr = self.pair(hg % 2), self.bpair(hg % 2)
                    Sv = Spr.rearrange("p (h k) -> p h k", h=4)
                    lo = 128 if t == 0 else 0
                    for i in range(4):
                        h = 4 * hg + i
                        c, hh = (h % 4) + 4 * (h // 8), (h // 4) % 2
                        kc = hg // 2
                        rows = slice(hh * 64, hh * 64 + 64)
                        bS = [bSpr[i // 2]]
                        first = (i % 2 == 0)
                        qcols = qT[rows, c, tl * 128:(tl + 1) * 128]
                        if not sample:
                            kb = bkTt[max(t - 1, 0):t + 1]
                            self.mm(Sv[:, i, lo:256], qcols, kT[rows, kc, (t - 1) * 128 + lo:(t + 1) * 128], first, False,
                                    [bqT] + kb, bS, inc=False)
                        else:
                            kcT, bkcT, vc, bvc, QM, bQM, bm16, bbm16 = smp
                            qs = (4 * hg + i) % 2
                            self.op("dve", "tensor_tensor", [bqT, bbm16], [bQM[qs]], out=QM[qs][rows],
                                    in0=qcols.unsqueeze(1).to_broadcast([64, 16, 128]), in1=bm16[rows], op=ALU.mult)
                            for b in range(16):
                                self.mm(Sv[:, i, 0:128], QM[qs][rows, b, :], kcT[rows, kc, b, :], first and b == 0, False,
                                        [bQM[qs], bkcT], bS, inc=False)
                            self.mm(Sv[:, i, 128:256], qcols, kT[rows, kc, 2048:2176], False, False, [bqT, bkTt[16]], bS,
                                    inc=False)
                        self.mm(Sv[:, i, lo:256], self.identb[:], bm[:, h, lo:256], False, True, [bbm, self.bconst], bS,
                                inc=True)
                    mx, nmx, rs, t4, es = (st2[:, 4 * q:4 * q + 4] for q in range(5))
                    hs = slice(4 * hg, 4 * hg + 4)
                    self.op("dve", "tensor_reduce", bSpr, [bst2], out=mx, in_=Sv[:, :, lo:256], axis=AX.X, op=ALU.max)
                    self.op("dve", "scalar_tensor_tensor", [bst2, bsm], [bst2], out=nmx, in0=mx, scalar=-1.0,
                            in1=nsnk[:, hs], op0=ALU.mult, op1=ALU.min)
                    for i in range(4):
                        self.act(P_[:, i, lo:256], Sv[:, i, lo:256], AF.Exp, [bSpr[i // 2], bst2], [bP, bst2],
                                 bias=nmx[:, i:i + 1], scale=1.0, accum_out=rs[:, i:i + 1])
                    self.op("dve", "tensor_tensor", [bst2, bsm], [bst2], out=t4, in0=snk[:, hs], in1=nmx, op=ALU.add)
                    self.act(es, t4, AF.Exp, [bst2], [bst2])
                    self.op("dve", "tensor_tensor", [bst2], [bst2], out=den[:, hs], in0=rs, in1=es, op=ALU.add)
                    pb, bpb = self.next_pb()
                    pv = pb.rearrange("p (h a n) -> p h a n", h=4, a=2)
                    kts = [1] if t == 0 else [0, 1]
                    for i in range(4):
                        for kt in kts:
                            self.tr(pv[:, i, kt, :], P_[:, i, kt * 128:(kt + 1) * 128], self.identb[:], [bP, self.bconst],
                                    [bpb], inc=(i == 3 and kt == 1))
                    if t == 0:
                        self.evac(PT[:, :, 1, :], pv[:, :, 1, :], [bpb], [bPT])
                    else:
                        self.evac(PT, pv, [bpb], [bPT])
                    for i in range(4):
                        h = 4 * hg + i
                        bO = [bOpr[h // 8]]
                        oc = Opr[:, h * 64:(h + 1) * 64]
                        firstb = (h % 8 == 0)
                        if not sample:
                            for kt in kts:
                                self.mm(oc, PT[:, i, kt, :], Vb[:, t - 1 + kt, hg * 64:(hg + 1) * 64],
                                        firstb and kt == kts[0], kt == 1, [bPT, bVb[t - 1 + kt]], bO,
                                        inc=(kt == 1 and i == 3))
                        else:
                            qs = (4 * hg + i) % 2
                            self.op("dve", "tensor_tensor", [bPT, bbm16], [bQM[qs]], out=QM[qs],
                                    in0=PT[:, i, 0, :].unsqueeze(1).to_broadcast([128, 16, 128]), in1=bm16, op=ALU.mult)
                            for b in range(16):
                                self.mm(oc, QM[qs][:, b, :], vc[:, b, hg * 64:(hg + 1) * 64], firstb and b == 0, False,
                                        [bQM[qs], bvc], bO, inc=False)
                            self.mm(oc, PT[:, i, 1, :], Vb[:, 16, hg * 64:(hg + 1) * 64], False, True, [bPT, bVb[16]], bO,
                                    inc=True)
                self.op("dve", "reciprocal", [bst2], [bst2], out=rden, in_=den)
                self.op("dve", "tensor_tensor", bOpr + [bst2], [bOb], out=Ob, in0=Opr.rearrange("p (h e) -> p h e", h=16),
                        in1=rden.unsqueeze(2).to_broadcast([128, 16, 64]), op=ALU.mult)
                pb, bpb = self.next_pb()
                pv = pb.rearrange("p (c n) -> p c n", c=8)
                Obf = Ob.rearrange("p h e -> p (h e)")
                for c in range(8):
                    self.tr(pv[:, c, :], Obf[:, c * 128:(c + 1) * 128], self.identb[:], [bOb, self.bconst], [bpb], inc=(c == 7))
                self.evac(OT, pv, [bpb], [bOT])
                ypr, bypr = self.pair(0), self.bpair(0)
                for half in range(2):
                    yh = ypr[:, half * 512:(half + 1) * 512]
                    pr_ = 32 * (half + 1)
                    self.mm(yh, self.onesf[pr_:pr_ + 1, :], brow[pr_:pr_ + 1, :], True, False,
                            [self.bconst, bbrow], [bypr[half]], inc=False)
                    for k in range(8):
                        self.mm(yh, OT[:, k, :], wo[:, k, half * 512:(half + 1) * 512], False, k == 7, [bOT, bwo], [bypr[half]])
                x = self.XR[:, t, :]
                self.op("dve", "scalar_tensor_tensor", [self.bxr[t]] + bypr, [self.bxr[t]], out=x, in0=x, scalar=ALPHA,
                        in1=ypr, op0=ALU.mult, op1=ALU.add)
                pend = self.ln_all(pend, t)
        if pend is not None:
            self.make_xt(*pend)

    def pool(self, l):
        d = self.dr
        self.release(0)
        mt, bmt = self.carve("poolmt", [128, 24, 128], F32)
        pw, bpw = self.carve("poolw", [128, 4, 2, 256], BF16)
        psc, bpsc = self.carve("poolsc", [128, D], F32)
        pfx, bpfx = self.carve("poolpfx", [128, 2, D], F32)
        dT, bdT = [], []
        for s in range(2):
            a, b = self.carve(f"dT{s}", [128, 8, 128], BF16)
            dT.append(a), bdT.append(b)
        tmp, btmp = self.carve("pooltmp", [128, D], F32)
        self.dma("sp", mt, d["c_poolmt"], [], [bmt])
        self.dma("pool", pw, d["pool_w"][0].rearrange("g (kk p) n -> p g kk n", p=128), [], [bpw])
        self.dma("sp", psc, d["pool_scale"][0].partition_broadcast(128), [], [bpsc])
        self.dma("sp", pfx[0:120, 0, :], d["state_pool"][0:8].rearrange("b r c -> (b r) c"), [], [bpfx])
        self.dma("sp", pfx[0:120, 1, :], d["state_pool"][8:16].rearrange("b r c -> (b r) c"), [], [bpfx])
        self.dma("sp", d["npool_p"], self.XR[113:128, 15, :], [self.bxr[15]], [])
        self.dma("sp", d["npool_s"][:, 0:7, :], d["state_pool"][:, 8:15, :], [self.bd2d], [], own=self.bd2d)
        for b in range(16):
            self.dma("sp", d["npool_s"][b, 7:15, :], self.XR[8 * b:8 * b + 8, 16, :], [self.bxr[16]], [])
        self.load_ln(l, 0)

        def diff(t):
            s = t % 2
            pr, bpr = self.pair(s), self.bpair(s)
            pv = pr.rearrange("p (c n) -> p c n", c=8)
            for c in range(8):
                g = c // 2
                cs = slice(c * 128, (c + 1) * 128)
                first = (c % 4 == 0)
                bb = [bpr[c // 4]]
                if t == 16:
                    self.mm(pv[:, c, :], pfx[0:120, 0, cs], mt[0:120, 16 + g, :], first, False, [bpfx, bmt], bb, inc=False)
                    self.mm(pv[:, c, :], pfx[0:120, 1, cs], mt[0:120, 20 + g, :], False, False, [bpfx, bmt], bb, inc=False)
                    self.mm(pv[:, c, :], self.XR[:, 16, cs], mt[:, 12 + g, :], False, True, [self.bxr[16], bmt], bb, inc=True)
                elif t == 0:
                    self.mm(pv[:, c, :], self.XR[:, 0, cs], mt[:, 8 + g, :], first, True, [self.bxr[0], bmt], bb, inc=True)
                else:
                    self.mm(pv[:, c, :], self.XR[:, t - 1, cs], mt[:, g, :], first, False, [self.bxr[t - 1], bmt], bb, inc=False)
                    self.mm(pv[:, c, :], self.XR[:, t, cs], mt[:, 4 + g, :], False, True, [self.bxr[t], bmt], bb, inc=True)
            self.evac(dT[s], pv, bpr, [bdT[s]])

        pend = [None]

        def update(t):
            s = t % 2
            ypr, bypr = self.pair(2), self.bpair(2)
            for g in range(4):
                for kk in range(2):
                    self.mm(ypr[:, g * 256:(g + 1) * 256], dT[s][:, 2 * g + kk, :], pw[:, g, kk, :], (g % 2 == 0) and kk == 0,
                            kk == 1, [bdT[s], bpw], [bypr[g // 2]], inc=(kk == 1))
            self.op("dve", "tensor_tensor", bypr + [bpsc], [btmp], out=tmp, in0=ypr, in1=psc, op=ALU.mult)
            x = self.XR[:, t, :]
            self.op("dve", "scalar_tensor_tensor", [self.bxr[t], btmp], [self.bxr[t]], out=x, in0=x, scalar=ALPHA, in1=tmp,
                    op0=ALU.mult, op1=ALU.add)
            pend[0] = self.ln_all(pend[0], t)

        for t in range(NT):
            diff(t)
            if t >= 1:
                update(t - 1)
        update(16)
        if pend[0] is not None:
            self.make_xt(*pend[0])

    def ssm(self, l):
        d = self.dr
        self.release(0)
        win = d["ssm_w_in"][0]
        negm_p, bnp = self.carve("negm_p", [128, 1024], BF16)
        negm_s, bns = self.carve("negm_s", [128, 1024], BF16)
        c8, bc8 = self.carve("c8", [8, 8 * 128 + 512 + 128 + 128 + 64], F32)
        sel8 = c8[:, 0:1024].rearrange("p (r n) -> p r n", r=8)
        scan_p, scan_s = c8[:, 1024:1536], c8[:, 1536:1664]
        apar = c8[:, 1664:1792]
        selj = c8[:, 1792:1856].rearrange("p (b j) -> p b j", b=16)
        seqm, bseqm = self.carve("seqm", [128, 16], F32)
        self.dma("sp", negm_p, d["c_negm_p"], [], [bnp])
        self.dma("sp", negm_s, d["c_negm_s"], [], [bns])
        self.dma("sp", sel8, d["c_sel8"], [], [bc8])
        self.dma("sp", scan_p, d["c_scan_p"], [], [bc8])
        self.dma("sp", scan_s, d["c_scan_s"], [], [bc8])
        self.dma("sp", apar, d["c_apar"], [], [bc8])
        self.dma("sp", selj, d["c_selj"], [], [bc8])
        self.dma("sp", seqm, d["c_seqm"], [], [bseqm])
        self.load_ln(l, 0)
        mL = self.mark()
        pend = None
        for g in range(4):
            self.release(mL)
            wx, bwx = self.carve("wx", [128, 8, 512], BF16)
            wz, bwz = self.carve("wz", [128, 8, 512], BF16)
            wBC, bwBC = self.carve("wBC", [128, 8, 256], BF16)
            wdt, bwdt = self.carve("wdt", [128, 8, 8], BF16)
            wout, bwout = self.carve("wout", [128, 4, 1024], BF16)
            cw, bcw = self.carve("convw", [128, 6, 5], F32)
            hp, bhp = self.carve("headp", [8, 4], F32)
            dbc, bdbc = self.carve("dbc", [128, 8], F32)
            nw, bnw = self.carve("nw", [128, 512], F32)
            wr = lambda c0, w_: win[:, c0:c0 + w_].rearrange("(k p) n -> p k n", p=128)
            self.dma("pool", wx, wr(2048 + 512 * g, 512), [], [bwx])
            self.dma("pool", wBC[:, :, 0:128], wr(4096 + 128 * g, 128), [], [bwBC])
            self.dma("pool", wBC[:, :, 128:256], wr(4608 + 128 * g, 128), [], [bwBC])
            self.dma("pool", wdt, wr(5120 + 8 * g, 8), [], [bwdt])
            self.dma("pool", wz, wr(512 * g, 512), [], [bwz])
            self.dma("pool", wout, d["ssm_w_out"][0][512 * g:512 * g + 512, :].rearrange("(k p) n -> p k n", p=128), [], [bwout])
            chbase = [512 * g + 128 * i for i in range(4)] + [2048 + 128 * g, 2560 + 128 * g]
            cwsrc, cbsrc = d["ssm_conv_w"][0], d["ssm_conv_b"][0]
            for cc in range(6):
                cb = chbase[cc]
                self.dma("sp", cw[:, cc, 0:4], cwsrc[:, cb:cb + 128].rearrange("j p -> p j"), [], [bcw],
                         allow_slow_non_contiguous=True)
                self.dma("sp", cw[:, cc, 4:5], cbsrc[cb:cb + 128].rearrange("(p o) -> p o", o=1), [], [bcw])
            self.dma("sp", hp[:, 0:1], d["ssm_dt_bias"][0][8 * g:8 * g + 8].rearrange("(p o) -> p o", o=1), [], [bhp])
            self.dma("sp", hp[:, 1:2], d["ssm_a_log"][0][8 * g:8 * g + 8].rearrange("(p o) -> p o", o=1), [], [bhp])
            self.dma("sp", dbc, d["ssm_d"][0][8 * g:8 * g + 8].partition_broadcast(128), [], [bdbc])
            self.dma("sp", nw, d["ssm_norm_w"][0][512 * g:512 * g + 512].partition_broadcast(128), [], [bnw])
            self.act(hp[:, 2:3], hp[:, 1:2], AF.Exp, [bhp], [bhp])
            self.op("dve", "tensor_scalar", [bhp], [bhp], out=hp[:, 2:3], in0=hp[:, 2:3], scalar1=-1.0, scalar2=None,
                    op0=ALU.mult)
            cwk, _ = self.carve("convwork", [128, 1536], F32)
            acc = [cwk[:, 0:512], cwk[:, 512:1024]]
            ctmp = cwk[:, 1024:1536]
            bacc = [Buf("acc0", self.retired), Buf("acc1", self.retired)]
            bctmp = Buf("ctmp", self.retired)
            bcwk = bacc + [bctmp]
            for b in bcwk:
                self.abufs.append((self.aoff, b))
            xcT, bxcT = self.carve("xcT", [128, 6, 512], BF16)
            dtb_, bdt = self.carve("dtbuf", [8, 3, 512], F32)
            el, bel = self.carve("elast", [8, 16], F32)
            xs, bxs = self.carve("xs", [128, 640], BF16)
            tmd, btmd = self.carve("tmd", [128, 32], F32)
            Ef, bE_ = self.carve("E", [128, 1024], F32)
            E = Ef.rearrange("p (r n) -> p r n", r=8)
            cst_ = Ef[:, 0:768]
            bcst = bE_
            CM = Ef.bitcast(BF16).rearrange("p (b n) -> p b n", b=16)
            bCM = bE_
            WT, bWT = self.carve("WT", [128, 8, 128], BF16)
            xD, bxD = self.carve("xD", [128, 512], BF16)
            sz, bsz = self.carve("sz", [128, 512], F32)
            y1, by1 = self.carve("y1", [128, 512], F32)
            hout, bhout = y1.rearrange("p (j n) -> p j n", j=4), by1
            yn, byn = self.carve("yn", [128, 512], BF16)
            ynT, bynT = self.carve("ynT", [128, 4, 128], BF16)
            dg, bdg = self.carve("dg", [8, 64], F32)
            mP = self.mark()
            rawc, _ = self.carve("rawc", [128, 2, 515], F32)
            brawc = [Buf("rawc0", self.retired), Buf("rawc1", self.retired)]
            for b in brawc:
                self.abufs.append((self.aoff, b))
            hist, bhist = self.carve("hist", [128, 6, 3], F32)
            hT, bhT = self.carve("hT", [128, 512], F32)
            hTb, bhTb = self.carve("hTb", [128, 512], BF16)
            xd, bxd = self.carve("xd", [128, 512], BF16)
            dbs, bdbs = self.carve("dbs", [128, 64], F32)
            self.op("pool", "memset", [], [bhT], ap=hT, constant=0.0)
            self.op("pool", "memset", [], [bhTb], ap=hTb, constant=0.0)
            self.op("pool", "memset", [], [bhist], ap=hist, constant=0.0)
            wsl = [wx[:, :, 0:128], wx[:, :, 128:256], wx[:, :, 256:384], wx[:, :, 384:512], wBC[:, :, 0:128], wBC[:, :, 128:256]]
            wsb = [bwx, bwx, bwx, bwx, bwBC, bwBC]
            decT, bEt, acum = (dtb_[:, q, :] for q in range(3))

            def dt_path(psd, bpsd, n, L, scanm):
                nb = n // L
                t0, t1, ac = decT[:, 0:n], bEt[:, 0:n], acum[:, 0:n]
                v3 = lambda a_: a_.rearrange("p (b t) -> p b t", t=L)
                self.act(t0, psd, AF.Exp, [bpsd, bhp], [bdt], bias=hp[:, 0:1], scale=1.0)
                self.act(t0, t0, AF.Ln, [bdt], [bdt], bias=1.0, scale=1.0)
                self.act(t1, t0, AF.Ln, [bdt], [bdt])
                self.op("dve", "tensor_scalar", [bdt, bhp], [bdt], out=t0, in0=t0, scalar1=hp[:, 2:3], scalar2=None,
                        op0=ALU.mult)
                self.op("dve", "tensor_tensor_scan", [bdt, bc8], [bdt], out=ac, data0=scanm, data1=t0, initial=0.0,
                        op0=ALU.mult, op1=ALU.add)
                self.op("dve", "tensor_tensor", [bdt], [bdt], out=t1, in0=t1, in1=ac, op=ALU.subtract)
                self.op("dve", "tensor_tensor", [bdt], [bdt], out=v3(t0), in0=v3(t1),
                        in1=v3(ac)[:, :, L - 1:L].to_broadcast([8, nb, L]), op=ALU.add)
                self.act(t0, t0, AF.Exp, [bdt], [bdt])
                self.act(el[:, 0:nb], v3(ac)[:, :, L - 1], AF.Exp, [bdt], [bel])

            def conv_chunk(cc, src_views, bsrc, out_view, shape):
                s = cc % 2
                a = acc[s]
                av = a if shape is None else a[:, 0:shape[0] * shape[1]].rearrange("p (b t) -> p b t", b=shape[0])
                tv = ctmp if shape is None else ctmp[:, 0:shape[0] * shape[1]].rearrange("p (b t) -> p b t", b=shape[0])
                self.op("pool", "tensor_scalar", [bsrc, bcw], [bacc[s]], out=av, in0=src_views[0], scalar1=cw[:, cc, 0:1],
                        scalar2=cw[:, cc, 4:5], op0=ALU.mult, op1=ALU.add)
                for jj in range(1, 4):
                    self.op("pool", "tensor_scalar", [bsrc, bcw], [bctmp], out=tv, in0=src_views[jj], scalar1=cw[:, cc, jj:jj + 1],
                            scalar2=None, op0=ALU.mult)
                    self.op("pool", "tensor_tensor", [bacc[s], bctmp], [bacc[s]], out=av, in0=av, in1=tv, op=ALU.add)
                self.act(out_view, av, AF.Silu, [bacc[s]], [bxcT])

            def chunk_core(t, cols, negm, bnegm, yi_fn):
                nonlocal pend
                xtt = [self.bxt[t]]
                pb, bpb = self.next_pb()
                pv = pb[:, 0:640].rearrange("p (c n) -> p c n", c=5)
                for cc in range(5):
                    self.tr(pv[:, cc, :], xcT[:, cc, cols], self.identb[:], [bxcT, self.bconst], [bpb], inc=(cc == 4))
                self.evac(xs, pb[:, 0:640], [bpb], [bxs])
                p2, bp2 = self.psf[:, 2, :], self.bpf[2]
                for q, srcv in enumerate((bEt, acum, decT)):
                    self.tr(p2[:, 8 * q:8 * q + 8], srcv[:, cols], self.identf[0:8, 0:8], [bdt, self.bconst], [bp2], inc=(q == 2))
                self.evac(tmd[:, 0:24], p2[:, 0:24], [bp2], [btmd], "dve")
                self.act(tmd[:, 24:32], tmd[:, 8:16], AF.Exp, [btmd], [btmd])
                p3, bp3 = self.psf[:, 3, :], self.bpf[3]
                for k in range(8):
                    self.mm(p3, self.XT[:, k, t * 128:(t + 1) * 128], wz[:, k, :], k == 0, k == 7, [bwz] + xtt, [bp3])
                self.act(sz, p3, AF.Silu, [bp3], [bsz])
                self.mm(p2[:, 128:256], xcT[:, 4, cols], xcT[:, 5, cols], False, True, [bxcT], [bp2])
                spr, bspr = self.pair(0), self.bpair(0)
                sv = spr.rearrange("p (r n) -> p r n", r=8)
                for r in range(8):
                    self.mm(sv[:, r, :], sel8[:, r, :], acum[:, cols], r % 4 == 0, False, [bc8, bdt], [bspr[r // 4]], inc=False)
                for a2 in range(2):
                    self.mm(spr[:, a2 * 512:(a2 + 1) * 512], self.identb[:], negm[:, a2 * 512:(a2 + 1) * 512], False, True,
                            [self.bconst, bnegm], [bspr[a2]], inc=True)
                for r in range(8):
                    self.act(E[:, r, :], sv[:, r, :], AF.Exp, [bspr[r // 4], btmd], [bE_], bias=tmd[:, r:r + 1], scale=1.0)
                self.op("dve", "tensor_tensor", [bE_, bp2], [bWT], out=WT, in0=E,
                        in1=p2[:, 128:256].unsqueeze(1).to_broadcast([128, 8, 128]), op=ALU.mult)
                v8 = lambda a_: a_.rearrange("p (r e) -> p r e", r=8)
                self.op("dve", "tensor_tensor", [bxs, bdbc], [bxD], out=v8(xD), in0=v8(xs[:, 0:512]),
                        in1=dbc.unsqueeze(2).to_broadcast([128, 8, 64]), op=ALU.mult)
                p4, bp4 = self.psf[:, 4, :], self.bpf[4]
                for r in range(8):
                    self.mm(p4[:, r * 64:(r + 1) * 64], WT[:, r, :], xs[:, r * 64:(r + 1) * 64], r == 0, False, [bWT, bxs], [bp4],
                            inc=False)
                self.mm(p4, self.identb[:], xD, False, True, [self.bconst, bxD], [bp4], inc=True)
                p5, bp5 = self.psf[:, 5, :], self.bpf[5]
                yi_fn(p5, bp5)
                self.op("dve", "tensor_tensor", [bp5, btmd], [by1], out=v8(y1), in0=v8(p5),
                        in1=tmd[:, 24:32].unsqueeze(2).to_broadcast([128, 8, 64]), op=ALU.mult)
                self.op("dve", "tensor_tensor", [by1, bp4], [by1], out=y1, in0=y1, in1=p4, op=ALU.add)
                self.op("dve", "tensor_tensor", [by1, bsz], [by1], out=y1, in0=y1, in1=sz, op=ALU.mult)
                st, bst = self.next_stat()
                self.act(sz, y1, AF.Square, [by1, bsz], [bsz, bst], accum_out=st[:, 0:1])
                self.op("dve", "tensor_scalar", [bst], [bst], out=st[:, 1:2], in0=st[:, 0:1], scalar1=1.0 / 512.0, scalar2=RMS_EPS,
                        op0=ALU.mult, op1=ALU.add)
                self.op("pool", "tensor_tensor", [bst, self.bconst], [bst], out=st[:, 2:3], in0=st[:, 1:2], in1=self.mhalf[:],
                        op=ALU.pow)
                self.op("dve", "scalar_tensor_tensor", [by1, bst, bnw], [byn], out=yn, in0=y1, scalar=st[:, 2:3], in1=nw,
                        op0=ALU.mult, op1=ALU.mult)
                pb, bpb = self.next_pb()
                pv = pb[:, 0:512].rearrange("p (c n) -> p c n", c=4)
                for c in range(4):
                    self.tr(pv[:, c, :], yn[:, c * 128:(c + 1) * 128], self.identb[:], [byn, self.bconst], [bpb], inc=(c == 3))
                self.evac(ynT, pv, [bpb], [bynT])
                opr, bopr = self.pair(0), self.bpair(0)
                for half in range(2):
                    for k in range(4):
                        self.mm(opr[:, half * 512:(half + 1) * 512], ynT[:, k, :], wout[:, k, half * 512:(half + 1) * 512], k == 0,
                                k == 3, [bynT, bwout], [bopr[half]])
                x = self.XR[:, t, :]
                if g == 0:
                    self.op("dve", "scalar_tensor_tensor", [self.bxr[t]] + bopr, [self.bxr[t]], out=x, in0=x, scalar=ALPHA,
                            in1=opr, op0=ALU.mult, op1=ALU.add)
                else:
                    self.op("dve", "tensor_tensor", [self.bxr[t]] + bopr, [self.bxr[t]], out=x, in0=x, in1=opr, op=ALU.add)
                if g == 3:
                    pend = self.ln_all(pend, t)

            def conv_state_proj(t, rows):
                cpr, bcpr = self.pair(0), self.bpair(0)
                tc_ = slice(t * 128, (t + 1) * 128)
                for k in range(8):
                    self.mm(cpr[:, 0:512], self.XT[:, k, tc_], wx[:, k, :], k == 0, k == 7, [bwx, self.bxt[t]], [bcpr[0]])
                for k in range(8):
                    self.mm(cpr[:, 512:768], self.XT[:, k, tc_], wBC[:, k, :], k == 0, k == 7, [bwBC, self.bxt[t]], [bcpr[1]])
                self.evac(cst_[rows, :], cpr[rows, 0:768], bcpr, [bcst], "act")

            osl = ((512 * g, 512, 0), (2048 + 128 * g, 128, 512), (2560 + 128 * g, 128, 640))
            for bi, (t0, t1) in enumerate(BLKS[:4]):
                c0 = t0 * 128
                xtb = self.bxt[t0:t1]
                psd, bpsd = self.psf[0:8, 0, :], self.bpf[0]
                for k in range(8):
                    self.mm(psd, wdt[:, k, :], self.XT[:, k, c0:c0 + 512], k == 0, k == 7, [bwdt] + xtb, [bpsd])
                dt_path(psd, bpsd, 512, 128, scan_p)
                for cc in range(6):
                    bi_ = 2 + cc % 4
                    s_ = cc % 2
                    ps, bps = self.psf[:, bi_, :], self.bpf[bi_]
                    for k in range(8):
                        self.mm(ps, wsl[cc][:, k, :], self.XT[:, k, c0:c0 + 512], k == 0, k == 7, [wsb[cc]] + xtb, [bps])
                    self.op("pool", "tensor_copy", [bhist], [brawc[s_]], out=rawc[:, s_, 0:3], in_=hist[:, cc, :])
                    self.evac(rawc[:, s_, 3:515], ps, [bps], [brawc[s_]], "act")
                    self.op("pool", "tensor_copy", [brawc[s_]], [bhist], out=hist[:, cc, :], in_=rawc[:, s_, 512:515])
                    conv_chunk(cc, [rawc[:, s_, jj:jj + 512] for jj in range(4)], brawc[s_], xcT[:, cc, :], None)
                for t in range(t0, t1):
                    tl = t - t0
                    cols = slice(tl * 128, (tl + 1) * 128)

                    def yi_fn(p5, bp5, cols=cols):
                        self.mm(p5, xcT[:, 5, cols], hTb, True, True, [bxcT, bhTb], [bp5])
                    chunk_core(t, cols, negm_p, bnp, yi_fn)
                    v8 = lambda a_: a_.rearrange("p (r e) -> p r e", r=8)
                    self.op("dve", "tensor_tensor", [bxs, btmd], [bxd], out=v8(xd), in0=v8(xs[:, 0:512]),
                            in1=tmd[:, 16:24].unsqueeze(2).to_broadcast([128, 8, 64]), op=ALU.mult)
                    p3, bp3 = self.psf[:, 3, :], self.bpf[3]
                    self.mm(p3, xs[:, 512:640], xd, True, True, [bxs, bxd], [bp3])
                    self.op("dve", "tensor_scalar", [bel, self.bconst], [bdg], out=dg[:, 0:8], in0=self.identf[0:8, 0:8],
                            scalar1=el[:, tl:tl + 1], scalar2=None, op0=ALU.mult)
                    p2, bp2 = self.psf[:, 2, :], self.bpf[2]
                    self.mm(p2[:, 256:264], self.onesf[0:8, :], dg[:, 0:8], False, True, [self.bconst, bdg], [bp2])
                    self.evac(dbs[:, 0:8], p2[:, 256:264], [bp2], [bdbs], "dve")
                    self.op("dve", "tensor_tensor", [bhT, bdbs], [bhT], out=v8(hT), in0=v8(hT),
                            in1=dbs[:, 0:8].unsqueeze(2).to_broadcast([128, 8, 64]), op=ALU.mult)
                    self.op("dve", "tensor_tensor", [bhT, bp3], [bhT], out=hT, in0=hT, in1=p3, op=ALU.add)
                    self.act(hTb, hT, AF.Copy, [bhT], [bhTb])
                if bi == 3:
                    conv_state_proj(15, slice(96, 128))
                    for (o0, w_, s0) in osl:
                        self.dma("sp", d["nconv_p"][:, o0:o0 + w_], cst_[125:128, s0:s0 + w_], [bcst], [])
            p3, bp3 = self.psf[:, 3, :], self.bpf[3]
            pv = p3.rearrange("p (j n) -> p j n", j=4)
            for jj in range(4):
                self.tr(pv[:, jj, :], hT[:, jj * 128:(jj + 1) * 128], self.identf[:], [bhT, self.bconst], [bp3], inc=(jj == 3))
            self.evac(hout, pv, [bp3], [bhout], "act")
            self.dma("sp", d["nssm_p"][512 * g:512 * g + 512, :].rearrange("(j p) n -> p j n", p=128), hout, [bhout], [])

            self.release(mP)
            rawp, brawp = self.carve("rawp", [128, 6, 16, 11], F32)
            decs, bdecs = self.carve("decs", [128, 16, 4], F32)
            h0f, bh0f = self.carve("h0f", [128, 4, 128], F32)
            h0T, bh0T = self.carve("h0T", [128, 512], BF16)
            hnw, bhnw = self.carve("hnw", [128, 4, 128], F32)
            xdm, bxdm = self.carve("xdm", [128, 512], BF16)
            dcb, bdcb = self.carve("dcb", [128, 8], F32)
            bm16, bbm16 = self.carve("bm16", [128, 16, 128], BF16)
            self.dma("sp", bm16, d["c_bm16"], [], [bbm16])
            scv = cwk[0:48, 0:768]
            scs = d["state_conv"]
            for (o0, w_, s0) in osl:
                self.dma("sp", scv[:, s0:s0 + w_], scs[:, :, o0:o0 + w_].rearrange("b r c -> (b r) c"), [], bcwk)
            p2, bp2 = self.psf[:, 2, :], self.bpf[2]
            pvh = p2[:, 0:288].rearrange("p (c n) -> p c n", c=6)
            for cc in range(6):
                self.tr(pvh[:, cc, :], scv[:, cc * 128:(cc + 1) * 128], self.identf[0:48, 0:48], bcwk + [self.bconst], [bp2],
                        inc=(cc == 5))
            self.evac(rawp[:, :, :, 0:3], pvh.rearrange("p c (b r) -> p c b r", r=3), [bp2], [brawp], "dve")
            xtt = [self.bxt[16]]
            for cc in range(6):
                bi_ = 3 + cc % 3
                ps, bps = self.psf[:, bi_, 0:128], self.bpf[bi_]
                for k in range(8):
                    self.mm(ps, wsl[cc][:, k, :], self.XT[:, k, 2048:2176], k == 0, k == 7, [wsb[cc]] + xtt, [bps])
                self.evac(rawp[:, cc, :, 3:11], ps.rearrange("p (b t) -> p b t", t=8), [bps], [brawp], "act")
            psd, bpsd = self.psf[0:8, 0, 0:128], self.bpf[0]
            for k in range(8):
                self.mm(psd, wdt[:, k, :], self.XT[:, k, 2048:2176], k == 0, k == 7, [bwdt] + xtt, [bpsd])
            dt_path(psd, bpsd, 128, 8, scan_s)
            for cc in range(6):
                conv_chunk(cc, [rawp[:, cc, :, jj:jj + 8] for jj in range(4)], brawp,
                           xcT[:, cc, 0:128].rearrange("p (b t) -> p b t", t=8), (16, 8))
            conv_state_proj(16, slice(0, 128))
            for b in range(16):
                for (o0, w_, s0) in osl:
                    self.dma("sp", d["nconv_s"][b, :, o0:o0 + w_], cst_[8 * b + 5:8 * b + 8, s0:s0 + w_], [bcst], [])
            self.op("dve", "tensor_tensor", [bel, bc8], [bdg], out=dg.rearrange("p (b j) -> p b j", b=16), in0=selj,
                    in1=el[:, 0:16].unsqueeze(2).to_broadcast([8, 16, 4]), op=ALU.mult)
            self.mm(p2[:, 320:384], apar, dg, False, True, [bc8, bdg], [bp2])
            self.evac(decs.rearrange("p b j -> p (b j)"), p2[:, 320:384], [bp2], [bdecs], "dve")
            cols = slice(0, 128)
            ssrc = d["state_ssm"]
            v8 = lambda a_: a_.rearrange("p (r e) -> p r e", r=8)

            def yi_s(p5, bp5):
                self.op("dve", "tensor_tensor", [bxcT, bbm16], [bCM], out=CM,
                        in0=xcT[:, 5, 0:128].unsqueeze(1).to_broadcast([128, 16, 128]), in1=bm16, op=ALU.mult)
                for b in range(16):
                    self.dma("sp", h0f, ssrc[b, 512 * g:512 * g + 512, :].rearrange("(j p) n -> p j n", p=128), [], [bh0f])
                    p3, bp3 = self.psf[:, 3, :], self.bpf[3]
                    pv3 = p3.rearrange("p (j n) -> p j n", j=4)
                    for jj in range(4):
                        self.tr(pv3[:, jj, :], h0f[:, jj, :], self.identf[:], [bh0f, self.bconst], [bp3], inc=(jj == 3))
                    self.evac(h0T, p3, [bp3], [bh0T], "act")
                    self.mm(p5, CM[:, b, :], h0T, b == 0, b == 15, [bCM, bh0T], [bp5], inc=True)
                    self.op("dve", "tensor_scalar", [btmd, bseqm], [bdcb], out=dcb, in0=tmd[:, 16:24], scalar1=seqm[:, b:b + 1],
                            scalar2=None, op0=ALU.mult)
                    self.op("dve", "tensor_tensor", [bxs, bdcb], [bxdm], out=v8(xdm), in0=v8(xs[:, 0:512]),
                            in1=dcb.unsqueeze(2).to_broadcast([128, 8, 64]), op=ALU.mult)
                    p1, bp1 = self.psf[:, 1, :], self.bpf[1]
                    pv1 = p1.rearrange("p (j n) -> p j n", j=4)
                    for jj in range(4):
                        self.mm(pv1[:, jj, :], xdm[:, jj * 128:(jj + 1) * 128], xs[:, 512:640], jj == 0, jj == 3, [bxdm, bxs],
                                [bp1], inc=(jj == 3))
                    self.op("dve", "tensor_tensor", [bh0f, bdecs], [bhnw], out=hnw, in0=h0f,
                            in1=decs[:, b, :].unsqueeze(2).to_broadcast([128, 4, 128]), op=ALU.mult)
                    self.op("dve", "tensor_tensor", [bhnw, bp1], [bhnw], out=hnw, in0=hnw, in1=pv1, op=ALU.add)
                    self.dma("sp", d["nssm_s"][b, 512 * g:512 * g + 512, :].rearrange("(j p) n -> p j n", p=128), hnw,
                             [bhnw], [])

            chunk_core(16, cols, negm_s, bns, yi_s)
        if pend is not None:
            self.make_xt(*pend)


def build_nc(stop_after=None):
    nc = bass.Bass("TRN2", target_bir_lowering=False)
    dr = {}

    def din(name, shape, dt=F32):
        dr[name] = nc.dram_tensor(name, list(shape), dt, kind="ExternalInput").ap()

    def dout(name, shape):
        dr[name] = nc.dram_tensor(name, list(shape), F32, kind="ExternalOutput").ap()

    din("x_p", [SEQ, D]); din("x_s", [128, D])
    din("cache_k", [2, 16, 128, 256]); din("cache_v", [2, 16, 128, 256])
    din("state_conv", [16, 3, 3072]); din("state_ssm", [16, 2048, 128]); din("state_pool", [16, 15, D])
    din("rel_bias", [32, 16]); din("attn_w_qkv", [2, D, 1536]); din("attn_b_qkv", [2, 1536])
    din("attn_w_o", [2, D, D]); din("attn_b_o", [2, D]); din("attn_sinks", [2, 16])
    din("ssm_w_in", [1, D, 5152]); din("ssm_conv_w", [1, 4, 3072]); din("ssm_conv_b", [1, 3072])
    din("ssm_dt_bias", [1, 32]); din("ssm_a_log", [1, 32]); din("ssm_d", [1, 32]); din("ssm_norm_w", [1, 2048])
    din("ssm_w_out", [1, 2048, D]); din("pool_w", [1, 4, 256, 256]); din("pool_scale", [1, D])
    din("ffn_w_gate", [4, D, DFF]); din("ffn_w_up", [4, D, DFF]); din("ffn_w_down", [4, DFF, D])
    din("ln_g", [4, 2, D]); din("ln_b", [4, 2, D])
    for k, v in host_constants().items():
        din(k, v.shape, BF16 if v.dtype == ml_dtypes.bfloat16 else F32)
    dout("y_p", [SEQ, D]); dout("y_s", [128, D])
    dout("nk_p", [2, 128, 256]); dout("nv_p", [2, 128, 256]); dout("nconv_p", [3, 3072]); dout("nssm_p", [2048, 128])
    dout("npool_p", [15, D])
    dout("nk_s", [2, 16, 128, 256]); dout("nv_s", [2, 16, 128, 256]); dout("nconv_s", [16, 3, 3072])
    dout("nssm_s", [16, 2048, 128]); dout("npool_s", [16, 15, D])
    dr["gtab"] = nc.dram_tensor("gtab", [16, 384], F32, kind="Internal").ap()

    with ExitStack() as es:
        S = Sched(nc, es)
        kb = KB(nc, S, es, dr, stop_after)
        kb.load_consts()
        kb.build_bias_tables()
        kb.load_x()
        nl = DEPTH if stop_after is None else stop_after
        import os
        skip = os.environ.get("KSKIP", "")
        for l in range(nl):
            kind = l % 3
            if "mix" in skip:
                pass
            elif kind == 0:
                kb.attn(l // 3, l)
            elif kind == 1:
                kb.ssm(l)
            else:
                kb.pool(l)
            if "ffn" not in skip:
                kb.ffn(l, final=(l == nl - 1))
        allb = [b for b in _all_bufs(kb)]
        S.wait_all("sp", allb)
        S.emit()
    return nc


def _all_bufs(kb):
    out = list(kb.bxr) + list(kb.bxt) + [kb.blng, kb.blnb, kb.bconst, kb.bd2d] + kb.bxb16 + kb.bstat + kb.bpf + kb.bpb
    out += [b for _, b in kb.abufs]
    dummy = Buf("retired", kb.retired)
    out.append(dummy)
    return out


_NC_CACHE = {}


def shard_inputs(inputs):
    consts = host_constants()
    maps = []
    f = lambda a: np.ascontiguousarray(np.asarray(a, dtype=np.float32))
    shared = {k: f(inputs[k]) for k in ("rel_bias", "attn_w_qkv", "attn_b_qkv", "attn_w_o", "attn_b_o", "attn_sinks", "ssm_w_in",
                                        "ssm_conv_w", "ssm_conv_b", "ssm_dt_bias", "ssm_a_log", "ssm_d", "ssm_norm_w", "ssm_w_out",
                                        "pool_w", "pool_scale", "ffn_w_gate", "ffn_w_up", "ffn_w_down", "ln_g", "ln_b")}
    for c in range(NCORES):
        sl = slice(16 * c, 16 * c + 16)
        m = dict(shared)
        m.update(consts)
        m["x_p"] = f(inputs["x_prompt"][c])
        m["x_s"] = f(inputs["x_sample"][sl]).reshape(128, D)
        m["cache_k"] = f(inputs["cache_k"][:, sl]).reshape(2, 16, 128, 256)
        m["cache_v"] = f(inputs["cache_v"][:, sl]).reshape(2, 16, 128, 256)
        m["state_conv"] = f(inputs["state_conv"][0, sl])
        m["state_ssm"] = f(inputs["state_ssm"][0, sl]).reshape(16, 2048, 128)
        m["state_pool"] = f(inputs["state_pool"][0, sl])
        maps.append(m)
    return maps


def gather_outputs(res):
    R = res
    cat = lambda k: np.stack([r[k] for r in R], 0)
    y_p = cat("y_p")
    y_s = cat("y_s").reshape(128, 8, D)
    nk_p = cat("nk_p").transpose(1, 0, 2, 3).reshape(2, 8, 128, 4, 64)
    nv_p = cat("nv_p").transpose(1, 0, 2, 3).reshape(2, 8, 128, 4, 64)
    nconv_p = cat("nconv_p")[None]
    nssm_p = cat("nssm_p").reshape(1, 8, 32, 64, 128)
    npool_p = cat("npool_p")[None]
    nk_s = np.concatenate([r["nk_s"] for r in R], 1).reshape(2, 128, 128, 4, 64)
    nv_s = np.concatenate([r["nv_s"] for r in R], 1).reshape(2, 128, 128, 4, 64)
    nconv_s = np.concatenate([r["nconv_s"] for r in R], 0)[None]
    nssm_s = np.concatenate([r["nssm_s"] for r in R], 0).reshape(1, 128, 32, 64, 128)
    npool_s = np.concatenate([r["npool_s"] for r in R], 0)[None]
    outs = (y_p, y_s, nk_p, nv_p, nconv_p, nssm_p, npool_p, nk_s, nv_s, nconv_s, nssm_s, npool_s)
    return tuple(np.ascontiguousarray(o, dtype=np.float32) for o in outs)


def kernel(**inputs):
    if "nc" not in _NC_CACHE:
        _NC_CACHE["nc"] = build_nc()
    nc = _NC_CACHE["nc"]
    maps = shard_inputs(inputs)
    res = run_bass_kernel_spmd(nc, maps, core_ids=list(range(NCORES)))
    return gather_outputs(res.results)
```

```python
import math
from contextlib import ExitStack

import ml_dtypes
import numpy as np

import concourse.bass as bass
import concourse.mybir as mybir
from concourse.bass_utils import run_bass_kernel_spmd

F32 = mybir.dt.float32
BF16 = mybir.dt.bfloat16
AF = mybir.ActivationFunctionType
ALU = mybir.AluOpType
AX = mybir.AxisListType

NCORES = 8
D = 1024
SEQ = 2048
NT = 17
NTOK = NT * 128
DEPTH = 4
DFF = 2816
ALPHA = (2 * DEPTH) ** 0.25
LN_EPS = 1e-5
RMS_EPS = 1e-5
NEG = -30000.0
BLKS = [(0, 4), (4, 8), (8, 12), (12, 16), (16, 17)]
ARENA_F32 = 23552
QHEADS = [(0, 4), (1, 5), (2, 6), (3, 7), (8, 12), (9, 13), (10, 14), (11, 15)]
POOL_W = (2, 4, 8, 16)


class Buf:
    __slots__ = ("name", "w", "r", "dsem", "excl")

    def __init__(self, name, r=None, excl=False):
        self.name = name
        self.excl = excl
        self.w = None
        self.r = dict(r) if r else {}
        self.dsem = None


class Sched:
    ENG = ("pe", "act", "dve", "pool", "sp")

    def __init__(self, nc, es):
        self.nc = nc
        self.es = es
        self.sems = {}
        self.cnt = {}
        self.seen = {e: {} for e in self.ENG}
        self.ops = {e: [] for e in self.ENG}
        for e in ("pe", "act", "dve", "pool"):
            self._newsem(e)
        self.nd = 0
        self.free_dsems = []

    def _newsem(self, key):
        self.sems[key] = self.es.enter_context(self.nc.semaphore(f"s_{key}"))
        self.cnt[key] = 0

    def _deps(self, eng, reads, writes):
        need = {}

        def add(k, v):
            if v > need.get(k, 0):
                need[k] = v
        for b in reads:
            if b.w is not None:
                add(*b.w)
            if b.excl:
                for k, v in b.r.items():
                    if k != eng:
                        add(k, v)
        for b in writes:
            if b.w is not None and b.w[0] != eng:
                add(*b.w)
            for k, v in b.r.items():
                if k != eng:
                    add(k, v)
        waits = []
        seen = self.seen[eng]
        for k, v in need.items():
            if seen.get(k, 0) < v:
                seen[k] = v
                waits.append((k, v))
        return waits

    def op(self, eng, fn, reads=(), writes=(), inc=True):
        waits = self._deps(eng, reads, writes)
        val = self.cnt[eng] + 1
        if inc:
            self.cnt[eng] = val
        for b in reads:
            b.r[eng] = val
        for b in writes:
            b.w = (eng, val)
            b.r = {}
        self.ops[eng].append((waits, fn, (eng, 1) if inc else None))

    def dma(self, q, fn, reads=(), writes=(), own=None):
        waits = self._deps(q, reads, writes)
        own = own or (writes[0] if writes else reads[0])
        if own.dsem is None:
            if self.free_dsems:
                own.dsem = self.free_dsems.pop()
            else:
                own.dsem = f"d{self.nd}"
                self.nd += 1
                self._newsem(own.dsem)
        k = own.dsem
        self.cnt[k] += 16
        val = self.cnt[k]
        for b in reads:
            b.r[k] = val
        for b in writes:
            b.w = (k, val)
            b.r = {}
        self.ops[q].append((waits, fn, (k, 16)))

    def wait_all(self, eng, bufs):
        waits = self._deps(eng, (), bufs)
        self.ops[eng].append((waits, None, None))

    def emit(self):
        with self.nc.Block() as block:
            def run(name):
                def body(e):
                    for waits, fn, inc in self.ops[name]:
                        for k, v in waits:
                            e.wait_ge(self.sems[k], v)
                        if fn is not None:
                            ins = fn(e)
                            if inc is not None:
                                ins.then_inc(self.sems[inc[0]], inc[1])
                return body
            block.sync(run("sp"))
            block.scalar(run("act"))
            block.vector(run("dve"))
            block.gpsimd(run("pool"))
            block.tensor(run("pe"))


def _t5_bucket(n):
    n = np.maximum(n, 0)
    nf = np.maximum(n, 1).astype(np.float32)
    large = 16 + (np.log(nf / np.float32(16)) / np.float32(math.log(8.0)) * np.float32(16)).astype(np.int32)
    large = np.minimum(large, 31)
    return np.where(n < 16, n, large)


def host_constants():
    bf = ml_dtypes.bfloat16
    c = {}
    c["c_identb"] = np.eye(128, dtype=np.float32).astype(bf)
    c["c_identf"] = np.eye(128, dtype=np.float32)
    c["c_antib"] = np.eye(128, dtype=np.float32)[::-1].copy().astype(bf)
    c["c_onesf"] = np.ones((128, 128), np.float32)
    i = np.arange(384)
    dist = 255 - i
    valid = (dist >= 0) & (dist < 128)
    oh = np.zeros((33, 384), np.float32)
    bk = _t5_bucket(dist)
    for ii in range(384):
        if valid[ii]:
            oh[bk[ii], ii] = 1.0
        else:
            oh[32, ii] = NEG
    c["c_onehot"] = oh
    p = np.arange(128)
    sels = np.zeros((128, 128), np.float32)
    sels[127 - (p % 8), p] = 1.0
    c["c_sels"] = sels.astype(bf)
    negb = np.where((p[:, None] // 8) == (p[None, :] // 8), 0.0, NEG).astype(np.float32)
    c["c_negb"] = negb.astype(bf)
    bm16 = np.zeros((128, 16, 128), np.float32)
    for b in range(16):
        bm16[:, b, 8 * b:8 * b + 8] = 1.0
    c["c_bm16"] = bm16.astype(bf)
    s_ = p[:, None]
    t_ = p[None, :]
    negm_p = np.where(t_ < s_, NEG, 0.0).astype(np.float32)
    negm_s = np.where((t_ >= s_) & (t_ // 8 == s_ // 8), 0.0, NEG).astype(np.float32)
    c["c_negm_p"] = np.tile(negm_p[:, None, :], (1, 8, 1)).reshape(128, 1024).astype(bf)
    c["c_negm_s"] = np.tile(negm_s[:, None, :], (1, 8, 1)).reshape(128, 1024).astype(bf)
    sel8 = np.zeros((8, 8, 128), np.float32)
    for r in range(8):
        sel8[r, r, :] = 1.0
    c["c_sel8"] = sel8
    sm_p = np.ones((8, 512), np.float32)
    sm_p[:, ::128] = 0.0
    sm_s = np.ones((8, 128), np.float32)
    sm_s[:, ::8] = 0.0
    c["c_scan_p"] = sm_p
    c["c_scan_s"] = sm_s
    apar = np.zeros((8, 128), np.float32)
    for k in range(8):
        apar[k, (k % 2) * 64:(k % 2) * 64 + 64] = 1.0
    c["c_apar"] = apar
    selj = np.zeros((8, 16, 4), np.float32)
    for k in range(8):
        selj[k, :, k // 2] = 1.0
    c["c_selj"] = selj
    seqm = np.zeros((128, 16), np.float32)
    seqm[p, p // 8] = 1.0
    c["c_seqm"] = seqm
    mt = np.zeros((24, 128, 128), np.float32)
    for g, w in enumerate(POOL_W):
        for t in range(128):
            for s in range(t - w + 1, t + 1):
                if s >= 0:
                    mt[4 + g, s, t] += 1.0 / w
                else:
                    mt[0 + g, s + 128, t] += 1.0 / w
            mt[4 + g, t, t] -= 1.0
            cnt = min(t + 1, w)
            for s in range(max(0, t - w + 1), t + 1):
                mt[8 + g, s, t] += 1.0 / cnt
            mt[8 + g, t, t] -= 1.0
            b, tt = t // 8, t % 8
            for pos in range(tt - w + 1, tt + 1):
                if pos >= 0:
                    mt[12 + g, b * 8 + pos, t] += 1.0 / w
                else:
                    j = pos + 15
                    if b < 8:
                        mt[16 + g, b * 15 + j, t] += 1.0 / w
                    else:
                        mt[20 + g, (b - 8) * 15 + j, t] += 1.0 / w
            mt[12 + g, t, t] -= 1.0
    c["c_poolmt"] = np.ascontiguousarray(mt.transpose(1, 0, 2))
    return c


class KB:
    def __init__(self, nc, S, es, dr, stop_after=None):
        self.nc, self.S, self.es, self.dr = nc, S, es, dr
        self.stop_after = stop_after
        sb = lambda name, shape, dt: es.enter_context(nc.sbuf_tensor(name, shape, dt))
        self.XR = sb("XR", [128, NT, D], F32)
        self.XT = sb("XT", [128, 8, NTOK], BF16)
        self.bxr = [Buf(f"xr{t}") for t in range(NT)]
        self.bxt = [Buf(f"xt{t}") for t in range(NT)]
        self.lng = sb("lng", [128, D], F32)
        self.lnb = sb("lnb", [128, D], F32)
        self.blng, self.blnb = Buf("lng"), Buf("lnb")
        self.xb16 = sb("xb16", [128, 2, D], BF16)
        self.bxb16 = [Buf("xb16_0"), Buf("xb16_1")]
        self.xbi = 0
        self.identb = sb("identb", [128, 128], BF16)
        self.identf = sb("identf", [128, 128], F32)
        self.onesf = sb("onesf", [128, 128], F32)
        self.mhalf = sb("mhalf", [128, 1], F32)
        self.bconst = Buf("const")
        self.stat = sb("stat", [128, 4, 16], F32)
        self.bstat = [Buf(f"stat{i}") for i in range(4)]
        self.sti = 0
        self.arena = sb("arena", [128, ARENA_F32], F32)
        self.aoff = 0
        self.abufs = []
        self.retired = {}
        self.psf = es.enter_context(nc.psum_tensor("psf", [128, 6, 512], F32))
        self.psb = es.enter_context(nc.psum_tensor("psb", [128, 2, 1024], BF16))
        self.bpf = [Buf(f"pf{i}", excl=True) for i in range(6)]
        self.bpb = [Buf("pb0", excl=True), Buf("pb1", excl=True)]
        self.pbi = 0
        self.bd2d = Buf("d2d")
        self.outbufs = [self.bd2d]
        self.evi = 0

    def op(self, eng, name, R, W, inc=True, **kw):
        self.S.op(eng, lambda e, n=name, kw=kw: getattr(e, n)(**kw), R, W, inc)

    def mm(self, out, lhsT, rhs, start, stop, R, W, inc=None):
        if inc is None:
            inc = stop
        self.S.op("pe", lambda e: e.matmul(out, lhsT=lhsT, rhs=rhs, start=start, stop=stop), R, W, inc)

    def tr(self, out, in_, ident, R, W, inc):
        self.S.op("pe", lambda e: e.transpose(out=out, in_=in_, identity=ident), R, W, inc)

    def act(self, out, in_, func, R, W, **kw):
        self.S.op("act", lambda e: e.activation(out=out, in_=in_, func=func, **kw), R, W)

    def dma(self, q, out, in_, R, W, own=None, **kw):
        self.S.dma(q, lambda e: e.dma_start(out=out, in_=in_, **kw), R, W, own)

    def evac(self, out, in_, R, W, eng=None):
        if eng is None:
            eng = ("act", "dve")[self.evi % 2]
            self.evi += 1
        if eng == "act":
            self.act(out, in_, AF.Copy, R, W)
        else:
            self.op(eng, "tensor_copy", R, W, out=out, in_=in_)

    def carve(self, name, shape, dt):
        n = int(np.prod(shape[1:]))
        nb = n * (2 if dt == BF16 else 4)
        n4 = (nb + 3) // 4
        assert self.aoff + n4 <= ARENA_F32, f"arena overflow at {name}: {self.aoff}+{n4}"
        v = self.arena[:, self.aoff:self.aoff + n4]
        self.aoff += n4
        if dt != F32:
            v = v.bitcast(dt)
        v = v[0:shape[0], 0:n]
        if len(shape) == 3:
            v = v.rearrange("p (a b) -> p a b", a=shape[1])
        elif len(shape) == 4:
            v = v.rearrange("p (a b c) -> p a b c", a=shape[1], b=shape[2])
        b = Buf(name, self.retired)
        self.abufs.append((self.aoff, b))
        return v, b

    def mark(self):
        return self.aoff

    def release(self, mark=0):
        keep = []
        for off, b in self.abufs:
            if off > mark:
                for k, v in ([b.w] if b.w else []) + list(b.r.items()):
                    if v > self.retired.get(k, 0):
                        self.retired[k] = v
                if b.dsem is not None:
                    self.S.free_dsems.append(b.dsem)
                    b.dsem = None
            else:
                keep.append((off, b))
        self.abufs = keep
        self.aoff = mark

    def pair(self, i):
        return self.psf[:, 2 * i:2 * i + 2, :].rearrange("p a b -> p (a b)")

    def bpair(self, i):
        return [self.bpf[2 * i], self.bpf[2 * i + 1]]

    def next_pb(self):
        i = self.pbi % 2
        self.pbi += 1
        return self.psb[:, i, :], self.bpb[i]

    def next_stat(self):
        i = self.sti % 4
        self.sti += 1
        return self.stat[:, i, :], self.bstat[i]

    def load_consts(self):
        d = self.dr
        W = [self.bconst]
        self.dma("sp", self.identb[:], d["c_identb"], [], W)
        self.dma("sp", self.identf[:], d["c_identf"], [], W)
        self.dma("sp", self.onesf[:], d["c_onesf"], [], W)
        self.op("pool", "memset", [], W, ap=self.mhalf[:], constant=-0.5)

    def load_x(self):
        d = self.dr
        self.dma("sp", self.XR[:, 0:16, :], d["x_p"].rearrange("(i p) d -> p i d", p=128), [], self.bxr[0:16])
        self.dma("sp", self.XR[:, 16, :], d["x_s"], [], [self.bxr[16]])
        for t in range(NT):
            slot = self.cast_xb(t)
            self.make_xt(t, slot)

    def cast_xb(self, t):
        slot = self.xbi % 2
        self.xbi += 1
        self.act(self.xb16[:, slot, :], self.XR[:, t, :], AF.Copy, [self.bxr[t]], [self.bxb16[slot]])
        return slot

    def make_xt(self, t, slot):
        pb, bpb = self.next_pb()
        pv = pb.rearrange("p (c n) -> p c n", c=8)
        for c in range(8):
            self.tr(pv[:, c, :], self.xb16[:, slot, c * 128:(c + 1) * 128], self.identb[:],
                    [self.bxb16[slot], self.bconst], [bpb], inc=(c == 7))
        self.evac(self.XT[:, :, t * 128:(t + 1) * 128], pv, [bpb], [self.bxt[t]])

    def load_ln(self, l, which):
        d = self.dr
        self.dma("sp", self.lng[:], d["ln_g"][l, which].partition_broadcast(128), [], [self.blng])
        self.dma("sp", self.lnb[:], d["ln_b"][l, which].partition_broadcast(128), [], [self.blnb])

    def ln1(self, t, final=False):
        st, bst = self.next_stat()
        x = self.XR[:, t, :]
        bx = self.bxr[t]
        for i in range(2):
            self.op("dve", "bn_stats", [bx], [bst], out=st[:, 6 * i:6 * i + 6], in_=x[:, i * 512:(i + 1) * 512])
        self.op("dve", "bn_aggr", [bst], [bst], out=st[:, 12:14], in_=st[:, 0:12])
        self.op("dve", "tensor_scalar", [bst], [bst], out=st[:, 14:15], in0=st[:, 13:14], scalar1=LN_EPS,
                scalar2=None, op0=ALU.add)
        self.op("pool", "tensor_tensor", [bst, self.bconst], [bst], out=st[:, 15:16], in0=st[:, 14:15],
                in1=self.mhalf[:], op=ALU.pow)
        self.op("dve", "tensor_scalar", [bx, bst], [bx], out=x, in0=x, scalar1=st[:, 12:13], scalar2=st[:, 15:16],
                op0=ALU.subtract, op1=ALU.mult)
        self.op("dve", "tensor_tensor", [bx, self.blng], [bx], out=x, in0=x, in1=self.lng[:], op=ALU.mult)
        self.op("dve", "tensor_tensor", [bx, self.blnb], [bx], out=x, in0=x, in1=self.lnb[:], op=ALU.add)
        if final:
            d = self.dr
            if t < 16:
                self.dma("sp", d["y_p"][t * 128:(t + 1) * 128, :], x, [bx], [])
            else:
                self.dma("sp", d["y_s"], x, [bx], [])
            return None
        return self.cast_xb(t)

    def ln_all(self, pend, t, final=False):
        slot = self.ln1(t, final)
        if pend is not None:
            self.make_xt(*pend)
        return None if final else (t, slot)

    def ffn(self, l, final):
        d = self.dr
        self.release(0)
        groups = [(i * 512, 512) for i in range(5)] + [(2560, 256)]
        wg, wu, wd, bw = [], [], [], []
        for s in range(2):
            a, b1 = self.carve(f"wg{s}", [128, 8, 512], BF16)
            u, b2 = self.carve(f"wu{s}", [128, 8, 512], BF16)
            dd, b3 = self.carve(f"wd{s}", [128, 4, 1024], BF16)
            wg.append(a), wu.append(u), wd.append(dd), bw.append((b1, b2, b3))
        at, bat, sg, bsg = [], [], [], []
        for s in range(2):
            a, b = self.carve(f"at{s}", [128, 4, 512], BF16)
            at.append(a), bat.append(b)
            a, b = self.carve(f"sg{s}", [128, 512], F32)
            sg.append(a), bsg.append(b)

        def load(gi):
            ff0, fw = groups[gi]
            s = gi % 2
            nfc = fw // 128
            self.dma("pool", wg[s][:, :, 0:fw], d["ffn_w_gate"][l][:, ff0:ff0 + fw].rearrange("(c p) n -> p c n", p=128),
                     [], [bw[s][0]])
            self.dma("pool", wu[s][:, :, 0:fw], d["ffn_w_up"][l][:, ff0:ff0 + fw].rearrange("(c p) n -> p c n", p=128),
                     [], [bw[s][1]])
            self.dma("pool", wd[s][:, 0:nfc, :], d["ffn_w_down"][l][ff0:ff0 + fw, :].rearrange("(c p) n -> p c n", p=128),
                     [], [bw[s][2]])

        load(0)
        load(1)
        self.load_ln(l, 1)
        units = [(gi, bi) for gi in range(len(groups)) for bi in range(len(BLKS))]
        state = {"pg": 0, "pend": None}

        def gu_steps(u):
            gi, bi = units[u]
            ff0, fw = groups[gi]
            s = gi % 2
            t0, t1 = BLKS[bi]
            c0, n = t0 * 128, (t1 - t0) * 128
            a_s = u % 2
            steps = []
            for fc in range(fw // 128):
                def step(fc=fc):
                    pgi = state["pg"] % 2
                    state["pg"] += 1
                    pg, bpg = self.psf[:, pgi, 0:n], self.bpf[pgi]
                    pu, bpu = self.psf[:, 2 + pgi, 0:n], self.bpf[2 + pgi]
                    for k in range(8):
                        self.mm(pg, wg[s][:, k, fc * 128:(fc + 1) * 128], self.XT[:, k, c0:c0 + n], k == 0, k == 7,
                                [bw[s][0]] + self.bxt[t0:t1], [bpg])
                    for k in range(8):
                        self.mm(pu, wu[s][:, k, fc * 128:(fc + 1) * 128], self.XT[:, k, c0:c0 + n], k == 0, k == 7,
                                [bw[s][1]] + self.bxt[t0:t1], [bpu])
                    self.act(sg[pgi][:, 0:n], pg, AF.Silu, [bpg], [bsg[pgi]])
                    self.op("dve", "tensor_tensor", [bsg[pgi], bpu], [bat[a_s]], out=at[a_s][:, fc, 0:n],
                            in0=sg[pgi][:, 0:n], in1=pu, op=ALU.mult)
                steps.append(step)
            return steps

        def d_steps(u):
            gi, bi = units[u]
            ff0, fw = groups[gi]
            s = gi % 2
            nfc = fw // 128
            t0, t1 = BLKS[bi]
            a_s = u % 2
            last = gi == len(groups) - 1
            steps = []
            for t in range(t0, t1):
                def step(t=t):
                    tl = t - t0
                    for half in range(2):
                        pd, bpd = self.psf[:, 4 + half, :], self.bpf[4 + half]
                        for fc in range(nfc):
                            self.mm(pd, at[a_s][:, fc, tl * 128:(tl + 1) * 128], wd[s][:, fc, half * 512:(half + 1) * 512],
                                    fc == 0, fc == nfc - 1, [bat[a_s], bw[s][2]], [bpd])
                        xs_ = self.XR[:, t, half * 512:(half + 1) * 512]
                        if gi == 0:
                            self.op("dve", "scalar_tensor_tensor", [self.bxr[t], bpd], [self.bxr[t]], out=xs_, in0=xs_,
                                    scalar=ALPHA, in1=pd, op0=ALU.mult, op1=ALU.add)
                        else:
                            self.op("dve", "tensor_tensor", [self.bxr[t], bpd], [self.bxr[t]], out=xs_, in0=xs_, in1=pd,
                                    op=ALU.add)
                    if last:
                        state["pend"] = self.ln_all(state["pend"], t, final)
                steps.append(step)
            if bi == len(BLKS) - 1 and gi + 2 < len(groups):
                steps.append(lambda: load(gi + 2))
            return steps

        prev_d = []
        for u in range(len(units)):
            g_ = gu_steps(u)
            m_ = max(len(g_), len(prev_d))
            for i in range(m_):
                if i < len(g_):
                    g_[i]()
                if i < len(prev_d):
                    prev_d[i]()
            prev_d = d_steps(u)
        for st_ in prev_d:
            st_()
        if state["pend"] is not None:
            self.make_xt(*state["pend"])

    def build_bias_tables(self):
        d = self.dr
        m = self.mark()
        rb, brb = self.carve("rb33", [33, 16], F32)
        oh, boh = self.carve("oh33", [33, 384], F32)
        gs, bgs = self.carve("gsb", [16, 384], F32)
        self.dma("sp", rb[0:32, :], d["rel_bias"], [], [brb])
        self.op("pool", "memset", [], [brb], ap=rb[32:33, :], constant=1.0)
        self.dma("sp", oh, d["c_onehot"], [], [boh])
        ps, bps = self.psf[0:16, 0, 0:384], self.bpf[0]
        self.mm(ps, rb, oh, True, True, [brb, boh], [bps])
        self.evac(gs, ps, [bps], [bgs], "act")
        self.bgtab = Buf("gtab")
        self.dma("sp", d["gtab"], gs, [bgs], [self.bgtab], own=bgs)
        self.release(m)

    def build_bm(self, bm, bbm, U, bU, sample, cst):
        bU = bU if isinstance(bU, list) else [bU]
        antib, sels, negb, bcs = cst
        for q4 in range(4):
            pr = self.pair(q4 % 2)
            bpr = self.bpair(q4 % 2)
            pv = pr.rearrange("p (h k) -> p h k", h=4)
            if not sample:
                for hh2 in range(2):
                    self.mm(pr[:, hh2 * 512:(hh2 + 1) * 512], antib,
                            U[:, 4 * q4 + 2 * hh2:4 * q4 + 2 * hh2 + 2, :].rearrange("p h k -> p (h k)"), True, True,
                            bU + [bcs], [bpr[hh2]])
            else:
                for i in range(4):
                    h = 4 * q4 + i
                    first = (i % 2 == 0)
                    self.mm(pv[:, i, 0:128], sels, U[:, h, 0:128], first, False, bU + [bcs], [bpr[i // 2]], inc=False)
                    for b in range(16):
                        self.mm(pv[:, i, 128 + 8 * b:136 + 8 * b], sels, U[:, h, 128:136], False, False, bU + [bcs],
                                [bpr[i // 2]], inc=False)
                    self.mm(pv[:, i, 128:256], self.identb[:], negb, False, True, [bcs, self.bconst], [bpr[i // 2]],
                            inc=True)
            self.evac(bm[:, 4 * q4:4 * q4 + 4, :], pv, bpr, [bbm])

    def attn(self, j, l):
        d = self.dr
        self.release(0)
        wo, bwo = self.carve("wo", [128, 8, 1024], BF16)
        kT, bkT = self.carve("kT", [128, 2, NTOK], BF16)
        Vb, _ = self.carve("Vb", [128, NT, 256], BF16)
        bVb = [Buf(f"Vb{t}", self.retired) for t in range(NT)]
        bkTt = [Buf(f"kT{t}", self.retired) for t in range(NT)]
        for b in bVb + bkTt:
            self.abufs.append((self.aoff, b))
        qTf, bqT = self.carve("qT", [128, 4096], BF16)
        qT = qTf.rearrange("p (c n) -> p c n", c=8)
        bm, bbm = self.carve("bm", [128, 16, 256], BF16)
        sm, bsm = self.carve("attn_small", [128, 64], F32)
        bq, bk, snk, nsnk = sm[:, 0:8], sm[:, 8:10], sm[:, 16:32], sm[:, 32:48]
        brow, bbrow = self.carve("brow", [65, 512], F32)
        cstt, bcs = self.carve("acst", [128, 3, 128], BF16)
        w8, _ = self.carve("w8", [128, 4096], BF16)
        P_ = w8[:, 0:1024].rearrange("p (h k) -> p h k", h=4)
        PT = w8[:, 1024:2048].rearrange("p (h a n) -> p h a n", h=4, a=2)
        Ob = w8[:, 2048:3072].rearrange("p (h e) -> p h e", h=16)
        OT = w8[:, 3072:4096].rearrange("p (c n) -> p c n", c=8)
        bP, bPT, bOb, bOT = (Buf(n_, self.retired) for n_ in ("P", "PT", "Ob", "OT"))
        bw8 = [bP, bPT, bOb, bOT]
        for b in bw8:
            self.abufs.append((self.aoff, b))
        st2, bst2 = self.carve("ast", [128, 64], F32)
        kvo, bkvo = self.carve("kvo", [128, 512], F32)
        mW = self.mark()
        wq, bwq = self.carve("wq", [128, 8, 8, 128], BF16)
        wkv, bwkv = self.carve("wkv", [128, 8, 512], BF16)
        U, bU = qTf.rearrange("p (h k) -> p h k", h=16), bqT

        wsrc = d["attn_w_qkv"][j]
        for hh in range(2):
            for cg in range(2):
                c0 = cg * 512 + hh * 256
                for ci in range(4):
                    self.dma("pool", wq[:, :, 4 * cg + ci, hh * 64:(hh + 1) * 64],
                             wsrc[:, c0 + 64 * ci:c0 + 64 * ci + 64].rearrange("(k p) n -> p k n", p=128), [], [bwq])
        self.dma("pool", wkv, wsrc[:, 1024:1536].rearrange("(k p) n -> p k n", p=128), [], [bwkv])
        self.dma("pool", wo, d["attn_w_o"][j].rearrange("(k p) n -> p k n", p=128), [], [bwo])
        bsrc = d["attn_b_qkv"][j]
        for hh in range(2):
            for cg in range(2):
                c0 = cg * 512 + hh * 256
                self.dma("sp", bq[hh * 64:(hh + 1) * 64, 4 * cg:4 * cg + 4],
                         bsrc[c0:c0 + 256].rearrange("(c n) -> n c", n=64), [], [bsm], allow_slow_non_contiguous=True)
        self.dma("sp", bk, bsrc[1024:1280].rearrange("(c p) -> p c", p=128), [], [bsm], allow_slow_non_contiguous=True)
        self.dma("sp", snk, d["attn_sinks"][j].partition_broadcast(128), [], [bsm])
        self.dma("sp", brow[0:1, :], bsrc[1024:1536].rearrange("(o n) -> o n", o=1), [], [bbrow])
        self.dma("sp", brow[32:33, :], d["attn_b_o"][j, 0:512].rearrange("(o n) -> o n", o=1), [], [bbrow])
        self.dma("sp", brow[64:65, :], d["attn_b_o"][j, 512:1024].rearrange("(o n) -> o n", o=1), [], [bbrow])
        self.dma("sp", cstt[:, 0, :], d["c_antib"], [], [bcs])
        self.dma("sp", cstt[:, 1, :], d["c_sels"], [], [bcs])
        self.dma("sp", cstt[:, 2, :], d["c_negb"], [], [bcs])
        self.op("dve", "tensor_scalar", [bsm], [bsm], out=bq, in0=bq, scalar1=0.125, scalar2=None, op0=ALU.mult)
        self.op("dve", "tensor_scalar", [bsm], [bsm], out=nsnk, in0=snk, scalar1=-1.0, scalar2=None, op0=ALU.mult)
        gt = d["gtab"]
        self.dma("pool", U, bass.AP(gt.tensor, 0, [[1, 128], [384, 16], [1, 256]]), [self.bgtab], [bU])
        cst = (cstt[:, 0, :], cstt[:, 1, :], cstt[:, 2, :], bcs)
        import os
        dbg = os.environ.get("KATT", "")
        if dbg == "a0":
            return
        self.build_bm(bm, bbm, U, bU, False, cst)
        self.load_ln(l, 0)
        if dbg == "a1":
            return

        pend = None
        pri = 0
        for bi, (t0, t1) in enumerate(BLKS):
            c0, n = t0 * 128, (t1 - t0) * 128
            xtb = self.bxt[t0:t1]
            for c in range(2):
                bi_ = pri % 4
                pri += 1
                ps, bps = self.psf[:, bi_, 0:n], self.bpf[bi_]
                for k in range(8):
                    self.mm(ps, wkv[:, k, c * 128:(c + 1) * 128], self.XT[:, k, c0:c0 + n], k == 0, k == 7,
                            [bwkv] + xtb, [bps])
                self.act(kT[:, c, c0:c0 + n], ps, AF.Identity, [bps, bsm], bkTt[t0:t1], bias=bk[:, c:c + 1], scale=1.0)
            for c in range(8):
                bi_ = pri % 4
                pri += 1
                ps, bps = self.psf[:, bi_, 0:n], self.bpf[bi_]
                for k in range(8):
                    self.mm(ps, wq[:, k, c, :], self.XT[:, k, c0:c0 + n], k == 0, k == 7, [bwq] + xtb, [bps])
                self.act(qT[:, c, 0:n], ps, AF.Identity, [bps, bsm], [bqT], bias=bq[:, c:c + 1], scale=0.125)
            for t in range(t0, t1):
                if dbg == "p1":
                    continue
                bi_ = pri % 4
                pri += 1
                ps, bps = self.psf[:, bi_, :], self.bpf[bi_]
                self.mm(ps, self.onesf[0:1, :], brow[0:1, :], True, False, [self.bconst, bbrow], [bps], inc=False)
                for k in range(8):
                    self.mm(ps, self.XT[:, k, t * 128:(t + 1) * 128], wkv[:, k, :], False, k == 7, [bwkv, self.bxt[t]], [bps])
                if t < 15:
                    self.evac(Vb[:, t, :], ps[:, 256:512], [bps], [bVb[t]])
                if t >= 15 and dbg != "p2":
                    self.evac(kvo, ps, [bps], [bkvo], "act")
                    self.op("dve", "tensor_copy", [bkvo], [bVb[t]], out=Vb[:, t, :], in_=kvo[:, 256:512])
                    if dbg == "p3a" and t == 16:
                        continue
                    if dbg == "p3c":
                        continue
                    if dbg == "p3d":
                        if t == 15:
                            self.dma("sp", d["nk_p"][j], kvo[:, 0:256], [bkvo], [])
                        continue
                    if dbg == "p3b" and t == 15:
                        continue
                    if t == 15:
                        self.dma("sp", d["nk_p"][j], kvo[:, 0:256], [bkvo], [])
                        self.dma("sp", d["nv_p"][j], kvo[:, 256:512], [bkvo], [])
                    else:
                        for b in range(16):
                            self.dma("sp", d["nk_s"][j, b, 120:128, :], kvo[8 * b:8 * b + 8, 0:256], [bkvo], [])
                            self.dma("sp", d["nv_s"][j, b, 120:128, :], kvo[8 * b:8 * b + 8, 256:512], [bkvo], [])
            if dbg in ("p1", "p2", "p3", "p3a", "p3b", "p3c", "p3d"):
                continue
            if t0 == 16:
                self.dma("sp", d["nk_s"][j, :, 0:120, :], d["cache_k"][j, :, 8:128, :], [self.bd2d], [], own=self.bd2d)
                self.dma("sp", d["nv_s"][j, :, 0:120, :], d["cache_v"][j, :, 8:128, :], [self.bd2d], [], own=self.bd2d)
                if dbg == "p4":
                    continue
                kcr = w8.rearrange("p (b c) -> p b c", b=16)
                self.dma("pool", kcr, bass.AP(gt.tensor, 0, [[1, 128], [384, 16], [1, 256]]), [self.bgtab], bw8)
                self.build_bm(bm, bbm, kcr, bw8, True, cst)
                self.release(mW)
                vc, bvc = self.carve("vc", [128, 16, 256], BF16)
                kcT, bkcT = self.carve("kcT", [128, 2, 16, 128], BF16)
                bm16, bbm16 = self.carve("bm16", [128, 16, 128], BF16)
                QM1, bQM1 = self.carve("QM", [128, 16, 128], BF16)
                QM, bQM = [QM1, QM1], [bQM1, bQM1]
                self.dma("sp", bm16, d["c_bm16"], [], [bbm16])
                self.dma("pool", kcr, d["cache_k"][j].rearrange("b k c -> k b c"), [], bw8)
                self.dma("pool", vc, d["cache_v"][j].rearrange("b k c -> k b c"), [], [bvc])
                for kc in range(2):
                    for b8 in range(2):
                        pb, bpb = self.next_pb()
                        pv = pb.rearrange("p (b n) -> p b n", b=8)
                        for bb in range(8):
                            b = b8 * 8 + bb
                            self.tr(pv[:, bb, :], kcr[:, b, kc * 128:(kc + 1) * 128], self.identb[:], bw8 + [self.bconst],
                                    [bpb], inc=(bb == 7))
                        self.evac(kcT[:, kc, b8 * 8:b8 * 8 + 8, :], pv, [bpb], [bkcT])
                smp = (kcT, bkcT, vc, bvc, QM, bQM, bm16, bbm16)
            if dbg == "a2":
                continue
            for t in range(t0, t1):
                tl = t - t0
                sample = t == 16
                if dbg == "a3" and sample:
                    continue
                den, rden = st2[:, 32:48], st2[:, 48:64]
                Opr, bOpr = self.pair(2), self.bpair(2)
                for hg in range(4):
                    Spr, bSpr = self.pair(hg % 2), self.bpair(hg % 2)
                    Sv = Spr.rearrange("p (h k) -> p h k", h=4)
                    lo = 128 if t == 0 else 0
                    for i in range(4):
                        h = 4 * hg + i
                        c, hh = (h % 4) + 4 * (h // 8), (h // 4) % 2
                        kc = hg // 2
                        rows = slice(hh * 64, hh * 64 + 64)
                        bS = [bSpr[i // 2]]
                        first = (i % 2 == 0)
                        qcols = qT[rows, c, tl * 128:(tl + 1) * 128]
                        if not sample:
                            kb = bkTt[max(t - 1, 0):t + 1]
                            self.mm(Sv[:, i, lo:256], qcols, kT[rows, kc, (t - 1) * 128 + lo:(t + 1) * 128], first, False,
                                    [bqT] + kb, bS, inc=False)
                        else:
                            kcT, bkcT, vc, bvc, QM, bQM, bm16, bbm16 = smp
                            qs = (4 * hg + i) % 2
                            self.op("dve", "tensor_tensor", [bqT, bbm16], [bQM[qs]], out=QM[qs][rows],
                                    in0=qcols.unsqueeze(1).to_broadcast([64, 16, 128]), in1=bm16[rows], op=ALU.mult)
                            for b in range(16):
                                self.mm(Sv[:, i, 0:128], QM[qs][rows, b, :], kcT[rows, kc, b, :], first and b == 0, False,
                                        [bQM[qs], bkcT], bS, inc=False)
                            self.mm(Sv[:, i, 128:256], qcols, kT[rows, kc, 2048:2176], False, False, [bqT, bkTt[16]], bS,
                                    inc=False)
                        self.mm(Sv[:, i, lo:256], self.identb[:], bm[:, h, lo:256], False, True, [bbm, self.bconst], bS,
                                inc=True)
                    mx, nmx, rs, t4, es = (st2[:, 4 * q:4 * q + 4] for q in range(5))
                    hs = slice(4 * hg, 4 * hg + 4)
                    self.op("dve", "tensor_reduce", bSpr, [bst2], out=mx, in_=Sv[:, :, lo:256], axis=AX.X, op=ALU.max)
                    self.op("dve", "scalar_tensor_tensor", [bst2, bsm], [bst2], out=nmx, in0=mx, scalar=-1.0,
                            in1=nsnk[:, hs], op0=ALU.mult, op1=ALU.min)
                    for i in range(4):
                        self.act(P_[:, i, lo:256], Sv[:, i, lo:256], AF.Exp, [bSpr[i // 2], bst2], [bP, bst2],
                                 bias=nmx[:, i:i + 1], scale=1.0, accum_out=rs[:, i:i + 1])
                    self.op("dve", "tensor_tensor", [bst2, bsm], [bst2], out=t4, in0=snk[:, hs], in1=nmx, op=ALU.add)
                    self.act(es, t4, AF.Exp, [bst2], [bst2])
                    self.op("dve", "tensor_tensor", [bst2], [bst2], out=den[:, hs], in0=rs, in1=es, op=ALU.add)
                    pb, bpb = self.next_pb()
                    pv = pb.rearrange("p (h a n) -> p h a n", h=4, a=2)
                    kts = [1] if t == 0 else [0, 1]
                    for i in range(4):
                        for kt in kts:
                            self.tr(pv[:, i, kt, :], P_[:, i, kt * 128:(kt + 1) * 128], self.identb[:], [bP, self.bconst],
                                    [bpb], inc=(i == 3 and kt == 1))
                    if t == 0:
                        self.evac(PT[:, :, 1, :], pv[:, :, 1, :], [bpb], [bPT])
                    else:
                        self.evac(PT, pv, [bpb], [bPT])
                    for i in range(4):
                        h = 4 * hg + i
                        bO = [bOpr[h // 8]]
                        oc = Opr[:, h * 64:(h + 1) * 64]
                        firstb = (h % 8 == 0)
                        if not sample:
                            for kt in kts:
                                self.mm(oc, PT[:, i, kt, :], Vb[:, t - 1 + kt, hg * 64:(hg + 1) * 64],
                                        firstb and kt == kts[0], kt == 1, [bPT, bVb[t - 1 + kt]], bO,
                                        inc=(kt == 1 and i == 3))
                        else:
                            qs = (4 * hg + i) % 2
                            self.op("dve", "tensor_tensor", [bPT, bbm16], [bQM[qs]], out=QM[qs],
                                    in0=PT[:, i, 0, :].unsqueeze(1).to_broadcast([128, 16, 128]), in1=bm16, op=ALU.mult)
                            for b in range(16):
                                self.mm(oc, QM[qs][:, b, :], vc[:, b, hg * 64:(hg + 1) * 64], firstb and b == 0, False,
                                        [bQM[qs], bvc], bO, inc=False)
                            self.mm(oc, PT[:, i, 1, :], Vb[:, 16, hg * 64:(hg + 1) * 64], False, True, [bPT, bVb[16]], bO,
                                    inc=True)
                self.op("dve", "reciprocal", [bst2], [bst2], out=rden, in_=den)
                self.op("dve", "tensor_tensor", bOpr + [bst2], [bOb], out=Ob, in0=Opr.rearrange("p (h e) -> p h e", h=16),
                        in1=rden.unsqueeze(2).to_broadcast([128, 16, 64]), op=ALU.mult)
                pb, bpb = self.next_pb()
                pv = pb.rearrange("p (c n) -> p c n", c=8)
                Obf = Ob.rearrange("p h e -> p (h e)")
                for c in range(8):
                    self.tr(pv[:, c, :], Obf[:, c * 128:(c + 1) * 128], self.identb[:], [bOb, self.bconst], [bpb], inc=(c == 7))
                self.evac(OT, pv, [bpb], [bOT])
                ypr, bypr = self.pair(0), self.bpair(0)
                for half in range(2):
                    yh = ypr[:, half * 512:(half + 1) * 512]
                    pr_ = 32 * (half + 1)
                    self.mm(yh, self.onesf[pr_:pr_ + 1, :], brow[pr_:pr_ + 1, :], True, False,
                            [self.bconst, bbrow], [bypr[half]], inc=False)
                    for k in range(8):
                        self.mm(yh, OT[:, k, :], wo[:, k, half * 512:(half + 1) * 512], False, k == 7, [bOT, bwo], [bypr[half]])
                x = self.XR[:, t, :]
                self.op("dve", "scalar_tensor_tensor", [self.bxr[t]] + bypr, [self.bxr[t]], out=x, in0=x, scalar=ALPHA,
                        in1=ypr, op0=ALU.mult, op1=ALU.add)
                pend = self.ln_all(pend, t)
        if pend is not None:
            self.make_xt(*pend)

    def pool(self, l):
        d = self.dr
        self.release(0)
        mt, bmt = self.carve("poolmt", [128, 24, 128], F32)
        pw, bpw = self.carve("poolw", [128, 4, 2, 256], BF16)
        psc, bpsc = self.carve("poolsc", [128, D], F32)
        pfx, bpfx = self.carve("poolpfx", [128, 2, D], F32)
        dT, bdT = [], []
        for s in range(2):
            a, b = self.carve(f"dT{s}", [128, 8, 128], BF16)
            dT.append(a), bdT.append(b)
        tmp, btmp = self.carve("pooltmp", [128, D], F32)
        self.dma("sp", mt, d["c_poolmt"], [], [bmt])
        self.dma("pool", pw, d["pool_w"][0].rearrange("g (kk p) n -> p g kk n", p=128), [], [bpw])
        self.dma("sp", psc, d["pool_scale"][0].partition_broadcast(128), [], [bpsc])
        self.dma("sp", pfx[0:120, 0, :], d["state_pool"][0:8].rearrange("b r c -> (b r) c"), [], [bpfx])
        self.dma("sp", pfx[0:120, 1, :], d["state_pool"][8:16].rearrange("b r c -> (b r) c"), [], [bpfx])
        self.dma("sp", d["npool_p"], self.XR[113:128, 15, :], [self.bxr[15]], [])
        self.dma("sp", d["npool_s"][:, 0:7, :], d["state_pool"][:, 8:15, :], [self.bd2d], [], own=self.bd2d)
        for b in range(16):
            self.dma("sp", d["npool_s"][b, 7:15, :], self.XR[8 * b:8 * b + 8, 16, :], [self.bxr[16]], [])
        self.load_ln(l, 0)

        def diff(t):
            s = t % 2
            pr, bpr = self.pair(s), self.bpair(s)
            pv = pr.rearrange("p (c n) -> p c n", c=8)
            for c in range(8):
                g = c // 2
                cs = slice(c * 128, (c + 1) * 128)
                first = (c % 4 == 0)
                bb = [bpr[c // 4]]
                if t == 16:
                    self.mm(pv[:, c, :], pfx[0:120, 0, cs], mt[0:120, 16 + g, :], first, False, [bpfx, bmt], bb, inc=False)
                    self.mm(pv[:, c, :], pfx[0:120, 1, cs], mt[0:120, 20 + g, :], False, False, [bpfx, bmt], bb, inc=False)
                    self.mm(pv[:, c, :], self.XR[:, 16, cs], mt[:, 12 + g, :], False, True, [self.bxr[16], bmt], bb, inc=True)
                elif t == 0:
                    self.mm(pv[:, c, :], self.XR[:, 0, cs], mt[:, 8 + g, :], first, True, [self.bxr[0], bmt], bb, inc=True)
                else:
                    self.mm(pv[:, c, :], self.XR[:, t - 1, cs], mt[:, g, :], first, False, [self.bxr[t - 1], bmt], bb, inc=False)
                    self.mm(pv[:, c, :], self.XR[:, t, cs], mt[:, 4 + g, :], False, True, [self.bxr[t], bmt], bb, inc=True)
            self.evac(dT[s], pv, bpr, [bdT[s]])

        pend = [None]

        def update(t):
            s = t % 2
            ypr, bypr = self.pair(2), self.bpair(2)
            for g in range(4):
                for kk in range(2):
                    self.mm(ypr[:, g * 256:(g + 1) * 256], dT[s][:, 2 * g + kk, :], pw[:, g, kk, :], (g % 2 == 0) and kk == 0,
                            kk == 1, [bdT[s], bpw], [bypr[g // 2]], inc=(kk == 1))
            self.op("dve", "tensor_tensor", bypr + [bpsc], [btmp], out=tmp, in0=ypr, in1=psc, op=ALU.mult)
            x = self.XR[:, t, :]
            self.op("dve", "scalar_tensor_tensor", [self.bxr[t], btmp], [self.bxr[t]], out=x, in0=x, scalar=ALPHA, in1=tmp,
                    op0=ALU.mult, op1=ALU.add)
            pend[0] = self.ln_all(pend[0], t)

        for t in range(NT):
            diff(t)
            if t >= 1:
                update(t - 1)
        update(16)
        if pend[0] is not None:
            self.make_xt(*pend[0])

    def ssm(self, l):
        d = self.dr
        self.release(0)
        win = d["ssm_w_in"][0]
        negm_p, bnp = self.carve("negm_p", [128, 1024], BF16)
        negm_s, bns = self.carve("negm_s", [128, 1024], BF16)
        c8, bc8 = self.carve("c8", [8, 8 * 128 + 512 + 128 + 128 + 64], F32)
        sel8 = c8[:, 0:1024].rearrange("p (r n) -> p r n", r=8)
        scan_p, scan_s = c8[:, 1024:1536], c8[:, 1536:1664]
        apar = c8[:, 1664:1792]
        selj = c8[:, 1792:1856].rearrange("p (b j) -> p b j", b=16)
        seqm, bseqm = self.carve("seqm", [128, 16], F32)
        self.dma("sp", negm_p, d["c_negm_p"], [], [bnp])
        self.dma("sp", negm_s, d["c_negm_s"], [], [bns])
        self.dma("sp", sel8, d["c_sel8"], [], [bc8])
        self.dma("sp", scan_p, d["c_scan_p"], [], [bc8])
        self.dma("sp", scan_s, d["c_scan_s"], [], [bc8])
        self.dma("sp", apar, d["c_apar"], [], [bc8])
        self.dma("sp", selj, d["c_selj"], [], [bc8])
        self.dma("sp", seqm, d["c_seqm"], [], [bseqm])
        self.load_ln(l, 0)
        mL = self.mark()
        pend = None
        for g in range(4):
            self.release(mL)
            wx, bwx = self.carve("wx", [128, 8, 512], BF16)
            wz, bwz = self.carve("wz", [128, 8, 512], BF16)
            wBC, bwBC = self.carve("wBC", [128, 8, 256], BF16)
            wdt, bwdt = self.carve("wdt", [128, 8, 8], BF16)
            wout, bwout = self.carve("wout", [128, 4, 1024], BF16)
            cw, bcw = self.carve("convw", [128, 6, 5], F32)
            hp, bhp = self.carve("headp", [8, 4], F32)
            dbc, bdbc = self.carve("dbc", [128, 8], F32)
            nw, bnw = self.carve("nw", [128, 512], F32)
            wr = lambda c0, w_: win[:, c0:c0 + w_].rearrange("(k p) n -> p k n", p=128)
            self.dma("pool", wx, wr(2048 + 512 * g, 512), [], [bwx])
            self.dma("pool", wBC[:, :, 0:128], wr(4096 + 128 * g, 128), [], [bwBC])
            self.dma("pool", wBC[:, :, 128:256], wr(4608 + 128 * g, 128), [], [bwBC])
            self.dma("pool", wdt, wr(5120 + 8 * g, 8), [], [bwdt])
            self.dma("pool", wz, wr(512 * g, 512), [], [bwz])
            self.dma("pool", wout, d["ssm_w_out"][0][512 * g:512 * g + 512, :].rearrange("(k p) n -> p k n", p=128), [], [bwout])
            chbase = [512 * g + 128 * i for i in range(4)] + [2048 + 128 * g, 2560 + 128 * g]
            cwsrc, cbsrc = d["ssm_conv_w"][0], d["ssm_conv_b"][0]
            for cc in range(6):
                cb = chbase[cc]
                self.dma("sp", cw[:, cc, 0:4], cwsrc[:, cb:cb + 128].rearrange("j p -> p j"), [], [bcw],
                         allow_slow_non_contiguous=True)
                self.dma("sp", cw[:, cc, 4:5], cbsrc[cb:cb + 128].rearrange("(p o) -> p o", o=1), [], [bcw])
            self.dma("sp", hp[:, 0:1], d["ssm_dt_bias"][0][8 * g:8 * g + 8].rearrange("(p o) -> p o", o=1), [], [bhp])
            self.dma("sp", hp[:, 1:2], d["ssm_a_log"][0][8 * g:8 * g + 8].rearrange("(p o) -> p o", o=1), [], [bhp])
            self.dma("sp", dbc, d["ssm_d"][0][8 * g:8 * g + 8].partition_broadcast(128), [], [bdbc])
            self.dma("sp", nw, d["ssm_norm_w"][0][512 * g:512 * g + 512].partition_broadcast(128), [], [bnw])
            self.act(hp[:, 2:3], hp[:, 1:2], AF.Exp, [bhp], [bhp])
            self.op("dve", "tensor_scalar", [bhp], [bhp], out=hp[:, 2:3], in0=hp[:, 2:3], scalar1=-1.0, scalar2=None,
                    op0=ALU.mult)
            cwk, _ = self.carve("convwork", [128, 1536], F32)
            acc = [cwk[:, 0:512], cwk[:, 512:1024]]
            ctmp = cwk[:, 1024:1536]
            bacc = [Buf("acc0", self.retired), Buf("acc1", self.retired)]
            bctmp = Buf("ctmp", self.retired)
            bcwk = bacc + [bctmp]
            for b in bcwk:
                self.abufs.append((self.aoff, b))
            xcT, bxcT = self.carve("xcT", [128, 6, 512], BF16)
            dtb_, bdt = self.carve("dtbuf", [8, 3, 512], F32)
            el, bel = self.carve("elast", [8, 16], F32)
            xs, bxs = self.carve("xs", [128, 640], BF16)
            tmd, btmd = self.carve("tmd", [128, 32], F32)
            Ef, bE_ = self.carve("E", [128, 1024], F32)
            E = Ef.rearrange("p (r n) -> p r n", r=8)
            cst_ = Ef[:, 0:768]
            bcst = bE_
            CM = Ef.bitcast(BF16).rearrange("p (b n) -> p b n", b=16)
            bCM = bE_
            WT, bWT = self.carve("WT", [128, 8, 128], BF16)
            xD, bxD = self.carve("xD", [128, 512], BF16)
            sz, bsz = self.carve("sz", [128, 512], F32)
            y1, by1 = self.carve("y1", [128, 512], F32)
            hout, bhout = y1.rearrange("p (j n) -> p j n", j=4), by1
            yn, byn = self.carve("yn", [128, 512], BF16)
            ynT, bynT = self.carve("ynT", [128, 4, 128], BF16)
            dg, bdg = self.carve("dg", [8, 64], F32)
            mP = self.mark()
            rawc, _ = self.carve("rawc", [128, 2, 515], F32)
            brawc = [Buf("rawc0", self.retired), Buf("rawc1", self.retired)]
            for b in brawc:
                self.abufs.append((self.aoff, b))
            hist, bhist = self.carve("hist", [128, 6, 3], F32)
            hT, bhT = self.carve("hT", [128, 512], F32)
            hTb, bhTb = self.carve("hTb", [128, 512], BF16)
            xd, bxd = self.carve("xd", [128, 512], BF16)
            dbs, bdbs = self.carve("dbs", [128, 64], F32)
            self.op("pool", "memset", [], [bhT], ap=hT, constant=0.0)
            self.op("pool", "memset", [], [bhTb], ap=hTb, constant=0.0)
            self.op("pool", "memset", [], [bhist], ap=hist, constant=0.0)
            wsl = [wx[:, :, 0:128], wx[:, :, 128:256], wx[:, :, 256:384], wx[:, :, 384:512], wBC[:, :, 0:128], wBC[:, :, 128:256]]
            wsb = [bwx, bwx, bwx, bwx, bwBC, bwBC]
            decT, bEt, acum = (dtb_[:, q, :] for q in range(3))

            def dt_path(psd, bpsd, n, L, scanm):
                nb = n // L
                t0, t1, ac = decT[:, 0:n], bEt[:, 0:n], acum[:, 0:n]
                v3 = lambda a_: a_.rearrange("p (b t) -> p b t", t=L)
                self.act(t0, psd, AF.Exp, [bpsd, bhp], [bdt], bias=hp[:, 0:1], scale=1.0)
                self.act(t0, t0, AF.Ln, [bdt], [bdt], bias=1.0, scale=1.0)
                self.act(t1, t0, AF.Ln, [bdt], [bdt])
                self.op("dve", "tensor_scalar", [bdt, bhp], [bdt], out=t0, in0=t0, scalar1=hp[:, 2:3], scalar2=None,
                        op0=ALU.mult)
                self.op("dve", "tensor_tensor_scan", [bdt, bc8], [bdt], out=ac, data0=scanm, data1=t0, initial=0.0,
                        op0=ALU.mult, op1=ALU.add)
                self.op("dve", "tensor_tensor", [bdt], [bdt], out=t1, in0=t1, in1=ac, op=ALU.subtract)
                self.op("dve", "tensor_tensor", [bdt], [bdt], out=v3(t0), in0=v3(t1),
                        in1=v3(ac)[:, :, L - 1:L].to_broadcast([8, nb, L]), op=ALU.add)
                self.act(t0, t0, AF.Exp, [bdt], [bdt])
                self.act(el[:, 0:nb], v3(ac)[:, :, L - 1], AF.Exp, [bdt], [bel])

            def conv_chunk(cc, src_views, bsrc, out_view, shape):
                s = cc % 2
                a = acc[s]
                av = a if shape is None else a[:, 0:shape[0] * shape[1]].rearrange("p (b t) -> p b t", b=shape[0])
                tv = ctmp if shape is None else ctmp[:, 0:shape[0] * shape[1]].rearrange("p (b t) -> p b t", b=shape[0])
                self.op("dve", "tensor_scalar", [bsrc, bcw], [bacc[s]], out=av, in0=src_views[0], scalar1=cw[:, cc, 0:1],
                        scalar2=cw[:, cc, 4:5], op0=ALU.mult, op1=ALU.add)
                for jj in range(1, 4):
                    self.op("dve", "scalar_tensor_tensor", [bsrc, bcw, bacc[s]], [bacc[s]], out=av, in0=src_views[jj],
                            scalar=cw[:, cc, jj:jj + 1], in1=av, op0=ALU.mult, op1=ALU.add)
                self.act(out_view, av, AF.Silu, [bacc[s]], [bxcT])

            v8 = lambda a_: a_.rearrange("p (r e) -> p r e", r=8)

            def ph1_pe(t, cols, negm, bnegm):
                xtt = [self.bxt[t]]
                pb, bpb = self.next_pb()
                pv = pb[:, 0:640].rearrange("p (c n) -> p c n", c=5)
                for cc in range(5):
                    self.tr(pv[:, cc, :], xcT[:, cc, cols], self.identb[:], [bxcT, self.bconst], [bpb], inc=(cc == 4))
                p2, bp2 = self.psf[:, 2, :], self.bpf[2]
                for q, srcv in enumerate((bEt, acum, decT)):
                    self.tr(p2[:, 8 * q:8 * q + 8], srcv[:, cols], self.identf[0:8, 0:8], [bdt, self.bconst], [bp2], inc=(q == 2))
                self.mm(p2[:, 128:256], xcT[:, 4, cols], xcT[:, 5, cols], False, True, [bxcT], [bp2])
                spr, bspr = self.pair(0), self.bpair(0)
                sv = spr.rearrange("p (r n) -> p r n", r=8)
                for r in range(8):
                    self.mm(sv[:, r, :], sel8[:, r, :], acum[:, cols], r % 4 == 0, False, [bc8, bdt], [bspr[r // 4]], inc=False)
                for a2 in range(2):
                    self.mm(spr[:, a2 * 512:(a2 + 1) * 512], self.identb[:], negm[:, a2 * 512:(a2 + 1) * 512], False, True,
                            [self.bconst, bnegm], [bspr[a2]], inc=True)
                p3, bp3 = self.psf[:, 3, :], self.bpf[3]
                for k in range(8):
                    self.mm(p3, self.XT[:, k, t * 128:(t + 1) * 128], wz[:, k, :], k == 0, k == 7, [bwz] + xtt, [bp3])
                return (pb, bpb)

            def ph1_early(t, pbt):
                pb, bpb = pbt
                p2, bp2 = self.psf[:, 2, :], self.bpf[2]
                self.evac(tmd[:, 0:24], p2[:, 0:24], [bp2], [btmd], "dve")
                self.act(tmd[:, 24:32], tmd[:, 8:16], AF.Exp, [btmd], [btmd])
                self.evac(xs, pb[:, 0:640], [bpb], [bxs], "dve")

            def ph1_late(t):
                p2, bp2 = self.psf[:, 2, :], self.bpf[2]
                spr, bspr = self.pair(0), self.bpair(0)
                sv = spr.rearrange("p (r n) -> p r n", r=8)
                for r in range(8):
                    self.act(E[:, r, :], sv[:, r, :], AF.Exp, [bspr[r // 4], btmd], [bE_], bias=tmd[:, r:r + 1], scale=1.0)
                p3, bp3 = self.psf[:, 3, :], self.bpf[3]
                self.act(sz, p3, AF.Silu, [bp3], [bsz])
                self.op("dve", "tensor_tensor", [bE_, bp2], [bWT], out=WT, in0=E,
                        in1=p2[:, 128:256].unsqueeze(1).to_broadcast([128, 8, 128]), op=ALU.mult)
                self.op("dve", "tensor_tensor", [bxs, bdbc], [bxD], out=v8(xD), in0=v8(xs[:, 0:512]),
                        in1=dbc.unsqueeze(2).to_broadcast([128, 8, 64]), op=ALU.mult)

            def ph2(t, yi_fn, state_fn, hoist):
                nonlocal pend
                p4, bp4 = self.psf[:, 4, :], self.bpf[4]
                p5, bp5 = self.psf[:, 5, :], self.bpf[5]
                for r in range(8):
                    self.mm(p4[:, r * 64:(r + 1) * 64], WT[:, r, :], xs[:, r * 64:(r + 1) * 64], r == 0, False, [bWT, bxs], [bp4],
                            inc=False)
                self.mm(p4, self.identb[:], xD, False, True, [self.bconst, bxD], [bp4], inc=True)
                if state_fn is not None:
                    state_fn(0)
                yi_fn(p5, bp5)
                if state_fn is not None:
                    state_fn(1)
                self.op("dve", "tensor_tensor", [bp5, btmd], [by1], out=v8(y1), in0=v8(p5),
                        in1=tmd[:, 24:32].unsqueeze(2).to_broadcast([128, 8, 64]), op=ALU.mult)
                self.op("dve", "tensor_tensor", [by1, bp4], [by1], out=y1, in0=y1, in1=p4, op=ALU.add)
                self.op("dve", "tensor_tensor", [by1, bsz], [by1], out=y1, in0=y1, in1=sz, op=ALU.mult)
                st, bst = self.next_stat()
                self.act(p5, y1, AF.Square, [by1], [bp5, bst], accum_out=st[:, 0:1])
                pbt = None
                if hoist is not None:
                    pbt = hoist[0]()
                    hoist[1](pbt)
                self.op("dve", "tensor_scalar", [bst], [bst], out=st[:, 1:2], in0=st[:, 0:1], scalar1=1.0 / 512.0, scalar2=RMS_EPS,
                        op0=ALU.mult, op1=ALU.add)
                self.op("pool", "tensor_tensor", [bst, self.bconst], [bst], out=st[:, 2:3], in0=st[:, 1:2], in1=self.mhalf[:],
                        op=ALU.pow)
                self.op("dve", "scalar_tensor_tensor", [by1, bst, bnw], [byn], out=yn, in0=y1, scalar=st[:, 2:3], in1=nw,
                        op0=ALU.mult, op1=ALU.mult)
                pb, bpb = self.next_pb()
                pv = pb[:, 0:512].rearrange("p (c n) -> p c n", c=4)
                for c in range(4):
                    self.tr(pv[:, c, :], yn[:, c * 128:(c + 1) * 128], self.identb[:], [byn, self.bconst], [bpb], inc=(c == 3))
                self.evac(ynT, pv, [bpb], [bynT], "act")
                if hoist is not None:
                    hoist[2]()
                opr, bopr = self.pair(2), self.bpair(2)
                for half in range(2):
                    for k in range(4):
                        self.mm(opr[:, half * 512:(half + 1) * 512], ynT[:, k, :], wout[:, k, half * 512:(half + 1) * 512], k == 0,
                                k == 3, [bynT, bwout], [bopr[half]])
                x = self.XR[:, t, :]
                if g == 0:
                    self.op("dve", "scalar_tensor_tensor", [self.bxr[t]] + bopr, [self.bxr[t]], out=x, in0=x, scalar=ALPHA,
                            in1=opr, op0=ALU.mult, op1=ALU.add)
                else:
                    self.op("dve", "tensor_tensor", [self.bxr[t]] + bopr, [self.bxr[t]], out=x, in0=x, in1=opr, op=ALU.add)
                if g == 3:
                    pend = self.ln_all(pend, t)

            def conv_state_proj(t, rows):
                cpr, bcpr = self.pair(0), self.bpair(0)
                tc_ = slice(t * 128, (t + 1) * 128)
                for k in range(8):
                    self.mm(cpr[:, 0:512], self.XT[:, k, tc_], wx[:, k, :], k == 0, k == 7, [bwx, self.bxt[t]], [bcpr[0]])
                for k in range(8):
                    self.mm(cpr[:, 512:768], self.XT[:, k, tc_], wBC[:, k, :], k == 0, k == 7, [bwBC, self.bxt[t]], [bcpr[1]])
                self.evac(cst_[rows, :], cpr[rows, 0:768], bcpr, [bcst], "act")

            osl = ((512 * g, 512, 0), (2048 + 128 * g, 128, 512), (2560 + 128 * g, 128, 640))
            for bi, (t0, t1) in enumerate(BLKS[:4]):
                c0 = t0 * 128
                xtb = self.bxt[t0:t1]
                psd, bpsd = self.psf[0:8, 0, :], self.bpf[0]
                for k in range(8):
                    self.mm(psd, wdt[:, k, :], self.XT[:, k, c0:c0 + 512], k == 0, k == 7, [bwdt] + xtb, [bpsd])
                dt_path(psd, bpsd, 512, 128, scan_p)
                for cc in range(6):
                    bi_ = 2 + cc % 4
                    s_ = cc % 2
                    ps, bps = self.psf[:, bi_, :], self.bpf[bi_]
                    for k in range(8):
                        self.mm(ps, wsl[cc][:, k, :], self.XT[:, k, c0:c0 + 512], k == 0, k == 7, [wsb[cc]] + xtb, [bps])
                    self.op("dve", "tensor_copy", [bhist], [brawc[s_]], out=rawc[:, s_, 0:3], in_=hist[:, cc, :])
                    self.evac(rawc[:, s_, 3:515], ps, [bps], [brawc[s_]], "act")
                    self.op("dve", "tensor_copy", [brawc[s_]], [bhist], out=hist[:, cc, :], in_=rawc[:, s_, 512:515])
                    conv_chunk(cc, [rawc[:, s_, jj:jj + 512] for jj in range(4)], brawc[s_], xcT[:, cc, :], None)
                colsl = [slice(tl * 128, (tl + 1) * 128) for tl in range(4)]
                pbt0 = ph1_pe(t0, colsl[0], negm_p, bnp)
                ph1_early(t0, pbt0)
                ph1_late(t0)
                for t in range(t0, t1):
                    tl = t - t0
                    cols = colsl[tl]

                    def yi_fn(p5, bp5, cols=cols):
                        self.mm(p5, xcT[:, 5, cols], hTb, True, True, [bxcT, bhTb], [bp5])

                    def state_fn(stage, tl=tl):
                        p3, bp3 = self.psf[:, 3, :], self.bpf[3]
                        if stage == 0:
                            self.op("dve", "tensor_tensor", [bxs, btmd], [bxd], out=v8(xd), in0=v8(xs[:, 0:512]),
                                    in1=tmd[:, 16:24].unsqueeze(2).to_broadcast([128, 8, 64]), op=ALU.mult)
                            self.mm(p3, xs[:, 512:640], xd, True, True, [bxs, bxd], [bp3])
                            self.op("dve", "tensor_scalar", [bel, self.bconst], [bdg], out=dg[:, 0:8], in0=self.identf[0:8, 0:8],
                                    scalar1=el[:, tl:tl + 1], scalar2=None, op0=ALU.mult)
                            p2, bp2 = self.psf[:, 2, :], self.bpf[2]
                            self.mm(p2[:, 256:264], self.onesf[0:8, :], dg[:, 0:8], False, True, [self.bconst, bdg], [bp2])
                            self.evac(dbs[:, 0:8], p2[:, 256:264], [bp2], [bdbs], "dve")
                        else:
                            self.op("dve", "tensor_tensor", [bhT, bdbs], [bhT], out=v8(hT), in0=v8(hT),
                                    in1=dbs[:, 0:8].unsqueeze(2).to_broadcast([128, 8, 64]), op=ALU.mult)
                            self.op("dve", "tensor_tensor", [bhT, bp3], [bhT], out=hT, in0=hT, in1=p3, op=ALU.add)
                            self.act(hTb, hT, AF.Copy, [bhT], [bhTb])

                    hoist = None
                    if t + 1 < t1:
                        hoist = (lambda t=t, tl=tl: ph1_pe(t + 1, colsl[tl + 1], negm_p, bnp),
                                 lambda pbt, t=t: ph1_early(t + 1, pbt),
                                 lambda t=t: ph1_late(t + 1))
                    ph2(t, yi_fn, state_fn, hoist)
                if bi == 3:
                    conv_state_proj(15, slice(96, 128))
                    for (o0, w_, s0) in osl:
                        self.dma("sp", d["nconv_p"][:, o0:o0 + w_], cst_[125:128, s0:s0 + w_], [bcst], [])
            p3, bp3 = self.psf[:, 3, :], self.bpf[3]
            pv = p3.rearrange("p (j n) -> p j n", j=4)
            for jj in range(4):
                self.tr(pv[:, jj, :], hT[:, jj * 128:(jj + 1) * 128], self.identf[:], [bhT, self.bconst], [bp3], inc=(jj == 3))
            self.evac(hout, pv, [bp3], [bhout], "act")
            self.dma("sp", d["nssm_p"][512 * g:512 * g + 512, :].rearrange("(j p) n -> p j n", p=128), hout, [bhout], [])

            self.release(mP)
            rawp, brawp = self.carve("rawp", [128, 6, 16, 11], F32)
            decs, bdecs = self.carve("decs", [128, 16, 4], F32)
            h0f, bh0f = self.carve("h0f", [128, 4, 128], F32)
            h0T, bh0T = self.carve("h0T", [128, 512], BF16)
            hnw, bhnw = self.carve("hnw", [128, 4, 128], F32)
            xdm, bxdm = self.carve("xdm", [128, 512], BF16)
            dcb, bdcb = self.carve("dcb", [128, 8], F32)
            bm16, bbm16 = self.carve("bm16", [128, 16, 128], BF16)
            self.dma("sp", bm16, d["c_bm16"], [], [bbm16])
            scv = cwk[0:48, 0:768]
            scs = d["state_conv"]
            for (o0, w_, s0) in osl:
                self.dma("sp", scv[:, s0:s0 + w_], scs[:, :, o0:o0 + w_].rearrange("b r c -> (b r) c"), [], bcwk)
            p2, bp2 = self.psf[:, 2, :], self.bpf[2]
            pvh = p2[:, 0:288].rearrange("p (c n) -> p c n", c=6)
            for cc in range(6):
                self.tr(pvh[:, cc, :], scv[:, cc * 128:(cc + 1) * 128], self.identf[0:48, 0:48], bcwk + [self.bconst], [bp2],
                        inc=(cc == 5))
            self.evac(rawp[:, :, :, 0:3], pvh.rearrange("p c (b r) -> p c b r", r=3), [bp2], [brawp], "dve")
            xtt = [self.bxt[16]]
            for cc in range(6):
                bi_ = 3 + cc % 3
                ps, bps = self.psf[:, bi_, 0:128], self.bpf[bi_]
                for k in range(8):
                    self.mm(ps, wsl[cc][:, k, :], self.XT[:, k, 2048:2176], k == 0, k == 7, [wsb[cc]] + xtt, [bps])
                self.evac(rawp[:, cc, :, 3:11], ps.rearrange("p (b t) -> p b t", t=8), [bps], [brawp], "act")
            psd, bpsd = self.psf[0:8, 0, 0:128], self.bpf[0]
            for k in range(8):
                self.mm(psd, wdt[:, k, :], self.XT[:, k, 2048:2176], k == 0, k == 7, [bwdt] + xtt, [bpsd])
            dt_path(psd, bpsd, 128, 8, scan_s)
            for cc in range(6):
                conv_chunk(cc, [rawp[:, cc, :, jj:jj + 8] for jj in range(4)], brawp,
                           xcT[:, cc, 0:128].rearrange("p (b t) -> p b t", t=8), (16, 8))
            conv_state_proj(16, slice(0, 128))
            for b in range(16):
                for (o0, w_, s0) in osl:
                    self.dma("sp", d["nconv_s"][b, :, o0:o0 + w_], cst_[8 * b + 5:8 * b + 8, s0:s0 + w_], [bcst], [])
            self.op("dve", "tensor_tensor", [bel, bc8], [bdg], out=dg.rearrange("p (b j) -> p b j", b=16), in0=selj,
                    in1=el[:, 0:16].unsqueeze(2).to_broadcast([8, 16, 4]), op=ALU.mult)
            self.mm(p2[:, 320:384], apar, dg, False, True, [bc8, bdg], [bp2])
            self.evac(decs.rearrange("p b j -> p (b j)"), p2[:, 320:384], [bp2], [bdecs], "dve")
            cols = slice(0, 128)
            ssrc = d["state_ssm"]
            v8 = lambda a_: a_.rearrange("p (r e) -> p r e", r=8)

            def yi_s(p5, bp5):
                self.op("dve", "tensor_tensor", [bxcT, bbm16], [bCM], out=CM,
                        in0=xcT[:, 5, 0:128].unsqueeze(1).to_broadcast([128, 16, 128]), in1=bm16, op=ALU.mult)
                for b in range(16):
                    self.dma("sp", h0f, ssrc[b, 512 * g:512 * g + 512, :].rearrange("(j p) n -> p j n", p=128), [], [bh0f])
                    p3, bp3 = self.psf[:, 3, :], self.bpf[3]
                    pv3 = p3.rearrange("p (j n) -> p j n", j=4)
                    for jj in range(4):
                        self.tr(pv3[:, jj, :], h0f[:, jj, :], self.identf[:], [bh0f, self.bconst], [bp3], inc=(jj == 3))
                    self.evac(h0T, p3, [bp3], [bh0T], "act")
                    self.mm(p5, CM[:, b, :], h0T, b == 0, b == 15, [bCM, bh0T], [bp5], inc=True)
                    self.op("dve", "tensor_scalar", [btmd, bseqm], [bdcb], out=dcb, in0=tmd[:, 16:24], scalar1=seqm[:, b:b + 1],
                            scalar2=None, op0=ALU.mult)
                    self.op("dve", "tensor_tensor", [bxs, bdcb], [bxdm], out=v8(xdm), in0=v8(xs[:, 0:512]),
                            in1=dcb.unsqueeze(2).to_broadcast([128, 8, 64]), op=ALU.mult)
                    p1, bp1 = self.psf[:, 1, :], self.bpf[1]
                    pv1 = p1.rearrange("p (j n) -> p j n", j=4)
                    for jj in range(4):
                        self.mm(pv1[:, jj, :], xdm[:, jj * 128:(jj + 1) * 128], xs[:, 512:640], jj == 0, jj == 3, [bxdm, bxs],
                                [bp1], inc=(jj == 3))
                    self.op("dve", "tensor_tensor", [bh0f, bdecs], [bhnw], out=hnw, in0=h0f,
                            in1=decs[:, b, :].unsqueeze(2).to_broadcast([128, 4, 128]), op=ALU.mult)
                    self.op("dve", "tensor_tensor", [bhnw, bp1], [bhnw], out=hnw, in0=hnw, in1=pv1, op=ALU.add)
                    self.dma("sp", d["nssm_s"][b, 512 * g:512 * g + 512, :].rearrange("(j p) n -> p j n", p=128), hnw,
                             [bhnw], [])

            pbt = ph1_pe(16, cols, negm_s, bns)
            ph1_early(16, pbt)
            ph1_late(16)
            ph2(16, yi_s, None, None)
        if pend is not None:
            self.make_xt(*pend)


def build_nc(stop_after=None):
    nc = bass.Bass("TRN2", target_bir_lowering=False)
    dr = {}

    def din(name, shape, dt=F32):
        dr[name] = nc.dram_tensor(name, list(shape), dt, kind="ExternalInput").ap()

    def dout(name, shape):
        dr[name] = nc.dram_tensor(name, list(shape), F32, kind="ExternalOutput").ap()

    din("x_p", [SEQ, D]); din("x_s", [128, D])
    din("cache_k", [2, 16, 128, 256]); din("cache_v", [2, 16, 128, 256])
    din("state_conv", [16, 3, 3072]); din("state_ssm", [16, 2048, 128]); din("state_pool", [16, 15, D])
    din("rel_bias", [32, 16]); din("attn_w_qkv", [2, D, 1536]); din("attn_b_qkv", [2, 1536])
    din("attn_w_o", [2, D, D]); din("attn_b_o", [2, D]); din("attn_sinks", [2, 16])
    din("ssm_w_in", [1, D, 5152]); din("ssm_conv_w", [1, 4, 3072]); din("ssm_conv_b", [1, 3072])
    din("ssm_dt_bias", [1, 32]); din("ssm_a_log", [1, 32]); din("ssm_d", [1, 32]); din("ssm_norm_w", [1, 2048])
    din("ssm_w_out", [1, 2048, D]); din("pool_w", [1, 4, 256, 256]); din("pool_scale", [1, D])
    din("ffn_w_gate", [4, D, DFF]); din("ffn_w_up", [4, D, DFF]); din("ffn_w_down", [4, DFF, D])
    din("ln_g", [4, 2, D]); din("ln_b", [4, 2, D])
    for k, v in host_constants().items():
        din(k, v.shape, BF16 if v.dtype == ml_dtypes.bfloat16 else F32)
    dout("y_p", [SEQ, D]); dout("y_s", [128, D])
    dout("nk_p", [2, 128, 256]); dout("nv_p", [2, 128, 256]); dout("nconv_p", [3, 3072]); dout("nssm_p", [2048, 128])
    dout("npool_p", [15, D])
    dout("nk_s", [2, 16, 128, 256]); dout("nv_s", [2, 16, 128, 256]); dout("nconv_s", [16, 3, 3072])
    dout("nssm_s", [16, 2048, 128]); dout("npool_s", [16, 15, D])
    dr["gtab"] = nc.dram_tensor("gtab", [16, 384], F32, kind="Internal").ap()

    with ExitStack() as es:
        S = Sched(nc, es)
        kb = KB(nc, S, es, dr, stop_after)
        kb.load_consts()
        kb.build_bias_tables()
        kb.load_x()
        nl = DEPTH if stop_after is None else stop_after
        import os
        skip = os.environ.get("KSKIP", "")
        for l in range(nl):
            kind = l % 3
            if "mix" in skip:
                pass
            elif kind == 0:
                kb.attn(l // 3, l)
            elif kind == 1:
                kb.ssm(l)
            else:
                kb.pool(l)
            if "ffn" not in skip:
                kb.ffn(l, final=(l == nl - 1))
        allb = [b for b in _all_bufs(kb)]
        S.wait_all("sp", allb)
        S.emit()
    return nc


def _all_bufs(kb):
    out = list(kb.bxr) + list(kb.bxt) + [kb.blng, kb.blnb, kb.bconst, kb.bd2d] + kb.bxb16 + kb.bstat + kb.bpf + kb.bpb
    out += [b for _, b in kb.abufs]
    dummy = Buf("retired", kb.retired)
    out.append(dummy)
    return out


_NC_CACHE = {}


def shard_inputs(inputs):
    consts = host_constants()
    maps = []
    f = lambda a: np.ascontiguousarray(np.asarray(a, dtype=np.float32))
    shared = {k: f(inputs[k]) for k in ("rel_bias", "attn_w_qkv", "attn_b_qkv", "attn_w_o", "attn_b_o", "attn_sinks", "ssm_w_in",
                                        "ssm_conv_w", "ssm_conv_b", "ssm_dt_bias", "ssm_a_log", "ssm_d", "ssm_norm_w", "ssm_w_out",
                                        "pool_w", "pool_scale", "ffn_w_gate", "ffn_w_up", "ffn_w_down", "ln_g", "ln_b")}
    for c in range(NCORES):
        sl = slice(16 * c, 16 * c + 16)
        m = dict(shared)
        m.update(consts)
        m["x_p"] = f(inputs["x_prompt"][c])
        m["x_s"] = f(inputs["x_sample"][sl]).reshape(128, D)
        m["cache_k"] = f(inputs["cache_k"][:, sl]).reshape(2, 16, 128, 256)
        m["cache_v"] = f(inputs["cache_v"][:, sl]).reshape(2, 16, 128, 256)
        m["state_conv"] = f(inputs["state_conv"][0, sl])
        m["state_ssm"] = f(inputs["state_ssm"][0, sl]).reshape(16, 2048, 128)
        m["state_pool"] = f(inputs["state_pool"][0, sl])
        maps.append(m)
    return maps


def gather_outputs(res):
    R = res
    cat = lambda k: np.stack([r[k] for r in R], 0)
    y_p = cat("y_p")
    y_s = cat("y_s").reshape(128, 8, D)
    nk_p = cat("nk_p").transpose(1, 0, 2, 3).reshape(2, 8, 128, 4, 64)
    nv_p = cat("nv_p").transpose(1, 0, 2, 3).reshape(2, 8, 128, 4, 64)
    nconv_p = cat("nconv_p")[None]
    nssm_p = cat("nssm_p").reshape(1, 8, 32, 64, 128)
    npool_p = cat("npool_p")[None]
    nk_s = np.concatenate([r["nk_s"] for r in R], 1).reshape(2, 128, 128, 4, 64)
    nv_s = np.concatenate([r["nv_s"] for r in R], 1).reshape(2, 128, 128, 4, 64)
    nconv_s = np.concatenate([r["nconv_s"] for r in R], 0)[None]
    nssm_s = np.concatenate([r["nssm_s"] for r in R], 0).reshape(1, 128, 32, 64, 128)
    npool_s = np.concatenate([r["npool_s"] for r in R], 0)[None]
    outs = (y_p, y_s, nk_p, nv_p, nconv_p, nssm_p, npool_p, nk_s, nv_s, nconv_s, nssm_s, npool_s)
    return tuple(np.ascontiguousarray(o, dtype=np.float32) for o in outs)


def kernel(**inputs):
    if "nc" not in _NC_CACHE:
        _NC_CACHE["nc"] = build_nc()
    nc = _NC_CACHE["nc"]
    maps = shard_inputs(inputs)
    res = run_bass_kernel_spmd(nc, maps, core_ids=list(range(NCORES)))
    return gather_outputs(res.results)
```

```python
import math
from contextlib import ExitStack

import ml_dtypes
import numpy as np

import concourse.bass as bass
import concourse.mybir as mybir
from concourse.bass_utils import run_bass_kernel_spmd

F32 = mybir.dt.float32
BF16 = mybir.dt.bfloat16
AF = mybir.ActivationFunctionType
ALU = mybir.AluOpType
AX = mybir.AxisListType

NCORES = 8
D = 1024
SEQ = 2048
NT = 17
NTOK = NT * 128
DEPTH = 4
DFF = 2816
ALPHA = (2 * DEPTH) ** 0.25
LN_EPS = 1e-5
RMS_EPS = 1e-5
NEG = -30000.0
BLKS = [(0, 4), (4, 8), (8, 12), (12, 16), (16, 17)]
ARENA_F32 = 23552
QHEADS = [(0, 4), (1, 5), (2, 6), (3, 7), (8, 12), (9, 13), (10, 14), (11, 15)]
POOL_W = (2, 4, 8, 16)


class Buf:
    __slots__ = ("name", "w", "r", "dsem", "excl")

    def __init__(self, name, r=None, excl=False):
        self.name = name
        self.excl = excl
        self.w = None
        self.r = dict(r) if r else {}
        self.dsem = None


class Sched:
    ENG = ("pe", "act", "dve", "pool", "sp")

    def __init__(self, nc, es):
        self.nc = nc
        self.es = es
        self.sems = {}
        self.cnt = {}
        self.seen = {e: {} for e in self.ENG}
        self.ops = {e: [] for e in self.ENG}
        for e in ("pe", "act", "dve", "pool"):
            self._newsem(e)
        self.nd = 0
        self.free_dsems = []

    def _newsem(self, key):
        self.sems[key] = self.es.enter_context(self.nc.semaphore(f"s_{key}"))
        self.cnt[key] = 0

    def _deps(self, eng, reads, writes):
        need = {}

        def add(k, v):
            if v > need.get(k, 0):
                need[k] = v
        for b in reads:
            if b.w is not None:
                add(*b.w)
            if b.excl:
                for k, v in b.r.items():
                    if k != eng:
                        add(k, v)
        for b in writes:
            if b.w is not None and b.w[0] != eng:
                add(*b.w)
            for k, v in b.r.items():
                if k != eng:
                    add(k, v)
        waits = []
        seen = self.seen[eng]
        for k, v in need.items():
            if seen.get(k, 0) < v:
                seen[k] = v
                waits.append((k, v))
        return waits

    def op(self, eng, fn, reads=(), writes=(), inc=True):
        waits = self._deps(eng, reads, writes)
        val = self.cnt[eng] + 1
        if inc:
            self.cnt[eng] = val
        for b in reads:
            b.r[eng] = val
        for b in writes:
            b.w = (eng, val)
            b.r = {}
        self.ops[eng].append((waits, fn, (eng, 1) if inc else None))

    def dma(self, q, fn, reads=(), writes=(), own=None):
        waits = self._deps(q, reads, writes)
        own = own or (writes[0] if writes else reads[0])
        if own.dsem is None:
            if self.free_dsems:
                own.dsem = self.free_dsems.pop()
            else:
                own.dsem = f"d{self.nd}"
                self.nd += 1
                self._newsem(own.dsem)
        k = own.dsem
        self.cnt[k] += 16
        val = self.cnt[k]
        for b in reads:
            b.r[k] = val
        for b in writes:
            b.w = (k, val)
            b.r = {}
        self.ops[q].append((waits, fn, (k, 16)))

    def wait_all(self, eng, bufs):
        waits = self._deps(eng, (), bufs)
        self.ops[eng].append((waits, None, None))

    def emit(self):
        with self.nc.Block() as block:
            def run(name):
                def body(e):
                    for waits, fn, inc in self.ops[name]:
                        for k, v in waits:
                            e.wait_ge(self.sems[k], v)
                        if fn is not None:
                            ins = fn(e)
                            if inc is not None:
                                ins.then_inc(self.sems[inc[0]], inc[1])
                return body
            block.sync(run("sp"))
            block.scalar(run("act"))
            block.vector(run("dve"))
            block.gpsimd(run("pool"))
            block.tensor(run("pe"))


def _t5_bucket(n):
    n = np.maximum(n, 0)
    nf = np.maximum(n, 1).astype(np.float32)
    large = 16 + (np.log(nf / np.float32(16)) / np.float32(math.log(8.0)) * np.float32(16)).astype(np.int32)
    large = np.minimum(large, 31)
    return np.where(n < 16, n, large)


def host_constants():
    bf = ml_dtypes.bfloat16
    c = {}
    c["c_identb"] = np.eye(128, dtype=np.float32).astype(bf)
    c["c_identf"] = np.eye(128, dtype=np.float32)
    c["c_antib"] = np.eye(128, dtype=np.float32)[::-1].copy().astype(bf)
    c["c_onesf"] = np.ones((128, 128), np.float32)
    i = np.arange(384)
    dist = 255 - i
    valid = (dist >= 0) & (dist < 128)
    oh = np.zeros((33, 384), np.float32)
    bk = _t5_bucket(dist)
    for ii in range(384):
        if valid[ii]:
            oh[bk[ii], ii] = 1.0
        else:
            oh[32, ii] = NEG
    c["c_onehot"] = oh
    p = np.arange(128)
    sels = np.zeros((128, 128), np.float32)
    sels[127 - (p % 8), p] = 1.0
    c["c_sels"] = sels.astype(bf)
    negb = np.where((p[:, None] // 8) == (p[None, :] // 8), 0.0, NEG).astype(np.float32)
    c["c_negb"] = negb.astype(bf)
    bm16 = np.zeros((128, 16, 128), np.float32)
    for b in range(16):
        bm16[:, b, 8 * b:8 * b + 8] = 1.0
    c["c_bm16"] = bm16.astype(bf)
    s_ = p[:, None]
    t_ = p[None, :]
    negm_p = np.where(t_ < s_, NEG, 0.0).astype(np.float32)
    negm_s = np.where((t_ >= s_) & (t_ // 8 == s_ // 8), 0.0, NEG).astype(np.float32)
    c["c_negm_p"] = np.tile(negm_p[:, None, :], (1, 8, 1)).reshape(128, 1024).astype(bf)
    c["c_negm_s"] = np.tile(negm_s[:, None, :], (1, 8, 1)).reshape(128, 1024).astype(bf)
    sel8 = np.zeros((8, 8, 128), np.float32)
    for r in range(8):
        sel8[r, r, :] = 1.0
    c["c_sel8"] = sel8
    sm_p = np.ones((8, 512), np.float32)
    sm_p[:, ::128] = 0.0
    sm_s = np.ones((8, 128), np.float32)
    sm_s[:, ::8] = 0.0
    c["c_scan_p"] = sm_p
    c["c_scan_s"] = sm_s
    apar = np.zeros((8, 128), np.float32)
    for k in range(8):
        apar[k, (k % 2) * 64:(k % 2) * 64 + 64] = 1.0
    c["c_apar"] = apar
    selj = np.zeros((8, 16, 4), np.float32)
    for k in range(8):
        selj[k, :, k // 2] = 1.0
    c["c_selj"] = selj
    seqm = np.zeros((128, 16), np.float32)
    seqm[p, p // 8] = 1.0
    c["c_seqm"] = seqm
    mt = np.zeros((24, 128, 128), np.float32)
    for g, w in enumerate(POOL_W):
        for t in range(128):
            for s in range(t - w + 1, t + 1):
                if s >= 0:
                    mt[4 + g, s, t] += 1.0 / w
                else:
                    mt[0 + g, s + 128, t] += 1.0 / w
            mt[4 + g, t, t] -= 1.0
            cnt = min(t + 1, w)
            for s in range(max(0, t - w + 1), t + 1):
                mt[8 + g, s, t] += 1.0 / cnt
            mt[8 + g, t, t] -= 1.0
            b, tt = t // 8, t % 8
            for pos in range(tt - w + 1, tt + 1):
                if pos >= 0:
                    mt[12 + g, b * 8 + pos, t] += 1.0 / w
                else:
                    j = pos + 15
                    if b < 8:
                        mt[16 + g, b * 15 + j, t] += 1.0 / w
                    else:
                        mt[20 + g, (b - 8) * 15 + j, t] += 1.0 / w
            mt[12 + g, t, t] -= 1.0
    c["c_poolmt"] = np.ascontiguousarray(mt.transpose(1, 0, 2))
    return c


class KB:
    def __init__(self, nc, S, es, dr, stop_after=None):
        self.nc, self.S, self.es, self.dr = nc, S, es, dr
        self.stop_after = stop_after
        sb = lambda name, shape, dt: es.enter_context(nc.sbuf_tensor(name, shape, dt))
        self.XR = sb("XR", [128, NT, D], F32)
        self.XT = sb("XT", [128, 8, NTOK], BF16)
        self.bxr = [Buf(f"xr{t}") for t in range(NT)]
        self.bxt = [Buf(f"xt{t}") for t in range(NT)]
        self.lng = sb("lng", [128, D], F32)
        self.lnb = sb("lnb", [128, D], F32)
        self.blng, self.blnb = Buf("lng"), Buf("lnb")
        self.xb16 = sb("xb16", [128, 2, D], BF16)
        self.bxb16 = [Buf("xb16_0"), Buf("xb16_1")]
        self.xbi = 0
        self.identb = sb("identb", [128, 128], BF16)
        self.identf = sb("identf", [128, 128], F32)
        self.onesf = sb("onesf", [128, 128], F32)
        self.mhalf = sb("mhalf", [128, 1], F32)
        self.bconst = Buf("const")
        self.stat = sb("stat", [128, 4, 16], F32)
        self.bstat = [Buf(f"stat{i}") for i in range(4)]
        self.sti = 0
        self.arena = sb("arena", [128, ARENA_F32], F32)
        self.aoff = 0
        self.abufs = []
        self.retired = {}
        self.psf = es.enter_context(nc.psum_tensor("psf", [128, 6, 512], F32))
        self.psb = es.enter_context(nc.psum_tensor("psb", [128, 2, 1024], BF16))
        self.bpf = [Buf(f"pf{i}", excl=True) for i in range(6)]
        self.bpb = [Buf("pb0", excl=True), Buf("pb1", excl=True)]
        self.pbi = 0
        self.bd2d = Buf("d2d")
        self.outbufs = [self.bd2d]
        self.evi = 0

    def op(self, eng, name, R, W, inc=True, **kw):
        self.S.op(eng, lambda e, n=name, kw=kw: getattr(e, n)(**kw), R, W, inc)

    def mm(self, out, lhsT, rhs, start, stop, R, W, inc=None):
        if inc is None:
            inc = stop
        self.S.op("pe", lambda e: e.matmul(out, lhsT=lhsT, rhs=rhs, start=start, stop=stop), R, W, inc)

    def tr(self, out, in_, ident, R, W, inc):
        self.S.op("pe", lambda e: e.transpose(out=out, in_=in_, identity=ident), R, W, inc)

    def act(self, out, in_, func, R, W, **kw):
        self.S.op("act", lambda e: e.activation(out=out, in_=in_, func=func, **kw), R, W)

    def dma(self, q, out, in_, R, W, own=None, **kw):
        self.S.dma(q, lambda e: e.dma_start(out=out, in_=in_, **kw), R, W, own)

    def evac(self, out, in_, R, W, eng=None):
        if eng is None:
            eng = ("act", "dve")[self.evi % 2]
            self.evi += 1
        if eng == "act":
            self.act(out, in_, AF.Copy, R, W)
        else:
            self.op(eng, "tensor_copy", R, W, out=out, in_=in_)

    def carve(self, name, shape, dt):
        n = int(np.prod(shape[1:]))
        nb = n * (2 if dt == BF16 else 4)
        n4 = (nb + 3) // 4
        assert self.aoff + n4 <= ARENA_F32, f"arena overflow at {name}: {self.aoff}+{n4}"
        v = self.arena[:, self.aoff:self.aoff + n4]
        self.aoff += n4
        if dt != F32:
            v = v.bitcast(dt)
        v = v[0:shape[0], 0:n]
        if len(shape) == 3:
            v = v.rearrange("p (a b) -> p a b", a=shape[1])
        elif len(shape) == 4:
            v = v.rearrange("p (a b c) -> p a b c", a=shape[1], b=shape[2])
        b = Buf(name, self.retired)
        self.abufs.append((self.aoff, b))
        return v, b

    def mark(self):
        return self.aoff

    def release(self, mark=0):
        keep = []
        for off, b in self.abufs:
            if off > mark:
                for k, v in ([b.w] if b.w else []) + list(b.r.items()):
                    if v > self.retired.get(k, 0):
                        self.retired[k] = v
                if b.dsem is not None:
                    self.S.free_dsems.append(b.dsem)
                    b.dsem = None
            else:
                keep.append((off, b))
        self.abufs = keep
        self.aoff = mark

    def pair(self, i):
        return self.psf[:, 2 * i:2 * i + 2, :].rearrange("p a b -> p (a b)")

    def bpair(self, i):
        return [self.bpf[2 * i], self.bpf[2 * i + 1]]

    def next_pb(self):
        i = self.pbi % 2
        self.pbi += 1
        return self.psb[:, i, :], self.bpb[i]

    def next_stat(self):
        i = self.sti % 4
        self.sti += 1
        return self.stat[:, i, :], self.bstat[i]

    def load_consts(self):
        d = self.dr
        W = [self.bconst]
        self.dma("sp", self.identb[:], d["c_identb"], [], W)
        self.dma("sp", self.identf[:], d["c_identf"], [], W)
        self.dma("sp", self.onesf[:], d["c_onesf"], [], W)
        self.op("pool", "memset", [], W, ap=self.mhalf[:], constant=-0.5)

    def load_x(self):
        d = self.dr
        self.dma("sp", self.XR[:, 0:16, :], d["x_p"].rearrange("(i p) d -> p i d", p=128), [], self.bxr[0:16])
        self.dma("sp", self.XR[:, 16, :], d["x_s"], [], [self.bxr[16]])
        for t in range(NT):
            slot = self.cast_xb(t)
            self.make_xt(t, slot)

    def cast_xb(self, t):
        slot = self.xbi % 2
        self.xbi += 1
        self.act(self.xb16[:, slot, :], self.XR[:, t, :], AF.Copy, [self.bxr[t]], [self.bxb16[slot]])
        return slot

    def make_xt(self, t, slot):
        pb, bpb = self.next_pb()
        pv = pb.rearrange("p (c n) -> p c n", c=8)
        for c in range(8):
            self.tr(pv[:, c, :], self.xb16[:, slot, c * 128:(c + 1) * 128], self.identb[:],
                    [self.bxb16[slot], self.bconst], [bpb], inc=(c == 7))
        self.evac(self.XT[:, :, t * 128:(t + 1) * 128], pv, [bpb], [self.bxt[t]])

    def load_ln(self, l, which):
        d = self.dr
        self.dma("sp", self.lng[:], d["ln_g"][l, which].partition_broadcast(128), [], [self.blng])
        self.dma("sp", self.lnb[:], d["ln_b"][l, which].partition_broadcast(128), [], [self.blnb])

    def ln1(self, t, final=False):
        st, bst = self.next_stat()
        x = self.XR[:, t, :]
        bx = self.bxr[t]
        for i in range(2):
            self.op("dve", "bn_stats", [bx], [bst], out=st[:, 6 * i:6 * i + 6], in_=x[:, i * 512:(i + 1) * 512])
        self.op("dve", "bn_aggr", [bst], [bst], out=st[:, 12:14], in_=st[:, 0:12])
        self.op("dve", "tensor_scalar", [bst], [bst], out=st[:, 14:15], in0=st[:, 13:14], scalar1=LN_EPS,
                scalar2=None, op0=ALU.add)
        self.op("pool", "tensor_tensor", [bst, self.bconst], [bst], out=st[:, 15:16], in0=st[:, 14:15],
                in1=self.mhalf[:], op=ALU.pow)
        self.op("dve", "tensor_scalar", [bx, bst], [bx], out=x, in0=x, scalar1=st[:, 12:13], scalar2=st[:, 15:16],
                op0=ALU.subtract, op1=ALU.mult)
        self.op("dve", "tensor_tensor", [bx, self.blng], [bx], out=x, in0=x, in1=self.lng[:], op=ALU.mult)
        self.op("dve", "tensor_tensor", [bx, self.blnb], [bx], out=x, in0=x, in1=self.lnb[:], op=ALU.add)
        if final:
            d = self.dr
            if t < 16:
                self.dma("sp", d["y_p"][t * 128:(t + 1) * 128, :], x, [bx], [])
            else:
                self.dma("sp", d["y_s"], x, [bx], [])
            return None
        return self.cast_xb(t)

    def ln_all(self, pend, t, final=False):
        slot = self.ln1(t, final)
        if pend is not None:
            self.make_xt(*pend)
        return None if final else (t, slot)

    def ffn(self, l, final):
        d = self.dr
        self.release(0)
        groups = [(i * 512, 512) for i in range(5)] + [(2560, 256)]
        wg, wu, wd, bw = [], [], [], []
        for s in range(2):
            a, b1 = self.carve(f"wg{s}", [128, 8, 512], BF16)
            u, b2 = self.carve(f"wu{s}", [128, 8, 512], BF16)
            dd, b3 = self.carve(f"wd{s}", [128, 4, 1024], BF16)
            wg.append(a), wu.append(u), wd.append(dd), bw.append((b1, b2, b3))
        at, bat, sg, bsg = [], [], [], []
        for s in range(2):
            a, b = self.carve(f"at{s}", [128, 4, 512], BF16)
            at.append(a), bat.append(b)
            a, b = self.carve(f"sg{s}", [128, 512], F32)
            sg.append(a), bsg.append(b)

        def load(gi):
            ff0, fw = groups[gi]
            s = gi % 2
            nfc = fw // 128
            self.dma("pool", wg[s][:, :, 0:fw], d["ffn_w_gate"][l][:, ff0:ff0 + fw].rearrange("(c p) n -> p c n", p=128),
                     [], [bw[s][0]])
            self.dma("pool", wu[s][:, :, 0:fw], d["ffn_w_up"][l][:, ff0:ff0 + fw].rearrange("(c p) n -> p c n", p=128),
                     [], [bw[s][1]])
            self.dma("pool", wd[s][:, 0:nfc, :], d["ffn_w_down"][l][ff0:ff0 + fw, :].rearrange("(c p) n -> p c n", p=128),
                     [], [bw[s][2]])

        load(0)
        load(1)
        self.load_ln(l, 1)
        units = [(gi, bi) for gi in range(len(groups)) for bi in range(len(BLKS))]
        state = {"pg": 0, "pend": None}

        def gu_steps(u):
            gi, bi = units[u]
            ff0, fw = groups[gi]
            s = gi % 2
            t0, t1 = BLKS[bi]
            c0, n = t0 * 128, (t1 - t0) * 128
            a_s = u % 2
            steps = []
            for fc in range(fw // 128):
                def step(fc=fc):
                    pgi = state["pg"] % 2
                    state["pg"] += 1
                    pg, bpg = self.psf[:, pgi, 0:n], self.bpf[pgi]
                    pu, bpu = self.psf[:, 2 + pgi, 0:n], self.bpf[2 + pgi]
                    for k in range(8):
                        self.mm(pg, wg[s][:, k, fc * 128:(fc + 1) * 128], self.XT[:, k, c0:c0 + n], k == 0, k == 7,
                                [bw[s][0]] + self.bxt[t0:t1], [bpg])
                    for k in range(8):
                        self.mm(pu, wu[s][:, k, fc * 128:(fc + 1) * 128], self.XT[:, k, c0:c0 + n], k == 0, k == 7,
                                [bw[s][1]] + self.bxt[t0:t1], [bpu])
                    self.act(sg[pgi][:, 0:n], pg, AF.Silu, [bpg], [bsg[pgi]])
                    self.op("dve", "tensor_tensor", [bsg[pgi], bpu], [bat[a_s]], out=at[a_s][:, fc, 0:n],
                            in0=sg[pgi][:, 0:n], in1=pu, op=ALU.mult)
                steps.append(step)
            return steps

        def d_steps(u):
            gi, bi = units[u]
            ff0, fw = groups[gi]
            s = gi % 2
            nfc = fw // 128
            t0, t1 = BLKS[bi]
            a_s = u % 2
            last = gi == len(groups) - 1
            steps = []
            for t in range(t0, t1):
                def step(t=t):
                    tl = t - t0
                    for half in range(2):
                        pd, bpd = self.psf[:, 4 + half, :], self.bpf[4 + half]
                        for fc in range(nfc):
                            self.mm(pd, at[a_s][:, fc, tl * 128:(tl + 1) * 128], wd[s][:, fc, half * 512:(half + 1) * 512],
                                    fc == 0, fc == nfc - 1, [bat[a_s], bw[s][2]], [bpd])
                        xs_ = self.XR[:, t, half * 512:(half + 1) * 512]
                        if gi == 0:
                            self.op("dve", "scalar_tensor_tensor", [self.bxr[t], bpd], [self.bxr[t]], out=xs_, in0=xs_,
                                    scalar=ALPHA, in1=pd, op0=ALU.mult, op1=ALU.add)
                        else:
                            self.op("dve", "tensor_tensor", [self.bxr[t], bpd], [self.bxr[t]], out=xs_, in0=xs_, in1=pd,
                                    op=ALU.add)
                    if last:
                        state["pend"] = self.ln_all(state["pend"], t, final)
                steps.append(step)
            if bi == len(BLKS) - 1 and gi + 2 < len(groups):
                steps.append(lambda: load(gi + 2))
            return steps

        prev_d = []
        for u in range(len(units)):
            g_ = gu_steps(u)
            m_ = max(len(g_), len(prev_d))
            for i in range(m_):
                if i < len(g_):
                    g_[i]()
                if i < len(prev_d):
                    prev_d[i]()
            prev_d = d_steps(u)
        for st_ in prev_d:
            st_()
        if state["pend"] is not None:
            self.make_xt(*state["pend"])

    def build_bias_tables(self):
        d = self.dr
        m = self.mark()
        rb, brb = self.carve("rb33", [33, 16], F32)
        oh, boh = self.carve("oh33", [33, 384], F32)
        gs, bgs = self.carve("gsb", [16, 384], F32)
        self.dma("sp", rb[0:32, :], d["rel_bias"], [], [brb])
        self.op("pool", "memset", [], [brb], ap=rb[32:33, :], constant=1.0)
        self.dma("sp", oh, d["c_onehot"], [], [boh])
        ps, bps = self.psf[0:16, 0, 0:384], self.bpf[0]
        self.mm(ps, rb, oh, True, True, [brb, boh], [bps])
        self.evac(gs, ps, [bps], [bgs], "act")
        self.bgtab = Buf("gtab")
        self.dma("sp", d["gtab"], gs, [bgs], [self.bgtab], own=bgs)
        self.release(m)

    def build_bm(self, bm, bbm, U, bU, sample, cst):
        bU = bU if isinstance(bU, list) else [bU]
        antib, sels, negb, bcs = cst
        for q4 in range(4):
            pr = self.pair(q4 % 2)
            bpr = self.bpair(q4 % 2)
            pv = pr.rearrange("p (h k) -> p h k", h=4)
            if not sample:
                for hh2 in range(2):
                    self.mm(pr[:, hh2 * 512:(hh2 + 1) * 512], antib,
                            U[:, 4 * q4 + 2 * hh2:4 * q4 + 2 * hh2 + 2, :].rearrange("p h k -> p (h k)"), True, True,
                            bU + [bcs], [bpr[hh2]])
            else:
                for i in range(4):
                    h = 4 * q4 + i
                    first = (i % 2 == 0)
                    self.mm(pv[:, i, 0:128], sels, U[:, h, 0:128], first, False, bU + [bcs], [bpr[i // 2]], inc=False)
                    for b in range(16):
                        self.mm(pv[:, i, 128 + 8 * b:136 + 8 * b], sels, U[:, h, 128:136], False, False, bU + [bcs],
                                [bpr[i // 2]], inc=False)
                    self.mm(pv[:, i, 128:256], self.identb[:], negb, False, True, [bcs, self.bconst], [bpr[i // 2]],
                            inc=True)
            self.evac(bm[:, 4 * q4:4 * q4 + 4, :], pv, bpr, [bbm])

    def attn(self, j, l):
        d = self.dr
        self.release(0)
        wo, bwo = self.carve("wo", [128, 8, 1024], BF16)
        kT, bkT = self.carve("kT", [128, 2, NTOK], BF16)
        Vb, _ = self.carve("Vb", [128, NT, 256], BF16)
        bVb = [Buf(f"Vb{t}", self.retired) for t in range(NT)]
        bkTt = [Buf(f"kT{t}", self.retired) for t in range(NT)]
        for b in bVb + bkTt:
            self.abufs.append((self.aoff, b))
        qTf, bqT = self.carve("qT", [128, 4096], BF16)
        qT = qTf.rearrange("p (c n) -> p c n", c=8)
        bm, bbm = self.carve("bm", [128, 16, 256], BF16)
        sm, bsm = self.carve("attn_small", [128, 64], F32)
        bq, bk, snk, nsnk = sm[:, 0:8], sm[:, 8:10], sm[:, 16:32], sm[:, 32:48]
        brow, bbrow = self.carve("brow", [65, 512], F32)
        cstt, bcs = self.carve("acst", [128, 3, 128], BF16)
        w8f, _ = self.carve("w8", [128, 6144], BF16)
        w8 = w8f[:, 0:4096]
        P2 = [w8f[:, 0:1024].rearrange("p (h k) -> p h k", h=4), w8f[:, 4096:5120].rearrange("p (h k) -> p h k", h=4)]
        PT2 = [w8f[:, 1024:2048].rearrange("p (h a n) -> p h a n", h=4, a=2),
               w8f[:, 5120:6144].rearrange("p (h a n) -> p h a n", h=4, a=2)]
        Ob = w8f[:, 2048:3072].rearrange("p (h e) -> p h e", h=16)
        OT = w8f[:, 3072:4096].rearrange("p (c n) -> p c n", c=8)
        bP2 = [Buf("P0", self.retired), Buf("P1", self.retired)]
        bPT2 = [Buf("PT0", self.retired), Buf("PT1", self.retired)]
        bOb, bOT = Buf("Ob", self.retired), Buf("OT", self.retired)
        bw8 = [bP2[0], bPT2[0], bOb, bOT]
        for b in bw8 + [bP2[1], bPT2[1]]:
            self.abufs.append((self.aoff, b))
        st2, bst2 = self.carve("ast", [128, 128], F32)
        kvo, bkvo = self.carve("kvo", [128, 512], F32)
        mW = self.mark()
        wq, bwq = self.carve("wq", [128, 8, 8, 128], BF16)
        wkv, bwkv = self.carve("wkv", [128, 8, 512], BF16)
        U, bU = qTf.rearrange("p (h k) -> p h k", h=16), bqT

        wsrc = d["attn_w_qkv"][j]
        for hh in range(2):
            for cg in range(2):
                c0 = cg * 512 + hh * 256
                for ci in range(4):
                    self.dma("pool", wq[:, :, 4 * cg + ci, hh * 64:(hh + 1) * 64],
                             wsrc[:, c0 + 64 * ci:c0 + 64 * ci + 64].rearrange("(k p) n -> p k n", p=128), [], [bwq])
        self.dma("pool", wkv, wsrc[:, 1024:1536].rearrange("(k p) n -> p k n", p=128), [], [bwkv])
        self.dma("pool", wo, d["attn_w_o"][j].rearrange("(k p) n -> p k n", p=128), [], [bwo])
        bsrc = d["attn_b_qkv"][j]
        for hh in range(2):
            for cg in range(2):
                c0 = cg * 512 + hh * 256
                self.dma("sp", bq[hh * 64:(hh + 1) * 64, 4 * cg:4 * cg + 4],
                         bsrc[c0:c0 + 256].rearrange("(c n) -> n c", n=64), [], [bsm], allow_slow_non_contiguous=True)
        self.dma("sp", bk, bsrc[1024:1280].rearrange("(c p) -> p c", p=128), [], [bsm], allow_slow_non_contiguous=True)
        self.dma("sp", snk, d["attn_sinks"][j].partition_broadcast(128), [], [bsm])
        self.dma("sp", brow[0:1, :], bsrc[1024:1536].rearrange("(o n) -> o n", o=1), [], [bbrow])
        self.dma("sp", brow[32:33, :], d["attn_b_o"][j, 0:512].rearrange("(o n) -> o n", o=1), [], [bbrow])
        self.dma("sp", brow[64:65, :], d["attn_b_o"][j, 512:1024].rearrange("(o n) -> o n", o=1), [], [bbrow])
        self.dma("sp", cstt[:, 0, :], d["c_antib"], [], [bcs])
        self.dma("sp", cstt[:, 1, :], d["c_sels"], [], [bcs])
        self.dma("sp", cstt[:, 2, :], d["c_negb"], [], [bcs])
        self.op("dve", "tensor_scalar", [bsm], [bsm], out=bq, in0=bq, scalar1=0.125, scalar2=None, op0=ALU.mult)
        self.op("dve", "tensor_scalar", [bsm], [bsm], out=nsnk, in0=snk, scalar1=-1.0, scalar2=None, op0=ALU.mult)
        gt = d["gtab"]
        self.dma("pool", U, bass.AP(gt.tensor, 0, [[1, 128], [384, 16], [1, 256]]), [self.bgtab], [bU])
        cst = (cstt[:, 0, :], cstt[:, 1, :], cstt[:, 2, :], bcs)
        import os
        dbg = os.environ.get("KATT", "")
        if dbg == "a0":
            return
        self.build_bm(bm, bbm, U, bU, False, cst)
        self.load_ln(l, 0)
        if dbg == "a1":
            return

        pend = None
        pst = {"pri": 0, "smp": None}

        def proj(bi):
            t0, t1 = BLKS[bi]
            pri = pst["pri"]
            c0, n = t0 * 128, (t1 - t0) * 128
            xtb = self.bxt[t0:t1]
            for c in range(2):
                bi_ = pri % 4
                pri += 1
                ps, bps = self.psf[:, bi_, 0:n], self.bpf[bi_]
                for k in range(8):
                    self.mm(ps, wkv[:, k, c * 128:(c + 1) * 128], self.XT[:, k, c0:c0 + n], k == 0, k == 7,
                            [bwkv] + xtb, [bps])
                self.act(kT[:, c, c0:c0 + n], ps, AF.Identity, [bps, bsm], bkTt[t0:t1], bias=bk[:, c:c + 1], scale=1.0)
            for c in range(8):
                bi_ = pri % 4
                pri += 1
                ps, bps = self.psf[:, bi_, 0:n], self.bpf[bi_]
                for k in range(8):
                    self.mm(ps, wq[:, k, c, :], self.XT[:, k, c0:c0 + n], k == 0, k == 7, [bwq] + xtb, [bps])
                self.act(qT[:, c, 0:n], ps, AF.Identity, [bps, bsm], [bqT], bias=bq[:, c:c + 1], scale=0.125)
            for t in range(t0, t1):
                if dbg == "p1":
                    return
                bi_ = pri % 4
                pri += 1
                ps, bps = self.psf[:, bi_, :], self.bpf[bi_]
                self.mm(ps, self.onesf[0:1, :], brow[0:1, :], True, False, [self.bconst, bbrow], [bps], inc=False)
                for k in range(8):
                    self.mm(ps, self.XT[:, k, t * 128:(t + 1) * 128], wkv[:, k, :], False, k == 7, [bwkv, self.bxt[t]], [bps])
                if t < 15:
                    self.evac(Vb[:, t, :], ps[:, 256:512], [bps], [bVb[t]])
                if t >= 15 and dbg != "p2":
                    self.evac(kvo, ps, [bps], [bkvo], "act")
                    self.op("dve", "tensor_copy", [bkvo], [bVb[t]], out=Vb[:, t, :], in_=kvo[:, 256:512])
                    if dbg == "p3a" and t == 16:
                        return
                    if dbg == "p3c":
                        return
                    if dbg == "p3d":
                        if t == 15:
                            self.dma("sp", d["nk_p"][j], kvo[:, 0:256], [bkvo], [])
                        return
                    if dbg == "p3b" and t == 15:
                        return
                    if t == 15:
                        self.dma("sp", d["nk_p"][j], kvo[:, 0:256], [bkvo], [])
                        self.dma("sp", d["nv_p"][j], kvo[:, 256:512], [bkvo], [])
                    else:
                        for b in range(16):
                            self.dma("sp", d["nk_s"][j, b, 120:128, :], kvo[8 * b:8 * b + 8, 0:256], [bkvo], [])
                            self.dma("sp", d["nv_s"][j, b, 120:128, :], kvo[8 * b:8 * b + 8, 256:512], [bkvo], [])
            if dbg in ("p1", "p2", "p3", "p3a", "p3b", "p3c", "p3d"):
                return
            if t0 == 16:
                self.dma("sp", d["nk_s"][j, :, 0:120, :], d["cache_k"][j, :, 8:128, :], [self.bd2d], [], own=self.bd2d)
                self.dma("sp", d["nv_s"][j, :, 0:120, :], d["cache_v"][j, :, 8:128, :], [self.bd2d], [], own=self.bd2d)
                if dbg == "p4":
                    return
                kcr = w8.rearrange("p (b c) -> p b c", b=16)
                self.dma("pool", kcr, bass.AP(gt.tensor, 0, [[1, 128], [384, 16], [1, 256]]), [self.bgtab], bw8)
                self.build_bm(bm, bbm, kcr, bw8, True, cst)
                self.release(mW)
                vc, bvc = self.carve("vc", [128, 16, 256], BF16)
                kcT, bkcT = self.carve("kcT", [128, 2, 16, 128], BF16)
                bm16, bbm16 = self.carve("bm16", [128, 16, 128], BF16)
                QM1, bQM1 = self.carve("QM", [128, 16, 128], BF16)
                QM, bQM = [QM1, QM1], [bQM1, bQM1]
                self.dma("sp", bm16, d["c_bm16"], [], [bbm16])
                self.dma("pool", kcr, d["cache_k"][j].rearrange("b k c -> k b c"), [], bw8)
                self.dma("pool", vc, d["cache_v"][j].rearrange("b k c -> k b c"), [], [bvc])
                for kc in range(2):
                    for b8 in range(2):
                        pb, bpb = self.next_pb()
                        pv = pb.rearrange("p (b n) -> p b n", b=8)
                        for bb in range(8):
                            b = b8 * 8 + bb
                            self.tr(pv[:, bb, :], kcr[:, b, kc * 128:(kc + 1) * 128], self.identb[:], bw8 + [self.bconst],
                                    [bpb], inc=(bb == 7))
                        self.evac(kcT[:, kc, b8 * 8:b8 * 8 + 8, :], pv, [bpb], [bkcT])
                pst["smp"] = (kcT, bkcT, vc, bvc, QM, bQM, bm16, bbm16)
            pst["pri"] = pri

        stg = [st2[:, 20 * hg:20 * hg + 20] for hg in range(4)]
        bstg = [Buf(f"astg{hg}", self.retired) for hg in range(4)]
        for b_ in bstg:
            self.abufs.append((self.aoff, b_))
        den, rden = st2[:, 96:112], st2[:, 112:128]
        Opr, bOpr = self.pair(2), self.bpair(2)

        def geom(t, hg, i):
            h = 4 * hg + i
            c, hh = (h % 4) + 4 * (h // 8), (h // 4) % 2
            return h, c, hh, hg // 2, slice(hh * 64, hh * 64 + 64)

        def stA(t, hg):
            t0 = BLKS[[i_ for i_, (a_, b_) in enumerate(BLKS) if a_ <= t < b_][0]][0]
            tl = t - t0
            sample = t == 16
            Spr, bSpr = self.pair(hg % 2), self.bpair(hg % 2)
            Sv = Spr.rearrange("p (h k) -> p h k", h=4)
            lo = 128 if t == 0 else 0
            for i in range(4):
                h, c, hh, kc, rows = geom(t, hg, i)
                bS = [bSpr[i // 2]]
                first = (i % 2 == 0)
                qcols = qT[rows, c, tl * 128:(tl + 1) * 128]
                if not sample:
                    kb = bkTt[max(t - 1, 0):t + 1]
                    self.mm(Sv[:, i, lo:256], qcols, kT[rows, kc, (t - 1) * 128 + lo:(t + 1) * 128], first, False,
                            [bqT] + kb, bS, inc=False)
                else:
                    kcT, bkcT, vc, bvc, QM, bQM, bm16, bbm16 = pst["smp"]
                    qs = (4 * hg + i) % 2
                    self.op("dve", "tensor_tensor", [bqT, bbm16], [bQM[qs]], out=QM[qs][rows],
                            in0=qcols.unsqueeze(1).to_broadcast([64, 16, 128]), in1=bm16[rows], op=ALU.mult)
                    for b in range(16):
                        self.mm(Sv[:, i, 0:128], QM[qs][rows, b, :], kcT[rows, kc, b, :], first and b == 0, False,
                                [bQM[qs], bkcT], bS, inc=False)
                    self.mm(Sv[:, i, 128:256], qcols, kT[rows, kc, 2048:2176], False, False, [bqT, bkTt[16]], bS,
                            inc=False)
                self.mm(Sv[:, i, lo:256], self.identb[:], bm[:, h, lo:256], False, True, [bbm, self.bconst], bS,
                        inc=True)

        def stB1(t, hg):
            Spr, bSpr = self.pair(hg % 2), self.bpair(hg % 2)
            Sv = Spr.rearrange("p (h k) -> p h k", h=4)
            lo = 128 if t == 0 else 0
            sg_, bsg_ = stg[hg], bstg[hg]
            mx, nmx, rs, t4, es = (sg_[:, 4 * q:4 * q + 4] for q in range(5))
            hs = slice(4 * hg, 4 * hg + 4)
            self.op("dve", "tensor_reduce", bSpr, [bsg_], out=mx, in_=Sv[:, :, lo:256], axis=AX.X, op=ALU.max)
            self.op("dve", "scalar_tensor_tensor", [bsg_, bsm], [bsg_], out=nmx, in0=mx, scalar=-1.0,
                    in1=nsnk[:, hs], op0=ALU.mult, op1=ALU.min)
            self.op("dve", "tensor_tensor", [bsg_, bsm], [bsg_], out=t4, in0=snk[:, hs], in1=nmx, op=ALU.add)

        def stB2(t, hg):
            Spr, bSpr = self.pair(hg % 2), self.bpair(hg % 2)
            Sv = Spr.rearrange("p (h k) -> p h k", h=4)
            lo = 128 if t == 0 else 0
            sg_, bsg_ = stg[hg], bstg[hg]
            mx, nmx, rs, t4, es = (sg_[:, 4 * q:4 * q + 4] for q in range(5))
            Pq, bPq = P2[hg % 2], bP2[hg % 2]
            for i in range(4):
                self.act(Pq[:, i, lo:256], Sv[:, i, lo:256], AF.Exp, [bSpr[i // 2], bsg_], [bPq, bsg_],
                         bias=nmx[:, i:i + 1], scale=1.0, accum_out=rs[:, i:i + 1])
            self.act(es, t4, AF.Exp, [bsg_], [bsg_])

        def stB3(t, hg):
            sg_, bsg_ = stg[hg], bstg[hg]
            mx, nmx, rs, t4, es = (sg_[:, 4 * q:4 * q + 4] for q in range(5))
            hs = slice(4 * hg, 4 * hg + 4)
            self.op("dve", "tensor_tensor", [bsg_], [bst2], out=den[:, hs], in0=rs, in1=es, op=ALU.add)

        def stC(t, hg):
            sample = t == 16
            Pq, bPq = P2[hg % 2], bP2[hg % 2]
            PTq, bPTq = PT2[hg % 2], bPT2[hg % 2]
            pb, bpb = self.next_pb()
            pv = pb.rearrange("p (h a n) -> p h a n", h=4, a=2)
            kts = [1] if t == 0 else [0, 1]
            for i in range(4):
                for kt in kts:
                    self.tr(pv[:, i, kt, :], Pq[:, i, kt * 128:(kt + 1) * 128], self.identb[:], [bPq, self.bconst],
                            [bpb], inc=(i == 3 and kt == 1))
            if t == 0:
                self.evac(PTq[:, :, 1, :], pv[:, :, 1, :], [bpb], [bPTq], "dve")
            else:
                self.evac(PTq, pv, [bpb], [bPTq], "dve")
            for i in range(4):
                h = 4 * hg + i
                bO = [bOpr[h // 8]]
                oc = Opr[:, h * 64:(h + 1) * 64]
                firstb = (h % 8 == 0)
                if not sample:
                    for kt in kts:
                        self.mm(oc, PTq[:, i, kt, :], Vb[:, t - 1 + kt, hg * 64:(hg + 1) * 64],
                                firstb and kt == kts[0], kt == 1, [bPTq, bVb[t - 1 + kt]], bO,
                                inc=(kt == 1 and i == 3))
                else:
                    kcT, bkcT, vc, bvc, QM, bQM, bm16, bbm16 = pst["smp"]
                    qs = (4 * hg + i) % 2
                    self.op("dve", "tensor_tensor", [bPTq, bbm16], [bQM[qs]], out=QM[qs],
                            in0=PTq[:, i, 0, :].unsqueeze(1).to_broadcast([128, 16, 128]), in1=bm16, op=ALU.mult)
                    for b in range(16):
                        self.mm(oc, QM[qs][:, b, :], vc[:, b, hg * 64:(hg + 1) * 64], firstb and b == 0, False,
                                [bQM[qs], bvc], bO, inc=False)
                    self.mm(oc, PTq[:, i, 1, :], Vb[:, 16, hg * 64:(hg + 1) * 64], False, True, [bPTq, bVb[16]], bO,
                            inc=True)

        def stD1(t):
            self.op("dve", "reciprocal", [bst2], [bst2], out=rden, in_=den)
            self.op("dve", "tensor_tensor", bOpr + [bst2], [bOb], out=Ob, in0=Opr.rearrange("p (h e) -> p h e", h=16),
                    in1=rden.unsqueeze(2).to_broadcast([128, 16, 64]), op=ALU.mult)
            pb, bpb = self.next_pb()
            pv = pb.rearrange("p (c n) -> p c n", c=8)
            Obf = Ob.rearrange("p h e -> p (h e)")
            for c in range(8):
                self.tr(pv[:, c, :], Obf[:, c * 128:(c + 1) * 128], self.identb[:], [bOb, self.bconst], [bpb], inc=(c == 7))
            self.evac(OT, pv, [bpb], [bOT], "act")
            for half in range(2):
                yh = Opr[:, half * 512:(half + 1) * 512]
                pr_ = 32 * (half + 1)
                self.mm(yh, self.onesf[pr_:pr_ + 1, :], brow[pr_:pr_ + 1, :], True, False,
                        [self.bconst, bbrow], [bOpr[half]], inc=False)
                for k in range(8):
                    self.mm(yh, OT[:, k, :], wo[:, k, half * 512:(half + 1) * 512], False, k == 7, [bOT, bwo], [bOpr[half]])

        def stD2(t):
            nonlocal pend
            x = self.XR[:, t, :]
            self.op("dve", "scalar_tensor_tensor", [self.bxr[t]] + bOpr, [self.bxr[t]], out=x, in0=x, scalar=ALPHA,
                    in1=Opr, op0=ALU.mult, op1=ALU.add)
            pend = self.ln_all(pend, t)

        done_proj, doneA = set(), set()
        prevD2 = None
        for bi, (t0, t1) in enumerate(BLKS):
            if bi not in done_proj:
                proj(bi)
                done_proj.add(bi)
            for t in range(t0, t1):
                if t not in doneA:
                    stA(t, 0)
                    stA(t, 1)
                stB1(t, 0); stB2(t, 0)
                if prevD2 is not None:
                    stD2(prevD2)
                    prevD2 = None
                stB1(t, 1); stB3(t, 0); stA(t, 2); stC(t, 0); stB2(t, 1)
                stB1(t, 2); stB3(t, 1); stA(t, 3); stC(t, 1); stB2(t, 2)
                stB1(t, 3); stB3(t, 2); stC(t, 2); stB2(t, 3)
                stB3(t, 3); stC(t, 3)
                nxt = t + 1
                if nxt < t1:
                    stA(nxt, 0); stA(nxt, 1); doneA.add(nxt)
                elif bi + 1 < len(BLKS) and BLKS[bi + 1][0] != 16:
                    proj(bi + 1)
                    done_proj.add(bi + 1)
                    stA(nxt, 0); stA(nxt, 1); doneA.add(nxt)
                stD1(t)
                prevD2 = t
        if prevD2 is not None:
            stD2(prevD2)
        if pend is not None:
            self.make_xt(*pend)

    def pool(self, l):
        d = self.dr
        self.release(0)
        mt, bmt = self.carve("poolmt", [128, 24, 128], F32)
        pw, bpw = self.carve("poolw", [128, 4, 2, 256], BF16)
        psc, bpsc = self.carve("poolsc", [128, D], F32)
        pfx, bpfx = self.carve("poolpfx", [128, 2, D], F32)
        dT, bdT = [], []
        for s in range(2):
            a, b = self.carve(f"dT{s}", [128, 8, 128], BF16)
            dT.append(a), bdT.append(b)
        tmp, btmp = self.carve("pooltmp", [128, D], F32)
        self.dma("sp", mt, d["c_poolmt"], [], [bmt])
        self.dma("pool", pw, d["pool_w"][0].rearrange("g (kk p) n -> p g kk n", p=128), [], [bpw])
        self.dma("sp", psc, d["pool_scale"][0].partition_broadcast(128), [], [bpsc])
        self.dma("sp", pfx[0:120, 0, :], d["state_pool"][0:8].rearrange("b r c -> (b r) c"), [], [bpfx])
        self.dma("sp", pfx[0:120, 1, :], d["state_pool"][8:16].rearrange("b r c -> (b r) c"), [], [bpfx])
        self.dma("sp", d["npool_p"], self.XR[113:128, 15, :], [self.bxr[15]], [])
        self.dma("sp", d["npool_s"][:, 0:7, :], d["state_pool"][:, 8:15, :], [self.bd2d], [], own=self.bd2d)
        for b in range(16):
            self.dma("sp", d["npool_s"][b, 7:15, :], self.XR[8 * b:8 * b + 8, 16, :], [self.bxr[16]], [])
        self.load_ln(l, 0)

        def diff(t):
            s = t % 2
            pr, bpr = self.pair(s), self.bpair(s)
            pv = pr.rearrange("p (c n) -> p c n", c=8)
            for c in range(8):
                g = c // 2
                cs = slice(c * 128, (c + 1) * 128)
                first = (c % 4 == 0)
                bb = [bpr[c // 4]]
                if t == 16:
                    self.mm(pv[:, c, :], pfx[0:120, 0, cs], mt[0:120, 16 + g, :], first, False, [bpfx, bmt], bb, inc=False)
                    self.mm(pv[:, c, :], pfx[0:120, 1, cs], mt[0:120, 20 + g, :], False, False, [bpfx, bmt], bb, inc=False)
                    self.mm(pv[:, c, :], self.XR[:, 16, cs], mt[:, 12 + g, :], False, True, [self.bxr[16], bmt], bb, inc=True)
                elif t == 0:
                    self.mm(pv[:, c, :], self.XR[:, 0, cs], mt[:, 8 + g, :], first, True, [self.bxr[0], bmt], bb, inc=True)
                else:
                    self.mm(pv[:, c, :], self.XR[:, t - 1, cs], mt[:, g, :], first, False, [self.bxr[t - 1], bmt], bb, inc=False)
                    self.mm(pv[:, c, :], self.XR[:, t, cs], mt[:, 4 + g, :], False, True, [self.bxr[t], bmt], bb, inc=True)
            self.evac(dT[s], pv, bpr, [bdT[s]])

        pend = [None]

        def update(t):
            s = t % 2
            ypr, bypr = self.pair(2), self.bpair(2)
            for g in range(4):
                for kk in range(2):
                    self.mm(ypr[:, g * 256:(g + 1) * 256], dT[s][:, 2 * g + kk, :], pw[:, g, kk, :], (g % 2 == 0) and kk == 0,
                            kk == 1, [bdT[s], bpw], [bypr[g // 2]], inc=(kk == 1))
            self.op("dve", "tensor_tensor", bypr + [bpsc], [btmp], out=tmp, in0=ypr, in1=psc, op=ALU.mult)
            x = self.XR[:, t, :]
            self.op("dve", "scalar_tensor_tensor", [self.bxr[t], btmp], [self.bxr[t]], out=x, in0=x, scalar=ALPHA, in1=tmp,
                    op0=ALU.mult, op1=ALU.add)
            pend[0] = self.ln_all(pend[0], t)

        for t in range(NT):
            diff(t)
            if t >= 1:
                update(t - 1)
        update(16)
        if pend[0] is not None:
            self.make_xt(*pend[0])

    def ssm(self, l):
        d = self.dr
        self.release(0)
        win = d["ssm_w_in"][0]
        negm_p, bnp = self.carve("negm_p", [128, 1024], BF16)
        negm_s, bns = self.carve("negm_s", [128, 1024], BF16)
        c8, bc8 = self.carve("c8", [8, 8 * 128 + 512 + 128 + 128 + 64], F32)
        sel8 = c8[:, 0:1024].rearrange("p (r n) -> p r n", r=8)
        scan_p, scan_s = c8[:, 1024:1536], c8[:, 1536:1664]
        apar = c8[:, 1664:1792]
        selj = c8[:, 1792:1856].rearrange("p (b j) -> p b j", b=16)
        seqm, bseqm = self.carve("seqm", [128, 16], F32)
        self.dma("sp", negm_p, d["c_negm_p"], [], [bnp])
        self.dma("sp", negm_s, d["c_negm_s"], [], [bns])
        self.dma("sp", sel8, d["c_sel8"], [], [bc8])
        self.dma("sp", scan_p, d["c_scan_p"], [], [bc8])
        self.dma("sp", scan_s, d["c_scan_s"], [], [bc8])
        self.dma("sp", apar, d["c_apar"], [], [bc8])
        self.dma("sp", selj, d["c_selj"], [], [bc8])
        self.dma("sp", seqm, d["c_seqm"], [], [bseqm])
        self.load_ln(l, 0)
        mL = self.mark()
        pend = None
        for g in range(4):
            self.release(mL)
            wx, bwx = self.carve("wx", [128, 8, 512], BF16)
            wz, bwz = self.carve("wz", [128, 8, 512], BF16)
            wBC, bwBC = self.carve("wBC", [128, 8, 256], BF16)
            wdt, bwdt = self.carve("wdt", [128, 8, 8], BF16)
            wout, bwout = self.carve("wout", [128, 4, 1024], BF16)
            cw, bcw = self.carve("convw", [128, 6, 5], F32)
            hp, bhp = self.carve("headp", [8, 4], F32)
            dbc, bdbc = self.carve("dbc", [128, 8], F32)
            nw, bnw = self.carve("nw", [128, 512], F32)
            wr = lambda c0, w_: win[:, c0:c0 + w_].rearrange("(k p) n -> p k n", p=128)
            self.dma("pool", wx, wr(2048 + 512 * g, 512), [], [bwx])
            self.dma("pool", wBC[:, :, 0:128], wr(4096 + 128 * g, 128), [], [bwBC])
            self.dma("pool", wBC[:, :, 128:256], wr(4608 + 128 * g, 128), [], [bwBC])
            self.dma("pool", wdt, wr(5120 + 8 * g, 8), [], [bwdt])
            self.dma("pool", wz, wr(512 * g, 512), [], [bwz])
            self.dma("pool", wout, d["ssm_w_out"][0][512 * g:512 * g + 512, :].rearrange("(k p) n -> p k n", p=128), [], [bwout])
            chbase = [512 * g + 128 * i for i in range(4)] + [2048 + 128 * g, 2560 + 128 * g]
            cwsrc, cbsrc = d["ssm_conv_w"][0], d["ssm_conv_b"][0]
            for cc in range(6):
                cb = chbase[cc]
                self.dma("sp", cw[:, cc, 0:4], cwsrc[:, cb:cb + 128].rearrange("j p -> p j"), [], [bcw],
                         allow_slow_non_contiguous=True)
                self.dma("sp", cw[:, cc, 4:5], cbsrc[cb:cb + 128].rearrange("(p o) -> p o", o=1), [], [bcw])
            self.dma("sp", hp[:, 0:1], d["ssm_dt_bias"][0][8 * g:8 * g + 8].rearrange("(p o) -> p o", o=1), [], [bhp])
            self.dma("sp", hp[:, 1:2], d["ssm_a_log"][0][8 * g:8 * g + 8].rearrange("(p o) -> p o", o=1), [], [bhp])
            self.dma("sp", dbc, d["ssm_d"][0][8 * g:8 * g + 8].partition_broadcast(128), [], [bdbc])
            self.dma("sp", nw, d["ssm_norm_w"][0][512 * g:512 * g + 512].partition_broadcast(128), [], [bnw])
            self.act(hp[:, 2:3], hp[:, 1:2], AF.Exp, [bhp], [bhp])
            self.op("dve", "tensor_scalar", [bhp], [bhp], out=hp[:, 2:3], in0=hp[:, 2:3], scalar1=-1.0, scalar2=None,
                    op0=ALU.mult)
            cwk, _ = self.carve("convwork", [128, 1536], F32)
            acc = [cwk[:, 0:512], cwk[:, 512:1024]]
            ctmp = cwk[:, 1024:1536]
            bacc = [Buf("acc0", self.retired), Buf("acc1", self.retired)]
            bctmp = Buf("ctmp", self.retired)
            bcwk = bacc + [bctmp]
            for b in bcwk:
                self.abufs.append((self.aoff, b))
            xcT, bxcT = self.carve("xcT", [128, 6, 512], BF16)
            dtb_, bdt = self.carve("dtbuf", [8, 3, 512], F32)
            el, bel = self.carve("elast", [8, 16], F32)
            xs, bxs = self.carve("xs", [128, 640], BF16)
            tmd, btmd = self.carve("tmd", [128, 32], F32)
            Ef, bE_ = self.carve("E", [128, 1024], F32)
            E = Ef.rearrange("p (r n) -> p r n", r=8)
            cst_ = Ef[:, 0:768]
            bcst = bE_
            CM = Ef.bitcast(BF16).rearrange("p (b n) -> p b n", b=16)
            bCM = bE_
            WT, bWT = self.carve("WT", [128, 8, 128], BF16)
            xD, bxD = self.carve("xD", [128, 512], BF16)
            sz, bsz = self.carve("sz", [128, 512], F32)
            y1, by1 = self.carve("y1", [128, 512], F32)
            hout, bhout = y1.rearrange("p (j n) -> p j n", j=4), by1
            yn, byn = self.carve("yn", [128, 512], BF16)
            ynT, bynT = self.carve("ynT", [128, 4, 128], BF16)
            dg, bdg = self.carve("dg", [8, 64], F32)
            mP = self.mark()
            rawc, _ = self.carve("rawc", [128, 2, 515], F32)
            brawc = [Buf("rawc0", self.retired), Buf("rawc1", self.retired)]
            for b in brawc:
                self.abufs.append((self.aoff, b))
            hist, bhist = self.carve("hist", [128, 6, 3], F32)
            hT, bhT = self.carve("hT", [128, 512], F32)
            hTb, bhTb = self.carve("hTb", [128, 512], BF16)
            xd, bxd = self.carve("xd", [128, 512], BF16)
            dbs, bdbs = self.carve("dbs", [128, 64], F32)
            self.op("pool", "memset", [], [bhT], ap=hT, constant=0.0)
            self.op("pool", "memset", [], [bhTb], ap=hTb, constant=0.0)
            self.op("pool", "memset", [], [bhist], ap=hist, constant=0.0)
            wsl = [wx[:, :, 0:128], wx[:, :, 128:256], wx[:, :, 256:384], wx[:, :, 384:512], wBC[:, :, 0:128], wBC[:, :, 128:256]]
            wsb = [bwx, bwx, bwx, bwx, bwBC, bwBC]
            decT, bEt, acum = (dtb_[:, q, :] for q in range(3))

            def dt_path(psd, bpsd, n, L, scanm):
                nb = n // L
                t0, t1, ac = decT[:, 0:n], bEt[:, 0:n], acum[:, 0:n]
                v3 = lambda a_: a_.rearrange("p (b t) -> p b t", t=L)
                self.act(t0, psd, AF.Exp, [bpsd, bhp], [bdt], bias=hp[:, 0:1], scale=1.0)
                self.act(t0, t0, AF.Ln, [bdt], [bdt], bias=1.0, scale=1.0)
                self.act(t1, t0, AF.Ln, [bdt], [bdt])
                self.op("dve", "tensor_scalar", [bdt, bhp], [bdt], out=t0, in0=t0, scalar1=hp[:, 2:3], scalar2=None,
                        op0=ALU.mult)
                self.op("dve", "tensor_tensor_scan", [bdt, bc8], [bdt], out=ac, data0=scanm, data1=t0, initial=0.0,
                        op0=ALU.mult, op1=ALU.add)
                self.op("dve", "tensor_tensor", [bdt], [bdt], out=t1, in0=t1, in1=ac, op=ALU.subtract)
                self.op("dve", "tensor_tensor", [bdt], [bdt], out=v3(t0), in0=v3(t1),
                        in1=v3(ac)[:, :, L - 1:L].to_broadcast([8, nb, L]), op=ALU.add)
                self.act(t0, t0, AF.Exp, [bdt], [bdt])
                self.act(el[:, 0:nb], v3(ac)[:, :, L - 1], AF.Exp, [bdt], [bel])

            def conv_chunk(cc, src_views, bsrc, out_view, shape):
                s = cc % 2
                a = acc[s]
                av = a if shape is None else a[:, 0:shape[0] * shape[1]].rearrange("p (b t) -> p b t", b=shape[0])
                tv = ctmp if shape is None else ctmp[:, 0:shape[0] * shape[1]].rearrange("p (b t) -> p b t", b=shape[0])
                self.op("dve", "tensor_scalar", [bsrc, bcw], [bacc[s]], out=av, in0=src_views[0], scalar1=cw[:, cc, 0:1],
                        scalar2=cw[:, cc, 4:5], op0=ALU.mult, op1=ALU.add)
                for jj in range(1, 4):
                    self.op("dve", "scalar_tensor_tensor", [bsrc, bcw, bacc[s]], [bacc[s]], out=av, in0=src_views[jj],
                            scalar=cw[:, cc, jj:jj + 1], in1=av, op0=ALU.mult, op1=ALU.add)
                self.act(out_view, av, AF.Silu, [bacc[s]], [bxcT])

            v8 = lambda a_: a_.rearrange("p (r e) -> p r e", r=8)

            def ph1_pe(t, cols, negm, bnegm):
                xtt = [self.bxt[t]]
                pb, bpb = self.next_pb()
                pv = pb[:, 0:640].rearrange("p (c n) -> p c n", c=5)
                for cc in range(5):
                    self.tr(pv[:, cc, :], xcT[:, cc, cols], self.identb[:], [bxcT, self.bconst], [bpb], inc=(cc == 4))
                p2, bp2 = self.psf[:, 2, :], self.bpf[2]
                for q, srcv in enumerate((bEt, acum, decT)):
                    self.tr(p2[:, 8 * q:8 * q + 8], srcv[:, cols], self.identf[0:8, 0:8], [bdt, self.bconst], [bp2], inc=(q == 2))
                self.mm(p2[:, 128:256], xcT[:, 4, cols], xcT[:, 5, cols], False, True, [bxcT], [bp2])
                spr, bspr = self.pair(0), self.bpair(0)
                sv = spr.rearrange("p (r n) -> p r n", r=8)
                for r in range(8):
                    self.mm(sv[:, r, :], sel8[:, r, :], acum[:, cols], r % 4 == 0, False, [bc8, bdt], [bspr[r // 4]], inc=False)
                for a2 in range(2):
                    self.mm(spr[:, a2 * 512:(a2 + 1) * 512], self.identb[:], negm[:, a2 * 512:(a2 + 1) * 512], False, True,
                            [self.bconst, bnegm], [bspr[a2]], inc=True)
                p3, bp3 = self.psf[:, 3, :], self.bpf[3]
                for k in range(8):
                    self.mm(p3, self.XT[:, k, t * 128:(t + 1) * 128], wz[:, k, :], k == 0, k == 7, [bwz] + xtt, [bp3])
                return (pb, bpb)

            def ph1_early(t, pbt):
                pb, bpb = pbt
                p2, bp2 = self.psf[:, 2, :], self.bpf[2]
                self.evac(tmd[:, 0:24], p2[:, 0:24], [bp2], [btmd], "dve")
                self.act(tmd[:, 24:32], tmd[:, 8:16], AF.Exp, [btmd], [btmd])
                self.evac(xs, pb[:, 0:640], [bpb], [bxs], "dve")

            def ph1_late(t):
                p2, bp2 = self.psf[:, 2, :], self.bpf[2]
                spr, bspr = self.pair(0), self.bpair(0)
                sv = spr.rearrange("p (r n) -> p r n", r=8)
                for r in range(8):
                    self.act(E[:, r, :], sv[:, r, :], AF.Exp, [bspr[r // 4], btmd], [bE_], bias=tmd[:, r:r + 1], scale=1.0)
                p3, bp3 = self.psf[:, 3, :], self.bpf[3]
                self.act(sz, p3, AF.Silu, [bp3], [bsz])
                self.op("dve", "tensor_tensor", [bE_, bp2], [bWT], out=WT, in0=E,
                        in1=p2[:, 128:256].unsqueeze(1).to_broadcast([128, 8, 128]), op=ALU.mult)
                self.op("dve", "tensor_tensor", [bxs, bdbc], [bxD], out=v8(xD), in0=v8(xs[:, 0:512]),
                        in1=dbc.unsqueeze(2).to_broadcast([128, 8, 64]), op=ALU.mult)

            def ph2(t, yi_fn, state_fn, hoist):
                nonlocal pend
                p4, bp4 = self.psf[:, 4, :], self.bpf[4]
                p5, bp5 = self.psf[:, 5, :], self.bpf[5]
                for r in range(8):
                    self.mm(p4[:, r * 64:(r + 1) * 64], WT[:, r, :], xs[:, r * 64:(r + 1) * 64], r == 0, False, [bWT, bxs], [bp4],
                            inc=False)
                self.mm(p4, self.identb[:], xD, False, True, [self.bconst, bxD], [bp4], inc=True)
                if state_fn is not None:
                    state_fn(0)
                yi_fn(p5, bp5)
                if state_fn is not None:
                    state_fn(1)
                self.op("dve", "tensor_tensor", [bp5, btmd], [by1], out=v8(y1), in0=v8(p5),
                        in1=tmd[:, 24:32].unsqueeze(2).to_broadcast([128, 8, 64]), op=ALU.mult)
                self.op("dve", "tensor_tensor", [by1, bp4], [by1], out=y1, in0=y1, in1=p4, op=ALU.add)
                self.op("dve", "tensor_tensor", [by1, bsz], [by1], out=y1, in0=y1, in1=sz, op=ALU.mult)
                st, bst = self.next_stat()
                self.act(p5, y1, AF.Square, [by1], [bp5, bst], accum_out=st[:, 0:1])
                pbt = None
                if hoist is not None:
                    pbt = hoist[0]()
                    hoist[1](pbt)
                self.op("dve", "tensor_scalar", [bst], [bst], out=st[:, 1:2], in0=st[:, 0:1], scalar1=1.0 / 512.0, scalar2=RMS_EPS,
                        op0=ALU.mult, op1=ALU.add)
                self.op("pool", "tensor_tensor", [bst, self.bconst], [bst], out=st[:, 2:3], in0=st[:, 1:2], in1=self.mhalf[:],
                        op=ALU.pow)
                self.op("dve", "scalar_tensor_tensor", [by1, bst, bnw], [byn], out=yn, in0=y1, scalar=st[:, 2:3], in1=nw,
                        op0=ALU.mult, op1=ALU.mult)
                pb, bpb = self.next_pb()
                pv = pb[:, 0:512].rearrange("p (c n) -> p c n", c=4)
                for c in range(4):
                    self.tr(pv[:, c, :], yn[:, c * 128:(c + 1) * 128], self.identb[:], [byn, self.bconst], [bpb], inc=(c == 3))
                self.evac(ynT, pv, [bpb], [bynT], "act")
                if hoist is not None:
                    hoist[2]()
                opr, bopr = self.pair(2), self.bpair(2)
                for half in range(2):
                    for k in range(4):
                        self.mm(opr[:, half * 512:(half + 1) * 512], ynT[:, k, :], wout[:, k, half * 512:(half + 1) * 512], k == 0,
                                k == 3, [bynT, bwout], [bopr[half]])
                x = self.XR[:, t, :]
                if g == 0:
                    self.op("dve", "scalar_tensor_tensor", [self.bxr[t]] + bopr, [self.bxr[t]], out=x, in0=x, scalar=ALPHA,
                            in1=opr, op0=ALU.mult, op1=ALU.add)
                else:
                    self.op("dve", "tensor_tensor", [self.bxr[t]] + bopr, [self.bxr[t]], out=x, in0=x, in1=opr, op=ALU.add)
                if g == 3:
                    pend = self.ln_all(pend, t)

            def conv_state_proj(t, rows):
                cpr, bcpr = self.pair(0), self.bpair(0)
                tc_ = slice(t * 128, (t + 1) * 128)
                for k in range(8):
                    self.mm(cpr[:, 0:512], self.XT[:, k, tc_], wx[:, k, :], k == 0, k == 7, [bwx, self.bxt[t]], [bcpr[0]])
                for k in range(8):
                    self.mm(cpr[:, 512:768], self.XT[:, k, tc_], wBC[:, k, :], k == 0, k == 7, [bwBC, self.bxt[t]], [bcpr[1]])
                self.evac(cst_[rows, :], cpr[rows, 0:768], bcpr, [bcst], "act")

            osl = ((512 * g, 512, 0), (2048 + 128 * g, 128, 512), (2560 + 128 * g, 128, 640))
            for bi, (t0, t1) in enumerate(BLKS[:4]):
                c0 = t0 * 128
                xtb = self.bxt[t0:t1]
                psd, bpsd = self.psf[0:8, 0, :], self.bpf[0]
                for k in range(8):
                    self.mm(psd, wdt[:, k, :], self.XT[:, k, c0:c0 + 512], k == 0, k == 7, [bwdt] + xtb, [bpsd])
                dt_path(psd, bpsd, 512, 128, scan_p)
                for cc in range(6):
                    bi_ = 2 + cc % 4
                    s_ = cc % 2
                    ps, bps = self.psf[:, bi_, :], self.bpf[bi_]
                    for k in range(8):
                        self.mm(ps, wsl[cc][:, k, :], self.XT[:, k, c0:c0 + 512], k == 0, k == 7, [wsb[cc]] + xtb, [bps])
                    self.op("dve", "tensor_copy", [bhist], [brawc[s_]], out=rawc[:, s_, 0:3], in_=hist[:, cc, :])
                    self.evac(rawc[:, s_, 3:515], ps, [bps], [brawc[s_]], "act")
                    self.op("dve", "tensor_copy", [brawc[s_]], [bhist], out=hist[:, cc, :], in_=rawc[:, s_, 512:515])
                    conv_chunk(cc, [rawc[:, s_, jj:jj + 512] for jj in range(4)], brawc[s_], xcT[:, cc, :], None)
                colsl = [slice(tl * 128, (tl + 1) * 128) for tl in range(4)]
                pbt0 = ph1_pe(t0, colsl[0], negm_p, bnp)
                ph1_early(t0, pbt0)
                ph1_late(t0)
                for t in range(t0, t1):
                    tl = t - t0
                    cols = colsl[tl]

                    def yi_fn(p5, bp5, cols=cols):
                        self.mm(p5, xcT[:, 5, cols], hTb, True, True, [bxcT, bhTb], [bp5])

                    def state_fn(stage, tl=tl):
                        p3, bp3 = self.psf[:, 3, :], self.bpf[3]
                        if stage == 0:
                            self.op("dve", "tensor_tensor", [bxs, btmd], [bxd], out=v8(xd), in0=v8(xs[:, 0:512]),
                                    in1=tmd[:, 16:24].unsqueeze(2).to_broadcast([128, 8, 64]), op=ALU.mult)
                            self.mm(p3, xs[:, 512:640], xd, True, True, [bxs, bxd], [bp3])
                            self.op("dve", "tensor_scalar", [bel, self.bconst], [bdg], out=dg[:, 0:8], in0=self.identf[0:8, 0:8],
                                    scalar1=el[:, tl:tl + 1], scalar2=None, op0=ALU.mult)
                            p2, bp2 = self.psf[:, 2, :], self.bpf[2]
                            self.mm(p2[:, 256:264], self.onesf[0:8, :], dg[:, 0:8], False, True, [self.bconst, bdg], [bp2])
                            self.evac(dbs[:, 0:8], p2[:, 256:264], [bp2], [bdbs], "dve")
                        else:
                            self.op("dve", "tensor_tensor", [bhT, bdbs], [bhT], out=v8(hT), in0=v8(hT),
                                    in1=dbs[:, 0:8].unsqueeze(2).to_broadcast([128, 8, 64]), op=ALU.mult)
                            self.op("dve", "tensor_tensor", [bhT, bp3], [bhT], out=hT, in0=hT, in1=p3, op=ALU.add)
                            self.act(hTb, hT, AF.Copy, [bhT], [bhTb])

                    hoist = None
                    if t + 1 < t1:
                        hoist = (lambda t=t, tl=tl: ph1_pe(t + 1, colsl[tl + 1], negm_p, bnp),
                                 lambda pbt, t=t: ph1_early(t + 1, pbt),
                                 lambda t=t: ph1_late(t + 1))
                    ph2(t, yi_fn, state_fn, hoist)
                if bi == 3:
                    conv_state_proj(15, slice(96, 128))
                    for (o0, w_, s0) in osl:
                        self.dma("sp", d["nconv_p"][:, o0:o0 + w_], cst_[125:128, s0:s0 + w_], [bcst], [])
            p3, bp3 = self.psf[:, 3, :], self.bpf[3]
            pv = p3.rearrange("p (j n) -> p j n", j=4)
            for jj in range(4):
                self.tr(pv[:, jj, :], hT[:, jj * 128:(jj + 1) * 128], self.identf[:], [bhT, self.bconst], [bp3], inc=(jj == 3))
            self.evac(hout, pv, [bp3], [bhout], "act")
            self.dma("sp", d["nssm_p"][512 * g:512 * g + 512, :].rearrange("(j p) n -> p j n", p=128), hout, [bhout], [])

            self.release(mP)
            rawp, brawp = self.carve("rawp", [128, 6, 16, 11], F32)
            decs, bdecs = self.carve("decs", [128, 16, 4], F32)
            h0f, bh0f = self.carve("h0f", [128, 4, 128], F32)
            h0T, bh0T = self.carve("h0T", [128, 512], BF16)
            hnw, bhnw = self.carve("hnw", [128, 4, 128], F32)
            xdm, bxdm = self.carve("xdm", [128, 512], BF16)
            dcb, bdcb = self.carve("dcb", [128, 8], F32)
            bm16, bbm16 = self.carve("bm16", [128, 16, 128], BF16)
            self.dma("sp", bm16, d["c_bm16"], [], [bbm16])
            scv = cwk[0:48, 0:768]
            scs = d["state_conv"]
            for (o0, w_, s0) in osl:
                self.dma("sp", scv[:, s0:s0 + w_], scs[:, :, o0:o0 + w_].rearrange("b r c -> (b r) c"), [], bcwk)
            p2, bp2 = self.psf[:, 2, :], self.bpf[2]
            pvh = p2[:, 0:288].rearrange("p (c n) -> p c n", c=6)
            for cc in range(6):
                self.tr(pvh[:, cc, :], scv[:, cc * 128:(cc + 1) * 128], self.identf[0:48, 0:48], bcwk + [self.bconst], [bp2],
                        inc=(cc == 5))
            self.evac(rawp[:, :, :, 0:3], pvh.rearrange("p c (b r) -> p c b r", r=3), [bp2], [brawp], "dve")
            xtt = [self.bxt[16]]
            for cc in range(6):
                bi_ = 3 + cc % 3
                ps, bps = self.psf[:, bi_, 0:128], self.bpf[bi_]
                for k in range(8):
                    self.mm(ps, wsl[cc][:, k, :], self.XT[:, k, 2048:2176], k == 0, k == 7, [wsb[cc]] + xtt, [bps])
                self.evac(rawp[:, cc, :, 3:11], ps.rearrange("p (b t) -> p b t", t=8), [bps], [brawp], "act")
            psd, bpsd = self.psf[0:8, 0, 0:128], self.bpf[0]
            for k in range(8):
                self.mm(psd, wdt[:, k, :], self.XT[:, k, 2048:2176], k == 0, k == 7, [bwdt] + xtt, [bpsd])
            dt_path(psd, bpsd, 128, 8, scan_s)
            for cc in range(6):
                conv_chunk(cc, [rawp[:, cc, :, jj:jj + 8] for jj in range(4)], brawp,
                           xcT[:, cc, 0:128].rearrange("p (b t) -> p b t", t=8), (16, 8))
            conv_state_proj(16, slice(0, 128))
            for b in range(16):
                for (o0, w_, s0) in osl:
                    self.dma("sp", d["nconv_s"][b, :, o0:o0 + w_], cst_[8 * b + 5:8 * b + 8, s0:s0 + w_], [bcst], [])
            self.op("dve", "tensor_tensor", [bel, bc8], [bdg], out=dg.rearrange("p (b j) -> p b j", b=16), in0=selj,
                    in1=el[:, 0:16].unsqueeze(2).to_broadcast([8, 16, 4]), op=ALU.mult)
            self.mm(p2[:, 320:384], apar, dg, False, True, [bc8, bdg], [bp2])
            self.evac(decs.rearrange("p b j -> p (b j)"), p2[:, 320:384], [bp2], [bdecs], "dve")
            cols = slice(0, 128)
            ssrc = d["state_ssm"]
            v8 = lambda a_: a_.rearrange("p (r e) -> p r e", r=8)

            def yi_s(p5, bp5):
                self.op("dve", "tensor_tensor", [bxcT, bbm16], [bCM], out=CM,
                        in0=xcT[:, 5, 0:128].unsqueeze(1).to_broadcast([128, 16, 128]), in1=bm16, op=ALU.mult)
                for b in range(16):
                    self.dma("sp", h0f, ssrc[b, 512 * g:512 * g + 512, :].rearrange("(j p) n -> p j n", p=128), [], [bh0f])
                    p3, bp3 = self.psf[:, 3, :], self.bpf[3]
                    pv3 = p3.rearrange("p (j n) -> p j n", j=4)
                    for jj in range(4):
                        self.tr(pv3[:, jj, :], h0f[:, jj, :], self.identf[:], [bh0f, self.bconst], [bp3], inc=(jj == 3))
                    self.evac(h0T, p3, [bp3], [bh0T], "act")
                    self.mm(p5, CM[:, b, :], h0T, b == 0, b == 15, [bCM, bh0T], [bp5], inc=True)
                    self.op("dve", "tensor_scalar", [btmd, bseqm], [bdcb], out=dcb, in0=tmd[:, 16:24], scalar1=seqm[:, b:b + 1],
                            scalar2=None, op0=ALU.mult)
                    self.op("dve", "tensor_tensor", [bxs, bdcb], [bxdm], out=v8(xdm), in0=v8(xs[:, 0:512]),
                            in1=dcb.unsqueeze(2).to_broadcast([128, 8, 64]), op=ALU.mult)
                    p1, bp1 = self.psf[:, 1, :], self.bpf[1]
                    pv1 = p1.rearrange("p (j n) -> p j n", j=4)
                    for jj in range(4):
                        self.mm(pv1[:, jj, :], xdm[:, jj * 128:(jj + 1) * 128], xs[:, 512:640], jj == 0, jj == 3, [bxdm, bxs],
                                [bp1], inc=(jj == 3))
                    self.op("dve", "tensor_tensor", [bh0f, bdecs], [bhnw], out=hnw, in0=h0f,
                            in1=decs[:, b, :].unsqueeze(2).to_broadcast([128, 4, 128]), op=ALU.mult)
                    self.op("dve", "tensor_tensor", [bhnw, bp1], [bhnw], out=hnw, in0=hnw, in1=pv1, op=ALU.add)
                    self.dma("sp", d["nssm_s"][b, 512 * g:512 * g + 512, :].rearrange("(j p) n -> p j n", p=128), hnw,
                             [bhnw], [])

            pbt = ph1_pe(16, cols, negm_s, bns)
            ph1_early(16, pbt)
            ph1_late(16)
            ph2(16, yi_s, None, None)
        if pend is not None:
            self.make_xt(*pend)


def build_nc(stop_after=None):
    nc = bass.Bass("TRN2", target_bir_lowering=False)
    dr = {}

    def din(name, shape, dt=F32):
        dr[name] = nc.dram_tensor(name, list(shape), dt, kind="ExternalInput").ap()

    def dout(name, shape):
        dr[name] = nc.dram_tensor(name, list(shape), F32, kind="ExternalOutput").ap()

    din("x_p", [SEQ, D]); din("x_s", [128, D])
    din("cache_k", [2, 16, 128, 256]); din("cache_v", [2, 16, 128, 256])
    din("state_conv", [16, 3, 3072]); din("state_ssm", [16, 2048, 128]); din("state_pool", [16, 15, D])
    din("rel_bias", [32, 16]); din("attn_w_qkv", [2, D, 1536]); din("attn_b_qkv", [2, 1536])
    din("attn_w_o", [2, D, D]); din("attn_b_o", [2, D]); din("attn_sinks", [2, 16])
    din("ssm_w_in", [1, D, 5152]); din("ssm_conv_w", [1, 4, 3072]); din("ssm_conv_b", [1, 3072])
    din("ssm_dt_bias", [1, 32]); din("ssm_a_log", [1, 32]); din("ssm_d", [1, 32]); din("ssm_norm_w", [1, 2048])
    din("ssm_w_out", [1, 2048, D]); din("pool_w", [1, 4, 256, 256]); din("pool_scale", [1, D])
    din("ffn_w_gate", [4, D, DFF]); din("ffn_w_up", [4, D, DFF]); din("ffn_w_down", [4, DFF, D])
    din("ln_g", [4, 2, D]); din("ln_b", [4, 2, D])
    for k, v in host_constants().items():
        din(k, v.shape, BF16 if v.dtype == ml_dtypes.bfloat16 else F32)
    dout("y_p", [SEQ, D]); dout("y_s", [128, D])
    dout("nk_p", [2, 128, 256]); dout("nv_p", [2, 128, 256]); dout("nconv_p", [3, 3072]); dout("nssm_p", [2048, 128])
    dout("npool_p", [15, D])
    dout("nk_s", [2, 16, 128, 256]); dout("nv_s", [2, 16, 128, 256]); dout("nconv_s", [16, 3, 3072])
    dout("nssm_s", [16, 2048, 128]); dout("npool_s", [16, 15, D])
    dr["gtab"] = nc.dram_tensor("gtab", [16, 384], F32, kind="Internal").ap()

    with ExitStack() as es:
        S = Sched(nc, es)
        kb = KB(nc, S, es, dr, stop_after)
        kb.load_consts()
        kb.build_bias_tables()
        kb.load_x()
        nl = DEPTH if stop_after is None else stop_after
        import os
        skip = os.environ.get("KSKIP", "")
        for l in range(nl):
            kind = l % 3
            if "mix" in skip:
                pass
            elif kind == 0:
                kb.attn(l // 3, l)
            elif kind == 1:
                kb.ssm(l)
            else:
                kb.pool(l)
            if "ffn" not in skip:
                kb.ffn(l, final=(l == nl - 1))
        allb = [b for b in _all_bufs(kb)]
        S.wait_all("sp", allb)
        S.emit()
    return nc


def _all_bufs(kb):
    out = list(kb.bxr) + list(kb.bxt) + [kb.blng, kb.blnb, kb.bconst, kb.bd2d] + kb.bxb16 + kb.bstat + kb.bpf + kb.bpb
    out += [b for _, b in kb.abufs]
    dummy = Buf("retired", kb.retired)
    out.append(dummy)
    return out


_NC_CACHE = {}


def shard_inputs(inputs):
    consts = host_constants()
    maps = []
    f = lambda a: np.ascontiguousarray(np.asarray(a, dtype=np.float32))
    shared = {k: f(inputs[k]) for k in ("rel_bias", "attn_w_qkv", "attn_b_qkv", "attn_w_o", "attn_b_o", "attn_sinks", "ssm_w_in",
                                        "ssm_conv_w", "ssm_conv_b", "ssm_dt_bias", "ssm_a_log", "ssm_d", "ssm_norm_w", "ssm_w_out",
                                        "pool_w", "pool_scale", "ffn_w_gate", "ffn_w_up", "ffn_w_down", "ln_g", "ln_b")}
    for c in range(NCORES):
        sl = slice(16 * c, 16 * c + 16)
        m = dict(shared)
        m.update(consts)
        m["x_p"] = f(inputs["x_prompt"][c])
        m["x_s"] = f(inputs["x_sample"][sl]).reshape(128, D)
        m["cache_k"] = f(inputs["cache_k"][:, sl]).reshape(2, 16, 128, 256)
        m["cache_v"] = f(inputs["cache_v"][:, sl]).reshape(2, 16, 128, 256)
        m["state_conv"] = f(inputs["state_conv"][0, sl])
        m["state_ssm"] = f(inputs["state_ssm"][0, sl]).reshape(16, 2048, 128)
        m["state_pool"] = f(inputs["state_pool"][0, sl])
        maps.append(m)
    return maps


def gather_outputs(res):
    R = res
    cat = lambda k: np.stack([r[k] for r in R], 0)
    y_p = cat("y_p")
    y_s = cat("y_s").reshape(128, 8, D)
    nk_p = cat("nk_p").transpose(1, 0, 2, 3).reshape(2, 8, 128, 4, 64)
    nv_p = cat("nv_p").transpose(1, 0, 2, 3).reshape(2, 8, 128, 4, 64)
    nconv_p = cat("nconv_p")[None]
    nssm_p = cat("nssm_p").reshape(1, 8, 32, 64, 128)
    npool_p = cat("npool_p")[None]
    nk_s = np.concatenate([r["nk_s"] for r in R], 1).reshape(2, 128, 128, 4, 64)
    nv_s = np.concatenate([r["nv_s"] for r in R], 1).reshape(2, 128, 128, 4, 64)
    nconv_s = np.concatenate([r["nconv_s"] for r in R], 0)[None]
    nssm_s = np.concatenate([r["nssm_s"] for r in R], 0).reshape(1, 128, 32, 64, 128)
    npool_s = np.concatenate([r["npool_s"] for r in R], 0)[None]
    outs = (y_p, y_s, nk_p, nv_p, nconv_p, nssm_p, npool_p, nk_s, nv_s, nconv_s, nssm_s, npool_s)
    return tuple(np.ascontiguousarray(o, dtype=np.float32) for o in outs)


def kernel(**inputs):
    if "nc" not in _NC_CACHE:
        _NC_CACHE["nc"] = build_nc()
    nc = _NC_CACHE["nc"]
    maps = shard_inputs(inputs)
    res = run_bass_kernel_spmd(nc, maps, core_ids=list(range(NCORES)))
    return gather_outputs(res.results)
```

```python
import math
from contextlib import ExitStack

import ml_dtypes
import numpy as np

import concourse.bass as bass
import concourse.mybir as mybir
from concourse.bass_utils import run_bass_kernel_spmd

F32 = mybir.dt.float32
BF16 = mybir.dt.bfloat16
AF = mybir.ActivationFunctionType
ALU = mybir.AluOpType
AX = mybir.AxisListType

NCORES = 8
D = 1024
SEQ = 2048
NT = 17
NTOK = NT * 128
DEPTH = 4
DFF = 2816
ALPHA = (2 * DEPTH) ** 0.25
LN_EPS = 1e-5
RMS_EPS = 1e-5
NEG = -30000.0
BLKS = [(0, 4), (4, 8), (8, 12), (12, 16), (16, 17)]
ARENA_F32 = 23552
QHEADS = [(0, 4), (1, 5), (2, 6), (3, 7), (8, 12), (9, 13), (10, 14), (11, 15)]
POOL_W = (2, 4, 8, 16)


class Buf:
    __slots__ = ("name", "w", "r", "dsem", "excl")

    def __init__(self, name, r=None, excl=False):
        self.name = name
        self.excl = excl
        self.w = None
        self.r = dict(r) if r else {}
        self.dsem = None


class Sched:
    ENG = ("pe", "act", "dve", "pool", "sp")

    def __init__(self, nc, es):
        self.nc = nc
        self.es = es
        self.sems = {}
        self.cnt = {}
        self.seen = {e: {} for e in self.ENG}
        self.ops = {e: [] for e in self.ENG}
        for e in ("pe", "act", "dve", "pool"):
            self._newsem(e)
        self.nd = 0
        self.free_dsems = []

    def _newsem(self, key):
        self.sems[key] = self.es.enter_context(self.nc.semaphore(f"s_{key}"))
        self.cnt[key] = 0

    def _deps(self, eng, reads, writes):
        need = {}

        def add(k, v):
            if v > need.get(k, 0):
                need[k] = v
        for b in reads:
            if b.w is not None:
                add(*b.w)
            if b.excl:
                for k, v in b.r.items():
                    if k != eng:
                        add(k, v)
        for b in writes:
            if b.w is not None and b.w[0] != eng:
                add(*b.w)
            for k, v in b.r.items():
                if k != eng:
                    add(k, v)
        waits = []
        seen = self.seen[eng]
        for k, v in need.items():
            if seen.get(k, 0) < v:
                seen[k] = v
                waits.append((k, v))
        return waits

    def op(self, eng, fn, reads=(), writes=(), inc=True):
        waits = self._deps(eng, reads, writes)
        val = self.cnt[eng] + 1
        if inc:
            self.cnt[eng] = val
        for b in reads:
            b.r[eng] = val
        for b in writes:
            b.w = (eng, val)
            b.r = {}
        self.ops[eng].append((waits, fn, (eng, 1) if inc else None))

    def dma(self, q, fn, reads=(), writes=(), own=None):
        waits = self._deps(q, reads, writes)
        own = own or (writes[0] if writes else reads[0])
        if own.dsem is None:
            if self.free_dsems:
                own.dsem = self.free_dsems.pop()
            else:
                own.dsem = f"d{self.nd}"
                self.nd += 1
                self._newsem(own.dsem)
        k = own.dsem
        self.cnt[k] += 16
        val = self.cnt[k]
        for b in reads:
            b.r[k] = val
        for b in writes:
            b.w = (k, val)
            b.r = {}
        self.ops[q].append((waits, fn, (k, 16)))

    def wait_all(self, eng, bufs):
        waits = self._deps(eng, (), bufs)
        self.ops[eng].append((waits, None, None))

    def emit(self):
        with self.nc.Block() as block:
            def run(name):
                def body(e):
                    for waits, fn, inc in self.ops[name]:
                        for k, v in waits:
                            e.wait_ge(self.sems[k], v)
                        if fn is not None:
                            ins = fn(e)
                            if inc is not None:
                                ins.then_inc(self.sems[inc[0]], inc[1])
                return body
            block.sync(run("sp"))
            block.scalar(run("act"))
            block.vector(run("dve"))
            block.gpsimd(run("pool"))
            block.tensor(run("pe"))


def _t5_bucket(n):
    n = np.maximum(n, 0)
    nf = np.maximum(n, 1).astype(np.float32)
    large = 16 + (np.log(nf / np.float32(16)) / np.float32(math.log(8.0)) * np.float32(16)).astype(np.int32)
    large = np.minimum(large, 31)
    return np.where(n < 16, n, large)


def host_constants():
    bf = ml_dtypes.bfloat16
    c = {}
    c["c_identb"] = np.eye(128, dtype=np.float32).astype(bf)
    c["c_identf"] = np.eye(128, dtype=np.float32)
    c["c_antib"] = np.eye(128, dtype=np.float32)[::-1].copy().astype(bf)
    c["c_onesf"] = np.ones((128, 128), np.float32)
    i = np.arange(384)
    dist = 255 - i
    valid = (dist >= 0) & (dist < 128)
    oh = np.zeros((33, 384), np.float32)
    bk = _t5_bucket(dist)
    for ii in range(384):
        if valid[ii]:
            oh[bk[ii], ii] = 1.0
        else:
            oh[32, ii] = NEG
    c["c_onehot"] = oh
    p = np.arange(128)
    sels = np.zeros((128, 128), np.float32)
    sels[127 - (p % 8), p] = 1.0
    c["c_sels"] = sels.astype(bf)
    negb = np.where((p[:, None] // 8) == (p[None, :] // 8), 0.0, NEG).astype(np.float32)
    c["c_negb"] = negb.astype(bf)
    bm16 = np.zeros((128, 16, 128), np.float32)
    for b in range(16):
        bm16[:, b, 8 * b:8 * b + 8] = 1.0
    c["c_bm16"] = bm16.astype(bf)
    s_ = p[:, None]
    t_ = p[None, :]
    negm_p = np.where(t_ < s_, NEG, 0.0).astype(np.float32)
    negm_s = np.where((t_ >= s_) & (t_ // 8 == s_ // 8), 0.0, NEG).astype(np.float32)
    c["c_negm_p"] = np.tile(negm_p[:, None, :], (1, 8, 1)).reshape(128, 1024).astype(bf)
    c["c_negm_s"] = np.tile(negm_s[:, None, :], (1, 8, 1)).reshape(128, 1024).astype(bf)
    sel8 = np.zeros((8, 8, 128), np.float32)
    for r in range(8):
        sel8[r, r, :] = 1.0
    c["c_sel8"] = sel8
    sm_p = np.ones((8, 512), np.float32)
    sm_p[:, ::128] = 0.0
    sm_s = np.ones((8, 128), np.float32)
    sm_s[:, ::8] = 0.0
    c["c_scan_p"] = sm_p
    c["c_scan_s"] = sm_s
    apar = np.zeros((8, 128), np.float32)
    for k in range(8):
        apar[k, (k % 2) * 64:(k % 2) * 64 + 64] = 1.0
    c["c_apar"] = apar
    selj = np.zeros((8, 16, 4), np.float32)
    for k in range(8):
        selj[k, :, k // 2] = 1.0
    c["c_selj"] = selj
    seqm = np.zeros((128, 16), np.float32)
    seqm[p, p // 8] = 1.0
    c["c_seqm"] = seqm
    mt = np.zeros((24, 128, 128), np.float32)
    for g, w in enumerate(POOL_W):
        for t in range(128):
            for s in range(t - w + 1, t + 1):
                if s >= 0:
                    mt[4 + g, s, t] += 1.0 / w
                else:
                    mt[0 + g, s + 128, t] += 1.0 / w
            mt[4 + g, t, t] -= 1.0
            cnt = min(t + 1, w)
            for s in range(max(0, t - w + 1), t + 1):
                mt[8 + g, s, t] += 1.0 / cnt
            mt[8 + g, t, t] -= 1.0
            b, tt = t // 8, t % 8
            for pos in range(tt - w + 1, tt + 1):
                if pos >= 0:
                    mt[12 + g, b * 8 + pos, t] += 1.0 / w
                else:
                    j = pos + 15
                    if b < 8:
                        mt[16 + g, b * 15 + j, t] += 1.0 / w
                    else:
                        mt[20 + g, (b - 8) * 15 + j, t] += 1.0 / w
            mt[12 + g, t, t] -= 1.0
    c["c_poolmt"] = np.ascontiguousarray(mt.transpose(1, 0, 2))
    return c


class KB:
    def __init__(self, nc, S, es, dr, stop_after=None):
        self.nc, self.S, self.es, self.dr = nc, S, es, dr
        self.stop_after = stop_after
        sb = lambda name, shape, dt: es.enter_context(nc.sbuf_tensor(name, shape, dt))
        self.XR = sb("XR", [128, NT, D], F32)
        self.XT = sb("XT", [128, 8, NTOK], BF16)
        self.bxr = [Buf(f"xr{t}") for t in range(NT)]
        self.bxt = [Buf(f"xt{t}") for t in range(NT)]
        self.lng = sb("lng", [128, D], F32)
        self.lnb = sb("lnb", [128, D], F32)
        self.blng, self.blnb = Buf("lng"), Buf("lnb")
        self.xb16 = sb("xb16", [128, 2, D], BF16)
        self.bxb16 = [Buf("xb16_0"), Buf("xb16_1")]
        self.xbi = 0
        self.identb = sb("identb", [128, 128], BF16)
        self.identf = sb("identf", [128, 128], F32)
        self.onesf = sb("onesf", [128, 128], F32)
        self.mhalf = sb("mhalf", [128, 1], F32)
        self.bconst = Buf("const")
        self.stat = sb("stat", [128, 4, 16], F32)
        self.bstat = [Buf(f"stat{i}") for i in range(4)]
        self.sti = 0
        self.arena = sb("arena", [128, ARENA_F32], F32)
        self.aoff = 0
        self.abufs = []
        self.retired = {}
        self.psf = es.enter_context(nc.psum_tensor("psf", [128, 6, 512], F32))
        self.psb = es.enter_context(nc.psum_tensor("psb", [128, 2, 1024], BF16))
        self.bpf = [Buf(f"pf{i}", excl=True) for i in range(6)]
        self.bpb = [Buf("pb0", excl=True), Buf("pb1", excl=True)]
        self.pbi = 0
        self.bd2d = Buf("d2d")
        self.outbufs = [self.bd2d]
        self.evi = 0

    def op(self, eng, name, R, W, inc=True, **kw):
        self.S.op(eng, lambda e, n=name, kw=kw: getattr(e, n)(**kw), R, W, inc)

    def mm(self, out, lhsT, rhs, start, stop, R, W, inc=None):
        if inc is None:
            inc = stop
        self.S.op("pe", lambda e: e.matmul(out, lhsT=lhsT, rhs=rhs, start=start, stop=stop), R, W, inc)

    def tr(self, out, in_, ident, R, W, inc):
        self.S.op("pe", lambda e: e.transpose(out=out, in_=in_, identity=ident), R, W, inc)

    def act(self, out, in_, func, R, W, **kw):
        self.S.op("act", lambda e: e.activation(out=out, in_=in_, func=func, **kw), R, W)

    def dma(self, q, out, in_, R, W, own=None, **kw):
        self.S.dma(q, lambda e: e.dma_start(out=out, in_=in_, **kw), R, W, own)

    def evac(self, out, in_, R, W, eng=None):
        if eng is None:
            eng = ("act", "dve")[self.evi % 2]
            self.evi += 1
        if eng == "act":
            self.act(out, in_, AF.Copy, R, W)
        else:
            self.op(eng, "tensor_copy", R, W, out=out, in_=in_)

    def carve(self, name, shape, dt):
        n = int(np.prod(shape[1:]))
        nb = n * (2 if dt == BF16 else 4)
        n4 = (nb + 3) // 4
        assert self.aoff + n4 <= ARENA_F32, f"arena overflow at {name}: {self.aoff}+{n4}"
        v = self.arena[:, self.aoff:self.aoff + n4]
        self.aoff += n4
        if dt != F32:
            v = v.bitcast(dt)
        v = v[0:shape[0], 0:n]
        if len(shape) == 3:
            v = v.rearrange("p (a b) -> p a b", a=shape[1])
        elif len(shape) == 4:
            v = v.rearrange("p (a b c) -> p a b c", a=shape[1], b=shape[2])
        b = Buf(name, self.retired)
        self.abufs.append((self.aoff, b))
        return v, b

    def mark(self):
        return self.aoff

    def release(self, mark=0):
        keep = []
        for off, b in self.abufs:
            if off > mark:
                for k, v in ([b.w] if b.w else []) + list(b.r.items()):
                    if v > self.retired.get(k, 0):
                        self.retired[k] = v
                if b.dsem is not None:
                    self.S.free_dsems.append(b.dsem)
                    b.dsem = None
            else:
                keep.append((off, b))
        self.abufs = keep
        self.aoff = mark

    def pair(self, i):
        return self.psf[:, 2 * i:2 * i + 2, :].rearrange("p a b -> p (a b)")

    def bpair(self, i):
        return [self.bpf[2 * i], self.bpf[2 * i + 1]]

    def next_pb(self):
        i = self.pbi % 2
        self.pbi += 1
        return self.psb[:, i, :], self.bpb[i]

    def next_stat(self):
        i = self.sti % 4
        self.sti += 1
        return self.stat[:, i, :], self.bstat[i]

    def load_consts(self):
        d = self.dr
        W = [self.bconst]
        self.dma("sp", self.identb[:], d["c_identb"], [], W)
        self.dma("sp", self.identf[:], d["c_identf"], [], W)
        self.dma("sp", self.onesf[:], d["c_onesf"], [], W)
        self.op("pool", "memset", [], W, ap=self.mhalf[:], constant=-0.5)

    def load_x(self):
        d = self.dr
        self.dma("sp", self.XR[:, 0:16, :], d["x_p"].rearrange("(i p) d -> p i d", p=128), [], self.bxr[0:16])
        self.dma("sp", self.XR[:, 16, :], d["x_s"], [], [self.bxr[16]])
        for t in range(NT):
            slot = self.cast_xb(t)
            self.make_xt(t, slot)

    def cast_xb(self, t):
        slot = self.xbi % 2
        self.xbi += 1
        self.act(self.xb16[:, slot, :], self.XR[:, t, :], AF.Copy, [self.bxr[t]], [self.bxb16[slot]])
        return slot

    def make_xt(self, t, slot):
        pb, bpb = self.next_pb()
        pv = pb.rearrange("p (c n) -> p c n", c=8)
        for c in range(8):
            self.tr(pv[:, c, :], self.xb16[:, slot, c * 128:(c + 1) * 128], self.identb[:],
                    [self.bxb16[slot], self.bconst], [bpb], inc=(c == 7))
        self.evac(self.XT[:, :, t * 128:(t + 1) * 128], pv, [bpb], [self.bxt[t]])

    def load_ln(self, l, which):
        d = self.dr
        self.dma("sp", self.lng[:], d["ln_g"][l, which].partition_broadcast(128), [], [self.blng])
        self.dma("sp", self.lnb[:], d["ln_b"][l, which].partition_broadcast(128), [], [self.blnb])

    def ln1(self, t, final=False):
        st, bst = self.next_stat()
        x = self.XR[:, t, :]
        bx = self.bxr[t]
        for i in range(2):
            self.op("dve", "bn_stats", [bx], [bst], out=st[:, 6 * i:6 * i + 6], in_=x[:, i * 512:(i + 1) * 512])
        self.op("dve", "bn_aggr", [bst], [bst], out=st[:, 12:14], in_=st[:, 0:12])
        self.op("dve", "tensor_scalar", [bst], [bst], out=st[:, 14:15], in0=st[:, 13:14], scalar1=LN_EPS,
                scalar2=None, op0=ALU.add)
        self.op("pool", "tensor_tensor", [bst, self.bconst], [bst], out=st[:, 15:16], in0=st[:, 14:15],
                in1=self.mhalf[:], op=ALU.pow)
        self.op("dve", "tensor_scalar", [bx, bst], [bx], out=x, in0=x, scalar1=st[:, 12:13], scalar2=st[:, 15:16],
                op0=ALU.subtract, op1=ALU.mult)
        self.op("dve", "tensor_tensor", [bx, self.blng], [bx], out=x, in0=x, in1=self.lng[:], op=ALU.mult)
        self.op("dve", "tensor_tensor", [bx, self.blnb], [bx], out=x, in0=x, in1=self.lnb[:], op=ALU.add)
        if final:
            d = self.dr
            if t < 16:
                self.dma("sp", d["y_p"][t * 128:(t + 1) * 128, :], x, [bx], [])
            else:
                self.dma("sp", d["y_s"], x, [bx], [])
            return None
        return self.cast_xb(t)

    def ln_all(self, pend, t, final=False):
        slot = self.ln1(t, final)
        if pend is not None:
            self.make_xt(*pend)
        return None if final else (t, slot)

    def ffn(self, l, final):
        d = self.dr
        self.release(0)
        groups = [(i * 512, 512) for i in range(5)] + [(2560, 256)]
        wg, wu, wd, bw = [], [], [], []
        for s in range(2):
            a, b1 = self.carve(f"wg{s}", [128, 8, 512], BF16)
            u, b2 = self.carve(f"wu{s}", [128, 8, 512], BF16)
            dd, b3 = self.carve(f"wd{s}", [128, 4, 1024], BF16)
            wg.append(a), wu.append(u), wd.append(dd), bw.append((b1, b2, b3))
        at, bat, sg, bsg = [], [], [], []
        for s in range(2):
            a, b = self.carve(f"at{s}", [128, 4, 512], BF16)
            at.append(a), bat.append(b)
            a, b = self.carve(f"sg{s}", [128, 512], F32)
            sg.append(a), bsg.append(b)

        def load(gi):
            ff0, fw = groups[gi]
            s = gi % 2
            nfc = fw // 128
            self.dma("pool", wg[s][:, :, 0:fw], d["ffn_w_gate"][l][:, ff0:ff0 + fw].rearrange("(c p) n -> p c n", p=128),
                     [], [bw[s][0]])
            self.dma("pool", wu[s][:, :, 0:fw], d["ffn_w_up"][l][:, ff0:ff0 + fw].rearrange("(c p) n -> p c n", p=128),
                     [], [bw[s][1]])
            self.dma("pool", wd[s][:, 0:nfc, :], d["ffn_w_down"][l][ff0:ff0 + fw, :].rearrange("(c p) n -> p c n", p=128),
                     [], [bw[s][2]])

        load(0)
        load(1)
        self.load_ln(l, 1)
        units = [(gi, bi) for gi in range(len(groups)) for bi in range(len(BLKS))]
        state = {"pg": 0, "pend": None}

        def gu_steps(u):
            gi, bi = units[u]
            ff0, fw = groups[gi]
            s = gi % 2
            t0, t1 = BLKS[bi]
            c0, n = t0 * 128, (t1 - t0) * 128
            a_s = u % 2
            steps = []
            for fc in range(fw // 128):
                def step(fc=fc):
                    pgi = state["pg"] % 2
                    state["pg"] += 1
                    pg, bpg = self.psf[:, pgi, 0:n], self.bpf[pgi]
                    pu, bpu = self.psf[:, 2 + pgi, 0:n], self.bpf[2 + pgi]
                    for k in range(8):
                        self.mm(pg, wg[s][:, k, fc * 128:(fc + 1) * 128], self.XT[:, k, c0:c0 + n], k == 0, k == 7,
                                [bw[s][0]] + self.bxt[t0:t1], [bpg])
                    for k in range(8):
                        self.mm(pu, wu[s][:, k, fc * 128:(fc + 1) * 128], self.XT[:, k, c0:c0 + n], k == 0, k == 7,
                                [bw[s][1]] + self.bxt[t0:t1], [bpu])
                    self.act(sg[pgi][:, 0:n], pg, AF.Silu, [bpg], [bsg[pgi]])
                    self.op("dve", "tensor_tensor", [bsg[pgi], bpu], [bat[a_s]], out=at[a_s][:, fc, 0:n],
                            in0=sg[pgi][:, 0:n], in1=pu, op=ALU.mult)
                steps.append(step)
            return steps

        def d_steps(u):
            gi, bi = units[u]
            ff0, fw = groups[gi]
            s = gi % 2
            nfc = fw // 128
            t0, t1 = BLKS[bi]
            a_s = u % 2
            last = gi == len(groups) - 1
            steps = []
            for t in range(t0, t1):
                def step(t=t):
                    tl = t - t0
                    for half in range(2):
                        pd, bpd = self.psf[:, 4 + half, :], self.bpf[4 + half]
                        for fc in range(nfc):
                            self.mm(pd, at[a_s][:, fc, tl * 128:(tl + 1) * 128], wd[s][:, fc, half * 512:(half + 1) * 512],
                                    fc == 0, fc == nfc - 1, [bat[a_s], bw[s][2]], [bpd])
                        xs_ = self.XR[:, t, half * 512:(half + 1) * 512]
                        if gi == 0:
                            self.op("dve", "scalar_tensor_tensor", [self.bxr[t], bpd], [self.bxr[t]], out=xs_, in0=xs_,
                                    scalar=ALPHA, in1=pd, op0=ALU.mult, op1=ALU.add)
                        else:
                            self.op("dve", "tensor_tensor", [self.bxr[t], bpd], [self.bxr[t]], out=xs_, in0=xs_, in1=pd,
                                    op=ALU.add)
                    if last:
                        state["pend"] = self.ln_all(state["pend"], t, final)
                steps.append(step)
            if bi == len(BLKS) - 1 and gi + 2 < len(groups):
                steps.append(lambda: load(gi + 2))
            return steps

        prev_d = []
        for u in range(len(units)):
            g_ = gu_steps(u)
            m_ = max(len(g_), len(prev_d))
            for i in range(m_):
                if i < len(g_):
                    g_[i]()
                if i < len(prev_d):
                    prev_d[i]()
            prev_d = d_steps(u)
        for st_ in prev_d:
            st_()
        if state["pend"] is not None:
            self.make_xt(*state["pend"])

    def build_bias_tables(self):
        d = self.dr
        m = self.mark()
        rb, brb = self.carve("rb33", [33, 16], F32)
        oh, boh = self.carve("oh33", [33, 384], F32)
        gs, bgs = self.carve("gsb", [16, 384], F32)
        self.dma("sp", rb[0:32, :], d["rel_bias"], [], [brb])
        self.op("pool", "memset", [], [brb], ap=rb[32:33, :], constant=1.0)
        self.dma("sp", oh, d["c_onehot"], [], [boh])
        ps, bps = self.psf[0:16, 0, 0:384], self.bpf[0]
        self.mm(ps, rb, oh, True, True, [brb, boh], [bps])
        self.evac(gs, ps, [bps], [bgs], "act")
        self.bgtab = Buf("gtab")
        self.dma("sp", d["gtab"], gs, [bgs], [self.bgtab], own=bgs)
        self.release(m)

    def build_bm(self, bm, bbm, U, bU, sample, cst):
        bU = bU if isinstance(bU, list) else [bU]
        antib, sels, negb, bcs = cst
        for q4 in range(4):
            pr = self.pair(q4 % 2)
            bpr = self.bpair(q4 % 2)
            pv = pr.rearrange("p (h k) -> p h k", h=4)
            if not sample:
                for hh2 in range(2):
                    self.mm(pr[:, hh2 * 512:(hh2 + 1) * 512], antib,
                            U[:, 4 * q4 + 2 * hh2:4 * q4 + 2 * hh2 + 2, :].rearrange("p h k -> p (h k)"), True, True,
                            bU + [bcs], [bpr[hh2]])
            else:
                for i in range(4):
                    h = 4 * q4 + i
                    first = (i % 2 == 0)
                    self.mm(pv[:, i, 0:128], sels, U[:, h, 0:128], first, False, bU + [bcs], [bpr[i // 2]], inc=False)
                    for b in range(16):
                        self.mm(pv[:, i, 128 + 8 * b:136 + 8 * b], sels, U[:, h, 128:136], False, False, bU + [bcs],
                                [bpr[i // 2]], inc=False)
                    self.mm(pv[:, i, 128:256], self.identb[:], negb, False, True, [bcs, self.bconst], [bpr[i // 2]],
                            inc=True)
            self.evac(bm[:, 4 * q4:4 * q4 + 4, :], pv, bpr, [bbm])

    def attn(self, j, l):
        d = self.dr
        self.release(0)
        wo, bwo = self.carve("wo", [128, 8, 1024], BF16)
        kT, bkT = self.carve("kT", [128, 2, NTOK], BF16)
        Vb, _ = self.carve("Vb", [128, NT, 256], BF16)
        bVb = [Buf(f"Vb{t}", self.retired) for t in range(NT)]
        bkTt = [Buf(f"kT{t}", self.retired) for t in range(NT)]
        for b in bVb + bkTt:
            self.abufs.append((self.aoff, b))
        qTf, bqT = self.carve("qT", [128, 4096], BF16)
        qT = qTf.rearrange("p (c n) -> p c n", c=8)
        bm, bbm = self.carve("bm", [128, 16, 256], BF16)
        sm, bsm = self.carve("attn_small", [128, 64], F32)
        bq, bk, snk, nsnk = sm[:, 0:8], sm[:, 8:10], sm[:, 16:32], sm[:, 32:48]
        brow, bbrow = self.carve("brow", [65, 512], F32)
        cstt, bcs = self.carve("acst", [128, 3, 128], BF16)
        w8f, _ = self.carve("w8", [128, 6144], BF16)
        w8 = w8f[:, 0:4096]
        P2 = [w8f[:, 0:1024].rearrange("p (h k) -> p h k", h=4), w8f[:, 4096:5120].rearrange("p (h k) -> p h k", h=4)]
        PT2 = [w8f[:, 1024:2048].rearrange("p (h a n) -> p h a n", h=4, a=2),
               w8f[:, 5120:6144].rearrange("p (h a n) -> p h a n", h=4, a=2)]
        Ob = w8f[:, 2048:3072].rearrange("p (h e) -> p h e", h=16)
        OT = w8f[:, 3072:4096].rearrange("p (c n) -> p c n", c=8)
        bP2 = [Buf("P0", self.retired), Buf("P1", self.retired)]
        bPT2 = [Buf("PT0", self.retired), Buf("PT1", self.retired)]
        bOb, bOT = Buf("Ob", self.retired), Buf("OT", self.retired)
        bw8 = [bP2[0], bPT2[0], bOb, bOT]
        for b in bw8 + [bP2[1], bPT2[1]]:
            self.abufs.append((self.aoff, b))
        st2, bst2 = self.carve("ast", [128, 128], F32)
        kvo, bkvo = self.carve("kvo", [128, 512], F32)
        mW = self.mark()
        wq, bwq = self.carve("wq", [128, 8, 8, 128], BF16)
        wkv, bwkv = self.carve("wkv", [128, 8, 512], BF16)
        U, bU = qTf.rearrange("p (h k) -> p h k", h=16), bqT

        wsrc = d["attn_w_qkv"][j]
        for hh in range(2):
            for cg in range(2):
                c0 = cg * 512 + hh * 256
                for ci in range(4):
                    self.dma("pool", wq[:, :, 4 * cg + ci, hh * 64:(hh + 1) * 64],
                             wsrc[:, c0 + 64 * ci:c0 + 64 * ci + 64].rearrange("(k p) n -> p k n", p=128), [], [bwq])
        self.dma("pool", wkv, wsrc[:, 1024:1536].rearrange("(k p) n -> p k n", p=128), [], [bwkv])
        self.dma("pool", wo, d["attn_w_o"][j].rearrange("(k p) n -> p k n", p=128), [], [bwo])
        bsrc = d["attn_b_qkv"][j]
        for hh in range(2):
            for cg in range(2):
                c0 = cg * 512 + hh * 256
                self.dma("sp", bq[hh * 64:(hh + 1) * 64, 4 * cg:4 * cg + 4],
                         bsrc[c0:c0 + 256].rearrange("(c n) -> n c", n=64), [], [bsm], allow_slow_non_contiguous=True)
        self.dma("sp", bk, bsrc[1024:1280].rearrange("(c p) -> p c", p=128), [], [bsm], allow_slow_non_contiguous=True)
        self.dma("sp", snk, d["attn_sinks"][j].partition_broadcast(128), [], [bsm])
        self.dma("sp", brow[0:1, :], bsrc[1024:1536].rearrange("(o n) -> o n", o=1), [], [bbrow])
        self.dma("sp", brow[32:33, :], d["attn_b_o"][j, 0:512].rearrange("(o n) -> o n", o=1), [], [bbrow])
        self.dma("sp", brow[64:65, :], d["attn_b_o"][j, 512:1024].rearrange("(o n) -> o n", o=1), [], [bbrow])
        self.dma("sp", cstt[:, 0, :], d["c_antib"], [], [bcs])
        self.dma("sp", cstt[:, 1, :], d["c_sels"], [], [bcs])
        self.dma("sp", cstt[:, 2, :], d["c_negb"], [], [bcs])
        self.op("dve", "tensor_scalar", [bsm], [bsm], out=bq, in0=bq, scalar1=0.125, scalar2=None, op0=ALU.mult)
        self.op("dve", "tensor_scalar", [bsm], [bsm], out=nsnk, in0=snk, scalar1=-1.0, scalar2=None, op0=ALU.mult)
        gt = d["gtab"]
        self.dma("pool", U, bass.AP(gt.tensor, 0, [[1, 128], [384, 16], [1, 256]]), [self.bgtab], [bU])
        cst = (cstt[:, 0, :], cstt[:, 1, :], cstt[:, 2, :], bcs)
        import os
        dbg = os.environ.get("KATT", "")
        if dbg == "a0":
            return
        self.build_bm(bm, bbm, U, bU, False, cst)
        self.load_ln(l, 0)
        if dbg == "a1":
            return

        pend = None
        pst = {"pri": 0, "smp": None}

        def proj(bi):
            t0, t1 = BLKS[bi]
            pri = pst["pri"]
            c0, n = t0 * 128, (t1 - t0) * 128
            xtb = self.bxt[t0:t1]
            for c in range(2):
                bi_ = pri % 4
                pri += 1
                ps, bps = self.psf[:, bi_, 0:n], self.bpf[bi_]
                for k in range(8):
                    self.mm(ps, wkv[:, k, c * 128:(c + 1) * 128], self.XT[:, k, c0:c0 + n], k == 0, k == 7,
                            [bwkv] + xtb, [bps])
                self.act(kT[:, c, c0:c0 + n], ps, AF.Identity, [bps, bsm], bkTt[t0:t1], bias=bk[:, c:c + 1], scale=1.0)
            for c in range(8):
                bi_ = pri % 4
                pri += 1
                ps, bps = self.psf[:, bi_, 0:n], self.bpf[bi_]
                for k in range(8):
                    self.mm(ps, wq[:, k, c, :], self.XT[:, k, c0:c0 + n], k == 0, k == 7, [bwq] + xtb, [bps])
                self.act(qT[:, c, 0:n], ps, AF.Identity, [bps, bsm], [bqT], bias=bq[:, c:c + 1], scale=0.125)
            for t in range(t0, t1):
                if dbg == "p1":
                    return
                bi_ = pri % 4
                pri += 1
                ps, bps = self.psf[:, bi_, :], self.bpf[bi_]
                self.mm(ps, self.onesf[0:1, :], brow[0:1, :], True, False, [self.bconst, bbrow], [bps], inc=False)
                for k in range(8):
                    self.mm(ps, self.XT[:, k, t * 128:(t + 1) * 128], wkv[:, k, :], False, k == 7, [bwkv, self.bxt[t]], [bps])
                if t < 15:
                    self.evac(Vb[:, t, :], ps[:, 256:512], [bps], [bVb[t]])
                if t >= 15 and dbg != "p2":
                    self.evac(kvo, ps, [bps], [bkvo], "act")
                    self.op("dve", "tensor_copy", [bkvo], [bVb[t]], out=Vb[:, t, :], in_=kvo[:, 256:512])
                    if dbg == "p3a" and t == 16:
                        return
                    if dbg == "p3c":
                        return
                    if dbg == "p3d":
                        if t == 15:
                            self.dma("sp", d["nk_p"][j], kvo[:, 0:256], [bkvo], [])
                        return
                    if dbg == "p3b" and t == 15:
                        return
                    if t == 15:
                        self.dma("sp", d["nk_p"][j], kvo[:, 0:256], [bkvo], [])
                        self.dma("sp", d["nv_p"][j], kvo[:, 256:512], [bkvo], [])
                    else:
                        for b in range(16):
                            self.dma("sp", d["nk_s"][j, b, 120:128, :], kvo[8 * b:8 * b + 8, 0:256], [bkvo], [])
                            self.dma("sp", d["nv_s"][j, b, 120:128, :], kvo[8 * b:8 * b + 8, 256:512], [bkvo], [])
            if dbg in ("p1", "p2", "p3", "p3a", "p3b", "p3c", "p3d"):
                return
            if t0 == 16:
                self.dma("sp", d["nk_s"][j, :, 0:120, :], d["cache_k"][j, :, 8:128, :], [self.bd2d], [], own=self.bd2d)
                self.dma("sp", d["nv_s"][j, :, 0:120, :], d["cache_v"][j, :, 8:128, :], [self.bd2d], [], own=self.bd2d)
                if dbg == "p4":
                    return
                kcr = w8.rearrange("p (b c) -> p b c", b=16)
                self.dma("pool", kcr, bass.AP(gt.tensor, 0, [[1, 128], [384, 16], [1, 256]]), [self.bgtab], bw8)
                self.build_bm(bm, bbm, kcr, bw8, True, cst)
                self.release(mW)
                vc, bvc = self.carve("vc", [128, 16, 256], BF16)
                kcT, bkcT = self.carve("kcT", [128, 2, 16, 128], BF16)
                bm16, bbm16 = self.carve("bm16", [128, 16, 128], BF16)
                QM1, bQM1 = self.carve("QM", [128, 16, 128], BF16)
                QM, bQM = [QM1, QM1], [bQM1, bQM1]
                self.dma("sp", bm16, d["c_bm16"], [], [bbm16])
                self.dma("pool", kcr, d["cache_k"][j].rearrange("b k c -> k b c"), [], bw8)
                self.dma("pool", vc, d["cache_v"][j].rearrange("b k c -> k b c"), [], [bvc])
                for kc in range(2):
                    for b8 in range(2):
                        pb, bpb = self.next_pb()
                        pv = pb.rearrange("p (b n) -> p b n", b=8)
                        for bb in range(8):
                            b = b8 * 8 + bb
                            self.tr(pv[:, bb, :], kcr[:, b, kc * 128:(kc + 1) * 128], self.identb[:], bw8 + [self.bconst],
                                    [bpb], inc=(bb == 7))
                        self.evac(kcT[:, kc, b8 * 8:b8 * 8 + 8, :], pv, [bpb], [bkcT])
                pst["smp"] = (kcT, bkcT, vc, bvc, QM, bQM, bm16, bbm16)
            pst["pri"] = pri

        stg = [st2[:, 20 * hg:20 * hg + 20] for hg in range(4)]
        bstg = [Buf(f"astg{hg}", self.retired) for hg in range(4)]
        for b_ in bstg:
            self.abufs.append((self.aoff, b_))
        den, rden = st2[:, 96:112], st2[:, 112:128]
        Opr, bOpr = self.pair(2), self.bpair(2)

        def geom(t, hg, i):
            h = 4 * hg + i
            c, hh = (h % 4) + 4 * (h // 8), (h // 4) % 2
            return h, c, hh, hg // 2, slice(hh * 64, hh * 64 + 64)

        def stA(t, hg):
            t0 = BLKS[[i_ for i_, (a_, b_) in enumerate(BLKS) if a_ <= t < b_][0]][0]
            tl = t - t0
            sample = t == 16
            Spr, bSpr = self.pair(hg % 2), self.bpair(hg % 2)
            Sv = Spr.rearrange("p (h k) -> p h k", h=4)
            lo = 128 if t == 0 else 0
            for i in range(4):
                h, c, hh, kc, rows = geom(t, hg, i)
                bS = [bSpr[i // 2]]
                first = (i % 2 == 0)
                qcols = qT[rows, c, tl * 128:(tl + 1) * 128]
                if not sample:
                    kb = bkTt[max(t - 1, 0):t + 1]
                    self.mm(Sv[:, i, lo:256], qcols, kT[rows, kc, (t - 1) * 128 + lo:(t + 1) * 128], first, False,
                            [bqT] + kb, bS, inc=False)
                else:
                    kcT, bkcT, vc, bvc, QM, bQM, bm16, bbm16 = pst["smp"]
                    qs = (4 * hg + i) % 2
                    self.op("dve", "tensor_tensor", [bqT, bbm16], [bQM[qs]], out=QM[qs][rows],
                            in0=qcols.unsqueeze(1).to_broadcast([64, 16, 128]), in1=bm16[rows], op=ALU.mult)
                    for b in range(16):
                        self.mm(Sv[:, i, 0:128], QM[qs][rows, b, :], kcT[rows, kc, b, :], first and b == 0, False,
                                [bQM[qs], bkcT], bS, inc=False)
                    self.mm(Sv[:, i, 128:256], qcols, kT[rows, kc, 2048:2176], False, False, [bqT, bkTt[16]], bS,
                            inc=False)
                self.mm(Sv[:, i, lo:256], self.identb[:], bm[:, h, lo:256], False, True, [bbm, self.bconst], bS,
                        inc=True)

        def stB1(t, hg):
            Spr, bSpr = self.pair(hg % 2), self.bpair(hg % 2)
            Sv = Spr.rearrange("p (h k) -> p h k", h=4)
            lo = 128 if t == 0 else 0
            sg_, bsg_ = stg[hg], bstg[hg]
            mx, nmx, rs, t4, es = (sg_[:, 4 * q:4 * q + 4] for q in range(5))
            hs = slice(4 * hg, 4 * hg + 4)
            self.op("dve", "tensor_reduce", bSpr, [bsg_], out=mx, in_=Sv[:, :, lo:256], axis=AX.X, op=ALU.max)
            self.op("dve", "scalar_tensor_tensor", [bsg_, bsm], [bsg_], out=nmx, in0=mx, scalar=-1.0,
                    in1=nsnk[:, hs], op0=ALU.mult, op1=ALU.min)
            self.op("dve", "tensor_tensor", [bsg_, bsm], [bsg_], out=t4, in0=snk[:, hs], in1=nmx, op=ALU.add)

        def stB2(t, hg):
            Spr, bSpr = self.pair(hg % 2), self.bpair(hg % 2)
            Sv = Spr.rearrange("p (h k) -> p h k", h=4)
            lo = 128 if t == 0 else 0
            sg_, bsg_ = stg[hg], bstg[hg]
            mx, nmx, rs, t4, es = (sg_[:, 4 * q:4 * q + 4] for q in range(5))
            Pq, bPq = P2[hg % 2], bP2[hg % 2]
            for i in range(4):
                self.act(Pq[:, i, lo:256], Sv[:, i, lo:256], AF.Exp, [bSpr[i // 2], bsg_], [bPq, bsg_],
                         bias=nmx[:, i:i + 1], scale=1.0, accum_out=rs[:, i:i + 1])
            self.act(es, t4, AF.Exp, [bsg_], [bsg_])

        def stB3(t, hg):
            sg_, bsg_ = stg[hg], bstg[hg]
            mx, nmx, rs, t4, es = (sg_[:, 4 * q:4 * q + 4] for q in range(5))
            hs = slice(4 * hg, 4 * hg + 4)
            self.op("dve", "tensor_tensor", [bsg_], [bst2], out=den[:, hs], in0=rs, in1=es, op=ALU.add)

        def stC(t, hg):
            sample = t == 16
            Pq, bPq = P2[hg % 2], bP2[hg % 2]
            PTq, bPTq = PT2[hg % 2], bPT2[hg % 2]
            pb, bpb = self.next_pb()
            pv = pb.rearrange("p (h a n) -> p h a n", h=4, a=2)
            kts = [1] if t == 0 else [0, 1]
            for i in range(4):
                for kt in kts:
                    self.tr(pv[:, i, kt, :], Pq[:, i, kt * 128:(kt + 1) * 128], self.identb[:], [bPq, self.bconst],
                            [bpb], inc=(i == 3 and kt == 1))
            if t == 0:
                self.evac(PTq[:, :, 1, :], pv[:, :, 1, :], [bpb], [bPTq], "dve")
            else:
                self.evac(PTq, pv, [bpb], [bPTq], "dve")
            for i in range(4):
                h = 4 * hg + i
                bO = [bOpr[h // 8]]
                oc = Opr[:, h * 64:(h + 1) * 64]
                firstb = (h % 8 == 0)
                if not sample:
                    for kt in kts:
                        self.mm(oc, PTq[:, i, kt, :], Vb[:, t - 1 + kt, hg * 64:(hg + 1) * 64],
                                firstb and kt == kts[0], kt == 1, [bPTq, bVb[t - 1 + kt]], bO,
                                inc=(kt == 1 and i == 3))
                else:
                    kcT, bkcT, vc, bvc, QM, bQM, bm16, bbm16 = pst["smp"]
                    qs = (4 * hg + i) % 2
                    self.op("dve", "tensor_tensor", [bPTq, bbm16], [bQM[qs]], out=QM[qs],
                            in0=PTq[:, i, 0, :].unsqueeze(1).to_broadcast([128, 16, 128]), in1=bm16, op=ALU.mult)
                    for b in range(16):
                        self.mm(oc, QM[qs][:, b, :], vc[:, b, hg * 64:(hg + 1) * 64], firstb and b == 0, False,
                                [bQM[qs], bvc], bO, inc=False)
                    self.mm(oc, PTq[:, i, 1, :], Vb[:, 16, hg * 64:(hg + 1) * 64], False, True, [bPTq, bVb[16]], bO,
                            inc=True)

        def stD1(t):
            self.op("dve", "reciprocal", [bst2], [bst2], out=rden, in_=den)
            self.op("dve", "tensor_tensor", bOpr + [bst2], [bOb], out=Ob, in0=Opr.rearrange("p (h e) -> p h e", h=16),
                    in1=rden.unsqueeze(2).to_broadcast([128, 16, 64]), op=ALU.mult)
            pb, bpb = self.next_pb()
            pv = pb.rearrange("p (c n) -> p c n", c=8)
            Obf = Ob.rearrange("p h e -> p (h e)")
            for c in range(8):
                self.tr(pv[:, c, :], Obf[:, c * 128:(c + 1) * 128], self.identb[:], [bOb, self.bconst], [bpb], inc=(c == 7))
            self.evac(OT, pv, [bpb], [bOT], "act")
            for half in range(2):
                yh = Opr[:, half * 512:(half + 1) * 512]
                pr_ = 32 * (half + 1)
                self.mm(yh, self.onesf[pr_:pr_ + 1, :], brow[pr_:pr_ + 1, :], True, False,
                        [self.bconst, bbrow], [bOpr[half]], inc=False)
                for k in range(8):
                    self.mm(yh, OT[:, k, :], wo[:, k, half * 512:(half + 1) * 512], False, k == 7, [bOT, bwo], [bOpr[half]])

        def stD2(t):
            nonlocal pend
            x = self.XR[:, t, :]
            self.op("dve", "scalar_tensor_tensor", [self.bxr[t]] + bOpr, [self.bxr[t]], out=x, in0=x, scalar=ALPHA,
                    in1=Opr, op0=ALU.mult, op1=ALU.add)
            pend = self.ln_all(pend, t)

        done_proj, doneA = set(), set()
        prevD2 = None
        for bi, (t0, t1) in enumerate(BLKS):
            if bi not in done_proj:
                proj(bi)
                done_proj.add(bi)
            for t in range(t0, t1):
                if t not in doneA:
                    stA(t, 0)
                    stA(t, 1)
                stB1(t, 0); stB2(t, 0)
                if prevD2 is not None:
                    stD2(prevD2)
                    prevD2 = None
                stB1(t, 1); stB3(t, 0); stA(t, 2); stC(t, 0); stB2(t, 1)
                stB1(t, 2); stB3(t, 1); stA(t, 3); stC(t, 1); stB2(t, 2)
                stB1(t, 3); stB3(t, 2); stC(t, 2); stB2(t, 3)
                stB3(t, 3); stC(t, 3)
                nxt = t + 1
                if nxt < t1:
                    stA(nxt, 0); stA(nxt, 1); doneA.add(nxt)
                elif bi + 1 < len(BLKS) and BLKS[bi + 1][0] != 16:
                    proj(bi + 1)
                    done_proj.add(bi + 1)
                    stA(nxt, 0); stA(nxt, 1); doneA.add(nxt)
                stD1(t)
                prevD2 = t
        if prevD2 is not None:
            stD2(prevD2)
        if pend is not None:
            self.make_xt(*pend)

    def pool(self, l):
        d = self.dr
        self.release(0)
        mt, bmt = self.carve("poolmt", [128, 24, 128], F32)
        pw, bpw = self.carve("poolw", [128, 4, 2, 256], BF16)
        psc, bpsc = self.carve("poolsc", [128, D], F32)
        pfx, bpfx = self.carve("poolpfx", [128, 2, D], F32)
        dT, bdT = [], []
        for s in range(2):
            a, b = self.carve(f"dT{s}", [128, 8, 128], BF16)
            dT.append(a), bdT.append(b)
        tmp, btmp = self.carve("pooltmp", [128, D], F32)
        self.dma("sp", mt, d["c_poolmt"], [], [bmt])
        self.dma("pool", pw, d["pool_w"][0].rearrange("g (kk p) n -> p g kk n", p=128), [], [bpw])
        self.dma("sp", psc, d["pool_scale"][0].partition_broadcast(128), [], [bpsc])
        self.dma("sp", pfx[0:120, 0, :], d["state_pool"][0:8].rearrange("b r c -> (b r) c"), [], [bpfx])
        self.dma("sp", pfx[0:120, 1, :], d["state_pool"][8:16].rearrange("b r c -> (b r) c"), [], [bpfx])
        self.dma("sp", d["npool_p"], self.XR[113:128, 15, :], [self.bxr[15]], [])
        self.dma("sp", d["npool_s"][:, 0:7, :], d["state_pool"][:, 8:15, :], [self.bd2d], [], own=self.bd2d)
        for b in range(16):
            self.dma("sp", d["npool_s"][b, 7:15, :], self.XR[8 * b:8 * b + 8, 16, :], [self.bxr[16]], [])
        self.load_ln(l, 0)

        def diff(t):
            s = t % 2
            pr, bpr = self.pair(s), self.bpair(s)
            pv = pr.rearrange("p (c n) -> p c n", c=8)
            for c in range(8):
                g = c // 2
                cs = slice(c * 128, (c + 1) * 128)
                first = (c % 4 == 0)
                bb = [bpr[c // 4]]
                if t == 16:
                    self.mm(pv[:, c, :], pfx[0:120, 0, cs], mt[0:120, 16 + g, :], first, False, [bpfx, bmt], bb, inc=False)
                    self.mm(pv[:, c, :], pfx[0:120, 1, cs], mt[0:120, 20 + g, :], False, False, [bpfx, bmt], bb, inc=False)
                    self.mm(pv[:, c, :], self.XR[:, 16, cs], mt[:, 12 + g, :], False, True, [self.bxr[16], bmt], bb, inc=True)
                elif t == 0:
                    self.mm(pv[:, c, :], self.XR[:, 0, cs], mt[:, 8 + g, :], first, True, [self.bxr[0], bmt], bb, inc=True)
                else:
                    self.mm(pv[:, c, :], self.XR[:, t - 1, cs], mt[:, g, :], first, False, [self.bxr[t - 1], bmt], bb, inc=False)
                    self.mm(pv[:, c, :], self.XR[:, t, cs], mt[:, 4 + g, :], False, True, [self.bxr[t], bmt], bb, inc=True)
            self.evac(dT[s], pv, bpr, [bdT[s]])

        pend = [None]

        def update(t):
            s = t % 2
            ypr, bypr = self.pair(2), self.bpair(2)
            for g in range(4):
                for kk in range(2):
                    self.mm(ypr[:, g * 256:(g + 1) * 256], dT[s][:, 2 * g + kk, :], pw[:, g, kk, :], (g % 2 == 0) and kk == 0,
                            kk == 1, [bdT[s], bpw], [bypr[g // 2]], inc=(kk == 1))
            self.op("dve", "tensor_tensor", bypr + [bpsc], [btmp], out=tmp, in0=ypr, in1=psc, op=ALU.mult)
            x = self.XR[:, t, :]
            self.op("dve", "scalar_tensor_tensor", [self.bxr[t], btmp], [self.bxr[t]], out=x, in0=x, scalar=ALPHA, in1=tmp,
                    op0=ALU.mult, op1=ALU.add)
            pend[0] = self.ln_all(pend[0], t)

        for t in range(NT):
            diff(t)
            if t >= 1:
                update(t - 1)
        update(16)
        if pend[0] is not None:
            self.make_xt(*pend[0])

    def ssm(self, l):
        d = self.dr
        self.release(0)
        win = d["ssm_w_in"][0]
        negm_p, bnp = self.carve("negm_p", [128, 1024], BF16)
        negm_s, bns = self.carve("negm_s", [128, 1024], BF16)
        c8, bc8 = self.carve("c8", [8, 8 * 128 + 512 + 128 + 128 + 64], F32)
        sel8 = c8[:, 0:1024].rearrange("p (r n) -> p r n", r=8)
        scan_p, scan_s = c8[:, 1024:1536], c8[:, 1536:1664]
        apar = c8[:, 1664:1792]
        selj = c8[:, 1792:1856].rearrange("p (b j) -> p b j", b=16)
        seqm, bseqm = self.carve("seqm", [128, 16], F32)
        self.dma("sp", negm_p, d["c_negm_p"], [], [bnp])
        self.dma("sp", negm_s, d["c_negm_s"], [], [bns])
        self.dma("sp", sel8, d["c_sel8"], [], [bc8])
        self.dma("sp", scan_p, d["c_scan_p"], [], [bc8])
        self.dma("sp", scan_s, d["c_scan_s"], [], [bc8])
        self.dma("sp", apar, d["c_apar"], [], [bc8])
        self.dma("sp", selj, d["c_selj"], [], [bc8])
        self.dma("sp", seqm, d["c_seqm"], [], [bseqm])
        self.load_ln(l, 0)
        mL = self.mark()
        pend = None
        for g in range(4):
            self.release(mL)
            wx, bwx = self.carve("wx", [128, 8, 512], BF16)
            wz, bwz = self.carve("wz", [128, 8, 512], BF16)
            wBC, bwBC = self.carve("wBC", [128, 8, 256], BF16)
            wdt, bwdt = self.carve("wdt", [128, 8, 8], BF16)
            wout, bwout = self.carve("wout", [128, 4, 1024], BF16)
            cw, bcw = self.carve("convw", [128, 6, 5], F32)
            hp, bhp = self.carve("headp", [8, 4], F32)
            dbc, bdbc = self.carve("dbc", [128, 8], F32)
            nw, bnw = self.carve("nw", [128, 512], F32)
            wr = lambda c0, w_: win[:, c0:c0 + w_].rearrange("(k p) n -> p k n", p=128)
            self.dma("pool", wx, wr(2048 + 512 * g, 512), [], [bwx])
            self.dma("pool", wBC[:, :, 0:128], wr(4096 + 128 * g, 128), [], [bwBC])
            self.dma("pool", wBC[:, :, 128:256], wr(4608 + 128 * g, 128), [], [bwBC])
            self.dma("pool", wdt, wr(5120 + 8 * g, 8), [], [bwdt])
            self.dma("pool", wz, wr(512 * g, 512), [], [bwz])
            self.dma("pool", wout, d["ssm_w_out"][0][512 * g:512 * g + 512, :].rearrange("(k p) n -> p k n", p=128), [], [bwout])
            chbase = [512 * g + 128 * i for i in range(4)] + [2048 + 128 * g, 2560 + 128 * g]
            cwsrc, cbsrc = d["ssm_conv_w"][0], d["ssm_conv_b"][0]
            for cc in range(6):
                cb = chbase[cc]
                self.dma("sp", cw[:, cc, 0:4], cwsrc[:, cb:cb + 128].rearrange("j p -> p j"), [], [bcw],
                         allow_slow_non_contiguous=True)
                self.dma("sp", cw[:, cc, 4:5], cbsrc[cb:cb + 128].rearrange("(p o) -> p o", o=1), [], [bcw])
            self.dma("sp", hp[:, 0:1], d["ssm_dt_bias"][0][8 * g:8 * g + 8].rearrange("(p o) -> p o", o=1), [], [bhp])
            self.dma("sp", hp[:, 1:2], d["ssm_a_log"][0][8 * g:8 * g + 8].rearrange("(p o) -> p o", o=1), [], [bhp])
            self.dma("sp", dbc, d["ssm_d"][0][8 * g:8 * g + 8].partition_broadcast(128), [], [bdbc])
            self.dma("sp", nw, d["ssm_norm_w"][0][512 * g:512 * g + 512].partition_broadcast(128), [], [bnw])
            self.act(hp[:, 2:3], hp[:, 1:2], AF.Exp, [bhp], [bhp])
            self.op("dve", "tensor_scalar", [bhp], [bhp], out=hp[:, 2:3], in0=hp[:, 2:3], scalar1=-1.0, scalar2=None,
                    op0=ALU.mult)
            cwk, _ = self.carve("convwork", [128, 1536], F32)
            acc = [cwk[:, 0:512], cwk[:, 512:1024]]
            ctmp = cwk[:, 1024:1536]
            bacc = [Buf("acc0", self.retired), Buf("acc1", self.retired)]
            bctmp = Buf("ctmp", self.retired)
            bcwk = bacc + [bctmp]
            for b in bcwk:
                self.abufs.append((self.aoff, b))
            xcT, bxcT = self.carve("xcT", [128, 6, 512], BF16)
            dtb_, bdt = self.carve("dtbuf", [8, 3, 512], F32)
            el, bel = self.carve("elast", [8, 16], F32)
            xs, bxs = self.carve("xs", [128, 640], BF16)
            tmd, btmd = self.carve("tmd", [128, 32], F32)
            Ef, bE_ = self.carve("E", [128, 1024], F32)
            E = Ef.rearrange("p (r n) -> p r n", r=8)
            cst_ = Ef[:, 0:768]
            bcst = bE_
            CM = Ef.bitcast(BF16).rearrange("p (b n) -> p b n", b=16)
            bCM = bE_
            WT, bWT = self.carve("WT", [128, 8, 128], BF16)
            xD, bxD = self.carve("xD", [128, 512], BF16)
            sz, bsz = self.carve("sz", [128, 512], F32)
            y1, by1 = self.carve("y1", [128, 512], F32)
            hout, bhout = y1.rearrange("p (j n) -> p j n", j=4), by1
            yn, byn = self.carve("yn", [128, 512], BF16)
            ynT, bynT = self.carve("ynT", [128, 4, 128], BF16)
            dg, bdg = self.carve("dg", [8, 64], F32)
            mP = self.mark()
            rawc, _ = self.carve("rawc", [128, 2, 515], F32)
            brawc = [Buf("rawc0", self.retired), Buf("rawc1", self.retired)]
            for b in brawc:
                self.abufs.append((self.aoff, b))
            hist, bhist = self.carve("hist", [128, 6, 3], F32)
            hT, bhT = self.carve("hT", [128, 512], F32)
            hTb, bhTb = self.carve("hTb", [128, 512], BF16)
            xd, bxd = self.carve("xd", [128, 512], BF16)
            dbs, bdbs = self.carve("dbs", [128, 64], F32)
            self.op("pool", "memset", [], [bhT], ap=hT, constant=0.0)
            self.op("pool", "memset", [], [bhTb], ap=hTb, constant=0.0)
            self.op("pool", "memset", [], [bhist], ap=hist, constant=0.0)
            wsl = [wx[:, :, 0:128], wx[:, :, 128:256], wx[:, :, 256:384], wx[:, :, 384:512], wBC[:, :, 0:128], wBC[:, :, 128:256]]
            wsb = [bwx, bwx, bwx, bwx, bwBC, bwBC]
            decT, bEt, acum = (dtb_[:, q, :] for q in range(3))

            def dt_path(psd, bpsd, n, L, scanm):
                nb = n // L
                t0, t1, ac = decT[:, 0:n], bEt[:, 0:n], acum[:, 0:n]
                v3 = lambda a_: a_.rearrange("p (b t) -> p b t", t=L)
                self.act(t0, psd, AF.Exp, [bpsd, bhp], [bdt], bias=hp[:, 0:1], scale=1.0)
                self.act(t0, t0, AF.Ln, [bdt], [bdt], bias=1.0, scale=1.0)
                self.act(t1, t0, AF.Ln, [bdt], [bdt])
                self.op("dve", "tensor_scalar", [bdt, bhp], [bdt], out=t0, in0=t0, scalar1=hp[:, 2:3], scalar2=None,
                        op0=ALU.mult)
                self.op("dve", "tensor_tensor_scan", [bdt, bc8], [bdt], out=ac, data0=scanm, data1=t0, initial=0.0,
                        op0=ALU.mult, op1=ALU.add)
                self.op("dve", "tensor_tensor", [bdt], [bdt], out=t1, in0=t1, in1=ac, op=ALU.subtract)
                self.op("dve", "tensor_tensor", [bdt], [bdt], out=v3(t0), in0=v3(t1),
                        in1=v3(ac)[:, :, L - 1:L].to_broadcast([8, nb, L]), op=ALU.add)
                self.act(t0, t0, AF.Exp, [bdt], [bdt])
                self.act(el[:, 0:nb], v3(ac)[:, :, L - 1], AF.Exp, [bdt], [bel])

            def conv_chunk(cc, src_views, bsrc, out_view, shape):
                s = cc % 2
                a = acc[s]
                av = a if shape is None else a[:, 0:shape[0] * shape[1]].rearrange("p (b t) -> p b t", b=shape[0])
                tv = ctmp if shape is None else ctmp[:, 0:shape[0] * shape[1]].rearrange("p (b t) -> p b t", b=shape[0])
                self.op("dve", "tensor_scalar", [bsrc, bcw], [bacc[s]], out=av, in0=src_views[0], scalar1=cw[:, cc, 0:1],
                        scalar2=cw[:, cc, 4:5], op0=ALU.mult, op1=ALU.add)
                for jj in range(1, 4):
                    self.op("dve", "scalar_tensor_tensor", [bsrc, bcw, bacc[s]], [bacc[s]], out=av, in0=src_views[jj],
                            scalar=cw[:, cc, jj:jj + 1], in1=av, op0=ALU.mult, op1=ALU.add)
                self.act(out_view, av, AF.Silu, [bacc[s]], [bxcT])

            v8 = lambda a_: a_.rearrange("p (r e) -> p r e", r=8)

            def ph1_pe(t, cols, negm, bnegm):
                xtt = [self.bxt[t]]
                pb, bpb = self.next_pb()
                pv = pb[:, 0:640].rearrange("p (c n) -> p c n", c=5)
                for cc in range(5):
                    self.tr(pv[:, cc, :], xcT[:, cc, cols], self.identb[:], [bxcT, self.bconst], [bpb], inc=(cc == 4))
                p2, bp2 = self.psf[:, 2, :], self.bpf[2]
                for q, srcv in enumerate((bEt, acum, decT)):
                    self.tr(p2[:, 8 * q:8 * q + 8], srcv[:, cols], self.identf[0:8, 0:8], [bdt, self.bconst], [bp2], inc=(q == 2))
                self.mm(p2[:, 128:256], xcT[:, 4, cols], xcT[:, 5, cols], False, True, [bxcT], [bp2])
                spr, bspr = self.pair(0), self.bpair(0)
                sv = spr.rearrange("p (r n) -> p r n", r=8)
                for r in range(8):
                    self.mm(sv[:, r, :], sel8[:, r, :], acum[:, cols], r % 4 == 0, False, [bc8, bdt], [bspr[r // 4]], inc=False)
                for a2 in range(2):
                    self.mm(spr[:, a2 * 512:(a2 + 1) * 512], self.identb[:], negm[:, a2 * 512:(a2 + 1) * 512], False, True,
                            [self.bconst, bnegm], [bspr[a2]], inc=True)
                p3, bp3 = self.psf[:, 3, :], self.bpf[3]
                for k in range(8):
                    self.mm(p3, self.XT[:, k, t * 128:(t + 1) * 128], wz[:, k, :], k == 0, k == 7, [bwz] + xtt, [bp3])
                return (pb, bpb)

            def ph1_early(t, pbt):
                pb, bpb = pbt
                p2, bp2 = self.psf[:, 2, :], self.bpf[2]
                self.evac(tmd[:, 0:24], p2[:, 0:24], [bp2], [btmd], "dve")
                self.act(tmd[:, 24:32], tmd[:, 8:16], AF.Exp, [btmd], [btmd])
                self.evac(xs, pb[:, 0:640], [bpb], [bxs], "dve")

            def ph1_late(t):
                p2, bp2 = self.psf[:, 2, :], self.bpf[2]
                spr, bspr = self.pair(0), self.bpair(0)
                sv = spr.rearrange("p (r n) -> p r n", r=8)
                for r in range(8):
                    self.act(E[:, r, :], sv[:, r, :], AF.Exp, [bspr[r // 4], btmd], [bE_], bias=tmd[:, r:r + 1], scale=1.0)
                p3, bp3 = self.psf[:, 3, :], self.bpf[3]
                self.act(sz, p3, AF.Silu, [bp3], [bsz])
                self.op("dve", "tensor_tensor", [bE_, bp2], [bWT], out=WT, in0=E,
                        in1=p2[:, 128:256].unsqueeze(1).to_broadcast([128, 8, 128]), op=ALU.mult)
                self.op("dve", "tensor_tensor", [bxs, bdbc], [bxD], out=v8(xD), in0=v8(xs[:, 0:512]),
                        in1=dbc.unsqueeze(2).to_broadcast([128, 8, 64]), op=ALU.mult)

            def ph2(t, yi_fn, state_fn, hoist):
                nonlocal pend
                p4, bp4 = self.psf[:, 4, :], self.bpf[4]
                p5, bp5 = self.psf[:, 5, :], self.bpf[5]
                for r in range(8):
                    self.mm(p4[:, r * 64:(r + 1) * 64], WT[:, r, :], xs[:, r * 64:(r + 1) * 64], r == 0, False, [bWT, bxs], [bp4],
                            inc=False)
                self.mm(p4, self.identb[:], xD, False, True, [self.bconst, bxD], [bp4], inc=True)
                if state_fn is not None:
                    state_fn(0)
                yi_fn(p5, bp5)
                if state_fn is not None:
                    state_fn(1)
                self.op("dve", "tensor_tensor", [bp5, btmd], [by1], out=v8(y1), in0=v8(p5),
                        in1=tmd[:, 24:32].unsqueeze(2).to_broadcast([128, 8, 64]), op=ALU.mult)
                self.op("dve", "tensor_tensor", [by1, bp4], [by1], out=y1, in0=y1, in1=p4, op=ALU.add)
                self.op("dve", "tensor_tensor", [by1, bsz], [by1], out=y1, in0=y1, in1=sz, op=ALU.mult)
                st, bst = self.next_stat()
                self.act(p5, y1, AF.Square, [by1], [bp5, bst], accum_out=st[:, 0:1])
                pbt = None
                if hoist is not None:
                    pbt = hoist[0]()
                    hoist[1](pbt)
                self.op("dve", "tensor_scalar", [bst], [bst], out=st[:, 1:2], in0=st[:, 0:1], scalar1=1.0 / 512.0, scalar2=RMS_EPS,
                        op0=ALU.mult, op1=ALU.add)
                self.op("pool", "tensor_tensor", [bst, self.bconst], [bst], out=st[:, 2:3], in0=st[:, 1:2], in1=self.mhalf[:],
                        op=ALU.pow)
                self.op("dve", "scalar_tensor_tensor", [by1, bst, bnw], [byn], out=yn, in0=y1, scalar=st[:, 2:3], in1=nw,
                        op0=ALU.mult, op1=ALU.mult)
                pb, bpb = self.next_pb()
                pv = pb[:, 0:512].rearrange("p (c n) -> p c n", c=4)
                for c in range(4):
                    self.tr(pv[:, c, :], yn[:, c * 128:(c + 1) * 128], self.identb[:], [byn, self.bconst], [bpb], inc=(c == 3))
                self.evac(ynT, pv, [bpb], [bynT], "act")
                if hoist is not None:
                    hoist[2]()
                opr, bopr = self.pair(2), self.bpair(2)
                for half in range(2):
                    for k in range(4):
                        self.mm(opr[:, half * 512:(half + 1) * 512], ynT[:, k, :], wout[:, k, half * 512:(half + 1) * 512], k == 0,
                                k == 3, [bynT, bwout], [bopr[half]])
                x = self.XR[:, t, :]
                if g == 0:
                    self.op("dve", "scalar_tensor_tensor", [self.bxr[t]] + bopr, [self.bxr[t]], out=x, in0=x, scalar=ALPHA,
                            in1=opr, op0=ALU.mult, op1=ALU.add)
                else:
                    self.op("dve", "tensor_tensor", [self.bxr[t]] + bopr, [self.bxr[t]], out=x, in0=x, in1=opr, op=ALU.add)
                if g == 3:
                    pend = self.ln_all(pend, t)

            def conv_state_proj(t, rows):
                cpr, bcpr = self.pair(0), self.bpair(0)
                tc_ = slice(t * 128, (t + 1) * 128)
                for k in range(8):
                    self.mm(cpr[:, 0:512], self.XT[:, k, tc_], wx[:, k, :], k == 0, k == 7, [bwx, self.bxt[t]], [bcpr[0]])
                for k in range(8):
                    self.mm(cpr[:, 512:768], self.XT[:, k, tc_], wBC[:, k, :], k == 0, k == 7, [bwBC, self.bxt[t]], [bcpr[1]])
                self.evac(cst_[rows, :], cpr[rows, 0:768], bcpr, [bcst], "act")

            osl = ((512 * g, 512, 0), (2048 + 128 * g, 128, 512), (2560 + 128 * g, 128, 640))
            for bi, (t0, t1) in enumerate(BLKS[:4]):
                c0 = t0 * 128
                xtb = self.bxt[t0:t1]
                psd, bpsd = self.psf[0:8, 0, :], self.bpf[0]
                for k in range(8):
                    self.mm(psd, wdt[:, k, :], self.XT[:, k, c0:c0 + 512], k == 0, k == 7, [bwdt] + xtb, [bpsd])
                dt_path(psd, bpsd, 512, 128, scan_p)
                for cc in range(6):
                    bi_ = 2 + cc % 4
                    s_ = cc % 2
                    ps, bps = self.psf[:, bi_, :], self.bpf[bi_]
                    for k in range(8):
                        self.mm(ps, wsl[cc][:, k, :], self.XT[:, k, c0:c0 + 512], k == 0, k == 7, [wsb[cc]] + xtb, [bps])
                    self.op("dve", "tensor_copy", [bhist], [brawc[s_]], out=rawc[:, s_, 0:3], in_=hist[:, cc, :])
                    self.evac(rawc[:, s_, 3:515], ps, [bps], [brawc[s_]], "act")
                    self.op("dve", "tensor_copy", [brawc[s_]], [bhist], out=hist[:, cc, :], in_=rawc[:, s_, 512:515])
                    conv_chunk(cc, [rawc[:, s_, jj:jj + 512] for jj in range(4)], brawc[s_], xcT[:, cc, :], None)
                colsl = [slice(tl * 128, (tl + 1) * 128) for tl in range(4)]
                pbt0 = ph1_pe(t0, colsl[0], negm_p, bnp)
                ph1_early(t0, pbt0)
                ph1_late(t0)
                for t in range(t0, t1):
                    tl = t - t0
                    cols = colsl[tl]

                    def yi_fn(p5, bp5, cols=cols):
                        self.mm(p5, xcT[:, 5, cols], hTb, True, True, [bxcT, bhTb], [bp5])

                    def state_fn(stage, tl=tl):
                        p3, bp3 = self.psf[:, 3, :], self.bpf[3]
                        if stage == 0:
                            self.op("dve", "tensor_tensor", [bxs, btmd], [bxd], out=v8(xd), in0=v8(xs[:, 0:512]),
                                    in1=tmd[:, 16:24].unsqueeze(2).to_broadcast([128, 8, 64]), op=ALU.mult)
                            self.mm(p3, xs[:, 512:640], xd, True, True, [bxs, bxd], [bp3])
                            self.op("dve", "tensor_scalar", [bel, self.bconst], [bdg], out=dg[:, 0:8], in0=self.identf[0:8, 0:8],
                                    scalar1=el[:, tl:tl + 1], scalar2=None, op0=ALU.mult)
                            p2, bp2 = self.psf[:, 2, :], self.bpf[2]
                            self.mm(p2[:, 256:264], self.onesf[0:8, :], dg[:, 0:8], False, True, [self.bconst, bdg], [bp2])
                            self.evac(dbs[:, 0:8], p2[:, 256:264], [bp2], [bdbs], "dve")
                        else:
                            self.op("dve", "tensor_tensor", [bhT, bdbs], [bhT], out=v8(hT), in0=v8(hT),
                                    in1=dbs[:, 0:8].unsqueeze(2).to_broadcast([128, 8, 64]), op=ALU.mult)
                            self.op("dve", "tensor_tensor", [bhT, bp3], [bhT], out=hT, in0=hT, in1=p3, op=ALU.add)
                            self.act(hTb, hT, AF.Copy, [bhT], [bhTb])

                    hoist = None
                    if t + 1 < t1:
                        hoist = (lambda t=t, tl=tl: ph1_pe(t + 1, colsl[tl + 1], negm_p, bnp),
                                 lambda pbt, t=t: ph1_early(t + 1, pbt),
                                 lambda t=t: ph1_late(t + 1))
                    ph2(t, yi_fn, state_fn, hoist)
                if bi == 3:
                    conv_state_proj(15, slice(96, 128))
                    for (o0, w_, s0) in osl:
                        self.dma("sp", d["nconv_p"][:, o0:o0 + w_], cst_[125:128, s0:s0 + w_], [bcst], [])
            p3, bp3 = self.psf[:, 3, :], self.bpf[3]
            pv = p3.rearrange("p (j n) -> p j n", j=4)
            for jj in range(4):
                self.tr(pv[:, jj, :], hT[:, jj * 128:(jj + 1) * 128], self.identf[:], [bhT, self.bconst], [bp3], inc=(jj == 3))
            self.evac(hout, pv, [bp3], [bhout], "act")
            self.dma("sp", d["nssm_p"][512 * g:512 * g + 512, :].rearrange("(j p) n -> p j n", p=128), hout, [bhout], [])

            self.release(mP)
            rawp, brawp = self.carve("rawp", [128, 6, 16, 11], F32)
            decs, bdecs = self.carve("decs", [128, 16, 4], F32)
            h0f, bh0f = self.carve("h0f", [128, 4, 128], F32)
            h0T, bh0T = self.carve("h0T", [128, 512], BF16)
            hnw, bhnw = self.carve("hnw", [128, 4, 128], F32)
            xdm, bxdm = self.carve("xdm", [128, 512], BF16)
            dcb, bdcb = self.carve("dcb", [128, 8], F32)
            dcb1, bdcb1 = self.carve("dcb1", [128, 8], F32)
            bm16, bbm16 = self.carve("bm16", [128, 16, 128], BF16)
            self.dma("sp", bm16, d["c_bm16"], [], [bbm16])
            scv = cwk[0:48, 0:768]
            scs = d["state_conv"]
            for (o0, w_, s0) in osl:
                self.dma("sp", scv[:, s0:s0 + w_], scs[:, :, o0:o0 + w_].rearrange("b r c -> (b r) c"), [], bcwk)
            p2, bp2 = self.psf[:, 2, :], self.bpf[2]
            pvh = p2[:, 0:288].rearrange("p (c n) -> p c n", c=6)
            for cc in range(6):
                self.tr(pvh[:, cc, :], scv[:, cc * 128:(cc + 1) * 128], self.identf[0:48, 0:48], bcwk + [self.bconst], [bp2],
                        inc=(cc == 5))
            self.evac(rawp[:, :, :, 0:3], pvh.rearrange("p c (b r) -> p c b r", r=3), [bp2], [brawp], "dve")
            xtt = [self.bxt[16]]
            for cc in range(6):
                bi_ = 3 + cc % 3
                ps, bps = self.psf[:, bi_, 0:128], self.bpf[bi_]
                for k in range(8):
                    self.mm(ps, wsl[cc][:, k, :], self.XT[:, k, 2048:2176], k == 0, k == 7, [wsb[cc]] + xtt, [bps])
                self.evac(rawp[:, cc, :, 3:11], ps.rearrange("p (b t) -> p b t", t=8), [bps], [brawp], "act")
            psd, bpsd = self.psf[0:8, 0, 0:128], self.bpf[0]
            for k in range(8):
                self.mm(psd, wdt[:, k, :], self.XT[:, k, 2048:2176], k == 0, k == 7, [bwdt] + xtt, [bpsd])
            dt_path(psd, bpsd, 128, 8, scan_s)
            for cc in range(6):
                conv_chunk(cc, [rawp[:, cc, :, jj:jj + 8] for jj in range(4)], brawp,
                           xcT[:, cc, 0:128].rearrange("p (b t) -> p b t", t=8), (16, 8))
            conv_state_proj(16, slice(0, 128))
            for b in range(16):
                for (o0, w_, s0) in osl:
                    self.dma("sp", d["nconv_s"][b, :, o0:o0 + w_], cst_[8 * b + 5:8 * b + 8, s0:s0 + w_], [bcst], [])
            self.op("dve", "tensor_tensor", [bel, bc8], [bdg], out=dg.rearrange("p (b j) -> p b j", b=16), in0=selj,
                    in1=el[:, 0:16].unsqueeze(2).to_broadcast([8, 16, 4]), op=ALU.mult)
            self.mm(p2[:, 320:384], apar, dg, False, True, [bc8, bdg], [bp2])
            self.evac(decs.rearrange("p b j -> p (b j)"), p2[:, 320:384], [bp2], [bdecs], "dve")
            cols = slice(0, 128)
            ssrc = d["state_ssm"]
            v8 = lambda a_: a_.rearrange("p (r e) -> p r e", r=8)

            def alias_buf(name, src):
                m = dict(src.r)
                if src.w is not None and src.w[1] > m.get(src.w[0], 0):
                    m[src.w[0]] = src.w[1]
                nb_ = Buf(name, m)
                self.abufs.append((self.aoff, nb_))
                return nb_

            def yi_s(p5, bp5):
                self.op("dve", "tensor_tensor", [bxcT, bbm16], [bCM], out=CM,
                        in0=xcT[:, 5, 0:128].unsqueeze(1).to_broadcast([128, 16, 128]), in1=bm16, op=ALU.mult)
                rflat = rawp.rearrange("p c b t -> p (c b t)")
                bflat = bm16.rearrange("p b n -> p (b n)")
                h0f_ = [h0f, rflat[:, 0:512].rearrange("p (j n) -> p j n", j=4)]
                hnw_ = [hnw, rflat[:, 512:1024].rearrange("p (j n) -> p j n", j=4)]
                h0T_ = [h0T, bflat[:, 0:512]]
                xdm_ = [xdm, bflat[:, 512:1024]]
                dcb_ = [dcb, dcb1]
                bh0f_ = [bh0f, alias_buf("h0f1", brawp)]
                bhnw_ = [bhnw, alias_buf("hnw1", brawp)]
                bh0T_ = [bh0T, alias_buf("h0T1", bbm16)]
                bxdm_ = [bxdm, alias_buf("xdm1", bbm16)]
                bdcb_ = [bdcb, bdcb1]
                for b in range(16):
                    s_ = b % 2
                    self.dma("pool", h0f_[s_], ssrc[b, 512 * g:512 * g + 512, :].rearrange("(j p) n -> p j n", p=128), [],
                             [bh0f_[s_]])
                    p3, bp3 = self.psf[:, 3, :], self.bpf[3]
                    pv3 = p3.rearrange("p (j n) -> p j n", j=4)
                    for jj in range(4):
                        self.tr(pv3[:, jj, :], h0f_[s_][:, jj, :], self.identf[:], [bh0f_[s_], self.bconst], [bp3], inc=(jj == 3))
                    self.evac(h0T_[s_], p3, [bp3], [bh0T_[s_]], "act")
                    self.mm(p5, CM[:, b, :], h0T_[s_], b == 0, b == 15, [bCM, bh0T_[s_]], [bp5], inc=True)
                    self.op("dve", "tensor_scalar", [btmd, bseqm], [bdcb_[s_]], out=dcb_[s_], in0=tmd[:, 16:24],
                            scalar1=seqm[:, b:b + 1], scalar2=None, op0=ALU.mult)
                    self.op("dve", "tensor_tensor", [bxs, bdcb_[s_]], [bxdm_[s_]], out=v8(xdm_[s_]), in0=v8(xs[:, 0:512]),
                            in1=dcb_[s_].unsqueeze(2).to_broadcast([128, 8, 64]), op=ALU.mult)
                    p1, bp1 = self.psf[:, 1, :], self.bpf[1]
                    pv1 = p1.rearrange("p (j n) -> p j n", j=4)
                    for jj in range(4):
                        self.mm(pv1[:, jj, :], xdm_[s_][:, jj * 128:(jj + 1) * 128], xs[:, 512:640], jj == 0, jj == 3,
                                [bxdm_[s_], bxs], [bp1], inc=(jj == 3))
                    self.op("dve", "tensor_tensor", [bh0f_[s_], bdecs], [bhnw_[s_]], out=hnw_[s_], in0=h0f_[s_],
                            in1=decs[:, b, :].unsqueeze(2).to_broadcast([128, 4, 128]), op=ALU.mult)
                    self.op("dve", "tensor_tensor", [bhnw_[s_], bp1], [bhnw_[s_]], out=hnw_[s_], in0=hnw_[s_], in1=pv1, op=ALU.add)
                    self.dma("sp", d["nssm_s"][b, 512 * g:512 * g + 512, :].rearrange("(j p) n -> p j n", p=128), hnw_[s_],
                             [bhnw_[s_]], [])

            pbt = ph1_pe(16, cols, negm_s, bns)
            ph1_early(16, pbt)
            ph1_late(16)
            ph2(16, yi_s, None, None)
        if pend is not None:
            self.make_xt(*pend)


def build_nc(stop_after=None):
    nc = bass.Bass("TRN2", target_bir_lowering=False)
    dr = {}

    def din(name, shape, dt=F32):
        dr[name] = nc.dram_tensor(name, list(shape), dt, kind="ExternalInput").ap()

    def dout(name, shape):
        dr[name] = nc.dram_tensor(name, list(shape), F32, kind="ExternalOutput").ap()

    din("x_p", [SEQ, D]); din("x_s", [128, D])
    din("cache_k", [2, 16, 128, 256]); din("cache_v", [2, 16, 128, 256])
    din("state_conv", [16, 3, 3072]); din("state_ssm", [16, 2048, 128]); din("state_pool", [16, 15, D])
    din("rel_bias", [32, 16]); din("attn_w_qkv", [2, D, 1536]); din("attn_b_qkv", [2, 1536])
    din("attn_w_o", [2, D, D]); din("attn_b_o", [2, D]); din("attn_sinks", [2, 16])
    din("ssm_w_in", [1, D, 5152]); din("ssm_conv_w", [1, 4, 3072]); din("ssm_conv_b", [1, 3072])
    din("ssm_dt_bias", [1, 32]); din("ssm_a_log", [1, 32]); din("ssm_d", [1, 32]); din("ssm_norm_w", [1, 2048])
    din("ssm_w_out", [1, 2048, D]); din("pool_w", [1, 4, 256, 256]); din("pool_scale", [1, D])
    din("ffn_w_gate", [4, D, DFF]); din("ffn_w_up", [4, D, DFF]); din("ffn_w_down", [4, DFF, D])
    din("ln_g", [4, 2, D]); din("ln_b", [4, 2, D])
    for k, v in host_constants().items():
        din(k, v.shape, BF16 if v.dtype == ml_dtypes.bfloat16 else F32)
    dout("y_p", [SEQ, D]); dout("y_s", [128, D])
    dout("nk_p", [2, 128, 256]); dout("nv_p", [2, 128, 256]); dout("nconv_p", [3, 3072]); dout("nssm_p", [2048, 128])
    dout("npool_p", [15, D])
    dout("nk_s", [2, 16, 128, 256]); dout("nv_s", [2, 16, 128, 256]); dout("nconv_s", [16, 3, 3072])
    dout("nssm_s", [16, 2048, 128]); dout("npool_s", [16, 15, D])
    dr["gtab"] = nc.dram_tensor("gtab", [16, 384], F32, kind="Internal").ap()

    with ExitStack() as es:
        S = Sched(nc, es)
        kb = KB(nc, S, es, dr, stop_after)
        kb.load_consts()
        kb.build_bias_tables()
        kb.load_x()
        nl = DEPTH if stop_after is None else stop_after
        import os
        skip = os.environ.get("KSKIP", "")
        for l in range(nl):
            kind = l % 3
            if "mix" in skip:
                pass
            elif kind == 0:
                kb.attn(l // 3, l)
            elif kind == 1:
                kb.ssm(l)
            else:
                kb.pool(l)
            if "ffn" not in skip:
                kb.ffn(l, final=(l == nl - 1))
        allb = [b for b in _all_bufs(kb)]
        S.wait_all("sp", allb)
        S.emit()
    return nc


def _all_bufs(kb):
    out = list(kb.bxr) + list(kb.bxt) + [kb.blng, kb.blnb, kb.bconst, kb.bd2d] + kb.bxb16 + kb.bstat + kb.bpf + kb.bpb
    out += [b for _, b in kb.abufs]
    dummy = Buf("retired", kb.retired)
    out.append(dummy)
    return out


_NC_CACHE = {}


def shard_inputs(inputs):
    consts = host_constants()
    maps = []
    f = lambda a: np.ascontiguousarray(np.asarray(a, dtype=np.float32))
    shared = {k: f(inputs[k]) for k in ("rel_bias", "attn_w_qkv", "attn_b_qkv", "attn_w_o", "attn_b_o", "attn_sinks", "ssm_w_in",
                                        "ssm_conv_w", "ssm_conv_b", "ssm_dt_bias", "ssm_a_log", "ssm_d", "ssm_norm_w", "ssm_w_out",
                                        "pool_w", "pool_scale", "ffn_w_gate", "ffn_w_up", "ffn_w_down", "ln_g", "ln_b")}
    for c in range(NCORES):
        sl = slice(16 * c, 16 * c + 16)
        m = dict(shared)
        m.update(consts)
        m["x_p"] = f(inputs["x_prompt"][c])
        m["x_s"] = f(inputs["x_sample"][sl]).reshape(128, D)
        m["cache_k"] = f(inputs["cache_k"][:, sl]).reshape(2, 16, 128, 256)
        m["cache_v"] = f(inputs["cache_v"][:, sl]).reshape(2, 16, 128, 256)
        m["state_conv"] = f(inputs["state_conv"][0, sl])
        m["state_ssm"] = f(inputs["state_ssm"][0, sl]).reshape(16, 2048, 128)
        m["state_pool"] = f(inputs["state_pool"][0, sl])
        maps.append(m)
    return maps


def gather_outputs(res):
    R = res
    cat = lambda k: np.stack([r[k] for r in R], 0)
    y_p = cat("y_p")
    y_s = cat("y_s").reshape(128, 8, D)
    nk_p = cat("nk_p").transpose(1, 0, 2, 3).reshape(2, 8, 128, 4, 64)
    nv_p = cat("nv_p").transpose(1, 0, 2, 3).reshape(2, 8, 128, 4, 64)
    nconv_p = cat("nconv_p")[None]
    nssm_p = cat("nssm_p").reshape(1, 8, 32, 64, 128)
    npool_p = cat("npool_p")[None]
    nk_s = np.concatenate([r["nk_s"] for r in R], 1).reshape(2, 128, 128, 4, 64)
    nv_s = np.concatenate([r["nv_s"] for r in R], 1).reshape(2, 128, 128, 4, 64)
    nconv_s = np.concatenate([r["nconv_s"] for r in R], 0)[None]
    nssm_s = np.concatenate([r["nssm_s"] for r in R], 0).reshape(1, 128, 32, 64, 128)
    npool_s = np.concatenate([r["npool_s"] for r in R], 0)[None]
    outs = (y_p, y_s, nk_p, nv_p, nconv_p, nssm_p, npool_p, nk_s, nv_s, nconv_s, nssm_s, npool_s)
    return tuple(np.ascontiguousarray(o, dtype=np.float32) for o in outs)


def kernel(**inputs):
    if "nc" not in _NC_CACHE:
        _NC_CACHE["nc"] = build_nc()
    nc = _NC_CACHE["nc"]
    maps = shard_inputs(inputs)
    res = run_bass_kernel_spmd(nc, maps, core_ids=list(range(NCORES)))
    return gather_outputs(res.results)
```

```python
import math
from contextlib import ExitStack

import ml_dtypes
import numpy as np

import concourse.bass as bass
import concourse.mybir as mybir
from concourse.bass_utils import run_bass_kernel_spmd

F32 = mybir.dt.float32
BF16 = mybir.dt.bfloat16
AF = mybir.ActivationFunctionType
ALU = mybir.AluOpType
AX = mybir.AxisListType

NCORES = 8
D = 1024
SEQ = 2048
NT = 17
NTOK = NT * 128
DEPTH = 4
DFF = 2816
ALPHA = (2 * DEPTH) ** 0.25
LN_EPS = 1e-5
RMS_EPS = 1e-5
NEG = -30000.0
BLKS = [(0, 4), (4, 8), (8, 12), (12, 16), (16, 17)]
ARENA_F32 = 23552
QHEADS = [(0, 4), (1, 5), (2, 6), (3, 7), (8, 12), (9, 13), (10, 14), (11, 15)]
POOL_W = (2, 4, 8, 16)


class Buf:
    __slots__ = ("name", "w", "r", "dsem", "excl")

    def __init__(self, name, r=None, excl=False):
        self.name = name
        self.excl = excl
        self.w = None
        self.r = dict(r) if r else {}
        self.dsem = None


class Sched:
    ENG = ("pe", "act", "dve", "pool", "sp")

    def __init__(self, nc, es):
        self.nc = nc
        self.es = es
        self.sems = {}
        self.cnt = {}
        self.seen = {e: {} for e in self.ENG}
        self.ops = {e: [] for e in self.ENG}
        for e in ("pe", "act", "dve", "pool"):
            self._newsem(e)
        self.nd = 0
        self.free_dsems = []

    def _newsem(self, key):
        self.sems[key] = self.es.enter_context(self.nc.semaphore(f"s_{key}"))
        self.cnt[key] = 0

    def _deps(self, eng, reads, writes):
        need = {}

        def add(k, v):
            if v > need.get(k, 0):
                need[k] = v
        for b in reads:
            if b.w is not None:
                add(*b.w)
            if b.excl:
                for k, v in b.r.items():
                    if k != eng:
                        add(k, v)
        for b in writes:
            if b.w is not None and b.w[0] != eng:
                add(*b.w)
            for k, v in b.r.items():
                if k != eng:
                    add(k, v)
        waits = []
        seen = self.seen[eng]
        for k, v in need.items():
            if seen.get(k, 0) < v:
                seen[k] = v
                waits.append((k, v))
        return waits

    def op(self, eng, fn, reads=(), writes=(), inc=True):
        waits = self._deps(eng, reads, writes)
        val = self.cnt[eng] + 1
        if inc:
            self.cnt[eng] = val
        for b in reads:
            b.r[eng] = val
        for b in writes:
            b.w = (eng, val)
            b.r = {}
        self.ops[eng].append((waits, fn, (eng, 1) if inc else None))

    def dma(self, q, fn, reads=(), writes=(), own=None):
        waits = self._deps(q, reads, writes)
        own = own or (writes[0] if writes else reads[0])
        if own.dsem is None:
            if self.free_dsems:
                own.dsem = self.free_dsems.pop()
                k0 = own.dsem
                if self.seen[q].get(k0, 0) < self.cnt[k0]:
                    self.seen[q][k0] = self.cnt[k0]
                    waits.append((k0, self.cnt[k0]))
            else:
                own.dsem = f"d{self.nd}"
                self.nd += 1
                self._newsem(own.dsem)
        k = own.dsem
        self.cnt[k] += 16
        val = self.cnt[k]
        for b in reads:
            b.r[k] = val
        for b in writes:
            b.w = (k, val)
            b.r = {}
        self.ops[q].append((waits, fn, (k, 16)))

    def wait_all(self, eng, bufs):
        waits = self._deps(eng, (), bufs)
        self.ops[eng].append((waits, None, None))

    def emit(self):
        with self.nc.Block() as block:
            def run(name):
                def body(e):
                    for waits, fn, inc in self.ops[name]:
                        for k, v in waits:
                            e.wait_ge(self.sems[k], v)
                        if fn is not None:
                            ins = fn(e)
                            if inc is not None:
                                ins.then_inc(self.sems[inc[0]], inc[1])
                return body
            block.sync(run("sp"))
            block.scalar(run("act"))
            block.vector(run("dve"))
            block.gpsimd(run("pool"))
            block.tensor(run("pe"))


def _t5_bucket(n):
    n = np.maximum(n, 0)
    nf = np.maximum(n, 1).astype(np.float32)
    large = 16 + (np.log(nf / np.float32(16)) / np.float32(math.log(8.0)) * np.float32(16)).astype(np.int32)
    large = np.minimum(large, 31)
    return np.where(n < 16, n, large)


def host_constants():
    bf = ml_dtypes.bfloat16
    c = {}
    c["c_identb"] = np.eye(128, dtype=np.float32).astype(bf)
    c["c_identf"] = np.eye(128, dtype=np.float32)
    c["c_antib"] = np.eye(128, dtype=np.float32)[::-1].copy().astype(bf)
    c["c_onesf"] = np.ones((128, 128), np.float32)
    i = np.arange(384)
    dist = 255 - i
    valid = (dist >= 0) & (dist < 128)
    oh = np.zeros((33, 384), np.float32)
    bk = _t5_bucket(dist)
    for ii in range(384):
        if valid[ii]:
            oh[bk[ii], ii] = 1.0
        else:
            oh[32, ii] = NEG
    c["c_onehot"] = oh
    p = np.arange(128)
    sels = np.zeros((128, 128), np.float32)
    sels[127 - (p % 8), p] = 1.0
    c["c_sels"] = sels.astype(bf)
    negb = np.where((p[:, None] // 8) == (p[None, :] // 8), 0.0, NEG).astype(np.float32)
    c["c_negb"] = negb.astype(bf)
    bm16 = np.zeros((128, 16, 128), np.float32)
    for b in range(16):
        bm16[:, b, 8 * b:8 * b + 8] = 1.0
    c["c_bm16"] = bm16.astype(bf)
    s_ = p[:, None]
    t_ = p[None, :]
    negm_p = np.where(t_ < s_, NEG, 0.0).astype(np.float32)
    negm_s = np.where((t_ >= s_) & (t_ // 8 == s_ // 8), 0.0, NEG).astype(np.float32)
    c["c_negm_p"] = np.tile(negm_p[:, None, :], (1, 8, 1)).reshape(128, 1024).astype(bf)
    c["c_negm_s"] = np.tile(negm_s[:, None, :], (1, 8, 1)).reshape(128, 1024).astype(bf)
    sel8 = np.zeros((8, 8, 128), np.float32)
    for r in range(8):
        sel8[r, r, :] = 1.0
    c["c_sel8"] = sel8
    sm_p = np.ones((8, 512), np.float32)
    sm_p[:, ::128] = 0.0
    sm_s = np.ones((8, 128), np.float32)
    sm_s[:, ::8] = 0.0
    c["c_scan_p"] = sm_p
    c["c_scan_s"] = sm_s
    apar = np.zeros((8, 128), np.float32)
    for k in range(8):
        apar[k, (k % 2) * 64:(k % 2) * 64 + 64] = 1.0
    c["c_apar"] = apar
    selj = np.zeros((8, 16, 4), np.float32)
    for k in range(8):
        selj[k, :, k // 2] = 1.0
    c["c_selj"] = selj
    seqm = np.zeros((128, 16), np.float32)
    seqm[p, p // 8] = 1.0
    c["c_seqm"] = seqm
    mt = np.zeros((24, 128, 128), np.float32)
    for g, w in enumerate(POOL_W):
        for t in range(128):
            for s in range(t - w + 1, t + 1):
                if s >= 0:
                    mt[4 + g, s, t] += 1.0 / w
                else:
                    mt[0 + g, s + 128, t] += 1.0 / w
            mt[4 + g, t, t] -= 1.0
            cnt = min(t + 1, w)
            for s in range(max(0, t - w + 1), t + 1):
                mt[8 + g, s, t] += 1.0 / cnt
            mt[8 + g, t, t] -= 1.0
            b, tt = t // 8, t % 8
            for pos in range(tt - w + 1, tt + 1):
                if pos >= 0:
                    mt[12 + g, b * 8 + pos, t] += 1.0 / w
                else:
                    j = pos + 15
                    if b < 8:
                        mt[16 + g, b * 15 + j, t] += 1.0 / w
                    else:
                        mt[20 + g, (b - 8) * 15 + j, t] += 1.0 / w
            mt[12 + g, t, t] -= 1.0
    c["c_poolmt"] = np.ascontiguousarray(mt.transpose(1, 0, 2))
    return c


class KB:
    def __init__(self, nc, S, es, dr, stop_after=None):
        self.nc, self.S, self.es, self.dr = nc, S, es, dr
        self.stop_after = stop_after
        sb = lambda name, shape, dt: es.enter_context(nc.sbuf_tensor(name, shape, dt))
        self.XR = sb("XR", [128, NT, D], F32)
        self.XT = sb("XT", [128, 8, NTOK], BF16)
        self.bxr = [Buf(f"xr{t}") for t in range(NT)]
        self.bxt = [Buf(f"xt{t}") for t in range(NT)]
        self.lng = sb("lng", [128, D], F32)
        self.lnb = sb("lnb", [128, D], F32)
        self.blng, self.blnb = Buf("lng"), Buf("lnb")
        self.xb16 = sb("xb16", [128, 2, D], BF16)
        self.bxb16 = [Buf("xb16_0"), Buf("xb16_1")]
        self.xbi = 0
        self.identb = sb("identb", [128, 128], BF16)
        self.identf = sb("identf", [128, 128], F32)
        self.onesf = sb("onesf", [128, 128], F32)
        self.mhalf = sb("mhalf", [128, 1], F32)
        self.bconst = Buf("const")
        self.stat = sb("stat", [128, 4, 16], F32)
        self.bstat = [Buf(f"stat{i}") for i in range(4)]
        self.sti = 0
        self.arena = sb("arena", [128, ARENA_F32], F32)
        self.aoff = 0
        self.abufs = []
        self.retired = {}
        self.rlist = []
        self.brange = {}
        self.last_range = (0, 0)
        self.psf = es.enter_context(nc.psum_tensor("psf", [128, 6, 512], F32))
        self.psb = es.enter_context(nc.psum_tensor("psb", [128, 2, 1024], BF16))
        self.bpf = [Buf(f"pf{i}", excl=True) for i in range(6)]
        self.bpb = [Buf("pb0", excl=True), Buf("pb1", excl=True)]
        self.pbi = 0
        self.bd2d = Buf("d2d")
        self.outbufs = [self.bd2d]
        self.evi = 0

    def op(self, eng, name, R, W, inc=True, **kw):
        self.S.op(eng, lambda e, n=name, kw=kw: getattr(e, n)(**kw), R, W, inc)

    def mm(self, out, lhsT, rhs, start, stop, R, W, inc=None):
        if inc is None:
            inc = stop
        self.S.op("pe", lambda e: e.matmul(out, lhsT=lhsT, rhs=rhs, start=start, stop=stop), R, W, inc)

    def tr(self, out, in_, ident, R, W, inc):
        self.S.op("pe", lambda e: e.transpose(out=out, in_=in_, identity=ident), R, W, inc)

    def act(self, out, in_, func, R, W, **kw):
        self.S.op("act", lambda e: e.activation(out=out, in_=in_, func=func, **kw), R, W)

    def dma(self, q, out, in_, R, W, own=None, **kw):
        self.S.dma(q, lambda e: e.dma_start(out=out, in_=in_, **kw), R, W, own)

    def evac(self, out, in_, R, W, eng=None):
        if eng is None:
            eng = ("act", "dve")[self.evi % 2]
            self.evi += 1
        if eng == "act":
            self.act(out, in_, AF.Copy, R, W)
        else:
            self.op(eng, "tensor_copy", R, W, out=out, in_=in_)

    def carve(self, name, shape, dt):
        n = int(np.prod(shape[1:]))
        nb = n * (2 if dt == BF16 else 4)
        n4 = (nb + 3) // 4
        assert self.aoff + n4 <= ARENA_F32, f"arena overflow at {name}: {self.aoff}+{n4}"
        v = self.arena[:, self.aoff:self.aoff + n4]
        self.aoff += n4
        if dt != F32:
            v = v.bitcast(dt)
        v = v[0:shape[0], 0:n]
        if len(shape) == 3:
            v = v.rearrange("p (a b) -> p a b", a=shape[1])
        elif len(shape) == 4:
            v = v.rearrange("p (a b c) -> p a b c", a=shape[1], b=shape[2])
        b = Buf(name, self._inherit(self.aoff - n4, self.aoff))
        self.abufs.append((self.aoff - n4, self.aoff, b))
        self.last_range = (self.aoff - n4, self.aoff)
        self.brange[id(b)] = self.last_range
        return v, b

    def mkbuf(self, name, rng=None, src=None):
        lo, hi = rng or self.last_range
        m = self._inherit(lo, hi)
        if src is not None:
            for k, v in list(src.r.items()) + ([src.w] if src.w else []):
                if v > m.get(k, 0):
                    m[k] = v
        b = Buf(name, m)
        self.abufs.append((lo, hi, b))
        self.brange[id(b)] = (lo, hi)
        return b

    def _inherit(self, lo, hi):
        m = {}
        for (a0, a1, mp) in self.rlist:
            if a0 < hi and lo < a1:
                for k, v in mp.items():
                    if v > m.get(k, 0):
                        m[k] = v
        return m

    def mark(self):
        return self.aoff

    def release(self, mark=0):
        keep = []
        for ent in self.abufs:
            if len(ent) == 2:
                off, b = ent
                a0, a1 = 0, ARENA_F32
            else:
                a0, a1, b = ent
                off = a1
            if off > mark:
                mp = dict(b.r)
                if b.w is not None and b.w[1] > mp.get(b.w[0], 0):
                    mp[b.w[0]] = b.w[1]
                for k, v in mp.items():
                    if v > self.retired.get(k, 0):
                        self.retired[k] = v
                if mp:
                    self.rlist.append((a0, a1, mp))
                if b.dsem is not None:
                    self.S.free_dsems.append(b.dsem)
                    b.dsem = None
            else:
                keep.append(ent)
        self.abufs = keep
        self.aoff = mark
        if len(self.rlist) > 400:
            merged = {}
            for (_, _, mp) in self.rlist:
                for k, v in mp.items():
                    if v > merged.get(k, 0):
                        merged[k] = v
            self.rlist = [(0, ARENA_F32, merged)]

    def pair(self, i):
        return self.psf[:, 2 * i:2 * i + 2, :].rearrange("p a b -> p (a b)")

    def bpair(self, i):
        return [self.bpf[2 * i], self.bpf[2 * i + 1]]

    def next_pb(self):
        i = self.pbi % 2
        self.pbi += 1
        return self.psb[:, i, :], self.bpb[i]

    def next_stat(self):
        i = self.sti % 4
        self.sti += 1
        return self.stat[:, i, :], self.bstat[i]

    def load_consts(self):
        d = self.dr
        W = [self.bconst]
        self.dma("sp", self.identb[:], d["c_identb"], [], W)
        self.dma("sp", self.identf[:], d["c_identf"], [], W)
        self.dma("sp", self.onesf[:], d["c_onesf"], [], W)
        self.op("pool", "memset", [], W, ap=self.mhalf[:], constant=-0.5)

    def load_x(self):
        d = self.dr
        self.dma("sp", self.XR[:, 0:16, :], d["x_p"].rearrange("(i p) d -> p i d", p=128), [], self.bxr[0:16])
        self.dma("sp", self.XR[:, 16, :], d["x_s"], [], [self.bxr[16]])
        for t in range(NT):
            slot = self.cast_xb(t)
            self.make_xt(t, slot)

    def cast_xb(self, t):
        slot = self.xbi % 2
        self.xbi += 1
        self.act(self.xb16[:, slot, :], self.XR[:, t, :], AF.Copy, [self.bxr[t]], [self.bxb16[slot]])
        return slot

    def make_xt(self, t, slot):
        pb, bpb = self.next_pb()
        pv = pb.rearrange("p (c n) -> p c n", c=8)
        for c in range(8):
            self.tr(pv[:, c, :], self.xb16[:, slot, c * 128:(c + 1) * 128], self.identb[:],
                    [self.bxb16[slot], self.bconst], [bpb], inc=(c == 7))
        self.evac(self.XT[:, :, t * 128:(t + 1) * 128], pv, [bpb], [self.bxt[t]])

    def load_ln(self, l, which):
        d = self.dr
        self.dma("sp", self.lng[:], d["ln_g"][l, which].partition_broadcast(128), [], [self.blng])
        self.dma("sp", self.lnb[:], d["ln_b"][l, which].partition_broadcast(128), [], [self.blnb])

    def ln1(self, t, final=False):
        st, bst = self.next_stat()
        x = self.XR[:, t, :]
        bx = self.bxr[t]
        for i in range(2):
            self.op("dve", "bn_stats", [bx], [bst], out=st[:, 6 * i:6 * i + 6], in_=x[:, i * 512:(i + 1) * 512])
        self.op("dve", "bn_aggr", [bst], [bst], out=st[:, 12:14], in_=st[:, 0:12])
        self.act(st[:, 14:15], st[:, 13:14], AF.Ln, [bst], [bst], bias=LN_EPS, scale=1.0)
        self.act(st[:, 15:16], st[:, 14:15], AF.Exp, [bst], [bst], scale=-0.5)
        self.op("dve", "tensor_scalar", [bx, bst], [bx], out=x, in0=x, scalar1=st[:, 12:13], scalar2=st[:, 15:16],
                op0=ALU.subtract, op1=ALU.mult)
        self.op("dve", "tensor_tensor", [bx, self.blng], [bx], out=x, in0=x, in1=self.lng[:], op=ALU.mult)
        self.op("dve", "tensor_tensor", [bx, self.blnb], [bx], out=x, in0=x, in1=self.lnb[:], op=ALU.add)
        if final:
            d = self.dr
            if t < 16:
                self.dma("sp", d["y_p"][t * 128:(t + 1) * 128, :], x, [bx], [])
            else:
                self.dma("sp", d["y_s"], x, [bx], [])
            return None
        return self.cast_xb(t)

    def ln_all(self, pend, t, final=False):
        slot = self.ln1(t, final)
        if pend is not None:
            self.make_xt(*pend)
        return None if final else (t, slot)

    def ffn(self, l, final):
        d = self.dr
        self.release(0)
        groups = [(i * 512, 512) for i in range(5)] + [(2560, 256)]
        wg, wu, wd, bw = [], [], [], []
        for s in range(2):
            a, b1 = self.carve(f"wg{s}", [128, 8, 512], BF16)
            u, b2 = self.carve(f"wu{s}", [128, 8, 512], BF16)
            dd, b3 = self.carve(f"wd{s}", [128, 4, 1024], BF16)
            wg.append(a), wu.append(u), wd.append(dd), bw.append((b1, b2, b3))
        at, bat, sg, bsg = [], [], [], []
        for s in range(2):
            a, b = self.carve(f"at{s}", [128, 4, 512], BF16)
            at.append(a), bat.append(b)
            a, b = self.carve(f"sg{s}", [128, 512], F32)
            sg.append(a), bsg.append(b)

        def load(gi):
            ff0, fw = groups[gi]
            s = gi % 2
            nfc = fw // 128
            self.dma("pool", wg[s][:, :, 0:fw], d["ffn_w_gate"][l][:, ff0:ff0 + fw].rearrange("(c p) n -> p c n", p=128),
                     [], [bw[s][0]])
            self.dma("pool", wu[s][:, :, 0:fw], d["ffn_w_up"][l][:, ff0:ff0 + fw].rearrange("(c p) n -> p c n", p=128),
                     [], [bw[s][1]])
            self.dma("pool", wd[s][:, 0:nfc, :], d["ffn_w_down"][l][ff0:ff0 + fw, :].rearrange("(c p) n -> p c n", p=128),
                     [], [bw[s][2]])

        load(0)
        load(1)
        self.load_ln(l, 1)
        units = [(gi, bi) for gi in range(len(groups)) for bi in range(len(BLKS))]
        state = {"pg": 0, "pend": None}

        def gu_steps(u):
            gi, bi = units[u]
            ff0, fw = groups[gi]
            s = gi % 2
            t0, t1 = BLKS[bi]
            c0, n = t0 * 128, (t1 - t0) * 128
            a_s = u % 2
            steps = []
            for fc in range(fw // 128):
                def step(fc=fc):
                    pgi = state["pg"] % 2
                    state["pg"] += 1
                    pg, bpg = self.psf[:, pgi, 0:n], self.bpf[pgi]
                    pu, bpu = self.psf[:, 2 + pgi, 0:n], self.bpf[2 + pgi]
                    for k in range(8):
                        self.mm(pg, wg[s][:, k, fc * 128:(fc + 1) * 128], self.XT[:, k, c0:c0 + n], k == 0, k == 7,
                                [bw[s][0]] + self.bxt[t0:t1], [bpg])
                    for k in range(8):
                        self.mm(pu, wu[s][:, k, fc * 128:(fc + 1) * 128], self.XT[:, k, c0:c0 + n], k == 0, k == 7,
                                [bw[s][1]] + self.bxt[t0:t1], [bpu])
                    self.act(sg[pgi][:, 0:n], pg, AF.Silu, [bpg], [bsg[pgi]])
                    self.op("dve", "tensor_tensor", [bsg[pgi], bpu], [bat[a_s]], out=at[a_s][:, fc, 0:n],
                            in0=sg[pgi][:, 0:n], in1=pu, op=ALU.mult)
                steps.append(step)
            return steps

        def d_steps(u):
            gi, bi = units[u]
            ff0, fw = groups[gi]
            s = gi % 2
            nfc = fw // 128
            t0, t1 = BLKS[bi]
            a_s = u % 2
            last = gi == len(groups) - 1
            steps = []
            for t in range(t0, t1):
                def step(t=t):
                    tl = t - t0
                    for half in range(2):
                        pd, bpd = self.psf[:, 4 + half, :], self.bpf[4 + half]
                        for fc in range(nfc):
                            self.mm(pd, at[a_s][:, fc, tl * 128:(tl + 1) * 128], wd[s][:, fc, half * 512:(half + 1) * 512],
                                    fc == 0, fc == nfc - 1, [bat[a_s], bw[s][2]], [bpd])
                        xs_ = self.XR[:, t, half * 512:(half + 1) * 512]
                        if gi == 0:
                            self.op("dve", "scalar_tensor_tensor", [self.bxr[t], bpd], [self.bxr[t]], out=xs_, in0=xs_,
                                    scalar=ALPHA, in1=pd, op0=ALU.mult, op1=ALU.add)
                        else:
                            self.op("dve", "tensor_tensor", [self.bxr[t], bpd], [self.bxr[t]], out=xs_, in0=xs_, in1=pd,
                                    op=ALU.add)
                    if last:
                        state["pend"] = self.ln_all(state["pend"], t, final)
                steps.append(step)
            if bi == len(BLKS) - 1 and gi + 2 < len(groups):
                steps.append(lambda: load(gi + 2))
            return steps

        prev_d = []
        for u in range(len(units)):
            g_ = gu_steps(u)
            m_ = max(len(g_), len(prev_d))
            for i in range(m_):
                if i < len(g_):
                    g_[i]()
                if i < len(prev_d):
                    prev_d[i]()
            prev_d = d_steps(u)
        for st_ in prev_d:
            st_()
        if state["pend"] is not None:
            self.make_xt(*state["pend"])

    def build_bias_tables(self):
        d = self.dr
        m = self.mark()
        rb, brb = self.carve("rb33", [33, 16], F32)
        oh, boh = self.carve("oh33", [33, 384], F32)
        gs, bgs = self.carve("gsb", [16, 384], F32)
        self.dma("sp", rb[0:32, :], d["rel_bias"], [], [brb])
        self.op("pool", "memset", [], [brb], ap=rb[32:33, :], constant=1.0)
        self.dma("sp", oh, d["c_onehot"], [], [boh])
        ps, bps = self.psf[0:16, 0, 0:384], self.bpf[0]
        self.mm(ps, rb, oh, True, True, [brb, boh], [bps])
        self.evac(gs, ps, [bps], [bgs], "act")
        self.bgtab = Buf("gtab")
        self.dma("sp", d["gtab"], gs, [bgs], [self.bgtab], own=bgs)
        self.release(m)

    def build_bm(self, bm, bbm, U, bU, sample, cst):
        bU = bU if isinstance(bU, list) else [bU]
        antib, sels, negb, bcs = cst
        for q4 in range(4):
            pr = self.pair(q4 % 2)
            bpr = self.bpair(q4 % 2)
            pv = pr.rearrange("p (h k) -> p h k", h=4)
            if not sample:
                for hh2 in range(2):
                    self.mm(pr[:, hh2 * 512:(hh2 + 1) * 512], antib,
                            U[:, 4 * q4 + 2 * hh2:4 * q4 + 2 * hh2 + 2, :].rearrange("p h k -> p (h k)"), True, True,
                            bU + [bcs], [bpr[hh2]])
            else:
                for i in range(4):
                    h = 4 * q4 + i
                    first = (i % 2 == 0)
                    self.mm(pv[:, i, 0:128], sels, U[:, h, 0:128], first, False, bU + [bcs], [bpr[i // 2]], inc=False)
                    for b in range(16):
                        self.mm(pv[:, i, 128 + 8 * b:136 + 8 * b], sels, U[:, h, 128:136], False, False, bU + [bcs],
                                [bpr[i // 2]], inc=False)
                    self.mm(pv[:, i, 128:256], self.identb[:], negb, False, True, [bcs, self.bconst], [bpr[i // 2]],
                            inc=True)
            self.evac(bm[:, 4 * q4:4 * q4 + 4, :], pv, bpr, [bbm])

    def attn(self, j, l):
        d = self.dr
        self.release(0)
        wo, bwo = self.carve("wo", [128, 8, 1024], BF16)
        kT, bkT = self.carve("kT", [128, 2, NTOK], BF16)
        Vb, _ = self.carve("Vb", [128, NT, 256], BF16)
        rng_V = self.last_range
        rng_kT = self.brange[id(bkT)]
        bVb = [self.mkbuf(f"Vb{t}", rng_V) for t in range(NT)]
        bkTt = [self.mkbuf(f"kT{t}", rng_kT) for t in range(NT)]
        qTf, bqT = self.carve("qT", [128, 4096], BF16)
        qT = qTf.rearrange("p (c n) -> p c n", c=8)
        bm, bbm = self.carve("bm", [128, 16, 256], BF16)
        sm, bsm = self.carve("attn_small", [128, 64], F32)
        bq, bk, snk, nsnk = sm[:, 0:8], sm[:, 8:10], sm[:, 16:32], sm[:, 32:48]
        brow, bbrow = self.carve("brow", [65, 512], F32)
        cstt, bcs = self.carve("acst", [128, 3, 128], BF16)
        w8f, _ = self.carve("w8", [128, 6144], BF16)
        w8 = w8f[:, 0:4096]
        P2 = [w8f[:, 0:1024].rearrange("p (h k) -> p h k", h=4), w8f[:, 4096:5120].rearrange("p (h k) -> p h k", h=4)]
        PT2 = [w8f[:, 1024:2048].rearrange("p (h a n) -> p h a n", h=4, a=2),
               w8f[:, 5120:6144].rearrange("p (h a n) -> p h a n", h=4, a=2)]
        Ob = w8f[:, 2048:3072].rearrange("p (h e) -> p h e", h=16)
        OT = w8f[:, 3072:4096].rearrange("p (c n) -> p c n", c=8)
        bP2 = [self.mkbuf("P0"), self.mkbuf("P1")]
        bPT2 = [self.mkbuf("PT0"), self.mkbuf("PT1")]
        bOb, bOT = self.mkbuf("Ob"), self.mkbuf("OT")
        bw8 = [bP2[0], bPT2[0], bOb, bOT]
        st2, bst2 = self.carve("ast", [128, 128], F32)
        kvo, bkvo = self.carve("kvo", [128, 512], F32)
        mW = self.mark()
        wq, bwq = self.carve("wq", [128, 8, 8, 128], BF16)
        wkv, bwkv = self.carve("wkv", [128, 8, 512], BF16)
        U, bU = qTf.rearrange("p (h k) -> p h k", h=16), bqT

        wsrc = d["attn_w_qkv"][j]
        for hh in range(2):
            for cg in range(2):
                c0 = cg * 512 + hh * 256
                for ci in range(4):
                    self.dma("pool", wq[:, :, 4 * cg + ci, hh * 64:(hh + 1) * 64],
                             wsrc[:, c0 + 64 * ci:c0 + 64 * ci + 64].rearrange("(k p) n -> p k n", p=128), [], [bwq])
        self.dma("pool", wkv, wsrc[:, 1024:1536].rearrange("(k p) n -> p k n", p=128), [], [bwkv])
        self.dma("pool", wo, d["attn_w_o"][j].rearrange("(k p) n -> p k n", p=128), [], [bwo])
        bsrc = d["attn_b_qkv"][j]
        for hh in range(2):
            for cg in range(2):
                c0 = cg * 512 + hh * 256
                self.dma("sp", bq[hh * 64:(hh + 1) * 64, 4 * cg:4 * cg + 4],
                         bsrc[c0:c0 + 256].rearrange("(c n) -> n c", n=64), [], [bsm], allow_slow_non_contiguous=True)
        self.dma("sp", bk, bsrc[1024:1280].rearrange("(c p) -> p c", p=128), [], [bsm], allow_slow_non_contiguous=True)
        self.dma("sp", snk, d["attn_sinks"][j].partition_broadcast(128), [], [bsm])
        self.dma("sp", brow[0:1, :], bsrc[1024:1536].rearrange("(o n) -> o n", o=1), [], [bbrow])
        self.dma("sp", brow[32:33, :], d["attn_b_o"][j, 0:512].rearrange("(o n) -> o n", o=1), [], [bbrow])
        self.dma("sp", brow[64:65, :], d["attn_b_o"][j, 512:1024].rearrange("(o n) -> o n", o=1), [], [bbrow])
        self.dma("sp", cstt[:, 0, :], d["c_antib"], [], [bcs])
        self.dma("sp", cstt[:, 1, :], d["c_sels"], [], [bcs])
        self.dma("sp", cstt[:, 2, :], d["c_negb"], [], [bcs])
        self.op("dve", "tensor_scalar", [bsm], [bsm], out=bq, in0=bq, scalar1=0.125, scalar2=None, op0=ALU.mult)
        self.op("dve", "tensor_scalar", [bsm], [bsm], out=nsnk, in0=snk, scalar1=-1.0, scalar2=None, op0=ALU.mult)
        gt = d["gtab"]
        self.dma("pool", U, bass.AP(gt.tensor, 0, [[1, 128], [384, 16], [1, 256]]), [self.bgtab], [bU])
        cst = (cstt[:, 0, :], cstt[:, 1, :], cstt[:, 2, :], bcs)
        import os
        dbg = os.environ.get("KATT", "")
        if dbg == "a0":
            return
        self.build_bm(bm, bbm, U, bU, False, cst)
        self.load_ln(l, 0)
        if dbg == "a1":
            return

        pend = None
        pst = {"pri": 0, "smp": None}

        def proj(bi):
            t0, t1 = BLKS[bi]
            pri = pst["pri"]
            c0, n = t0 * 128, (t1 - t0) * 128
            xtb = self.bxt[t0:t1]
            for c in range(2):
                bi_ = pri % 4
                pri += 1
                ps, bps = self.psf[:, bi_, 0:n], self.bpf[bi_]
                for k in range(8):
                    self.mm(ps, wkv[:, k, c * 128:(c + 1) * 128], self.XT[:, k, c0:c0 + n], k == 0, k == 7,
                            [bwkv] + xtb, [bps])
                self.act(kT[:, c, c0:c0 + n], ps, AF.Identity, [bps, bsm], bkTt[t0:t1], bias=bk[:, c:c + 1], scale=1.0)
            for c in range(8):
                bi_ = pri % 4
                pri += 1
                ps, bps = self.psf[:, bi_, 0:n], self.bpf[bi_]
                for k in range(8):
                    self.mm(ps, wq[:, k, c, :], self.XT[:, k, c0:c0 + n], k == 0, k == 7, [bwq] + xtb, [bps])
                self.act(qT[:, c, 0:n], ps, AF.Identity, [bps, bsm], [bqT], bias=bq[:, c:c + 1], scale=0.125)
            for t in range(t0, t1):
                if dbg == "p1":
                    return
                bi_ = pri % 4
                pri += 1
                ps, bps = self.psf[:, bi_, :], self.bpf[bi_]
                self.mm(ps, self.onesf[0:1, :], brow[0:1, :], True, False, [self.bconst, bbrow], [bps], inc=False)
                for k in range(8):
                    self.mm(ps, self.XT[:, k, t * 128:(t + 1) * 128], wkv[:, k, :], False, k == 7, [bwkv, self.bxt[t]], [bps])
                if t < 15:
                    self.evac(Vb[:, t, :], ps[:, 256:512], [bps], [bVb[t]])
                if t >= 15 and dbg != "p2":
                    self.evac(kvo, ps, [bps], [bkvo], "act")
                    self.op("dve", "tensor_copy", [bkvo], [bVb[t]], out=Vb[:, t, :], in_=kvo[:, 256:512])
                    if dbg == "p3a" and t == 16:
                        return
                    if dbg == "p3c":
                        return
                    if dbg == "p3d":
                        if t == 15:
                            self.dma("sp", d["nk_p"][j], kvo[:, 0:256], [bkvo], [])
                        return
                    if dbg == "p3b" and t == 15:
                        return
                    if t == 15:
                        self.dma("sp", d["nk_p"][j], kvo[:, 0:256], [bkvo], [])
                        self.dma("sp", d["nv_p"][j], kvo[:, 256:512], [bkvo], [])
                    else:
                        for b in range(16):
                            self.dma("sp", d["nk_s"][j, b, 120:128, :], kvo[8 * b:8 * b + 8, 0:256], [bkvo], [])
                            self.dma("sp", d["nv_s"][j, b, 120:128, :], kvo[8 * b:8 * b + 8, 256:512], [bkvo], [])
            if dbg in ("p1", "p2", "p3", "p3a", "p3b", "p3c", "p3d"):
                return
            if t0 == 16:
                self.dma("sp", d["nk_s"][j, :, 0:120, :], d["cache_k"][j, :, 8:128, :], [self.bd2d], [], own=self.bd2d)
                self.dma("sp", d["nv_s"][j, :, 0:120, :], d["cache_v"][j, :, 8:128, :], [self.bd2d], [], own=self.bd2d)
                if dbg == "p4":
                    return
                kcr = w8.rearrange("p (b c) -> p b c", b=16)
                self.dma("pool", kcr, bass.AP(gt.tensor, 0, [[1, 128], [384, 16], [1, 256]]), [self.bgtab], bw8)
                self.build_bm(bm, bbm, kcr, bw8, True, cst)
                self.release(mW)
                vc, bvc = self.carve("vc", [128, 16, 256], BF16)
                kcT, bkcT = self.carve("kcT", [128, 2, 16, 128], BF16)
                bm16, bbm16 = self.carve("bm16", [128, 16, 128], BF16)
                QM1, bQM1 = self.carve("QM", [128, 16, 128], BF16)
                QM, bQM = [QM1, QM1], [bQM1, bQM1]
                self.dma("sp", bm16, d["c_bm16"], [], [bbm16])
                self.dma("pool", kcr, d["cache_k"][j].rearrange("b k c -> k b c"), [], bw8)
                self.dma("pool", vc, d["cache_v"][j].rearrange("b k c -> k b c"), [], [bvc])
                for kc in range(2):
                    for b8 in range(2):
                        pb, bpb = self.next_pb()
                        pv = pb.rearrange("p (b n) -> p b n", b=8)
                        for bb in range(8):
                            b = b8 * 8 + bb
                            self.tr(pv[:, bb, :], kcr[:, b, kc * 128:(kc + 1) * 128], self.identb[:], bw8 + [self.bconst],
                                    [bpb], inc=(bb == 7))
                        self.evac(kcT[:, kc, b8 * 8:b8 * 8 + 8, :], pv, [bpb], [bkcT])
                pst["smp"] = (kcT, bkcT, vc, bvc, QM, bQM, bm16, bbm16)
            pst["pri"] = pri

        stg = [st2[:, 20 * hg:20 * hg + 20] for hg in range(4)]
        bstg = [self.mkbuf(f"astg{hg}", self.brange[id(bst2)]) for hg in range(4)]
        den, rden = st2[:, 96:112], st2[:, 112:128]
        Opr, bOpr = self.pair(2), self.bpair(2)

        def geom(t, hg, i):
            h = 4 * hg + i
            c, hh = (h % 4) + 4 * (h // 8), (h // 4) % 2
            return h, c, hh, hg // 2, slice(hh * 64, hh * 64 + 64)

        def stA(t, hg):
            t0 = BLKS[[i_ for i_, (a_, b_) in enumerate(BLKS) if a_ <= t < b_][0]][0]
            tl = t - t0
            sample = t == 16
            Spr, bSpr = self.pair(hg % 2), self.bpair(hg % 2)
            Sv = Spr.rearrange("p (h k) -> p h k", h=4)
            lo = 128 if t == 0 else 0
            for i in range(4):
                h, c, hh, kc, rows = geom(t, hg, i)
                bS = [bSpr[i // 2]]
                first = (i % 2 == 0)
                qcols = qT[rows, c, tl * 128:(tl + 1) * 128]
                if not sample:
                    kb = bkTt[max(t - 1, 0):t + 1]
                    self.mm(Sv[:, i, lo:256], qcols, kT[rows, kc, (t - 1) * 128 + lo:(t + 1) * 128], first, False,
                            [bqT] + kb, bS, inc=False)
                else:
                    kcT, bkcT, vc, bvc, QM, bQM, bm16, bbm16 = pst["smp"]
                    qs = (4 * hg + i) % 2
                    self.op("dve", "tensor_tensor", [bqT, bbm16], [bQM[qs]], out=QM[qs][rows],
                            in0=qcols.unsqueeze(1).to_broadcast([64, 16, 128]), in1=bm16[rows], op=ALU.mult)
                    for b in range(16):
                        self.mm(Sv[:, i, 0:128], QM[qs][rows, b, :], kcT[rows, kc, b, :], first and b == 0, False,
                                [bQM[qs], bkcT], bS, inc=False)
                    self.mm(Sv[:, i, 128:256], qcols, kT[rows, kc, 2048:2176], False, False, [bqT, bkTt[16]], bS,
                            inc=False)
                self.mm(Sv[:, i, lo:256], self.identb[:], bm[:, h, lo:256], False, True, [bbm, self.bconst], bS,
                        inc=True)

        def stB1(t, hg):
            Spr, bSpr = self.pair(hg % 2), self.bpair(hg % 2)
            Sv = Spr.rearrange("p (h k) -> p h k", h=4)
            lo = 128 if t == 0 else 0
            sg_, bsg_ = stg[hg], bstg[hg]
            mx, nmx, rs, t4, es = (sg_[:, 4 * q:4 * q + 4] for q in range(5))
            hs = slice(4 * hg, 4 * hg + 4)
            self.op("dve", "tensor_reduce", bSpr, [bsg_], out=mx, in_=Sv[:, :, lo:256], axis=AX.X, op=ALU.max)
            self.op("dve", "scalar_tensor_tensor", [bsg_, bsm], [bsg_], out=nmx, in0=mx, scalar=-1.0,
                    in1=nsnk[:, hs], op0=ALU.mult, op1=ALU.min)
            self.op("dve", "tensor_tensor", [bsg_, bsm], [bsg_], out=t4, in0=snk[:, hs], in1=nmx, op=ALU.add)

        def stB2(t, hg):
            Spr, bSpr = self.pair(hg % 2), self.bpair(hg % 2)
            Sv = Spr.rearrange("p (h k) -> p h k", h=4)
            lo = 128 if t == 0 else 0
            sg_, bsg_ = stg[hg], bstg[hg]
            mx, nmx, rs, t4, es = (sg_[:, 4 * q:4 * q + 4] for q in range(5))
            Pq, bPq = P2[hg % 2], bP2[hg % 2]
            for i in range(4):
                self.act(Pq[:, i, lo:256], Sv[:, i, lo:256], AF.Exp, [bSpr[i // 2], bsg_], [bPq, bsg_],
                         bias=nmx[:, i:i + 1], scale=1.0, accum_out=rs[:, i:i + 1])
            self.act(es, t4, AF.Exp, [bsg_], [bsg_])

        def stB3(t, hg):
            sg_, bsg_ = stg[hg], bstg[hg]
            mx, nmx, rs, t4, es = (sg_[:, 4 * q:4 * q + 4] for q in range(5))
            hs = slice(4 * hg, 4 * hg + 4)
            self.op("dve", "tensor_tensor", [bsg_], [bst2], out=den[:, hs], in0=rs, in1=es, op=ALU.add)

        def stC(t, hg):
            sample = t == 16
            Pq, bPq = P2[hg % 2], bP2[hg % 2]
            PTq, bPTq = PT2[hg % 2], bPT2[hg % 2]
            pb, bpb = self.next_pb()
            pv = pb.rearrange("p (h a n) -> p h a n", h=4, a=2)
            kts = [1] if t == 0 else [0, 1]
            for i in range(4):
                for kt in kts:
                    self.tr(pv[:, i, kt, :], Pq[:, i, kt * 128:(kt + 1) * 128], self.identb[:], [bPq, self.bconst],
                            [bpb], inc=(i == 3 and kt == 1))
            if t == 0:
                self.evac(PTq[:, :, 1, :], pv[:, :, 1, :], [bpb], [bPTq], "dve")
            else:
                self.evac(PTq, pv, [bpb], [bPTq], "dve")
            for i in range(4):
                h = 4 * hg + i
                bO = [bOpr[h // 8]]
                oc = Opr[:, h * 64:(h + 1) * 64]
                firstb = (h % 8 == 0)
                if not sample:
                    for kt in kts:
                        self.mm(oc, PTq[:, i, kt, :], Vb[:, t - 1 + kt, hg * 64:(hg + 1) * 64],
                                firstb and kt == kts[0], kt == 1, [bPTq, bVb[t - 1 + kt]], bO,
                                inc=(kt == 1 and i == 3))
                else:
                    kcT, bkcT, vc, bvc, QM, bQM, bm16, bbm16 = pst["smp"]
                    qs = (4 * hg + i) % 2
                    self.op("dve", "tensor_tensor", [bPTq, bbm16], [bQM[qs]], out=QM[qs],
                            in0=PTq[:, i, 0, :].unsqueeze(1).to_broadcast([128, 16, 128]), in1=bm16, op=ALU.mult)
                    for b in range(16):
                        self.mm(oc, QM[qs][:, b, :], vc[:, b, hg * 64:(hg + 1) * 64], firstb and b == 0, False,
                                [bQM[qs], bvc], bO, inc=False)
                    self.mm(oc, PTq[:, i, 1, :], Vb[:, 16, hg * 64:(hg + 1) * 64], False, True, [bPTq, bVb[16]], bO,
                            inc=True)

        def stD1(t):
            self.op("dve", "reciprocal", [bst2], [bst2], out=rden, in_=den)
            self.op("dve", "tensor_tensor", bOpr + [bst2], [bOb], out=Ob, in0=Opr.rearrange("p (h e) -> p h e", h=16),
                    in1=rden.unsqueeze(2).to_broadcast([128, 16, 64]), op=ALU.mult)
            pb, bpb = self.next_pb()
            pv = pb.rearrange("p (c n) -> p c n", c=8)
            Obf = Ob.rearrange("p h e -> p (h e)")
            for c in range(8):
                self.tr(pv[:, c, :], Obf[:, c * 128:(c + 1) * 128], self.identb[:], [bOb, self.bconst], [bpb], inc=(c == 7))
            self.evac(OT, pv, [bpb], [bOT], "act")
            for half in range(2):
                yh = Opr[:, half * 512:(half + 1) * 512]
                pr_ = 32 * (half + 1)
                self.mm(yh, self.onesf[pr_:pr_ + 1, :], brow[pr_:pr_ + 1, :], True, False,
                        [self.bconst, bbrow], [bOpr[half]], inc=False)
                for k in range(8):
                    self.mm(yh, OT[:, k, :], wo[:, k, half * 512:(half + 1) * 512], False, k == 7, [bOT, bwo], [bOpr[half]])

        def stD2(t):
            nonlocal pend
            x = self.XR[:, t, :]
            self.op("dve", "scalar_tensor_tensor", [self.bxr[t]] + bOpr, [self.bxr[t]], out=x, in0=x, scalar=ALPHA,
                    in1=Opr, op0=ALU.mult, op1=ALU.add)
            pend = self.ln_all(pend, t)

        done_proj, doneA = set(), set()
        prevD2 = None
        for bi, (t0, t1) in enumerate(BLKS):
            if bi not in done_proj:
                proj(bi)
                done_proj.add(bi)
            for t in range(t0, t1):
                if t not in doneA:
                    stA(t, 0)
                    stA(t, 1)
                stB1(t, 0); stB2(t, 0)
                if prevD2 is not None:
                    stD2(prevD2)
                    prevD2 = None
                stB1(t, 1); stB3(t, 0); stA(t, 2); stC(t, 0); stB2(t, 1)
                stB1(t, 2); stB3(t, 1); stA(t, 3); stC(t, 1); stB2(t, 2)
                stB1(t, 3); stB3(t, 2); stC(t, 2); stB2(t, 3)
                stB3(t, 3); stC(t, 3)
                nxt = t + 1
                if nxt < t1:
                    stA(nxt, 0); stA(nxt, 1); doneA.add(nxt)
                elif bi + 1 < len(BLKS) and BLKS[bi + 1][0] != 16:
                    proj(bi + 1)
                    done_proj.add(bi + 1)
                    stA(nxt, 0); stA(nxt, 1); doneA.add(nxt)
                stD1(t)
                prevD2 = t
        if prevD2 is not None:
            stD2(prevD2)
        if pend is not None:
            self.make_xt(*pend)

    def pool(self, l):
        d = self.dr
        self.release(0)
        mt, bmt = self.carve("poolmt", [128, 24, 128], F32)
        pw, bpw = self.carve("poolw", [128, 4, 2, 256], BF16)
        psc, bpsc = self.carve("poolsc", [128, D], F32)
        pfx, bpfx = self.carve("poolpfx", [128, 2, D], F32)
        dT, bdT = [], []
        for s in range(2):
            a, b = self.carve(f"dT{s}", [128, 8, 128], BF16)
            dT.append(a), bdT.append(b)
        tmp, btmp = self.carve("pooltmp", [128, D], F32)
        self.dma("sp", mt, d["c_poolmt"], [], [bmt])
        self.dma("pool", pw, d["pool_w"][0].rearrange("g (kk p) n -> p g kk n", p=128), [], [bpw])
        self.dma("sp", psc, d["pool_scale"][0].partition_broadcast(128), [], [bpsc])
        self.dma("sp", pfx[0:120, 0, :], d["state_pool"][0:8].rearrange("b r c -> (b r) c"), [], [bpfx])
        self.dma("sp", pfx[0:120, 1, :], d["state_pool"][8:16].rearrange("b r c -> (b r) c"), [], [bpfx])
        self.dma("sp", d["npool_p"], self.XR[113:128, 15, :], [self.bxr[15]], [])
        self.dma("sp", d["npool_s"][:, 0:7, :], d["state_pool"][:, 8:15, :], [self.bd2d], [], own=self.bd2d)
        for b in range(16):
            self.dma("sp", d["npool_s"][b, 7:15, :], self.XR[8 * b:8 * b + 8, 16, :], [self.bxr[16]], [])
        self.load_ln(l, 0)

        def diff(t):
            s = t % 2
            pr, bpr = self.pair(s), self.bpair(s)
            pv = pr.rearrange("p (c n) -> p c n", c=8)
            for c in range(8):
                g = c // 2
                cs = slice(c * 128, (c + 1) * 128)
                first = (c % 4 == 0)
                bb = [bpr[c // 4]]
                if t == 16:
                    self.mm(pv[:, c, :], pfx[0:120, 0, cs], mt[0:120, 16 + g, :], first, False, [bpfx, bmt], bb, inc=False)
                    self.mm(pv[:, c, :], pfx[0:120, 1, cs], mt[0:120, 20 + g, :], False, False, [bpfx, bmt], bb, inc=False)
                    self.mm(pv[:, c, :], self.XR[:, 16, cs], mt[:, 12 + g, :], False, True, [self.bxr[16], bmt], bb, inc=True)
                elif t == 0:
                    self.mm(pv[:, c, :], self.XR[:, 0, cs], mt[:, 8 + g, :], first, True, [self.bxr[0], bmt], bb, inc=True)
                else:
                    self.mm(pv[:, c, :], self.XR[:, t - 1, cs], mt[:, g, :], first, False, [self.bxr[t - 1], bmt], bb, inc=False)
                    self.mm(pv[:, c, :], self.XR[:, t, cs], mt[:, 4 + g, :], False, True, [self.bxr[t], bmt], bb, inc=True)
            self.evac(dT[s], pv, bpr, [bdT[s]])

        pend = [None]

        def update(t):
            s = t % 2
            ypr, bypr = self.pair(2), self.bpair(2)
            for g in range(4):
                for kk in range(2):
                    self.mm(ypr[:, g * 256:(g + 1) * 256], dT[s][:, 2 * g + kk, :], pw[:, g, kk, :], (g % 2 == 0) and kk == 0,
                            kk == 1, [bdT[s], bpw], [bypr[g // 2]], inc=(kk == 1))
            self.op("dve", "tensor_tensor", bypr + [bpsc], [btmp], out=tmp, in0=ypr, in1=psc, op=ALU.mult)
            x = self.XR[:, t, :]
            self.op("dve", "scalar_tensor_tensor", [self.bxr[t], btmp], [self.bxr[t]], out=x, in0=x, scalar=ALPHA, in1=tmp,
                    op0=ALU.mult, op1=ALU.add)
            pend[0] = self.ln_all(pend[0], t)

        for t in range(NT):
            diff(t)
            if t >= 1:
                update(t - 1)
        update(16)
        if pend[0] is not None:
            self.make_xt(*pend[0])

    def ssm(self, l):
        d = self.dr
        self.release(0)
        win = d["ssm_w_in"][0]
        negm_p, bnp = self.carve("negm_p", [128, 1024], BF16)
        negm_s, bns = self.carve("negm_s", [128, 1024], BF16)
        c8, bc8 = self.carve("c8", [8, 8 * 128 + 512 + 128 + 128 + 64], F32)
        sel8 = c8[:, 0:1024].rearrange("p (r n) -> p r n", r=8)
        scan_p, scan_s = c8[:, 1024:1536], c8[:, 1536:1664]
        apar = c8[:, 1664:1792]
        selj = c8[:, 1792:1856].rearrange("p (b j) -> p b j", b=16)
        seqm, bseqm = self.carve("seqm", [128, 16], F32)
        self.dma("sp", negm_p, d["c_negm_p"], [], [bnp])
        self.dma("sp", negm_s, d["c_negm_s"], [], [bns])
        self.dma("sp", sel8, d["c_sel8"], [], [bc8])
        self.dma("sp", scan_p, d["c_scan_p"], [], [bc8])
        self.dma("sp", scan_s, d["c_scan_s"], [], [bc8])
        self.dma("sp", apar, d["c_apar"], [], [bc8])
        self.dma("sp", selj, d["c_selj"], [], [bc8])
        self.dma("sp", seqm, d["c_seqm"], [], [bseqm])
        self.load_ln(l, 0)
        mL = self.mark()
        pend = None
        for g in range(4):
            self.release(mL)
            wx, bwx = self.carve("wx", [128, 8, 512], BF16)
            wz, bwz = self.carve("wz", [128, 8, 512], BF16)
            wBC, bwBC = self.carve("wBC", [128, 8, 256], BF16)
            wdt, bwdt = self.carve("wdt", [128, 8, 8], BF16)
            wout, bwout = self.carve("wout", [128, 4, 1024], BF16)
            cw, bcw = self.carve("convw", [128, 6, 5], F32)
            hp, bhp = self.carve("headp", [8, 4], F32)
            dbc, bdbc = self.carve("dbc", [128, 8], F32)
            nw, bnw = self.carve("nw", [128, 512], F32)
            wr = lambda c0, w_: win[:, c0:c0 + w_].rearrange("(k p) n -> p k n", p=128)
            self.dma("pool", wx, wr(2048 + 512 * g, 512), [], [bwx])
            self.dma("pool", wBC[:, :, 0:128], wr(4096 + 128 * g, 128), [], [bwBC])
            self.dma("pool", wBC[:, :, 128:256], wr(4608 + 128 * g, 128), [], [bwBC])
            self.dma("pool", wdt, wr(5120 + 8 * g, 8), [], [bwdt])
            self.dma("pool", wz, wr(512 * g, 512), [], [bwz])
            self.dma("pool", wout, d["ssm_w_out"][0][512 * g:512 * g + 512, :].rearrange("(k p) n -> p k n", p=128), [], [bwout])
            chbase = [512 * g + 128 * i for i in range(4)] + [2048 + 128 * g, 2560 + 128 * g]
            cwsrc, cbsrc = d["ssm_conv_w"][0], d["ssm_conv_b"][0]
            for cc in range(6):
                cb = chbase[cc]
                self.dma("sp", cw[:, cc, 0:4], cwsrc[:, cb:cb + 128].rearrange("j p -> p j"), [], [bcw],
                         allow_slow_non_contiguous=True)
                self.dma("sp", cw[:, cc, 4:5], cbsrc[cb:cb + 128].rearrange("(p o) -> p o", o=1), [], [bcw])
            self.dma("sp", hp[:, 0:1], d["ssm_dt_bias"][0][8 * g:8 * g + 8].rearrange("(p o) -> p o", o=1), [], [bhp])
            self.dma("sp", hp[:, 1:2], d["ssm_a_log"][0][8 * g:8 * g + 8].rearrange("(p o) -> p o", o=1), [], [bhp])
            self.dma("sp", dbc, d["ssm_d"][0][8 * g:8 * g + 8].partition_broadcast(128), [], [bdbc])
            self.dma("sp", nw, d["ssm_norm_w"][0][512 * g:512 * g + 512].partition_broadcast(128), [], [bnw])
            self.act(hp[:, 2:3], hp[:, 1:2], AF.Exp, [bhp], [bhp])
            self.op("dve", "tensor_scalar", [bhp], [bhp], out=hp[:, 2:3], in0=hp[:, 2:3], scalar1=-1.0, scalar2=None,
                    op0=ALU.mult)
            cwk, _ = self.carve("convwork", [128, 1536], F32)
            acc = [cwk[:, 0:512], cwk[:, 512:1024]]
            ctmp = cwk[:, 1024:1536]
            bacc = [self.mkbuf("acc0"), self.mkbuf("acc1")]
            bctmp = self.mkbuf("ctmp")
            bcwk = bacc + [bctmp]
            xcT, bxcT = self.carve("xcT", [128, 6, 512], BF16)
            dtb_, bdt = self.carve("dtbuf", [8, 3, 512], F32)
            el, bel = self.carve("elast", [8, 16], F32)
            xs, bxs = self.carve("xs", [128, 640], BF16)
            tmd, btmd = self.carve("tmd", [128, 32], F32)
            Ef, bE_ = self.carve("E", [128, 1024], F32)
            E = Ef.rearrange("p (r n) -> p r n", r=8)
            cst_ = Ef[:, 0:768]
            bcst = bE_
            CM = Ef.bitcast(BF16).rearrange("p (b n) -> p b n", b=16)
            bCM = bE_
            WT, bWT = self.carve("WT", [128, 8, 128], BF16)
            xD, bxD = self.carve("xD", [128, 512], BF16)
            sz, bsz = self.carve("sz", [128, 512], F32)
            y1, by1 = self.carve("y1", [128, 512], F32)
            hout, bhout = y1.rearrange("p (j n) -> p j n", j=4), by1
            yn, byn = self.carve("yn", [128, 512], BF16)
            ynT, bynT = self.carve("ynT", [128, 4, 128], BF16)
            dg, bdg = self.carve("dg", [8, 64], F32)
            mP = self.mark()
            rawc, _ = self.carve("rawc", [128, 2, 515], F32)
            brawc = [self.mkbuf("rawc0"), self.mkbuf("rawc1")]
            hist, bhist = self.carve("hist", [128, 6, 3], F32)
            hT, bhT = self.carve("hT", [128, 512], F32)
            hTb, bhTb = self.carve("hTb", [128, 512], BF16)
            xd, bxd = self.carve("xd", [128, 512], BF16)
            dbs, bdbs = self.carve("dbs", [128, 64], F32)
            self.op("pool", "memset", [], [bhT], ap=hT, constant=0.0)
            self.op("pool", "memset", [], [bhTb], ap=hTb, constant=0.0)
            self.op("pool", "memset", [], [bhist], ap=hist, constant=0.0)
            wsl = [wx[:, :, 0:128], wx[:, :, 128:256], wx[:, :, 256:384], wx[:, :, 384:512], wBC[:, :, 0:128], wBC[:, :, 128:256]]
            wsb = [bwx, bwx, bwx, bwx, bwBC, bwBC]
            decT, bEt, acum = (dtb_[:, q, :] for q in range(3))

            def dt_path(psd, bpsd, n, L, scanm):
                nb = n // L
                t0, t1, ac = decT[:, 0:n], bEt[:, 0:n], acum[:, 0:n]
                v3 = lambda a_: a_.rearrange("p (b t) -> p b t", t=L)
                self.act(t0, psd, AF.Exp, [bpsd, bhp], [bdt], bias=hp[:, 0:1], scale=1.0)
                self.act(t0, t0, AF.Ln, [bdt], [bdt], bias=1.0, scale=1.0)
                self.act(t1, t0, AF.Ln, [bdt], [bdt])
                self.op("dve", "tensor_scalar", [bdt, bhp], [bdt], out=t0, in0=t0, scalar1=hp[:, 2:3], scalar2=None,
                        op0=ALU.mult)
                self.op("dve", "tensor_tensor_scan", [bdt, bc8], [bdt], out=ac, data0=scanm, data1=t0, initial=0.0,
                        op0=ALU.mult, op1=ALU.add)
                self.op("dve", "tensor_tensor", [bdt], [bdt], out=t1, in0=t1, in1=ac, op=ALU.subtract)
                self.op("dve", "tensor_tensor", [bdt], [bdt], out=v3(t0), in0=v3(t1),
                        in1=v3(ac)[:, :, L - 1:L].to_broadcast([8, nb, L]), op=ALU.add)
                self.act(t0, t0, AF.Exp, [bdt], [bdt])
                self.act(el[:, 0:nb], v3(ac)[:, :, L - 1], AF.Exp, [bdt], [bel])

            def conv_chunk(cc, src_views, bsrc, out_view, shape):
                s = cc % 2
                a = acc[s]
                av = a if shape is None else a[:, 0:shape[0] * shape[1]].rearrange("p (b t) -> p b t", b=shape[0])
                tv = ctmp if shape is None else ctmp[:, 0:shape[0] * shape[1]].rearrange("p (b t) -> p b t", b=shape[0])
                self.op("dve", "tensor_scalar", [bsrc, bcw], [bacc[s]], out=av, in0=src_views[0], scalar1=cw[:, cc, 0:1],
                        scalar2=cw[:, cc, 4:5], op0=ALU.mult, op1=ALU.add)
                for jj in range(1, 4):
                    self.op("dve", "scalar_tensor_tensor", [bsrc, bcw, bacc[s]], [bacc[s]], out=av, in0=src_views[jj],
                            scalar=cw[:, cc, jj:jj + 1], in1=av, op0=ALU.mult, op1=ALU.add)
                self.act(out_view, av, AF.Silu, [bacc[s]], [bxcT])

            v8 = lambda a_: a_.rearrange("p (r e) -> p r e", r=8)

            def ph1_pe(t, cols, negm, bnegm):
                xtt = [self.bxt[t]]
                pb, bpb = self.next_pb()
                pv = pb[:, 0:640].rearrange("p (c n) -> p c n", c=5)
                for cc in range(5):
                    self.tr(pv[:, cc, :], xcT[:, cc, cols], self.identb[:], [bxcT, self.bconst], [bpb], inc=(cc == 4))
                p2, bp2 = self.psf[:, 2, :], self.bpf[2]
                for q, srcv in enumerate((bEt, acum, decT)):
                    self.tr(p2[:, 8 * q:8 * q + 8], srcv[:, cols], self.identf[0:8, 0:8], [bdt, self.bconst], [bp2], inc=(q == 2))
                self.mm(p2[:, 128:256], xcT[:, 4, cols], xcT[:, 5, cols], False, True, [bxcT], [bp2])
                spr, bspr = self.pair(0), self.bpair(0)
                sv = spr.rearrange("p (r n) -> p r n", r=8)
                for r in range(8):
                    self.mm(sv[:, r, :], sel8[:, r, :], acum[:, cols], r % 4 == 0, False, [bc8, bdt], [bspr[r // 4]], inc=False)
                for a2 in range(2):
                    self.mm(spr[:, a2 * 512:(a2 + 1) * 512], self.identb[:], negm[:, a2 * 512:(a2 + 1) * 512], False, True,
                            [self.bconst, bnegm], [bspr[a2]], inc=True)
                p3, bp3 = self.psf[:, 3, :], self.bpf[3]
                for k in range(8):
                    self.mm(p3, self.XT[:, k, t * 128:(t + 1) * 128], wz[:, k, :], k == 0, k == 7, [bwz] + xtt, [bp3])
                return (pb, bpb)

            def ph1_early(t, pbt):
                pb, bpb = pbt
                p2, bp2 = self.psf[:, 2, :], self.bpf[2]
                self.evac(tmd[:, 0:24], p2[:, 0:24], [bp2], [btmd], "dve")
                self.act(tmd[:, 24:32], tmd[:, 8:16], AF.Exp, [btmd], [btmd])
                self.evac(xs, pb[:, 0:640], [bpb], [bxs], "dve")

            def ph1_late(t):
                p2, bp2 = self.psf[:, 2, :], self.bpf[2]
                spr, bspr = self.pair(0), self.bpair(0)
                sv = spr.rearrange("p (r n) -> p r n", r=8)
                for r in range(8):
                    self.act(E[:, r, :], sv[:, r, :], AF.Exp, [bspr[r // 4], btmd], [bE_], bias=tmd[:, r:r + 1], scale=1.0)
                p3, bp3 = self.psf[:, 3, :], self.bpf[3]
                self.act(sz, p3, AF.Silu, [bp3], [bsz])
                self.op("dve", "tensor_tensor", [bE_, bp2], [bWT], out=WT, in0=E,
                        in1=p2[:, 128:256].unsqueeze(1).to_broadcast([128, 8, 128]), op=ALU.mult)
                self.op("dve", "tensor_tensor", [bxs, bdbc], [bxD], out=v8(xD), in0=v8(xs[:, 0:512]),
                        in1=dbc.unsqueeze(2).to_broadcast([128, 8, 64]), op=ALU.mult)

            def ph2(t, yi_fn, state_fn, hoist):
                nonlocal pend
                p4, bp4 = self.psf[:, 4, :], self.bpf[4]
                p5, bp5 = self.psf[:, 5, :], self.bpf[5]
                for r in range(8):
                    self.mm(p4[:, r * 64:(r + 1) * 64], WT[:, r, :], xs[:, r * 64:(r + 1) * 64], r == 0, False, [bWT, bxs], [bp4],
                            inc=False)
                self.mm(p4, self.identb[:], xD, False, True, [self.bconst, bxD], [bp4], inc=True)
                if state_fn is not None:
                    state_fn(0)
                yi_fn(p5, bp5)
                if state_fn is not None:
                    state_fn(1)
                self.op("dve", "tensor_tensor", [bp5, btmd], [by1], out=v8(y1), in0=v8(p5),
                        in1=tmd[:, 24:32].unsqueeze(2).to_broadcast([128, 8, 64]), op=ALU.mult)
                self.op("dve", "tensor_tensor", [by1, bp4], [by1], out=y1, in0=y1, in1=p4, op=ALU.add)
                self.op("dve", "tensor_tensor", [by1, bsz], [by1], out=y1, in0=y1, in1=sz, op=ALU.mult)
                st, bst = self.next_stat()
                self.act(p5, y1, AF.Square, [by1], [bp5, bst], accum_out=st[:, 0:1])
                pbt = None
                if hoist is not None:
                    pbt = hoist[0]()
                    hoist[1](pbt)
                self.act(st[:, 1:2], st[:, 0:1], AF.Ln, [bst], [bst], bias=RMS_EPS, scale=1.0 / 512.0)
                self.act(st[:, 2:3], st[:, 1:2], AF.Exp, [bst], [bst], scale=-0.5)
                self.op("dve", "scalar_tensor_tensor", [by1, bst, bnw], [byn], out=yn, in0=y1, scalar=st[:, 2:3], in1=nw,
                        op0=ALU.mult, op1=ALU.mult)
                pb, bpb = self.next_pb()
                pv = pb[:, 0:512].rearrange("p (c n) -> p c n", c=4)
                for c in range(4):
                    self.tr(pv[:, c, :], yn[:, c * 128:(c + 1) * 128], self.identb[:], [byn, self.bconst], [bpb], inc=(c == 3))
                self.evac(ynT, pv, [bpb], [bynT], "act")
                if hoist is not None:
                    hoist[2]()
                opr, bopr = self.pair(2), self.bpair(2)
                for half in range(2):
                    for k in range(4):
                        self.mm(opr[:, half * 512:(half + 1) * 512], ynT[:, k, :], wout[:, k, half * 512:(half + 1) * 512], k == 0,
                                k == 3, [bynT, bwout], [bopr[half]])
                x = self.XR[:, t, :]
                if g == 0:
                    self.op("dve", "scalar_tensor_tensor", [self.bxr[t]] + bopr, [self.bxr[t]], out=x, in0=x, scalar=ALPHA,
                            in1=opr, op0=ALU.mult, op1=ALU.add)
                else:
                    self.op("dve", "tensor_tensor", [self.bxr[t]] + bopr, [self.bxr[t]], out=x, in0=x, in1=opr, op=ALU.add)
                if g == 3:
                    pend = self.ln_all(pend, t)

            def conv_state_proj(t, rows):
                cpr, bcpr = self.pair(0), self.bpair(0)
                tc_ = slice(t * 128, (t + 1) * 128)
                for k in range(8):
                    self.mm(cpr[:, 0:512], self.XT[:, k, tc_], wx[:, k, :], k == 0, k == 7, [bwx, self.bxt[t]], [bcpr[0]])
                for k in range(8):
                    self.mm(cpr[:, 512:768], self.XT[:, k, tc_], wBC[:, k, :], k == 0, k == 7, [bwBC, self.bxt[t]], [bcpr[1]])
                self.evac(cst_[rows, :], cpr[rows, 0:768], bcpr, [bcst], "act")

            osl = ((512 * g, 512, 0), (2048 + 128 * g, 128, 512), (2560 + 128 * g, 128, 640))
            for bi, (t0, t1) in enumerate(BLKS[:4]):
                c0 = t0 * 128
                xtb = self.bxt[t0:t1]
                psd, bpsd = self.psf[0:8, 0, :], self.bpf[0]
                for k in range(8):
                    self.mm(psd, wdt[:, k, :], self.XT[:, k, c0:c0 + 512], k == 0, k == 7, [bwdt] + xtb, [bpsd])
                dt_path(psd, bpsd, 512, 128, scan_p)
                for cc in range(6):
                    bi_ = 2 + cc % 4
                    s_ = cc % 2
                    ps, bps = self.psf[:, bi_, :], self.bpf[bi_]
                    for k in range(8):
                        self.mm(ps, wsl[cc][:, k, :], self.XT[:, k, c0:c0 + 512], k == 0, k == 7, [wsb[cc]] + xtb, [bps])
                    self.op("dve", "tensor_copy", [bhist], [brawc[s_]], out=rawc[:, s_, 0:3], in_=hist[:, cc, :])
                    self.evac(rawc[:, s_, 3:515], ps, [bps], [brawc[s_]], "act")
                    self.op("dve", "tensor_copy", [brawc[s_]], [bhist], out=hist[:, cc, :], in_=rawc[:, s_, 512:515])
                    conv_chunk(cc, [rawc[:, s_, jj:jj + 512] for jj in range(4)], brawc[s_], xcT[:, cc, :], None)
                colsl = [slice(tl * 128, (tl + 1) * 128) for tl in range(4)]
                pbt0 = ph1_pe(t0, colsl[0], negm_p, bnp)
                ph1_early(t0, pbt0)
                ph1_late(t0)
                for t in range(t0, t1):
                    tl = t - t0
                    cols = colsl[tl]

                    def yi_fn(p5, bp5, cols=cols):
                        self.mm(p5, xcT[:, 5, cols], hTb, True, True, [bxcT, bhTb], [bp5])

                    def state_fn(stage, tl=tl):
                        p3, bp3 = self.psf[:, 3, :], self.bpf[3]
                        if stage == 0:
                            self.op("dve", "tensor_tensor", [bxs, btmd], [bxd], out=v8(xd), in0=v8(xs[:, 0:512]),
                                    in1=tmd[:, 16:24].unsqueeze(2).to_broadcast([128, 8, 64]), op=ALU.mult)
                            self.mm(p3, xs[:, 512:640], xd, True, True, [bxs, bxd], [bp3])
                            self.op("dve", "tensor_scalar", [bel, self.bconst], [bdg], out=dg[:, 0:8], in0=self.identf[0:8, 0:8],
                                    scalar1=el[:, tl:tl + 1], scalar2=None, op0=ALU.mult)
                            p2, bp2 = self.psf[:, 2, :], self.bpf[2]
                            self.mm(p2[:, 256:264], self.onesf[0:8, :], dg[:, 0:8], False, True, [self.bconst, bdg], [bp2])
                            self.evac(dbs[:, 0:8], p2[:, 256:264], [bp2], [bdbs], "dve")
                        else:
                            self.op("dve", "tensor_tensor", [bhT, bdbs], [bhT], out=v8(hT), in0=v8(hT),
                                    in1=dbs[:, 0:8].unsqueeze(2).to_broadcast([128, 8, 64]), op=ALU.mult)
                            self.op("dve", "tensor_tensor", [bhT, bp3], [bhT], out=hT, in0=hT, in1=p3, op=ALU.add)
                            self.act(hTb, hT, AF.Copy, [bhT], [bhTb])

                    hoist = None
                    if t + 1 < t1:
                        hoist = (lambda t=t, tl=tl: ph1_pe(t + 1, colsl[tl + 1], negm_p, bnp),
                                 lambda pbt, t=t: ph1_early(t + 1, pbt),
                                 lambda t=t: ph1_late(t + 1))
                    ph2(t, yi_fn, state_fn, hoist)
                if bi == 3:
                    conv_state_proj(15, slice(96, 128))
                    for (o0, w_, s0) in osl:
                        self.dma("sp", d["nconv_p"][:, o0:o0 + w_], cst_[125:128, s0:s0 + w_], [bcst], [])
            p3, bp3 = self.psf[:, 3, :], self.bpf[3]
            pv = p3.rearrange("p (j n) -> p j n", j=4)
            for jj in range(4):
                self.tr(pv[:, jj, :], hT[:, jj * 128:(jj + 1) * 128], self.identf[:], [bhT, self.bconst], [bp3], inc=(jj == 3))
            self.evac(hout, pv, [bp3], [bhout], "act")
            self.dma("sp", d["nssm_p"][512 * g:512 * g + 512, :].rearrange("(j p) n -> p j n", p=128), hout, [bhout], [])

            self.release(mP)
            rawp, brawp = self.carve("rawp", [128, 6, 16, 11], F32)
            decs, bdecs = self.carve("decs", [128, 16, 4], F32)
            h0f, bh0f = self.carve("h0f", [128, 4, 128], F32)
            h0T, bh0T = self.carve("h0T", [128, 512], BF16)
            hnw, bhnw = self.carve("hnw", [128, 4, 128], F32)
            xdm, bxdm = self.carve("xdm", [128, 512], BF16)
            dcb, bdcb = self.carve("dcb", [128, 8], F32)
            dcb1, bdcb1 = self.carve("dcb1", [128, 8], F32)
            bm16, bbm16 = self.carve("bm16", [128, 16, 128], BF16)
            self.dma("sp", bm16, d["c_bm16"], [], [bbm16])
            scv = cwk[0:48, 0:768]
            scs = d["state_conv"]
            for (o0, w_, s0) in osl:
                self.dma("sp", scv[:, s0:s0 + w_], scs[:, :, o0:o0 + w_].rearrange("b r c -> (b r) c"), [], bcwk)
            p2, bp2 = self.psf[:, 2, :], self.bpf[2]
            pvh = p2[:, 0:288].rearrange("p (c n) -> p c n", c=6)
            for cc in range(6):
                self.tr(pvh[:, cc, :], scv[:, cc * 128:(cc + 1) * 128], self.identf[0:48, 0:48], bcwk + [self.bconst], [bp2],
                        inc=(cc == 5))
            self.evac(rawp[:, :, :, 0:3], pvh.rearrange("p c (b r) -> p c b r", r=3), [bp2], [brawp], "dve")
            xtt = [self.bxt[16]]
            for cc in range(6):
                bi_ = 3 + cc % 3
                ps, bps = self.psf[:, bi_, 0:128], self.bpf[bi_]
                for k in range(8):
                    self.mm(ps, wsl[cc][:, k, :], self.XT[:, k, 2048:2176], k == 0, k == 7, [wsb[cc]] + xtt, [bps])
                self.evac(rawp[:, cc, :, 3:11], ps.rearrange("p (b t) -> p b t", t=8), [bps], [brawp], "act")
            psd, bpsd = self.psf[0:8, 0, 0:128], self.bpf[0]
            for k in range(8):
                self.mm(psd, wdt[:, k, :], self.XT[:, k, 2048:2176], k == 0, k == 7, [bwdt] + xtt, [bpsd])
            dt_path(psd, bpsd, 128, 8, scan_s)
            for cc in range(6):
                conv_chunk(cc, [rawp[:, cc, :, jj:jj + 8] for jj in range(4)], brawp,
                           xcT[:, cc, 0:128].rearrange("p (b t) -> p b t", t=8), (16, 8))
            conv_state_proj(16, slice(0, 128))
            for b in range(16):
                for (o0, w_, s0) in osl:
                    self.dma("sp", d["nconv_s"][b, :, o0:o0 + w_], cst_[8 * b + 5:8 * b + 8, s0:s0 + w_], [bcst], [])
            self.op("dve", "tensor_tensor", [bel, bc8], [bdg], out=dg.rearrange("p (b j) -> p b j", b=16), in0=selj,
                    in1=el[:, 0:16].unsqueeze(2).to_broadcast([8, 16, 4]), op=ALU.mult)
            self.mm(p2[:, 320:384], apar, dg, False, True, [bc8, bdg], [bp2])
            self.evac(decs.rearrange("p b j -> p (b j)"), p2[:, 320:384], [bp2], [bdecs], "dve")
            cols = slice(0, 128)
            ssrc = d["state_ssm"]
            v8 = lambda a_: a_.rearrange("p (r e) -> p r e", r=8)

            def alias_buf(name, src):
                return self.mkbuf(name, self.brange[id(src)], src)

            def yi_s(p5, bp5):
                self.op("dve", "tensor_tensor", [bxcT, bbm16], [bCM], out=CM,
                        in0=xcT[:, 5, 0:128].unsqueeze(1).to_broadcast([128, 16, 128]), in1=bm16, op=ALU.mult)
                rflat = rawp.rearrange("p c b t -> p (c b t)")
                bflat = bm16.rearrange("p b n -> p (b n)")
                h0f_ = [h0f, rflat[:, 0:512].rearrange("p (j n) -> p j n", j=4)]
                hnw_ = [hnw, rflat[:, 512:1024].rearrange("p (j n) -> p j n", j=4)]
                h0T_ = [h0T, bflat[:, 0:512]]
                xdm_ = [xdm, bflat[:, 512:1024]]
                dcb_ = [dcb, dcb1]
                bh0f_ = [bh0f, alias_buf("h0f1", brawp)]
                bhnw_ = [bhnw, alias_buf("hnw1", brawp)]
                bh0T_ = [bh0T, alias_buf("h0T1", bbm16)]
                bxdm_ = [bxdm, alias_buf("xdm1", bbm16)]
                bdcb_ = [bdcb, bdcb1]
                for b in range(16):
                    s_ = b % 2
                    self.dma("act", h0f_[s_], ssrc[b, 512 * g:512 * g + 512, :].rearrange("(j p) n -> p j n", p=128), [],
                             [bh0f_[s_]])
                    p3, bp3 = self.psf[:, 3, :], self.bpf[3]
                    pv3 = p3.rearrange("p (j n) -> p j n", j=4)
                    for jj in range(4):
                        self.tr(pv3[:, jj, :], h0f_[s_][:, jj, :], self.identf[:], [bh0f_[s_], self.bconst], [bp3], inc=(jj == 3))
                    self.evac(h0T_[s_], p3, [bp3], [bh0T_[s_]], "act")
                    self.mm(p5, CM[:, b, :], h0T_[s_], b == 0, b == 15, [bCM, bh0T_[s_]], [bp5], inc=True)
                    self.op("dve", "tensor_scalar", [btmd, bseqm], [bdcb_[s_]], out=dcb_[s_], in0=tmd[:, 16:24],
                            scalar1=seqm[:, b:b + 1], scalar2=None, op0=ALU.mult)
                    self.op("dve", "tensor_tensor", [bxs, bdcb_[s_]], [bxdm_[s_]], out=v8(xdm_[s_]), in0=v8(xs[:, 0:512]),
                            in1=dcb_[s_].unsqueeze(2).to_broadcast([128, 8, 64]), op=ALU.mult)
                    p1, bp1 = self.psf[:, 1, :], self.bpf[1]
                    pv1 = p1.rearrange("p (j n) -> p j n", j=4)
                    for jj in range(4):
                        self.mm(pv1[:, jj, :], xdm_[s_][:, jj * 128:(jj + 1) * 128], xs[:, 512:640], jj == 0, jj == 3,
                                [bxdm_[s_], bxs], [bp1], inc=(jj == 3))
                    self.op("dve", "tensor_tensor", [bh0f_[s_], bdecs], [bhnw_[s_]], out=hnw_[s_], in0=h0f_[s_],
                            in1=decs[:, b, :].unsqueeze(2).to_broadcast([128, 4, 128]), op=ALU.mult)
                    self.op("dve", "tensor_tensor", [bhnw_[s_], bp1], [bhnw_[s_]], out=hnw_[s_], in0=hnw_[s_], in1=pv1, op=ALU.add)
                    self.dma("sp", d["nssm_s"][b, 512 * g:512 * g + 512, :].rearrange("(j p) n -> p j n", p=128), hnw_[s_],
                             [bhnw_[s_]], [])

            pbt = ph1_pe(16, cols, negm_s, bns)
            ph1_early(16, pbt)
            ph1_late(16)
            ph2(16, yi_s, None, None)
        if pend is not None:
            self.make_xt(*pend)


def build_nc(stop_after=None):
    nc = bass.Bass("TRN2", target_bir_lowering=False)
    dr = {}

    def din(name, shape, dt=F32):
        dr[name] = nc.dram_tensor(name, list(shape), dt, kind="ExternalInput").ap()

    def dout(name, shape):
        dr[name] = nc.dram_tensor(name, list(shape), F32, kind="ExternalOutput").ap()

    din("x_p", [SEQ, D]); din("x_s", [128, D])
    din("cache_k", [2, 16, 128, 256]); din("cache_v", [2, 16, 128, 256])
    din("state_conv", [16, 3, 3072]); din("state_ssm", [16, 2048, 128]); din("state_pool", [16, 15, D])
    din("rel_bias", [32, 16]); din("attn_w_qkv", [2, D, 1536]); din("attn_b_qkv", [2, 1536])
    din("attn_w_o", [2, D, D]); din("attn_b_o", [2, D]); din("attn_sinks", [2, 16])
    din("ssm_w_in", [1, D, 5152]); din("ssm_conv_w", [1, 4, 3072]); din("ssm_conv_b", [1, 3072])
    din("ssm_dt_bias", [1, 32]); din("ssm_a_log", [1, 32]); din("ssm_d", [1, 32]); din("ssm_norm_w", [1, 2048])
    din("ssm_w_out", [1, 2048, D]); din("pool_w", [1, 4, 256, 256]); din("pool_scale", [1, D])
    din("ffn_w_gate", [4, D, DFF]); din("ffn_w_up", [4, D, DFF]); din("ffn_w_down", [4, DFF, D])
    din("ln_g", [4, 2, D]); din("ln_b", [4, 2, D])
    for k, v in host_constants().items():
        din(k, v.shape, BF16 if v.dtype == ml_dtypes.bfloat16 else F32)
    dout("y_p", [SEQ, D]); dout("y_s", [128, D])
    dout("nk_p", [2, 128, 256]); dout("nv_p", [2, 128, 256]); dout("nconv_p", [3, 3072]); dout("nssm_p", [2048, 128])
    dout("npool_p", [15, D])
    dout("nk_s", [2, 16, 128, 256]); dout("nv_s", [2, 16, 128, 256]); dout("nconv_s", [16, 3, 3072])
    dout("nssm_s", [16, 2048, 128]); dout("npool_s", [16, 15, D])
    dr["gtab"] = nc.dram_tensor("gtab", [16, 384], F32, kind="Internal").ap()

    with ExitStack() as es:
        S = Sched(nc, es)
        kb = KB(nc, S, es, dr, stop_after)
        kb.load_consts()
        kb.build_bias_tables()
        kb.load_x()
        nl = DEPTH if stop_after is None else stop_after
        import os
        skip = os.environ.get("KSKIP", "")
        for l in range(nl):
            kind = l % 3
            if "mix" in skip:
                pass
            elif kind == 0:
                kb.attn(l // 3, l)
            elif kind == 1:
                kb.ssm(l)
            else:
                kb.pool(l)
            if "ffn" not in skip:
                kb.ffn(l, final=(l == nl - 1))
        allb = [b for b in _all_bufs(kb)]
        S.wait_all("sp", allb)
        S.emit()
    return nc


def _all_bufs(kb):
    out = list(kb.bxr) + list(kb.bxt) + [kb.blng, kb.blnb, kb.bconst, kb.bd2d] + kb.bxb16 + kb.bstat + kb.bpf + kb.bpb
    out += [ent[-1] for ent in kb.abufs]
    dummy = Buf("retired", kb.retired)
    out.append(dummy)
    return out


_NC_CACHE = {}


def shard_inputs(inputs):
    consts = host_constants()
    maps = []
    f = lambda a: np.ascontiguousarray(np.asarray(a, dtype=np.float32))
    shared = {k: f(inputs[k]) for k in ("rel_bias", "attn_w_qkv", "attn_b_qkv", "attn_w_o", "attn_b_o", "attn_sinks", "ssm_w_in",
                                        "ssm_conv_w", "ssm_conv_b", "ssm_dt_bias", "ssm_a_log", "ssm_d", "ssm_norm_w", "ssm_w_out",
                                        "pool_w", "pool_scale", "ffn_w_gate", "ffn_w_up", "ffn_w_down", "ln_g", "ln_b")}
    for c in range(NCORES):
        sl = slice(16 * c, 16 * c + 16)
        m = dict(shared)
        m.update(consts)
        m["x_p"] = f(inputs["x_prompt"][c])
        m["x_s"] = f(inputs["x_sample"][sl]).reshape(128, D)
        m["cache_k"] = f(inputs["cache_k"][:, sl]).reshape(2, 16, 128, 256)
        m["cache_v"] = f(inputs["cache_v"][:, sl]).reshape(2, 16, 128, 256)
        m["state_conv"] = f(inputs["state_conv"][0, sl])
        m["state_ssm"] = f(inputs["state_ssm"][0, sl]).reshape(16, 2048, 128)
        m["state_pool"] = f(inputs["state_pool"][0, sl])
        maps.append(m)
    return maps


def gather_outputs(res):
    R = res
    cat = lambda k: np.stack([r[k] for r in R], 0)
    y_p = cat("y_p")
    y_s = cat("y_s").reshape(128, 8, D)
    nk_p = cat("nk_p").transpose(1, 0, 2, 3).reshape(2, 8, 128, 4, 64)
    nv_p = cat("nv_p").transpose(1, 0, 2, 3).reshape(2, 8, 128, 4, 64)
    nconv_p = cat("nconv_p")[None]
    nssm_p = cat("nssm_p").reshape(1, 8, 32, 64, 128)
    npool_p = cat("npool_p")[None]
    nk_s = np.concatenate([r["nk_s"] for r in R], 1).reshape(2, 128, 128, 4, 64)
    nv_s = np.concatenate([r["nv_s"] for r in R], 1).reshape(2, 128, 128, 4, 64)
    nconv_s = np.concatenate([r["nconv_s"] for r in R], 0)[None]
    nssm_s = np.concatenate([r["nssm_s"] for r in R], 0).reshape(1, 128, 32, 64, 128)
    npool_s = np.concatenate([r["npool_s"] for r in R], 0)[None]
    outs = (y_p, y_s, nk_p, nv_p, nconv_p, nssm_p, npool_p, nk_s, nv_s, nconv_s, nssm_s, npool_s)
    return tuple(np.ascontiguousarray(o, dtype=np.float32) for o in outs)


def kernel(**inputs):
    if "nc" not in _NC_CACHE:
        _NC_CACHE["nc"] = build_nc()
    nc = _NC_CACHE["nc"]
    maps = shard_inputs(inputs)
    res = run_bass_kernel_spmd(nc, maps, core_ids=list(range(NCORES)))
    return gather_outputs(res.results)
```
